# Optimizing a Trainium2 kernel written in Bass

```python
import math
import jax, jax.numpy as jnp
from jax import lax
import numpy as np

D_MODEL = 2048
BATCH = 4
SEQ = 8192
DEPTH = 2

N_RET_HEADS = 8
RET_HEAD_DIM = 128
D_RET = N_RET_HEADS * RET_HEAD_DIM
RET_CHUNK = 128
ROPE_BASE = 10000.0
N_MOBA_HEADS = 8
MOBA_HEAD_DIM = 128
D_MOBA = N_MOBA_HEADS * MOBA_HEAD_DIM
MOBA_BLOCK = 256
MOBA_TOPK = 3
MOBA_Q_CHUNK = 16
D_ATTN_IN = 4 * D_RET + 3 * D_MOBA
D_RNN = 2816
N_RNN_BLOCKS = 16
RNN_BLOCK = D_RNN // N_RNN_BLOCKS
CONV_WIDTH = 4
LRU_C = 8.0
D_FF = 5632
LN_EPS = 1e-5
DEEPNORM_ALPHA = (2.0 * DEPTH) ** 0.25
DEEPNORM_BETA = (8.0 * DEPTH) ** -0.25
N_EVEN = (DEPTH + 1) // 2
N_ODD = DEPTH // 2

kernel_name = "hybrid_retention_moba_rglru_macaron_deepnorm"


def layer_norm(x, g, b):
    xf = x.astype(jnp.float32)
    mu = jnp.mean(xf, axis=-1, keepdims=True)
    var = jnp.mean(jnp.square(xf - mu), axis=-1, keepdims=True)
    return ((xf - mu) * lax.rsqrt(var + LN_EPS) * g + b).astype(x.dtype)


def swiglu(x, w_gate, w_up, w_down):
    return (jax.nn.silu(x @ w_gate) * (x @ w_up)) @ w_down


def to_heads(t, n_heads):
    b, s, _ = t.shape
    return t.reshape(b, s, n_heads, -1).transpose(0, 2, 1, 3)


def rotary(t, pos):
    d = t.shape[-1]
    half = d // 2
    inv_freq = ROPE_BASE ** (-jnp.arange(half, dtype=jnp.float32) / half)
    ang = pos[:, None] * inv_freq[None, :]
    cos, sin = jnp.cos(ang), jnp.sin(ang)
    t1, t2 = t[..., :half], t[..., half:]
    return jnp.concatenate([t1 * cos - t2 * sin, t2 * cos + t1 * sin], axis=-1)


def retention(q, k, v):
    B, H, S, dk = q.shape
    dv = v.shape[-1]
    C = RET_CHUNK
    N = S // C
    log_g = jnp.log1p(-jnp.exp2(-5.0 - jnp.arange(H, dtype=jnp.float32)))
    pos = jnp.arange(S, dtype=jnp.float32)
    q = rotary(q, pos)
    k = rotary(k, pos) * (dk ** -0.5)
    q = q.reshape(B, H, N, C, dk)
    k = k.reshape(B, H, N, C, dk)
    v = v.reshape(B, H, N, C, dv)
    c = jnp.arange(C, dtype=jnp.float32)
    rel = c[:, None] - c[None, :]
    intra = jnp.where(rel >= 0, jnp.exp(jnp.maximum(rel, 0.0)[None] * log_g[:, None, None]), 0.0)
    scores = jnp.einsum('bhncd,bhnsd->bhncs', q, k) * intra[None, :, None]
    out_inner = jnp.einsum('bhncs,bhnse->bhnce', scores, v)
    k_dec = jnp.exp((C - 1 - c)[None, :] * log_g[:, None])
    kv = jnp.einsum('bhnsd,bhnse,hs->nbhde', k, v, k_dec)
    chunk_dec = jnp.exp(C * log_g)[None, :, None, None]

    def step(state, kv_n):
        return state * chunk_dec + kv_n, state

    _, prev = lax.scan(step, jnp.zeros((B, H, dk, dv), jnp.float32), kv)
    q_dec = jnp.exp((c + 1)[None, :] * log_g[:, None])
    out_cross = jnp.einsum('bhncd,nbhde,hc->bhnce', q, prev, q_dec)
    return (out_inner + out_cross).reshape(B, H, S, dv)


def moba_attention(q, k, v):
    B, H, S, dh = q.shape
    BLK = MOBA_BLOCK
    S_pad = ((S + BLK - 1) // BLK) * BLK
    pad = ((0, 0), (0, 0), (0, S_pad - S), (0, 0))
    q, k, v = jnp.pad(q, pad), jnp.pad(k, pad), jnp.pad(v, pad)
    NB = S_pad // BLK
    k_sel_n = min(MOBA_TOPK, NB)
    scale = dh ** -0.5
    kb = k.reshape(B, H, NB, BLK, dh)
    vb = v.reshape(B, H, NB, BLK, dh)
    kmean = jnp.mean(kb.astype(jnp.float32), axis=3)
    gate = jnp.einsum('bhsd,bhnd->bhsn', q.astype(jnp.float32), kmean)
    q_block = jnp.arange(S_pad) // BLK
    past = jnp.arange(NB)[None, :] < q_block[:, None]
    gate = jnp.where(past, gate, -jnp.inf)
    _, idx = lax.top_k(gate, k_sel_n)
    valid = idx < q_block[:, None]
    bi = jnp.arange(B)[:, None, None, None]
    hi = jnp.arange(H)[None, :, None, None]
    QC = MOBA_Q_CHUNK
    n_chunks = S_pad // QC

    def chunk(ci):
        start = ci * QC
        qb = start // BLK
        qc = lax.dynamic_slice_in_dim(q, start, QC, axis=2)
        idc = lax.dynamic_slice_in_dim(idx, start, QC, axis=2)
        vac = lax.dynamic_slice_in_dim(valid, start, QC, axis=2)
        k_own = lax.dynamic_index_in_dim(kb, qb, axis=2, keepdims=False)
        v_own = lax.dynamic_index_in_dim(vb, qb, axis=2, keepdims=False)
        s_own = jnp.einsum('bhqd,bhkd->bhqk', qc, k_own).astype(jnp.float32) * scale
        qpos = start + jnp.arange(QC)
        kpos = qb * BLK + jnp.arange(BLK)
        s_own = jnp.where(kpos[None, :] <= qpos[:, None], s_own, -jnp.inf)
        k_g = kb[bi, hi, idc]
        v_g = vb[bi, hi, idc]
        s_sel = jnp.einsum('bhqd,bhqjkd->bhqjk', qc, k_g).astype(jnp.float32) * scale
        s_sel = jnp.where(vac[..., None], s_sel, -jnp.inf)
        logits = jnp.concatenate([s_own, s_sel.reshape(B, H, QC, k_sel_n * BLK)], axis=-1)
        p = jax.nn.softmax(logits, axis=-1).astype(v.dtype)
        p_own = p[..., :BLK]
        p_sel = p[..., BLK:].reshape(B, H, QC, k_sel_n, BLK)
        return (jnp.einsum('bhqk,bhkd->bhqd', p_own, v_own)
                + jnp.einsum('bhqjk,bhqjkd->bhqd', p_sel, v_g))

    out = lax.map(chunk, jnp.arange(n_chunks))
    out = jnp.moveaxis(out, 0, 2).reshape(B, H, S_pad, dh)
    return out[:, :, :S]


def attention_group_mixer(x, w_in, gn_g, gn_b, w_out):
    B, S, _ = x.shape
    proj = x @ w_in
    rq, rk, rv, rg, mq, mk, mv = jnp.split(
        proj, [D_RET, 2 * D_RET, 3 * D_RET, 4 * D_RET, 4 * D_RET + D_MOBA, 4 * D_RET + 2 * D_MOBA], axis=-1)
    ro = retention(to_heads(rq, N_RET_HEADS).astype(jnp.float32),
                   to_heads(rk, N_RET_HEADS).astype(jnp.float32),
                   to_heads(rv, N_RET_HEADS).astype(jnp.float32))
    mu = jnp.mean(ro, axis=-1, keepdims=True)
    var = jnp.mean(jnp.square(ro - mu), axis=-1, keepdims=True)
    ro = (ro - mu) * lax.rsqrt(var + LN_EPS)
    ro = ro.transpose(0, 2, 1, 3).reshape(B, S, D_RET) * gn_g + gn_b
    ro = (jax.nn.silu(rg.astype(jnp.float32)) * ro).astype(x.dtype)
    mo = moba_attention(to_heads(mq, N_MOBA_HEADS), to_heads(mk, N_MOBA_HEADS), to_heads(mv, N_MOBA_HEADS))
    mo = mo.transpose(0, 2, 1, 3).reshape(B, S, D_MOBA).astype(x.dtype)
    return jnp.concatenate([ro, mo], axis=-1) @ w_out


def rglru_mixer(x, w_in, conv_w, conv_b, ga_w, ga_b, gx_w, gx_b, lam, w_out):
    B, S, _ = x.shape
    proj = x @ w_in
    gate_br, rnn_br = jnp.split(proj, 2, axis=-1)
    gate = jax.nn.gelu(gate_br)
    u = lax.conv_general_dilated(rnn_br, conv_w[:, None, :].astype(rnn_br.dtype), window_strides=(1,),
                                 padding=[(CONV_WIDTH - 1, 0)],
                                 dimension_numbers=('NWC', 'WIO', 'NWC'),
                                 feature_group_count=D_RNN) + conv_b
    ub = u.reshape(B, S, N_RNN_BLOCKS, RNN_BLOCK)
    r = jax.nn.sigmoid((jnp.einsum('bsgi,gij->bsgj', ub, ga_w).reshape(B, S, D_RNN) + ga_b).astype(jnp.float32))
    i = jax.nn.sigmoid((jnp.einsum('bsgi,gij->bsgj', ub, gx_w).reshape(B, S, D_RNN) + gx_b).astype(jnp.float32))
    log_a = -LRU_C * r * jax.nn.softplus(-lam.astype(jnp.float32))
    a = jnp.exp(log_a)
    b = jnp.sqrt(-jnp.expm1(2.0 * log_a)) * (i * u.astype(jnp.float32))

    def combine(left, right):
        a1, b1 = left
        a2, b2 = right
        return a1 * a2, a2 * b1 + b2

    _, h = lax.associative_scan(combine, (a, b), axis=1)
    return (h.astype(x.dtype) * gate) @ w_out


def setup_inputs(seed: int = 0) -> dict:
    key = jax.random.key(seed)
    ks = jax.random.split(key, 20)
    f32 = jnp.float32
    nrm = lambda k, shape, scale: jax.random.normal(k, shape, f32) * scale
    x = jax.random.normal(ks[0], (BATCH, SEQ, D_MODEL), f32)
    ln_g = 1.0 + nrm(ks[1], (DEPTH, 3, D_MODEL), 0.01)
    ln_b = nrm(ks[2], (DEPTH, 3, D_MODEL), 0.01)
    ffn_w_gate = nrm(ks[3], (DEPTH, 2, D_MODEL, D_FF), D_MODEL ** -0.5)
    ffn_w_up = nrm(ks[4], (DEPTH, 2, D_MODEL, D_FF), D_MODEL ** -0.5)
    ffn_w_down = nrm(ks[5], (DEPTH, 2, D_FF, D_MODEL), DEEPNORM_BETA * D_FF ** -0.5)
    attn_w_in = nrm(ks[6], (N_EVEN, D_MODEL, D_ATTN_IN), D_MODEL ** -0.5)
    ret_gn_g = 1.0 + nrm(ks[7], (N_EVEN, D_RET), 0.01)
    ret_gn_b = nrm(ks[8], (N_EVEN, D_RET), 0.01)
    attn_w_out = nrm(ks[9], (N_EVEN, D_RET + D_MOBA, D_MODEL), DEEPNORM_BETA * (D_RET + D_MOBA) ** -0.5)
    rnn_w_in = nrm(ks[10], (N_ODD, D_MODEL, 2 * D_RNN), D_MODEL ** -0.5)
    rnn_conv_w = nrm(ks[11], (N_ODD, CONV_WIDTH, D_RNN), CONV_WIDTH ** -0.5)
    rnn_conv_b = nrm(ks[12], (N_ODD, D_RNN), 0.01)
    rnn_gate_a_w = nrm(ks[13], (N_ODD, N_RNN_BLOCKS, RNN_BLOCK, RNN_BLOCK), RNN_BLOCK ** -0.5)
    rnn_gate_a_b = nrm(ks[14], (N_ODD, D_RNN), 0.01)
    rnn_gate_x_w = nrm(ks[15], (N_ODD, N_RNN_BLOCKS, RNN_BLOCK, RNN_BLOCK), RNN_BLOCK ** -0.5)
    rnn_gate_x_b = nrm(ks[16], (N_ODD, D_RNN), 0.01)
    a_c = jax.random.uniform(ks[17], (N_ODD, D_RNN), f32, 0.9, 0.999)
    a0 = a_c ** (1.0 / LRU_C)
    rnn_lambda = jnp.log(a0) - jnp.log1p(-a0)
    rnn_w_out = nrm(ks[18], (N_ODD, D_RNN, D_MODEL), DEEPNORM_BETA * D_RNN ** -0.5)
    return {"x": x, "ln_g": ln_g, "ln_b": ln_b, "ffn_w_gate": ffn_w_gate, "ffn_w_up": ffn_w_up,
            "ffn_w_down": ffn_w_down, "attn_w_in": attn_w_in, "ret_gn_g": ret_gn_g, "ret_gn_b": ret_gn_b,
            "attn_w_out": attn_w_out, "rnn_w_in": rnn_w_in, "rnn_conv_w": rnn_conv_w, "rnn_conv_b": rnn_conv_b,
            "rnn_gate_a_w": rnn_gate_a_w, "rnn_gate_a_b": rnn_gate_a_b, "rnn_gate_x_w": rnn_gate_x_w,
            "rnn_gate_x_b": rnn_gate_x_b, "rnn_lambda": rnn_lambda, "rnn_w_out": rnn_w_out}


def reference(x, ln_g, ln_b, ffn_w_gate, ffn_w_up, ffn_w_down, attn_w_in, ret_gn_g, ret_gn_b, attn_w_out,
              rnn_w_in, rnn_conv_w, rnn_conv_b, rnn_gate_a_w, rnn_gate_a_b, rnn_gate_x_w, rnn_gate_x_b,
              rnn_lambda, rnn_w_out):
    h = x
    for layer in range(DEPTH):
        h = layer_norm(DEEPNORM_ALPHA * h + 0.5 * swiglu(h, ffn_w_gate[layer, 0], ffn_w_up[layer, 0],
                                                          ffn_w_down[layer, 0]), ln_g[layer, 0], ln_b[layer, 0])
        j = layer // 2
        if layer % 2 == 0:
            mix = attention_group_mixer(h, attn_w_in[j], ret_gn_g[j], ret_gn_b[j], attn_w_out[j])
        else:
            mix = rglru_mixer(h, rnn_w_in[j], rnn_conv_w[j], rnn_conv_b[j], rnn_gate_a_w[j], rnn_gate_a_b[j],
                              rnn_gate_x_w[j], rnn_gate_x_b[j], rnn_lambda[j], rnn_w_out[j])
        h = layer_norm(DEEPNORM_ALPHA * h + mix, ln_g[layer, 1], ln_b[layer, 1])
        h = layer_norm(DEEPNORM_ALPHA * h + 0.5 * swiglu(h, ffn_w_gate[layer, 1], ffn_w_up[layer, 1],
                                                          ffn_w_down[layer, 1]), ln_g[layer, 2], ln_b[layer, 2])
    return h
```

```python
import math
from contextlib import ExitStack

import numpy as np
import concourse.bass as bass
import concourse.mybir as mybir
from concourse.bass_utils import run_bass_kernel_spmd

F32 = mybir.dt.float32
BF16 = mybir.dt.bfloat16
AF = mybir.ActivationFunctionType
ALU = mybir.AluOpType

D = 2048
DFF = 5632
KC = D // 128
FC = DFF // 128
DEPTH = 2
ALPHA = (2.0 * DEPTH) ** 0.25
LN_EPS = 1e-5
NCORES = 8
TOK = 4096
TILE = 512


class EngW:
    def __init__(self, nc, es, name, eng):
        self.nc, self.name, self.eng = nc, name, eng
        self.sem = es.enter_context(nc.semaphore("pg_" + name))
        self.count = 0
        self.waited = {}

    def wait(self, *tickets):
        for t in tickets:
            if t is None:
                continue
            if isinstance(t, list):
                self.wait(*t)
                continue
            sem, val, key = t
            if self.waited.get(key, 0) >= val:
                continue
            self.eng.wait_ge(sem, val)
            self.waited[key] = val

    def done(self, inst):
        self.count += 1
        inst.then_inc(self.sem, 1)
        return (self.sem, self.count, self.name)

    def last(self):
        return (self.sem, self.count, self.name) if self.count else None


class DmaSem:
    _n = 0

    def __init__(self, nc, es, name):
        DmaSem._n += 1
        self.key = "dma_%d" % DmaSem._n
        self.sem = es.enter_context(nc.semaphore(self.key))
        self.count = 0

    def done(self, inst):
        self.count += 16
        inst.then_inc(self.sem, 16)
        return (self.sem, self.count, self.key)

    def last(self):
        return (self.sem, self.count, self.key) if self.count else None


class Ring:
    def __init__(self, bufs):
        self.bufs = bufs
        self.free = [None] * len(bufs)
        self.i = 0

    def next(self):
        j = self.i % len(self.bufs)
        self.i += 1
        return j


class K:
    def __init__(self, nc, es):
        self.nc, self.es = nc, es
        self.pe = EngW(nc, es, "pe", nc.tensor)
        self.act = EngW(nc, es, "act", nc.scalar)
        self.dve = EngW(nc, es, "dve", nc.vector)
        self.pool = EngW(nc, es, "pool", nc.gpsimd)
        self.sp = EngW(nc, es, "sp", nc.sync)
        self.sem_pool = []
        self.nsem = 0

    _uid = 0

    def sb(self, es, name, shape, dt):
        K._uid += 1
        return es.enter_context(self.nc.sbuf_tensor("%s_%d" % (name, K._uid), shape, dt))

    def ps(self, es, name, shape, dt):
        K._uid += 1
        return es.enter_context(self.nc.psum_tensor("%s_%d" % (name, K._uid), shape, dt))

    def dsem(self, es, name):
        if self.sem_pool and self.nsem >= 80:
            self.sem_pool.sort(key=lambda d: d.count)
            ds = self.sem_pool.pop(0)
        else:
            ds = DmaSem(self.nc, self.es, name)
            self.nsem += 1
        es.callback(self.sem_pool.append, ds)
        return ds


def dram(nc, name, shape, dt, kind="Internal"):
    return nc.dram_tensor(name, list(shape), dt, kind=kind).ap()


def prep_ws(k, es, w_src, name, kdim, ndim):
    nc = k.nc
    nkc, nnc = kdim // 128, ndim // 128
    out = dram(nc, name, [nnc, 128, nkc, 128], BF16)
    ds = k.dsem(es, name)
    t = None
    for c in range(nnc):
        src = w_src[:, c * 128:(c + 1) * 128].rearrange("(kc p) f -> p kc f", p=128)
        t = ds.done(nc.gpsimd.dma_start(out=out[c], in_=src))
    return out, t


def sublayer_phase(k, mode, X_in, X_out, wd, nk, wtick, g_row, b_row, ntok, cx, eps, wg=None, wu=None, Y=None):
    nc = k.nc
    pe, act, dve, pool, sp = k.pe, k.act, k.dve, k.pool, k.sp
    ntiles = ntok // TILE
    ffn = mode == "ffn"
    with ExitStack() as es:
        xs = k.sb(es, "f_xs", [128, 4, D], F32)
        xbw = D if mode != "proj_tm" else nk * 128
        xb = [k.sb(es, "f_xb%d" % i, [128, xbw], BF16) for i in range(4)]
        hid = k.sb(es, "f_hid", [128, nk, TILE], BF16)
        wds = [k.sb(es, "f_wd%d" % i, [128, nk, 128], BF16) for i in range(2)]
        ytmp = [k.sb(es, "f_yt%d" % i, [128, TILE], F32) for i in range(2)]
        grep = k.sb(es, "f_g", [128, D], F32)
        brep = k.sb(es, "f_b", [128, D], F32)
        identb = k.sb(es, "f_idb", [128, 128], BF16)
        identf = k.sb(es, "f_idf", [128, 128], F32)
        st = k.sb(es, "f_st", [128, 4, 6], F32)
        mv = k.sb(es, "f_mv", [128, 8], F32)
        pT = [k.ps(es, "f_pT%d" % i, [128, 512], BF16) for i in range(2)]
        acc = [k.ps(es, "f_acc%d" % i, [128, 512], F32) for i in range(4)]
        pZ = [k.ps(es, "f_pZ%d" % i, [128, 512], F32) for i in range(2)]
        if ffn:
            stage = [k.sb(es, "f_stage%d" % i, [128, D], F32) for i in range(2)]
            hT = k.sb(es, "f_hT", [128, KC, TILE], BF16)
            wgs = [k.sb(es, "f_wg%d" % i, [128, KC, 128], BF16) for i in range(3)]
            wus = [k.sb(es, "f_wu%d" % i, [128, KC, 128], BF16) for i in range(3)]
            sgb = [k.sb(es, "f_sg%d" % i, [128, TILE], F32) for i in range(2)]
            s_stage = [k.dsem(es, "stage%d" % i) for i in range(2)]
            s_wgu = [k.dsem(es, "wgu%d" % i) for i in range(3)]
        s_xb = [k.dsem(es, "xb%d" % i) for i in range(4)]
        s_wd = [k.dsem(es, "wd%d" % i) for i in range(2)]
        s_xs = k.dsem(es, "xs")
        s_gb = k.dsem(es, "gb")
        s_out = k.dsem(es, "out")
        s_hid = k.dsem(es, "hid")

        t_gb = s_gb.done(nc.sync.dma_start(out=grep[:], in_=g_row.partition_broadcast(128)))
        t_gb = s_gb.done(nc.sync.dma_start(out=brep[:], in_=b_row.partition_broadcast(128)))
        for idt in (identb, identf):
            t0 = pool.done(nc.gpsimd.memset(idt[:], 1.0))
            pool.wait(t0)
            t_id = pool.done(nc.gpsimd.affine_select(out=idt[:], in_=idt[:], pattern=[[-1, 128]],
                                                     compare_op=ALU.is_equal, fill=0.0, base=0,
                                                     channel_multiplier=1))

        stage_free = [None, None]
        xb_free = [None] * 4
        pT_free = [None, None]
        acc_free = [None] * 4
        pZ_free = [None, None]
        wgu_free = [None] * 3
        wd_free = [None] * 2
        sg_free = [None, None]
        yt_free = [None, None]
        cnt = dict(stage=0, xb=0, pT=0, acc=0, pZ=0, wgu=0, wd=0, sg=0, yt=0)
        state = dict(hT_free=None, hid_free=None, xs_free=None, st_free=None, xs_ready=None)

        def load_xs(t):
            sp.wait(state["xs_free"])
            for s in range(4):
                r0 = t * TILE + s * 128
                state["xs_ready"] = s_xs.done(nc.sync.dma_start(out=xs[:, s, :], in_=X_in[r0:r0 + 128, :]))

        def emit_T(t, src, is_f32, nkc, dest, dest_free_key):
            for s in range(4):
                r0 = t * TILE + s * 128
                xi = cnt["xb"] % 4
                cnt["xb"] += 1
                if is_f32:
                    si = cnt["stage"] % 2
                    cnt["stage"] += 1
                    sp.wait(stage_free[si])
                    t_ld = s_stage[si].done(nc.sync.dma_start(out=stage[si][:], in_=src[r0:r0 + 128, :]))
                    pool.wait(t_ld, xb_free[xi])
                    t_c = pool.done(nc.gpsimd.tensor_copy(out=xb[xi][:], in_=stage[si][:]))
                    stage_free[si] = t_c
                else:
                    sp.wait(xb_free[xi])
                    t_c = s_xb[xi].done(nc.sync.dma_start(out=xb[xi][:], in_=src[r0:r0 + 128, :]))
                ngrp = (nkc + 3) // 4
                for q in range(ngrp):
                    n4 = min(4, nkc - 4 * q)
                    pi = cnt["pT"] % 2
                    cnt["pT"] += 1
                    pe.wait(t_c, pT_free[pi], t_id)
                    for j in range(n4):
                        kc = 4 * q + j
                        ins = nc.tensor.transpose(pT[pi][:, j * 128:(j + 1) * 128],
                                                  xb[xi][:, kc * 128:(kc + 1) * 128], identb[:])
                    t_tr = pe.done(ins)
                    ev = dve if (q % 2 == 0) else act
                    ev.wait(t_tr, state[dest_free_key])
                    dst = dest[:, 4 * q:4 * q + n4, s * 128:(s + 1) * 128]
                    srcp = pT[pi][:, 0:n4 * 128].rearrange("p (a b) -> p a b", b=128)
                    if ev is dve:
                        t_ev = dve.done(nc.vector.tensor_copy(out=dst, in_=srcp))
                    else:
                        t_ev = act.done(nc.scalar.copy(out=dst, in_=srcp))
                    pT_free[pi] = t_ev
                xb_free[xi] = t_tr
            return [dve.last(), act.last()]

        def emit_GU(t, hT_ready):
            for fc in range(FC):
                wi = cnt["wgu"] % 3
                cnt["wgu"] += 1
                sp.wait(wgu_free[wi], wtick)
                s_wgu[wi].done(nc.sync.dma_start(out=wgs[wi][:], in_=wg[fc]))
                t_w = s_wgu[wi].done(nc.sync.dma_start(out=wus[wi][:], in_=wu[fc]))
                if fc == 22:
                    load_xs(t)
                ag = cnt["acc"] % 4
                au = (cnt["acc"] + 1) % 4
                cnt["acc"] += 2
                pe.wait(t_w, hT_ready, acc_free[ag])
                for kc in range(KC):
                    ins = nc.tensor.matmul(acc[ag][:], lhsT=wgs[wi][:, kc, :], rhs=hT[:, kc, :],
                                           start=(kc == 0), stop=(kc == KC - 1))
                t_g = pe.done(ins)
                pe.wait(acc_free[au])
                for kc in range(KC):
                    ins = nc.tensor.matmul(acc[au][:], lhsT=wus[wi][:, kc, :], rhs=hT[:, kc, :],
                                           start=(kc == 0), stop=(kc == KC - 1))
                t_u = pe.done(ins)
                wgu_free[wi] = t_u
                gi = cnt["sg"] % 2
                cnt["sg"] += 1
                act.wait(t_g, sg_free[gi])
                t_s = act.done(nc.scalar.activation(out=sgb[gi][:], in_=acc[ag][:], func=AF.Silu))
                acc_free[ag] = t_s
                dve.wait(t_s, t_u, state["hid_free"])
                t_h = dve.done(nc.vector.tensor_tensor(out=hid[:, fc, :], in0=sgb[gi][:], in1=acc[au][:],
                                                       op=ALU.mult))
                sg_free[gi] = t_h
                acc_free[au] = t_h
            state["hT_free"] = t_u
            return t_h

        def emit_D(t, hid_ready):
            pend = None
            t_res = None

            def emit_tr(p):
                oc, yi, t_cp = p
                zi = cnt["pZ"] % 2
                cnt["pZ"] += 1
                pe.wait(t_cp, pZ_free[zi], t_id)
                for s in range(4):
                    ins = nc.tensor.transpose(pZ[zi][:, s * 128:(s + 1) * 128], ytmp[yi][:, s * 128:(s + 1) * 128],
                                              identf[:])
                t_tr = pe.done(ins)
                yt_free[yi] = t_tr
                dve.wait(t_tr, state["xs_ready"])
                t_r = dve.done(nc.vector.scalar_tensor_tensor(
                    out=xs[:, :, oc * 128:(oc + 1) * 128], in0=xs[:, :, oc * 128:(oc + 1) * 128],
                    scalar=float(cx), in1=pZ[zi][:].rearrange("p (a b) -> p a b", b=128),
                    op0=ALU.mult, op1=ALU.add))
                pZ_free[zi] = t_r
                return t_r

            for oc in range(KC):
                wi = cnt["wd"] % 2
                cnt["wd"] += 1
                sp.wait(wd_free[wi], wtick)
                t_w = s_wd[wi].done(nc.sync.dma_start(out=wds[wi][:], in_=wd[oc]))
                ai = cnt["acc"] % 4
                cnt["acc"] += 1
                pe.wait(t_w, hid_ready, acc_free[ai])
                for fc in range(nk):
                    ins = nc.tensor.matmul(acc[ai][:], lhsT=wds[wi][:, fc, :], rhs=hid[:, fc, :],
                                           start=(fc == 0), stop=(fc == nk - 1))
                t_m = pe.done(ins)
                wd_free[wi] = t_m
                yi = cnt["yt"] % 2
                cnt["yt"] += 1
                act.wait(t_m, yt_free[yi])
                t_cp = act.done(nc.scalar.copy(out=ytmp[yi][:], in_=acc[ai][:]))
                acc_free[ai] = t_cp
                if pend is not None:
                    t_res = emit_tr(pend)
                pend = (oc, yi, t_cp)
            state["hid_free"] = t_m
            t_res = emit_tr(pend)
            return t_res

        def emit_LN(t, t_res):
            t_st = None
            for s in range(4):
                dve.wait(t_res, state["st_free"])
                for c in range(4):
                    t_b = dve.done(nc.vector.bn_stats(out=st[:, c, :], in_=xs[:, s, c * 512:(c + 1) * 512]))
                dve.wait(t_b)
                t_a = dve.done(nc.vector.bn_aggr(out=mv[:, 0:2], in_=st[:].rearrange("p a b -> p (a b)")))
                dve.wait(t_a)
                t_e = dve.done(nc.vector.tensor_scalar(out=mv[:, 2:3], in0=mv[:, 1:2], scalar1=float(eps),
                                                       scalar2=None, op0=ALU.add))
                act.wait(t_e)
                t_q = act.done(nc.scalar.activation(out=mv[:, 3:4], in_=mv[:, 2:3], func=AF.Sqrt))
                dve.wait(t_q)
                t_r = dve.done(nc.vector.reciprocal(out=mv[:, 4:5], in_=mv[:, 3:4]))
                dve.wait(t_r)
                t_n = dve.done(nc.vector.scalar_tensor_tensor(out=mv[:, 5:6], in0=mv[:, 0:1], scalar=-1.0,
                                                              in1=mv[:, 4:5], op0=ALU.mult, op1=ALU.mult))
                act.wait(t_n)
                t_x = act.done(nc.scalar.activation(out=xs[:, s, :], in_=xs[:, s, :], func=AF.Identity,
                                                    bias=mv[:, 5:6], scale=mv[:, 4:5]))
                state["st_free"] = t_x
                dve.wait(t_x, t_gb)
                t_p = dve.done(nc.vector.tensor_tensor(out=xs[:, s, :], in0=xs[:, s, :], in1=grep[:], op=ALU.mult))
                dve.wait(t_p)
                t_p = dve.done(nc.vector.tensor_tensor(out=xs[:, s, :], in0=xs[:, s, :], in1=brep[:], op=ALU.add))
                pool.wait(t_p)
                r0 = t * TILE + s * 128
                t_st = s_out.done(nc.gpsimd.dma_start(out=X_out[r0:r0 + 128, :], in_=xs[:, s, :]))
            state["xs_free"] = t_st
            return t_st

        t_out = None
        if ffn:
            hT_ready = emit_T(0, X_in, True, KC, hT, "hT_free")
            for t in range(ntiles):
                hid_ready = emit_GU(t, hT_ready)
                if t + 1 < ntiles:
                    hT_ready = emit_T(t + 1, X_in, True, KC, hT, "hT_free")
                t_res = emit_D(t, hid_ready)
                t_out = emit_LN(t, t_res)
        else:
            for t in range(ntiles):
                if mode == "proj_tm":
                    hid_ready = emit_T(t, Y, False, nk, hid, "hid_free")
                else:
                    sp.wait(state["hid_free"])
                    for c in range(nk):
                        hid_ready = s_hid.done(nc.sync.dma_start(out=hid[:, c, :],
                                                                 in_=Y[c, :, t * TILE:(t + 1) * TILE]))
                load_xs(t)
                t_res = emit_D(t, hid_ready)
                t_out = emit_LN(t, t_res)
        drain(k, [t_out])
        return t_out


def drain(k, extra=()):
    engs = (k.pe, k.act, k.dve, k.pool, k.sp)
    fin = [e.last() for e in engs] + list(extra)
    for e in engs:
        e.wait(fin)


NH = 8
KINDS = ["q", "k", "v", "g", "mq", "mk", "mv"]


def prep_tm(k, es, w_src, name, kdim, ndim, gw=512):
    nc = k.nc
    nkc, ncg = kdim // 128, ndim // gw
    out = dram(nc, name, [ncg, 128, nkc, gw], BF16)
    ds = k.dsem(es, name)
    t = None
    for c in range(ncg):
        src = w_src[:, c * gw:(c + 1) * gw].rearrange("(kc p) f -> p kc f", p=128)
        t = ds.done(nc.gpsimd.dma_start(out=out[c], in_=src))
    return out, t


def attn_inproj_phase(k, X_in, w_in, wtick, cos_d, sin_d, ksc_d, outs, ntok):
    nc = k.nc
    pe, act, dve, pool, sp = k.pe, k.act, k.dve, k.pool, k.sp
    QT, KT, KTM, VTM, GTM, MQT, MKT, MVTM, KMEAN = outs
    ntiles = ntok // TILE
    with ExitStack() as es:
        stage = [k.sb(es, "a_stage%d" % i, [128, D], F32) for i in range(2)]
        xb = [k.sb(es, "a_xb%d" % i, [128, D], BF16) for i in range(4)]
        hT = k.sb(es, "a_hT", [128, KC, TILE], BF16)
        wt = [k.sb(es, "a_wt%d" % i, [128, KC, 512], BF16) for i in range(2)]
        xsb = [k.sb(es, "a_xsb%d" % i, [128, 512], F32) for i in range(2)]
        ra = [k.sb(es, "a_ra%d" % i, [128, 512], F32) for i in range(2)]
        rb = [k.sb(es, "a_rb%d" % i, [128, 512], F32) for i in range(2)]
        tmb = [k.sb(es, "a_tmb%d" % i, [128, 512], BF16) for i in range(4)]
        fmt = [k.sb(es, "a_fmt%d" % i, [128, 4, TILE], BF16) for i in range(2)]
        cosr = [k.sb(es, "a_cos%d" % i, [128, 4, 64], F32) for i in range(2)]
        sinr = [k.sb(es, "a_sin%d" % i, [128, 4, 64], F32) for i in range(2)]
        ksc = k.sb(es, "a_ksc", [128, NH], F32)
        onesb = k.sb(es, "a_ones", [128, 2], BF16)
        kmrow = [k.sb(es, "a_kmrow%d" % i, [1, 512], F32) for i in range(2)]
        identb = k.sb(es, "a_idb", [128, 128], BF16)
        pT_t = k.ps(es, "a_pT", [128, 512], BF16)
        pT = [pT_t[:, :], pT_t[:, :]]
        acc = [k.ps(es, "a_acc%d" % i, [128, 512], F32) for i in range(4)]
        ptr = [k.ps(es, "a_ptr%d" % i, [128, 512], BF16)[:, :] for i in range(2)]
        pkm_t = k.ps(es, "a_pkm", [1, 512], F32)
        pkm = [pkm_t, pkm_t]

        s_stage = [k.dsem(es, "astage%d" % i) for i in range(2)]
        s_wt = [k.dsem(es, "awt%d" % i) for i in range(2)]
        s_cs = [k.dsem(es, "acs%d" % i) for i in range(2)]
        s_c = k.dsem(es, "aconst")
        s_st = [k.dsem(es, "ast%d" % i) for i in range(4)]
        s_fm = [k.dsem(es, "afm%d" % i) for i in range(2)]
        s_km = [k.dsem(es, "akm%d" % i) for i in range(2)]

        t_c = s_c.done(nc.sync.dma_start(out=ksc[:], in_=ksc_d[:, :]))
        t0 = pool.done(nc.gpsimd.memset(onesb[:], 1.0))
        t0 = pool.done(nc.gpsimd.memset(identb[:], 1.0))
        pool.wait(t0)
        t_id = pool.done(nc.gpsimd.affine_select(out=identb[:], in_=identb[:], pattern=[[-1, 128]],
                                                 compare_op=ALU.is_equal, fill=0.0, base=0, channel_multiplier=1))

        stage_free = [None, None]
        xb_free = [None] * 4
        pT_free = [None, None]
        acc_free = [None] * 4
        wt_free = [None, None]
        xsb_free = [None, None]
        ra_free = [None, None]
        rb_free = [None, None]
        tmb_free = [None] * 4
        ptr_free = [None, None]
        fmt_free = [None, None]
        cs_free = [None, None]
        pkm_free = [None, None]
        kmrow_free = [None, None]
        cnt = dict(stage=0, xb=0, pT=0, acc=0, wt=0, xsb=0, r=0, tmb=0, ptr=0, fmt=0, km=0)
        state = dict(hT_free=None)

        def emit_T(t):
            for s in range(4):
                r0 = t * TILE + s * 128
                xi = cnt["xb"] % 4
                cnt["xb"] += 1
                si = cnt["stage"] % 2
                cnt["stage"] += 1
                sp.wait(stage_free[si])
                t_ld = s_stage[si].done(nc.sync.dma_start(out=stage[si][:], in_=X_in[r0:r0 + 128, :]))
                pool.wait(t_ld, xb_free[xi])
                t_cc = pool.done(nc.gpsimd.tensor_copy(out=xb[xi][:], in_=stage[si][:]))
                stage_free[si] = t_cc
                for q in range(4):
                    pi = 0
                    pe.wait(t_cc, pT_free[pi], t_id)
                    for j in range(4):
                        kc = 4 * q + j
                        ins = nc.tensor.transpose(pT[pi][:, j * 128:(j + 1) * 128],
                                                  xb[xi][:, kc * 128:(kc + 1) * 128], identb[:])
                    t_tr = pe.done(ins)
                    ev = dve if (q % 2 == 0) else act
                    ev.wait(t_tr, state["hT_free"])
                    dst = hT[:, 4 * q:4 * q + 4, s * 128:(s + 1) * 128]
                    srcp = pT[pi].rearrange("p (a b) -> p a b", b=128)
                    if ev is dve:
                        t_ev = dve.done(nc.vector.tensor_copy(out=dst, in_=srcp))
                    else:
                        t_ev = act.done(nc.scalar.copy(out=dst, in_=srcp))
                    pT_free[pi] = t_ev
                xb_free[xi] = t_tr
            return [dve.last(), act.last()]

        TMDST = {"k": KTM, "v": VTM, "g": GTM, "mv": MVTM}
        FMDST = {"q": QT, "k": KT, "mq": MQT, "mk": MKT}

        def post(t, cg, s, ai, t_m, ci, fj):
            kind = KINDS[cg // 2]
            dbg = getattr(k, "dbg", 9)
            if dbg < 4 and kind in ("q", "k"):
                kind = "mq" if kind == "q" else "mk"
            hb = (cg % 2) * 4
            r0 = t * TILE + s * 128
            j = cnt["tmb"] % 4
            cnt["tmb"] += 1
            if kind in ("q", "k"):
                xj = cnt["xsb"] % 2
                cnt["xsb"] += 1
                act.wait(t_m, xsb_free[xj])
                t_x = act.done(nc.scalar.copy(out=xsb[xj][:], in_=acc[ai][:]))
                acc_free[ai] = t_x
                r = cnt["r"] % 2
                cnt["r"] += 1
                x8 = xsb[xj][:].rearrange("p (a b) -> p a b", b=64)
                x42 = xsb[xj][:].rearrange("p (h a b) -> p h a b", a=2, b=64)
                rb42 = rb[r][:].rearrange("p (h a b) -> p h a b", a=2, b=64)
                cosb = cosr[ci][:, s, :].unsqueeze(1).to_broadcast([128, 8, 64])
                sinb = sinr[ci][:, s, :].unsqueeze(1).to_broadcast([128, 4, 64])
                dve.wait(t_x, ra_free[r], state["cs_ready"])
                t_a = dve.done(nc.vector.tensor_tensor(out=ra[r][:].rearrange("p (a b) -> p a b", b=64), in0=x8,
                                                       in1=cosb, op=ALU.mult))
                dve.wait(t_x, rb_free[r], state["cs_ready"])
                dve.done(nc.vector.tensor_tensor(out=rb42[:, :, 0, :], in0=x42[:, :, 1, :], in1=sinb, op=ALU.mult))
                t_b = dve.done(nc.vector.tensor_tensor(out=rb42[:, :, 1, :], in0=x42[:, :, 0, :], in1=sinb,
                                                       op=ALU.mult))
                xsb_free[xj] = [t_a, t_b]
                ra42 = ra[r][:].rearrange("p (h a b) -> p h a b", a=2, b=64)
                dve.wait(t_a, t_b, tmb_free[j])
                if kind == "q":
                    o42 = tmb[j][:].rearrange("p (h a b) -> p h a b", a=2, b=64)
                else:
                    o42 = ra42
                dve.done(nc.vector.tensor_tensor(out=o42[:, :, 0, :], in0=ra42[:, :, 0, :], in1=rb42[:, :, 0, :],
                                                 op=ALU.subtract))
                t_tm = dve.done(nc.vector.tensor_tensor(out=o42[:, :, 1, :], in0=ra42[:, :, 1, :],
                                                        in1=rb42[:, :, 1, :], op=ALU.add))
                if kind == "k":
                    dve.wait(t_tm, t_c)
                    t_tm = dve.done(nc.vector.tensor_tensor(
                        out=tmb[j][:].rearrange("p (h d) -> p h d", d=128),
                        in0=ra[r][:].rearrange("p (h d) -> p h d", d=128),
                        in1=ksc[:, hb:hb + 4].unsqueeze(2).to_broadcast([128, 4, 128]), op=ALU.mult))
                ra_free[r] = t_tm
                rb_free[r] = t_tm
            else:
                act.wait(t_m, tmb_free[j])
                if kind == "g":
                    t_tm = act.done(nc.scalar.activation(out=tmb[j][:], in_=acc[ai][:], func=AF.Silu))
                else:
                    t_tm = act.done(nc.scalar.copy(out=tmb[j][:], in_=acc[ai][:]))
                acc_free[ai] = t_tm
            frees = []
            if kind in TMDST and dbg >= 2:
                pool.wait(t_tm)
                frees.append(s_st[j].done(nc.gpsimd.dma_start(
                    out=TMDST[kind][r0:r0 + 128, hb * 128:hb * 128 + 512], in_=tmb[j][:])))
            tmb_free[j] = frees
            if kind in FMDST and dbg >= 3:
                return (t, cg, s, j, t_tm, fj, kind, hb, frees)
            return None

        def pe_post(p):
            t, cg, s, j, t_tm, fj, kind, hb, frees = p
            pi = cnt["ptr"] % 2
            cnt["ptr"] += 1
            pe.wait(t_tm, ptr_free[pi], t_id)
            for hh in range(4):
                ins = nc.tensor.transpose(ptr[pi][:, hh * 128:(hh + 1) * 128], tmb[j][:, hh * 128:(hh + 1) * 128],
                                          identb[:])
            t_tr = pe.done(ins)
            frees.append(t_tr)
            ev = dve if (s % 2 == 0) else act
            ev.wait(t_tr, fmt_free[fj] if s == 0 else None)
            dst = fmt[fj][:, :, s * 128:(s + 1) * 128]
            srcp = ptr[pi].rearrange("p (a b) -> p a b", b=128)
            if ev is dve:
                t_ev = dve.done(nc.vector.tensor_copy(out=dst, in_=srcp))
            else:
                t_ev = act.done(nc.scalar.copy(out=dst, in_=srcp))
            ptr_free[pi] = t_ev
            state["fm_evs"].append(t_ev)
            if kind == "mk" and not getattr(k, "no_kmean", False):
                b = s // 2
                kmi = 0
                if s % 2 == 0:
                    pe.wait(pkm_free[kmi])
                ins = nc.tensor.matmul(pkm[kmi][0:1, :], lhsT=onesb[:, 0:1], rhs=tmb[j][:], start=(s % 2 == 0),
                                       stop=(s % 2 == 1))
                t_k = pe.done(ins)
                frees.append(t_k)
                if s % 2 == 1:
                    act.wait(t_k, kmrow_free[kmi])
                    t_r = act.done(nc.scalar.activation(out=kmrow[kmi][:], in_=pkm[kmi][0:1, :], func=AF.Copy,
                                                        scale=1.0 / 256.0))
                    pkm_free[kmi] = t_r
                    pool.wait(t_r)
                    kmrow_free[kmi] = s_km[kmi].done(nc.gpsimd.dma_start(
                        out=KMEAN[2 * t + b:2 * t + b + 1, hb * 128:hb * 128 + 512], in_=kmrow[kmi][:]))
            if s == 3:
                pool.wait(state["fm_evs"])
                fmt_free[fj] = s_fm[fj].done(nc.gpsimd.dma_start(
                    out=FMDST[kind][hb:hb + 4, :, t * TILE:(t + 1) * TILE].rearrange("h d t -> d h t"),
                    in_=fmt[fj][:]))
                state["fm_evs"] = []

        state["fm_evs"] = []
        if getattr(k, "dbg", 9) == -1:
            drain(k, [t_c])
            return
        hT_ready = emit_T(0)
        if getattr(k, "dbg", 9) == 0:
            drain(k, [t_c])
            return
        for t in range(ntiles):
            ci = t % 2
            sp.wait(cs_free[ci])
            s_cs[ci].done(nc.sync.dma_start(out=cosr[ci][:], in_=cos_d[t * TILE:(t + 1) * TILE, :]
                                            .rearrange("(s p) j -> p s j", p=128)))
            state["cs_ready"] = s_cs[ci].done(nc.sync.dma_start(
                out=sinr[ci][:], in_=sin_d[t * TILE:(t + 1) * TILE, :].rearrange("(s p) j -> p s j", p=128)))
            pend = None
            for cg in range(14):
                wi = cnt["wt"] % 2
                cnt["wt"] += 1
                sp.wait(wt_free[wi], wtick)
                t_w = s_wt[wi].done(nc.sync.dma_start(out=wt[wi][:], in_=w_in[cg]))
                kind = KINDS[cg // 2]
                fj = None
                if kind in FMDST:
                    fj = cnt["fmt"] % 2
                    cnt["fmt"] += 1
                for s in range(4):
                    ai = cnt["acc"] % 4
                    cnt["acc"] += 1
                    pe.wait(t_w, hT_ready, acc_free[ai])
                    for kc in range(KC):
                        ins = nc.tensor.matmul(acc[ai][:], lhsT=hT[:, kc, s * 128:(s + 1) * 128], rhs=wt[wi][:, kc, :],
                                               start=(kc == 0), stop=(kc == KC - 1))
                    t_m = pe.done(ins)
                    if pend is not None:
                        pe_post(pend)
                    pend = post(t, cg, s, ai, t_m, ci, fj)
                wt_free[wi] = t_m
            state["hT_free"] = t_m
            if pend is not None:
                pe_post(pend)
                pend = None
            cs_free[ci] = [dve.last(), pool.last()]
            if t + 1 < ntiles:
                hT_ready = emit_T(t + 1)
        drain(k, [x.last() for x in s_st + s_fm + s_km])


def retention_phase(k, QT, KT, KTM, VTM, GTM, gng_row, gnb_row, causal_d, dec_d, qdec_d, YATT, ntok):
    nc = k.nc
    pe, act, dve, pool, sp = k.pe, k.act, k.dve, k.pool, k.sp
    ngrp = ntok // TILE
    with ExitStack() as es:
        qT4 = [k.sb(es, "r_qT%d" % i, [128, NH, TILE], BF16) for i in range(2)]
        kT4 = [k.sb(es, "r_kT%d" % i, [128, NH, TILE], BF16) for i in range(2)]
        ktm4 = [k.sb(es, "r_ktm%d" % i, [128, 4, 1024], BF16) for i in range(2)]
        vtm4 = [k.sb(es, "r_vtm%d" % i, [128, 4, 1024], BF16) for i in range(2)]
        gtm4 = [k.sb(es, "r_gtm%d" % i, [128, 4, 1024], BF16) for i in range(2)]
        S = k.sb(es, "r_S", [128, 1024], F32)
        Sb = [k.sb(es, "r_Sb%d" % i, [128, 1024], BF16) for i in range(2)]
        PT = [k.sb(es, "r_PT%d" % i, [128, 1024], BF16) for i in range(2)]
        ro = [k.sb(es, "r_ro%d" % i, [128, 1024], F32) for i in range(2)]
        ob = [k.sb(es, "r_ob%d" % i, [128, 1024], BF16) for i in range(2)]
        causal = k.sb(es, "r_causal", [128, 128], F32)
        dec = k.sb(es, "r_dec", [128, NH], F32)
        qdec = k.sb(es, "r_qdec", [128, NH], F32)
        gng = k.sb(es, "r_gng", [128, 1024], F32)
        gnb = k.sb(es, "r_gnb", [128, 1024], F32)
        st8 = k.sb(es, "r_st8", [128, NH, 6], F32)
        mv8 = k.sb(es, "r_mv8", [128, NH, 2], F32)
        rs = k.sb(es, "r_rs", [128, 3, NH], F32)
        pS = k.ps(es, "r_pS", [128, 1024], F32)
        pO = k.ps(es, "r_pO", [128, 1024], F32)
        pKV = k.ps(es, "r_pKV", [128, 1024], F32)
        s_ld = [k.dsem(es, "rld%d" % i) for i in range(2)]
        s_c = k.dsem(es, "rconst")
        s_o = [k.dsem(es, "rout%d" % i) for i in range(2)]

        s_c.done(nc.sync.dma_start(out=causal[:], in_=causal_d[:, :]))
        s_c.done(nc.sync.dma_start(out=dec[:], in_=dec_d[:, :]))
        s_c.done(nc.sync.dma_start(out=qdec[:], in_=qdec_d[:, :]))
        s_c.done(nc.sync.dma_start(out=gng[:], in_=gng_row.partition_broadcast(128)))
        t_c = s_c.done(nc.sync.dma_start(out=gnb[:], in_=gnb_row.partition_broadcast(128)))
        t0 = pool.done(nc.gpsimd.memset(S[:], 0.0))
        t_sb = pool.done(nc.gpsimd.memset(Sb[0][:], 0.0))
        t_S = t_sb

        def v3(ap):
            return ap.rearrange("p (h e) -> p h e", e=128)

        ld_free = [None, None]
        pS_free = pO_free = pKV_free = None
        PT_free = [None, None]
        ro_free = [None, None]
        ob_free = [None, None]
        Sb_free = [None, None]
        rs_free = None
        nchunk = 0
        for g in range(ngrp):
            li = g % 2
            sp.wait(ld_free[li])
            sl = slice(g * TILE, (g + 1) * TILE)
            s_ld[li].done(nc.sync.dma_start(out=qT4[li][:], in_=QT[:, :, sl].rearrange("h d t -> d h t")))
            s_ld[li].done(nc.sync.dma_start(out=kT4[li][:], in_=KT[:, :, sl].rearrange("h d t -> d h t")))
            s_ld[li].done(nc.sync.dma_start(out=ktm4[li][:], in_=KTM[sl, :].rearrange("(c p) f -> p c f", p=128)))
            s_ld[li].done(nc.sync.dma_start(out=vtm4[li][:], in_=VTM[sl, :].rearrange("(c p) f -> p c f", p=128)))
            t_ld = s_ld[li].done(nc.sync.dma_start(out=gtm4[li][:],
                                                   in_=GTM[sl, :].rearrange("(c p) f -> p c f", p=128)))
            for cc in range(4):
                c0 = cc * 128
                r0 = g * TILE + c0
                bi = nchunk % 2
                nchunk += 1
                pe.wait(t_ld, pS_free)
                for h in range(NH):
                    ins = nc.tensor.matmul(pS[:, h * 128:(h + 1) * 128], lhsT=kT4[li][:, h, c0:c0 + 128],
                                           rhs=qT4[li][:, h, c0:c0 + 128], start=True, stop=True)
                t_s = pe.done(ins)
                dve.wait(t_s, PT_free[bi], t_c)
                t_pt = dve.done(nc.vector.tensor_tensor(out=v3(PT[bi][:]), in0=v3(pS[:]),
                                                        in1=causal[:].unsqueeze(1).to_broadcast([128, NH, 128]),
                                                        op=ALU.mult))
                pS_free = t_pt
                pe.wait(t_pt, t_sb, pO_free)
                for h in range(NH):
                    hs = slice(h * 128, (h + 1) * 128)
                    nc.tensor.matmul(pO[:, hs], lhsT=PT[bi][:, hs], rhs=vtm4[li][:, cc, hs], start=True, stop=False)
                    ins = nc.tensor.matmul(pO[:, hs], lhsT=qT4[li][:, h, c0:c0 + 128], rhs=Sb[bi][:, hs],
                                           start=False, stop=True)
                t_o = pe.done(ins)
                PT_free[bi] = t_o
                Sb_free[bi] = t_o
                pe.wait(pKV_free)
                for h in range(NH):
                    hs = slice(h * 128, (h + 1) * 128)
                    ins = nc.tensor.matmul(pKV[:, hs], lhsT=ktm4[li][:, cc, hs], rhs=vtm4[li][:, cc, hs],
                                           start=True, stop=True)
                t_kv = pe.done(ins)
                dve.wait(t_kv, t_S)
                t_1 = dve.done(nc.vector.tensor_tensor(out=S[:], in0=S[:], in1=pKV[:], op=ALU.add))
                pKV_free = t_1
                dve.wait(t_1, t_c)
                t_2 = dve.done(nc.vector.tensor_tensor(out=v3(S[:]), in0=v3(S[:]),
                                                       in1=dec[:].unsqueeze(2).to_broadcast([128, NH, 128]),
                                                       op=ALU.mult))
                act.wait(t_2, Sb_free[1 - bi])
                t_sb = act.done(nc.scalar.copy(out=Sb[1 - bi][:], in_=S[:]))
                t_S = t_sb
                dve.wait(t_o, ro_free[bi], rs_free)
                t_r = dve.done(nc.vector.tensor_tensor(out=v3(ro[bi][:]), in0=v3(pO[:]),
                                                       in1=qdec[:].unsqueeze(2).to_broadcast([128, NH, 128]),
                                                       op=ALU.mult))
                pO_free = t_r
                dve.wait(t_r)
                for h in range(NH):
                    t_b = dve.done(nc.vector.bn_stats(out=st8[:, h, :], in_=ro[bi][:, h * 128:(h + 1) * 128]))
                dve.wait(t_b)
                for h in range(NH):
                    t_a = dve.done(nc.vector.bn_aggr(out=mv8[:, h, :], in_=st8[:, h, :]))
                dve.wait(t_a)
                t_e = dve.done(nc.vector.tensor_scalar(out=rs[:, 0, :], in0=mv8[:, :, 1], scalar1=LN_EPS, scalar2=None,
                                                       op0=ALU.add))
                act.wait(t_e)
                t_q = act.done(nc.scalar.activation(out=rs[:, 1, :], in_=rs[:, 0, :], func=AF.Sqrt))
                dve.wait(t_q)
                t_i = dve.done(nc.vector.reciprocal(out=rs[:, 2, :], in_=rs[:, 1, :]))
                dve.wait(t_i)
                t_nb = dve.done(nc.vector.scalar_tensor_tensor(out=rs[:, 1, :], in0=mv8[:, :, 0], scalar=-1.0,
                                                               in1=rs[:, 2, :], op0=ALU.mult, op1=ALU.mult))
                act.wait(t_nb)
                for h in range(NH):
                    hs = slice(h * 128, (h + 1) * 128)
                    t_p = act.done(nc.scalar.activation(out=ro[bi][:, hs], in_=ro[bi][:, hs], func=AF.Identity,
                                                        scale=rs[:, 2, h:h + 1], bias=rs[:, 1, h:h + 1]))
                rs_free = t_p
                dve.wait(t_p)
                t_p = dve.done(nc.vector.tensor_tensor(out=ro[bi][:], in0=ro[bi][:], in1=gng[:], op=ALU.mult))
                dve.wait(t_p)
                t_p = dve.done(nc.vector.tensor_tensor(out=ro[bi][:], in0=ro[bi][:], in1=gnb[:], op=ALU.add))
                dve.wait(t_p, ob_free[bi])
                t_f = dve.done(nc.vector.tensor_tensor(out=ob[bi][:], in0=ro[bi][:], in1=gtm4[li][:, cc, :],
                                                       op=ALU.mult))
                ro_free[bi] = t_f
                pool.wait(t_f)
                ob_free[bi] = s_o[bi].done(nc.gpsimd.dma_start(out=YATT[r0:r0 + 128, 0:1024], in_=ob[bi][:]))
            ld_free[li] = [pe.last(), dve.last()]
        drain(k, [x.last() for x in s_o])


def moba_phase(k, MQT, MKT, MVTM, KMEAN, causalT_d, YATT, ntok):
    nc = k.nc
    pe, act, dve, pool, sp = k.pe, k.act, k.dve, k.pool, k.sp
    nq = ntok // 128
    nblk = ntok // 256
    SC = 128.0 ** -0.5
    with ExitStack() as es:
        mqT = [k.sb(es, "m_q%d" % i, [128, ntok], BF16) for i in range(2)]
        mkT = [k.sb(es, "m_k%d" % i, [128, ntok], BF16) for i in range(2)]
        mv = [k.sb(es, "m_v%d" % i, [128, nq, 129], BF16) for i in range(2)]
        km = k.sb(es, "m_km", [nblk, 1024], F32)
        kmT = k.sb(es, "m_kmT", [128, NH, 32], BF16)
        identf = k.sb(es, "m_idf", [128, 128], F32)
        causalT = k.sb(es, "m_causal", [128, 128], F32)
        gm = k.sb(es, "m_gm", [128, 32], F32)
        top8 = k.sb(es, "m_top8", [128, 8], F32)
        sel = [k.sb(es, "m_sel%d" % i, [128, 32], F32) for i in range(2)]
        eo = [k.sb(es, "m_eo%d" % i, [128, 128], F32) for i in range(2)]
        PTo = [k.sb(es, "m_PTo%d" % i, [128, 2, 128], BF16) for i in range(2)]
        PTg = [k.sb(es, "m_PTg%d" % i, [128, 4, 128], BF16) for i in range(3)]
        O = [k.sb(es, "m_O%d" % i, [128, 132], F32) for i in range(2)]
        rinv = k.sb(es, "m_rinv", [128, 2], F32)
        mob = [k.sb(es, "m_mob%d" % i, [128, 128], BF16) for i in range(2)]
        pG = k.ps(es, "m_pG", [128, 512], F32)
        pSo = k.ps(es, "m_pSo", [128, 512], F32)
        pOo = k.ps(es, "m_pOo", [128, 512], F32)
        pS = [k.ps(es, "m_pS%d" % i, [128, 512], F32) for i in range(2)]
        pO2 = [k.ps(es, "m_pO2%d" % i, [128, 512], F32) for i in range(2)]
        s_h = [k.dsem(es, "mh%d" % i) for i in range(2)]
        s_c = k.dsem(es, "mconst")
        s_o = [k.dsem(es, "mout%d" % i) for i in range(2)]

        s_c.done(nc.sync.dma_start(out=km[:], in_=KMEAN[:, :]))
        t_c = s_c.done(nc.sync.dma_start(out=causalT[:], in_=causalT_d[:, :]))
        t0 = pool.done(nc.gpsimd.memset(identf[:], 1.0))
        pool.wait(t0)
        t_id = pool.done(nc.gpsimd.affine_select(out=identf[:], in_=identf[:], pattern=[[-1, 128]],
                                                 compare_op=ALU.is_equal, fill=0.0, base=0, channel_multiplier=1))
        for i in range(2):
            t_ones = pool.done(nc.gpsimd.memset(mv[i][:, :, 128:129], 1.0))
        t_prev = None
        for h in range(NH):
            pe.wait(t_c, t_id, t_prev)
            t_t = pe.done(nc.tensor.transpose(pG[:, 0:nblk], km[:, h * 128:(h + 1) * 128], identf[0:nblk, 0:nblk]))
            dve.wait(t_t)
            t_prev = dve.done(nc.vector.tensor_copy(out=kmT[:, h, 0:nblk], in_=pG[:, 0:nblk]))
        pG_free = t_prev

        h_free = [None, None]
        pSo_free = pOo_free = None
        pS_free = [None, None]
        pO2_free = [None, None]
        PTo_free = [None, None]
        PTg_free = [None] * 3
        O_free = [None, None]
        eo_free = [None, None]
        sel_free = [None, None]
        mob_free = [None, None]
        gm_t = None
        top_free = None
        rinv_free = None
        cnt = dict(pS=0, pO2=0, PTg=0, qi=0)

        for h in range(NH):
            hi = h % 2
            sp.wait(h_free[hi], t_ones)
            s_h[hi].done(nc.sync.dma_start(out=mqT[hi][:], in_=MQT[h]))
            s_h[hi].done(nc.sync.dma_start(out=mkT[hi][:], in_=MKT[h]))
            t_ld = s_h[hi].done(nc.sync.dma_start(
                out=mv[hi][:, :, 0:128], in_=MVTM[:, h * 128:(h + 1) * 128].rearrange("(c p) d -> p c d", p=128)))
            dve.wait(gm_t)
            gm_t = dve.done(nc.vector.memset(gm[:], -1e30))
            for i in range(nq):
                nb = i // 2
                qi = cnt["qi"] % 2
                cnt["qi"] += 1
                qs = slice(i * 128, (i + 1) * 128)
                r0 = i * 128
                use_sel = nb > 3
                if use_sel:
                    pe.wait(t_ld, pG_free)
                    t_g = pe.done(nc.tensor.matmul(pG[:, 0:32], lhsT=mqT[hi][:, qs], rhs=kmT[:, h, :],
                                                   start=True, stop=True))
                    dve.wait(t_g, gm_t, top_free)
                    t_gm = dve.done(nc.vector.tensor_copy(out=gm[:, 0:nb], in_=pG[:, 0:nb]))
                    pG_free = t_gm
                    dve.wait(t_gm)
                    t_t8 = dve.done(nc.vector.max(out=top8[:], in_=gm[:]))
                    dve.wait(t_t8, sel_free[qi])
                    t_sel = dve.done(nc.vector.tensor_scalar(out=sel[qi][:], in0=gm[:], scalar1=top8[:, 2:3],
                                                             scalar2=None, op0=ALU.is_ge))
                    gm_t = t_sel
                    top_free = t_sel
                ncs = 1 + (i % 2)
                pe.wait(t_ld, pSo_free)
                ins = nc.tensor.matmul(pSo[:, 0:128], lhsT=mkT[hi][:, qs], rhs=mqT[hi][:, qs], start=True, stop=True)
                if ncs == 2:
                    ins = nc.tensor.matmul(pSo[:, 128:256], lhsT=mkT[hi][:, (i - 1) * 128:i * 128], rhs=mqT[hi][:, qs],
                                           start=True, stop=True)
                t_so = pe.done(ins)
                act.wait(t_so, eo_free[qi], PTo_free[qi])
                t_e = act.done(nc.scalar.activation(out=eo[qi][:], in_=pSo[:, 0:128], func=AF.Exp, scale=SC))
                if ncs == 2:
                    t_e2 = act.done(nc.scalar.activation(out=PTo[qi][:, 1, :], in_=pSo[:, 128:256], func=AF.Exp,
                                                         scale=SC))
                else:
                    t_e2 = t_e
                pSo_free = t_e2
                dve.wait(t_e, t_c)
                t_pd = dve.done(nc.vector.tensor_tensor(out=PTo[qi][:, 0, :], in0=eo[qi][:], in1=causalT[:],
                                                        op=ALU.mult))
                eo_free[qi] = t_pd
                ng = (nb + 1) // 2

                def emit_S(g):
                    nbg = min(2, nb - 2 * g)
                    si = cnt["pS"] % 2
                    cnt["pS"] += 1
                    pe.wait(pS_free[si])
                    for b in range(nbg):
                        n = 2 * g + b
                        for c2 in range(2):
                            ks = slice(n * 256 + c2 * 128, n * 256 + (c2 + 1) * 128)
                            ins_ = nc.tensor.matmul(pS[si][:, (b * 2 + c2) * 128:(b * 2 + c2 + 1) * 128],
                                                    lhsT=mkT[hi][:, ks], rhs=mqT[hi][:, qs], start=True, stop=True)
                    t_s = pe.done(ins_)
                    pj = cnt["PTg"] % 3
                    cnt["PTg"] += 1
                    act.wait(t_s, PTg_free[pj])
                    t_p = act.done(nc.scalar.activation(
                        out=PTg[pj][:, 0:2 * nbg, :].rearrange("p a b -> p (a b)"), in_=pS[si][:, 0:nbg * 256],
                        func=AF.Exp, scale=SC))
                    pS_free[si] = t_p
                    return (g, nbg, pj, t_p)

                pend = emit_S(0) if ng > 0 else None
                pe.wait(t_pd, t_e2, pOo_free)
                ins = nc.tensor.matmul(pOo[:, 0:129], lhsT=PTo[qi][:, 0, :], rhs=mv[hi][:, i, :], start=True,
                                       stop=(ncs == 1))
                if ncs == 2:
                    ins = nc.tensor.matmul(pOo[:, 0:129], lhsT=PTo[qi][:, 1, :], rhs=mv[hi][:, i - 1, :], start=False,
                                           stop=True)
                t_oo = pe.done(ins)
                PTo_free[qi] = t_oo
                dve.wait(t_oo, O_free[qi])
                t_O = dve.done(nc.vector.tensor_copy(out=O[qi][:, 0:129], in_=pOo[:, 0:129]))
                pOo_free = t_O
                for g in range(ng):
                    nxt = emit_S(g + 1) if g + 1 < ng else None
                    _, nbg, pj, t_p = pend
                    oi = cnt["pO2"] % 2
                    cnt["pO2"] += 1
                    pe.wait(t_p, pO2_free[oi])
                    for b in range(nbg):
                        n = 2 * g + b
                        for c2 in range(2):
                            ins = nc.tensor.matmul(pO2[oi][:, b * 132:b * 132 + 129], lhsT=PTg[pj][:, b * 2 + c2, :],
                                                   rhs=mv[hi][:, n * 2 + c2, :], start=(c2 == 0), stop=(c2 == 1))
                    t_pv = pe.done(ins)
                    PTg_free[pj] = t_pv
                    for b in range(nbg):
                        n = 2 * g + b
                        dve.wait(t_pv, t_O)
                        if use_sel:
                            t_O = dve.done(nc.vector.scalar_tensor_tensor(
                                out=O[qi][:, 0:129], in0=pO2[oi][:, b * 132:b * 132 + 129], scalar=sel[qi][:, n:n + 1],
                                in1=O[qi][:, 0:129], op0=ALU.mult, op1=ALU.add))
                        else:
                            t_O = dve.done(nc.vector.tensor_tensor(out=O[qi][:, 0:129], in0=O[qi][:, 0:129],
                                                                   in1=pO2[oi][:, b * 132:b * 132 + 129], op=ALU.add))
                    pO2_free[oi] = t_O
                    pend = nxt
                if use_sel:
                    sel_free[qi] = t_O
                dve.wait(t_O, rinv_free)
                t_r = dve.done(nc.vector.reciprocal(out=rinv[:, 0:1], in_=O[qi][:, 128:129]))
                dve.wait(t_r, mob_free[qi])
                t_m = dve.done(nc.vector.tensor_scalar(out=mob[qi][:], in0=O[qi][:, 0:128], scalar1=rinv[:, 0:1],
                                                       scalar2=None, op0=ALU.mult))
                rinv_free = t_m
                O_free[qi] = t_m
                pool.wait(t_m)
                mob_free[qi] = s_o[qi].done(nc.gpsimd.dma_start(
                    out=YATT[r0:r0 + 128, 1024 + h * 128:1024 + (h + 1) * 128], in_=mob[qi][:]))
            h_free[hi] = pe.last()
        drain(k, [x.last() for x in s_o])


def consts(ntok, pos0):
    pos = (pos0 + np.arange(ntok)).astype(np.float32)
    invf = (10000.0 ** (-np.arange(64, dtype=np.float32) / np.float32(64))).astype(np.float32)
    ang = (pos[:, None] * invf[None, :]).astype(np.float32)
    lg = np.log1p(-np.exp2(-5.0 - np.arange(NH, dtype=np.float64)))
    p = np.arange(128, dtype=np.float64)
    c = np.arange(128)
    return dict(
        cos=np.cos(ang).astype(np.float32), sin=np.sin(ang).astype(np.float32),
        ksc=(128.0 ** -0.5 * np.exp(-(p[:, None] + 1) * lg[None, :])).astype(np.float32),
        qdec=np.exp((p[:, None] + 1) * lg[None, :]).astype(np.float32),
        dec=np.tile(np.exp(128.0 * lg)[None, :], (128, 1)).astype(np.float32),
        causal=(c[None, :] >= c[:, None]).astype(np.float32),
    )


DRNN = 2816
RC = DRNN // 128
GK = 1.5957691216057308


def rnn_inproj_phase(k, X_in, w_in, wtick, GATE, RNNBR, ntok):
    nc = k.nc
    pe, act, dve, pool, sp = k.pe, k.act, k.dve, k.pool, k.sp
    ntiles = ntok // TILE
    with ExitStack() as es:
        stage = [k.sb(es, "n_stage%d" % i, [128, D], F32) for i in range(2)]
        xb = [k.sb(es, "n_xb%d" % i, [128, D], BF16) for i in range(4)]
        hT = k.sb(es, "n_hT", [128, KC, TILE], BF16)
        ws = [k.sb(es, "n_w%d" % i, [128, KC, 128], BF16) for i in range(3)]
        xf = [k.sb(es, "n_xf%d" % i, [128, TILE], F32) for i in range(3)]
        t1 = [k.sb(es, "n_t1%d" % i, [128, TILE], F32) for i in range(2)]
        gb = [k.sb(es, "n_gb%d" % i, [128, TILE], BF16) for i in range(2)]
        identb = k.sb(es, "n_idb", [128, 128], BF16)
        pT = k.ps(es, "n_pT", [128, 512], BF16)
        acc = [k.ps(es, "n_acc%d" % i, [128, 512], F32) for i in range(4)]
        s_stage = [k.dsem(es, "nstage%d" % i) for i in range(2)]
        s_w = [k.dsem(es, "nw%d" % i) for i in range(3)]
        s_og = [k.dsem(es, "nog%d" % i) for i in range(2)]
        s_ox = [k.dsem(es, "nox%d" % i) for i in range(3)]
        t0 = pool.done(nc.gpsimd.memset(identb[:], 1.0))
        pool.wait(t0)
        t_id = pool.done(nc.gpsimd.affine_select(out=identb[:], in_=identb[:], pattern=[[-1, 128]],
                                                 compare_op=ALU.is_equal, fill=0.0, base=0, channel_multiplier=1))
        stage_free = [None, None]
        xb_free = [None] * 4
        w_free = [None] * 3
        acc_free = [None] * 4
        xf_free = [None] * 3
        t1_free = [None, None]
        gb_free = [None, None]
        cnt = dict(stage=0, xb=0, w=0, acc=0, xf=0, t1=0, gb=0)
        state = dict(hT_free=None, pT_free=None)

        def emit_T(t):
            for s in range(4):
                r0 = t * TILE + s * 128
                xi = cnt["xb"] % 4
                cnt["xb"] += 1
                si = cnt["stage"] % 2
                cnt["stage"] += 1
                sp.wait(stage_free[si])
                t_ld = s_stage[si].done(nc.sync.dma_start(out=stage[si][:], in_=X_in[r0:r0 + 128, :]))
                pool.wait(t_ld, xb_free[xi])
                t_cc = pool.done(nc.gpsimd.tensor_copy(out=xb[xi][:], in_=stage[si][:]))
                stage_free[si] = t_cc
                for q in range(4):
                    pe.wait(t_cc, state["pT_free"], t_id)
                    for j in range(4):
                        kc = 4 * q + j
                        ins = nc.tensor.transpose(pT[:, j * 128:(j + 1) * 128],
                                                  xb[xi][:, kc * 128:(kc + 1) * 128], identb[:])
                    t_tr = pe.done(ins)
                    ev = dve if (q % 2 == 0) else act
                    ev.wait(t_tr, state["hT_free"])
                    dst = hT[:, 4 * q:4 * q + 4, s * 128:(s + 1) * 128]
                    srcp = pT[:].rearrange("p (a b) -> p a b", b=128)
                    if ev is dve:
                        t_ev = dve.done(nc.vector.tensor_copy(out=dst, in_=srcp))
                    else:
                        t_ev = act.done(nc.scalar.copy(out=dst, in_=srcp))
                    state["pT_free"] = t_ev
                xb_free[xi] = t_tr
            return [dve.last(), act.last()]

        hT_ready = emit_T(0)
        for t in range(ntiles):
            sl = slice(t * TILE, (t + 1) * TILE)
            for oc in range(2 * RC):
                wi = cnt["w"] % 3
                cnt["w"] += 1
                sp.wait(w_free[wi], wtick)
                t_w = s_w[wi].done(nc.sync.dma_start(out=ws[wi][:], in_=w_in[oc]))
                ai = cnt["acc"] % 4
                cnt["acc"] += 1
                pe.wait(t_w, hT_ready, acc_free[ai])
                for kc in range(KC):
                    ins = nc.tensor.matmul(acc[ai][:], lhsT=ws[wi][:, kc, :], rhs=hT[:, kc, :],
                                           start=(kc == 0), stop=(kc == KC - 1))
                t_m = pe.done(ins)
                w_free[wi] = t_m
                xi = cnt["xf"] % 3
                cnt["xf"] += 1
                act.wait(t_m, xf_free[xi])
                t_x = act.done(nc.scalar.copy(out=xf[xi][:], in_=acc[ai][:]))
                acc_free[ai] = t_x
                if oc >= RC:
                    pool.wait(t_x)
                    xf_free[xi] = s_ox[xi].done(nc.gpsimd.dma_start(out=RNNBR[oc - RC, :, sl], in_=xf[xi][:]))
                else:
                    ti = cnt["t1"] % 2
                    cnt["t1"] += 1
                    gi = cnt["gb"] % 2
                    cnt["gb"] += 1
                    dve.wait(t_x, t1_free[ti])
                    t_a = dve.done(nc.vector.tensor_tensor(out=t1[ti][:], in0=xf[xi][:], in1=xf[xi][:], op=ALU.mult))
                    dve.wait(t_a)
                    t_a = dve.done(nc.vector.tensor_scalar(out=t1[ti][:], in0=t1[ti][:], scalar1=0.044715, scalar2=1.0,
                                                           op0=ALU.mult, op1=ALU.add))
                    dve.wait(t_a)
                    t_a = dve.done(nc.vector.tensor_tensor(out=t1[ti][:], in0=t1[ti][:], in1=xf[xi][:], op=ALU.mult))
                    act.wait(t_a)
                    t_e = act.done(nc.scalar.activation(out=t1[ti][:], in_=t1[ti][:], func=AF.Exp, scale=-GK))
                    dve.wait(t_e)
                    t_a = dve.done(nc.vector.tensor_scalar(out=t1[ti][:], in0=t1[ti][:], scalar1=1.0, scalar2=None,
                                                           op0=ALU.add))
                    dve.wait(t_a)
                    t_a = dve.done(nc.vector.reciprocal(out=t1[ti][:], in_=t1[ti][:]))
                    dve.wait(t_a, gb_free[gi])
                    t_g = dve.done(nc.vector.tensor_tensor(out=gb[gi][:], in0=t1[ti][:], in1=xf[xi][:], op=ALU.mult))
                    t1_free[ti] = t_g
                    xf_free[xi] = t_g
                    pool.wait(t_g)
                    gb_free[gi] = s_og[gi].done(nc.gpsimd.dma_start(out=GATE[oc, :, sl], in_=gb[gi][:]))
            state["hT_free"] = t_m
            if t + 1 < ntiles:
                hT_ready = emit_T(t + 1)
        drain(k, [x.last() for x in s_og + s_ox])


def prep_bd(k, es, w_src, name):
    nc = k.nc
    out = dram(nc, name, [DRNN, DRNN], BF16)
    z = k.sb(es, name + "_z", [128, DRNN], BF16)
    ds = k.dsem(es, name + "z")
    t0 = k.pool.done(nc.gpsimd.memset(z[:], 0.0))
    k.pool.wait(t0)
    t = None
    for c in range(RC):
        t = ds.done(nc.gpsimd.dma_start(out=out[c * 128:(c + 1) * 128, :], in_=z[:]))
    k.pool.wait(t)
    ds2 = k.dsem(es, name + "b")
    for g in range(16):
        t = ds2.done(nc.gpsimd.dma_start(out=out[g * 176:(g + 1) * 176, g * 176:(g + 1) * 176], in_=w_src[g]))
    return out, t


def rnn_core_phase(k, GATE, RNNBR, wa_bd, wx_bd, wtick, vecs, YRNN, ntok):
    nc = k.nc
    pe, act, dve, pool, sp = k.pe, k.act, k.dve, k.pool, k.sp
    ntiles = ntok // TILE
    with ExitStack() as es:
        wband = [k.sb(es, "c_wband%d" % i, [128, RC, 5, 128], BF16) for i in range(2)]
        vt = k.sb(es, "c_vt", [128, 9, RC], F32)
        cst = k.sb(es, "c_cst", [128, 6, RC], F32)
        hst = k.sb(es, "c_hst", [128, RC], F32)
        ones1 = k.sb(es, "c_ones", [128, 2], F32)
        xr = [k.sb(es, "c_xr%d" % i, [128, TILE + 4], F32) for i in range(3)]
        u = k.sb(es, "c_u", [128, RC, TILE], F32)
        ub = k.sb(es, "c_ub", [128, RC, TILE], BF16)
        gt = [k.sb(es, "c_gt%d" % i, [128, TILE], BF16) for i in range(2)]
        r_ = [k.sb(es, "c_r%d" % i, [128, TILE], F32) for i in range(2)]
        i_ = [k.sb(es, "c_i%d" % i, [128, TILE], F32) for i in range(2)]
        a_ = [k.sb(es, "c_a%d" % i, [128, TILE], F32) for i in range(2)]
        b_ = [k.sb(es, "c_b%d" % i, [128, TILE], F32) for i in range(2)]
        h_ = [k.sb(es, "c_h%d" % i, [128, TILE], F32) for i in range(2)]
        yb = [k.sb(es, "c_yb%d" % i, [128, TILE], BF16) for i in range(2)]
        pR = [k.ps(es, "c_pR%d" % i, [128, 512], F32) for i in range(2)]
        pI = [k.ps(es, "c_pI%d" % i, [128, 512], F32) for i in range(2)]
        s_c = k.dsem(es, "cconst")
        s_x = [k.dsem(es, "cx%d" % i) for i in range(3)]
        s_g = [k.dsem(es, "cg%d" % i) for i in range(2)]
        s_o = [k.dsem(es, "co%d" % i) for i in range(2)]

        sp.wait(wtick)
        for gi, wsrc in enumerate((wa_bd, wx_bd)):
            t0 = pool.done(nc.gpsimd.memset(wband[gi][:], 0.0))
            sp.wait(t0)
            for c in range(RC):
                lo, hi = max(0, c - 2), min(RC - 1, c + 2)
                t_wb = s_c.done(nc.sync.dma_start(
                    out=wband[gi][:, c, lo - c + 2:hi - c + 3, :],
                    in_=wsrc[lo * 128:(hi + 1) * 128, c * 128:(c + 1) * 128].rearrange("(j p) f -> p j f", p=128)))
        t_v = s_c.done(nc.sync.dma_start(out=vt[:].rearrange("p v c -> p (v c)"), in_=vecs[:, :]))
        t_wb = t_v
        act.wait(t_v)
        dve.wait(t_v)
        t = dve.done(nc.vector.tensor_scalar(out=cst[:, 0:2, :], in0=vt[:, 5:7, :], scalar1=-1.0, scalar2=None,
                                             op0=ALU.mult))
        t_e = act.done(nc.scalar.activation(out=cst[:, 4, :], in_=vt[:, 7, :], func=AF.Exp, scale=-1.0))
        dve.wait(t_e)
        t = dve.done(nc.vector.tensor_scalar(out=cst[:, 4, :], in0=cst[:, 4, :], scalar1=1.0, scalar2=None,
                                             op0=ALU.add))
        act.wait(t)
        t_e = act.done(nc.scalar.activation(out=cst[:, 5, :], in_=cst[:, 4, :], func=AF.Ln))
        dve.wait(t_e)
        dve.done(nc.vector.tensor_scalar(out=cst[:, 2, :], in0=cst[:, 5, :], scalar1=-8.0, scalar2=None, op0=ALU.mult))
        dve.done(nc.vector.tensor_scalar(out=cst[:, 3, :], in0=cst[:, 5, :], scalar1=-16.0, scalar2=None,
                                         op0=ALU.mult))
        dve.done(nc.vector.memset(ones1[:], 1.0))
        t_cst = dve.done(nc.vector.memset(hst[:], 0.0))
        for xi in range(3):
            t_cst = dve.done(nc.vector.memset(xr[xi][:, 0:4], 0.0))

        xr_free = [None] * 3
        gt_free = [None, None]
        pR_free = [None, None]
        pI_free = [None, None]
        buf_free = [None, None]
        yb_free = [None, None]
        u_free = None
        cntx = 0
        cntc = 0
        t_h = t_cst
        for t in range(ntiles):
            t0 = t * TILE
            t_u = None
            for c in range(RC):
                xi = cntx % 3
                cntx += 1
                sp.wait(xr_free[xi], t_cst)
                if t == 0:
                    t_x = s_x[xi].done(nc.sync.dma_start(out=xr[xi][:, 4:4 + TILE], in_=RNNBR[c, :, 0:TILE]))
                else:
                    t_x = s_x[xi].done(nc.sync.dma_start(out=xr[xi][:, 0:4 + TILE],
                                                         in_=RNNBR[c, :, t0 - 4:t0 + TILE]))
                act.wait(t_x, u_free, t_cst)
                t_1 = act.done(nc.scalar.activation(out=u[:, c, :], in_=xr[xi][:, 4:4 + TILE], func=AF.Identity,
                                                    scale=vt[:, 3, c:c + 1], bias=vt[:, 4, c:c + 1]))
                for j in range(3):
                    dve.wait(t_1)
                    t_1 = dve.done(nc.vector.scalar_tensor_tensor(
                        out=u[:, c, :], in0=xr[xi][:, 1 + j:1 + j + TILE], scalar=vt[:, j, c:c + 1], in1=u[:, c, :],
                        op0=ALU.mult, op1=ALU.add))
                xr_free[xi] = t_1
                act.wait(t_1)
                t_u = act.done(nc.scalar.copy(out=ub[:, c, :], in_=u[:, c, :]))
            for c in range(RC):
                bi = cntc % 2
                cntc += 1
                sl = slice(t0, t0 + TILE)
                sp.wait(gt_free[bi])
                t_g = s_g[bi].done(nc.sync.dma_start(out=gt[bi][:], in_=GATE[c, :, sl]))
                lo, hi = max(0, c - 2), min(RC - 1, c + 2)
                pe.wait(t_u, t_wb, pR_free[bi], pI_free[bi])
                for cc in range(lo, hi + 1):
                    nc.tensor.matmul(pR[bi][:], lhsT=wband[0][:, c, cc - c + 2, :], rhs=ub[:, cc, :],
                                     start=(cc == lo), stop=(cc == hi))
                for cc in range(lo, hi + 1):
                    ins = nc.tensor.matmul(pI[bi][:], lhsT=wband[1][:, c, cc - c + 2, :], rhs=ub[:, cc, :],
                                           start=(cc == lo), stop=(cc == hi))
                t_m = pe.done(ins)
                act.wait(t_m, buf_free[bi], t_cst)
                t_r = act.done(nc.scalar.activation(out=r_[bi][:], in_=pR[bi][:], func=AF.Exp, scale=-1.0,
                                                    bias=cst[:, 0, c:c + 1]))
                t_i = act.done(nc.scalar.activation(out=i_[bi][:], in_=pI[bi][:], func=AF.Exp, scale=-1.0,
                                                    bias=cst[:, 1, c:c + 1]))
                pR_free[bi] = t_r
                pI_free[bi] = t_i
                dve.wait(t_r)
                t_1 = dve.done(nc.vector.tensor_scalar(out=r_[bi][:], in0=r_[bi][:], scalar1=1.0, scalar2=None,
                                                       op0=ALU.add))
                dve.wait(t_1)
                t_rr = dve.done(nc.vector.reciprocal(out=r_[bi][:], in_=r_[bi][:]))
                act.wait(t_rr)
                t_a = act.done(nc.scalar.activation(out=a_[bi][:], in_=r_[bi][:], func=AF.Exp,
                                                    scale=cst[:, 2, c:c + 1]))
                t_a2 = act.done(nc.scalar.activation(out=b_[bi][:], in_=r_[bi][:], func=AF.Exp,
                                                     scale=cst[:, 3, c:c + 1]))
                act.wait(t_a2)
                t_l = act.done(nc.scalar.activation(out=b_[bi][:], in_=b_[bi][:], func=AF.Ln, scale=-1.0,
                                                    bias=ones1[:, 0:1]))
                act.wait(t_l)
                t_s = act.done(nc.scalar.activation(out=b_[bi][:], in_=b_[bi][:], func=AF.Exp, scale=0.5))
                dve.wait(t_i)
                t_1 = dve.done(nc.vector.tensor_scalar(out=i_[bi][:], in0=i_[bi][:], scalar1=1.0, scalar2=None,
                                                       op0=ALU.add))
                dve.wait(t_1)
                t_1 = dve.done(nc.vector.reciprocal(out=i_[bi][:], in_=i_[bi][:]))
                dve.wait(t_1)
                t_iu = dve.done(nc.vector.tensor_tensor(out=i_[bi][:], in0=i_[bi][:], in1=u[:, c, :], op=ALU.mult))
                dve.wait(t_iu, t_s)
                t_b = dve.done(nc.vector.tensor_tensor(out=b_[bi][:], in0=b_[bi][:], in1=i_[bi][:], op=ALU.mult))
                dve.wait(t_b, t_a, t_h)
                t_sc = dve.done(nc.vector.tensor_tensor_scan(out=h_[bi][:], data0=a_[bi][:], data1=b_[bi][:],
                                                             initial=hst[:, c:c + 1], op0=ALU.mult, op1=ALU.add))
                dve.wait(t_sc)
                t_h = dve.done(nc.vector.tensor_copy(out=hst[:, c:c + 1], in_=h_[bi][:, TILE - 1:TILE]))
                pool.wait(t_sc, t_g, yb_free[bi])
                t_y = pool.done(nc.gpsimd.tensor_tensor(out=yb[bi][:], in0=h_[bi][:], in1=gt[bi][:], op=ALU.mult))
                gt_free[bi] = t_y
                buf_free[bi] = [t_y, t_h]
                pool.wait(t_y)
                yb_free[bi] = s_o[bi].done(nc.gpsimd.dma_start(out=YRNN[c, :, sl], in_=yb[bi][:]))
            u_free = [pe.last(), dve.last()]
        drain(k, [x.last() for x in s_o])


BETA_UNUSED = None
IN_SPECS = [
    ("x", None), ("ln_g", [2, 3, D]), ("ln_b", [2, 3, D]),
    ("ffn_w_gate", [2, 2, D, DFF]), ("ffn_w_up", [2, 2, D, DFF]), ("ffn_w_down", [2, 2, DFF, D]),
    ("attn_w_in", [1, D, 7168]), ("ret_gn_g", [1, 1024]), ("ret_gn_b", [1, 1024]), ("attn_w_out", [1, D, D]),
    ("rnn_w_in", [1, D, 2 * DRNN]), ("rnn_gate_a_w", [1, 16, 176, 176]), ("rnn_gate_x_w", [1, 16, 176, 176]),
    ("rnn_w_out", [1, DRNN, D]),
    ("c_cos", None), ("c_sin", None), ("c_ksc", [128, NH]), ("c_qdec", [128, NH]), ("c_dec", [128, NH]),
    ("c_causal", [128, 128]), ("c_vecs", [128, 9 * RC]),
]


_DBG = False


def build_program(ntok, upto=99):
    nc = bass.Bass("TRN2", target_bir_lowering=False)
    I = {}
    for name, shape in IN_SPECS:
        if name == "x":
            shape = [ntok, D]
        elif name in ("c_cos", "c_sin"):
            shape = [ntok, 64]
        I[name] = dram(nc, name, shape, F32, "ExternalInput")
    out = dram(nc, "out", [ntok, D], F32, "ExternalOutput")
    dk = "ExternalOutput" if _DBG else "Internal"
    XA = dram(nc, "XA", [ntok, D], F32, dk)
    XB = dram(nc, "XB", [ntok, D], F32, dk)
    with ExitStack() as es:
        k = K(nc, es)

        def prep_ffn(l, j):
            a, t1 = prep_ws(k, es, I["ffn_w_gate"][l, j], "wg%d%d" % (l, j), D, DFF)
            b, t2 = prep_ws(k, es, I["ffn_w_up"][l, j], "wu%d%d" % (l, j), D, DFF)
            c, t3 = prep_ws(k, es, I["ffn_w_down"][l, j], "wd%d%d" % (l, j), DFF, D)
            return a, b, c, [t1, t2, t3]

        def ffn(l, j, w, src, dst):
            sublayer_phase(k, "ffn", src, dst, w[2], FC, w[3], I["ln_g"][l, 2 * j], I["ln_b"][l, 2 * j], ntok,
                           2.0 * ALPHA, 4.0 * LN_EPS, wg=w[0], wu=w[1])

        w00 = prep_ffn(0, 0)
        ffn(0, 0, w00, I["x"], XA if upto > 0 else out)
        if upto <= 0:
            return nc
        win, t_win = prep_tm(k, es, I["attn_w_in"][0], "awin", D, 7168)
        wout, t_wout = prep_ws(k, es, I["attn_w_out"][0], "awout", D, D)
        w01 = prep_ffn(0, 1)
        sc = [dram(nc, "QT", [NH, 128, ntok], BF16), dram(nc, "KT", [NH, 128, ntok], BF16),
              dram(nc, "KTM", [ntok, 1024], BF16), dram(nc, "VTM", [ntok, 1024], BF16),
              dram(nc, "GTM", [ntok, 1024], BF16), dram(nc, "MQT", [NH, 128, ntok], BF16),
              dram(nc, "MKT", [NH, 128, ntok], BF16), dram(nc, "MVTM", [ntok, 1024], BF16),
              dram(nc, "KMEAN", [ntok // 256, 1024], F32)]
        YATT = dram(nc, "YATT", [ntok, 2048], BF16)
        attn_inproj_phase(k, XA, win, [t_win], I["c_cos"], I["c_sin"], I["c_ksc"], sc, ntok)
        QT, KT, KTM, VTM, GTM, MQT, MKT, MVTM, KMEAN = sc
        retention_phase(k, QT, KT, KTM, VTM, GTM, I["ret_gn_g"][0], I["ret_gn_b"][0], I["c_causal"], I["c_dec"],
                        I["c_qdec"], YATT, ntok)
        moba_phase(k, MQT, MKT, MVTM, KMEAN, I["c_causal"], YATT, ntok)
        sublayer_phase(k, "proj_tm", XA, XB if upto > 1 else out, wout, KC, [t_wout], I["ln_g"][0, 1], I["ln_b"][0, 1],
                       ntok, ALPHA, LN_EPS, Y=YATT)
        if upto <= 1:
            return nc
        w10 = prep_ffn(1, 0)
        ffn(0, 1, w01, XB, XA)
        rin, t_rin = prep_ws(k, es, I["rnn_w_in"][0], "rwin", D, 2 * DRNN)
        wa, t_wa = prep_bd(k, es, I["rnn_gate_a_w"][0], "rwa")
        wx, t_wx = prep_bd(k, es, I["rnn_gate_x_w"][0], "rwx")
        rout, t_rout = prep_ws(k, es, I["rnn_w_out"][0], "rwout", DRNN, D)
        w11 = prep_ffn(1, 1)
        ffn(1, 0, w10, XA, XB)
        GATE = dram(nc, "GATE", [RC, 128, ntok], BF16)
        RNNBR = dram(nc, "RNNBR", [RC, 128, ntok], F32)
        YRNN = dram(nc, "YRNN", [RC, 128, ntok], BF16, dk)
        rnn_inproj_phase(k, XB, rin, [t_rin], GATE, RNNBR, ntok)
        rnn_core_phase(k, GATE, RNNBR, wa, wx, [t_wa, t_wx], I["c_vecs"], YRNN, ntok)
        sublayer_phase(k, "proj_fm", XB, XA if upto > 2 else out, rout, RC, [t_rout], I["ln_g"][1, 1], I["ln_b"][1, 1],
                       ntok, ALPHA, LN_EPS, Y=YRNN)
        if upto <= 2:
            return nc
        ffn(1, 1, w11, XA, out)
    return nc


_PROG = {}
_LAST = {}
SPREAD = True


def kernel(x, ln_g, ln_b, ffn_w_gate, ffn_w_up, ffn_w_down, attn_w_in, ret_gn_g, ret_gn_b, attn_w_out,
           rnn_w_in, rnn_conv_w, rnn_conv_b, rnn_gate_a_w, rnn_gate_a_b, rnn_gate_x_w, rnn_gate_x_b,
           rnn_lambda, rnn_w_out):
    f32 = lambda a: np.ascontiguousarray(np.asarray(a, dtype=np.float32))
    x = f32(x)
    B, S, _ = x.shape
    if S not in _PROG:
        _PROG[S] = build_program(S)
    nc = _PROG[S]
    C = consts(S, 0)
    vecs = np.zeros((9, DRNN), np.float32)
    vecs[0:4] = f32(rnn_conv_w)[0]
    vecs[4] = f32(rnn_conv_b)[0]
    vecs[5] = f32(rnn_gate_a_b)[0]
    vecs[6] = f32(rnn_gate_x_b)[0]
    vecs[7] = f32(rnn_lambda)[0]
    shared = dict(ln_g=f32(ln_g), ln_b=f32(ln_b), ffn_w_gate=f32(ffn_w_gate), ffn_w_up=f32(ffn_w_up),
                  ffn_w_down=f32(ffn_w_down), attn_w_in=f32(attn_w_in), ret_gn_g=f32(ret_gn_g),
                  ret_gn_b=f32(ret_gn_b), attn_w_out=f32(attn_w_out), rnn_w_in=f32(rnn_w_in),
                  rnn_gate_a_w=f32(rnn_gate_a_w), rnn_gate_x_w=f32(rnn_gate_x_w), rnn_w_out=f32(rnn_w_out),
                  c_cos=C["cos"], c_sin=C["sin"], c_ksc=C["ksc"], c_qdec=C["qdec"], c_dec=C["dec"],
                  c_causal=C["causal"],
                  c_vecs=np.ascontiguousarray(vecs.reshape(9, RC, 128).transpose(2, 0, 1).reshape(128, 9 * RC)))
    if B == 4 and SPREAD:
        act_cores = [0, 1, 4, 5]
        zeros = {n: np.zeros_like(v) for n, v in shared.items()}
        zeros["x"] = np.zeros_like(x[0])
        in_maps = [zeros] * NCORES
        in_maps = list(in_maps)
        for b, c in enumerate(act_cores):
            in_maps[c] = dict(shared, x=x[b])
        res = run_bass_kernel_spmd(nc, in_maps, core_ids=list(range(NCORES)))
        outs = [res.results[c] for c in act_cores]
    else:
        in_maps = [dict(shared, x=x[b]) for b in range(B)]
        res = run_bass_kernel_spmd(nc, in_maps, core_ids=list(range(B)))
        outs = res.results
    _LAST["r"] = outs[0]
    return np.stack([np.asarray(r["out"], dtype=np.float32) for r in outs], axis=0)
```

```python
import math
from contextlib import ExitStack

import numpy as np
import concourse.bass as bass
import concourse.mybir as mybir
from concourse.bass_utils import run_bass_kernel_spmd

F32 = mybir.dt.float32
BF16 = mybir.dt.bfloat16
AF = mybir.ActivationFunctionType
ALU = mybir.AluOpType

D = 2048
DFF = 5632
KC = D // 128
FC = DFF // 128
DEPTH = 2
ALPHA = (2.0 * DEPTH) ** 0.25
LN_EPS = 1e-5
NCORES = 8
TOK = 4096
TILE = 512


class EngW:
    def __init__(self, nc, es, name, eng):
        self.nc, self.name, self.eng = nc, name, eng
        self.sem = es.enter_context(nc.semaphore("pg_" + name))
        self.count = 0
        self.waited = {}

    def wait(self, *tickets):
        for t in tickets:
            if t is None:
                continue
            if isinstance(t, list):
                self.wait(*t)
                continue
            sem, val, key = t
            if self.waited.get(key, 0) >= val:
                continue
            self.eng.wait_ge(sem, val)
            self.waited[key] = val

    def done(self, inst):
        self.count += 1
        inst.then_inc(self.sem, 1)
        return (self.sem, self.count, self.name)

    def last(self):
        return (self.sem, self.count, self.name) if self.count else None


class DmaSem:
    _n = 0

    def __init__(self, nc, es, name):
        DmaSem._n += 1
        self.key = "dma_%d" % DmaSem._n
        self.sem = es.enter_context(nc.semaphore(self.key))
        self.count = 0

    def done(self, inst):
        self.count += 16
        inst.then_inc(self.sem, 16)
        return (self.sem, self.count, self.key)

    def last(self):
        return (self.sem, self.count, self.key) if self.count else None


class Ring:
    def __init__(self, bufs):
        self.bufs = bufs
        self.free = [None] * len(bufs)
        self.i = 0

    def next(self):
        j = self.i % len(self.bufs)
        self.i += 1
        return j


class K:
    def __init__(self, nc, es):
        self.nc, self.es = nc, es
        self.pe = EngW(nc, es, "pe", nc.tensor)
        self.act = EngW(nc, es, "act", nc.scalar)
        self.dve = EngW(nc, es, "dve", nc.vector)
        self.pool = EngW(nc, es, "pool", nc.gpsimd)
        self.sp = EngW(nc, es, "sp", nc.sync)
        self.sem_pool = []
        self.nsem = 0

    _uid = 0

    def sb(self, es, name, shape, dt):
        K._uid += 1
        return es.enter_context(self.nc.sbuf_tensor("%s_%d" % (name, K._uid), shape, dt))

    def ps(self, es, name, shape, dt):
        K._uid += 1
        return es.enter_context(self.nc.psum_tensor("%s_%d" % (name, K._uid), shape, dt))

    def dsem(self, es, name):
        if self.sem_pool and self.nsem >= 80:
            self.sem_pool.sort(key=lambda d: d.count)
            ds = self.sem_pool.pop(0)
        else:
            ds = DmaSem(self.nc, self.es, name)
            self.nsem += 1
        es.callback(self.sem_pool.append, ds)
        return ds


def dram(nc, name, shape, dt, kind="Internal"):
    return nc.dram_tensor(name, list(shape), dt, kind=kind).ap()


def prep_ws(k, es, w_src, name, kdim, ndim):
    nc = k.nc
    nkc, nnc = kdim // 128, ndim // 128
    out = dram(nc, name, [nnc, 128, nkc, 128], BF16)
    ds = k.dsem(es, name)
    t = None
    for c in range(nnc):
        src = w_src[:, c * 128:(c + 1) * 128].rearrange("(kc p) f -> p kc f", p=128)
        t = ds.done(nc.gpsimd.dma_start(out=out[c], in_=src))
    return out, t


def sublayer_phase(k, mode, X_in, X_out, wd, nk, wtick, g_row, b_row, ntok, cx, eps, wg=None, wu=None, Y=None):
    nc = k.nc
    pe, act, dve, pool, sp = k.pe, k.act, k.dve, k.pool, k.sp
    ntiles = ntok // TILE
    ffn = mode == "ffn"
    with ExitStack() as es:
        xs = k.sb(es, "f_xs", [128, 4, D], F32)
        xbw = D if mode != "proj_tm" else nk * 128
        xb = [k.sb(es, "f_xb%d" % i, [128, xbw], BF16) for i in range(4)]
        hid = k.sb(es, "f_hid", [128, nk, TILE], BF16)
        wds = [k.sb(es, "f_wd%d" % i, [128, nk, 128], BF16) for i in range(2)]
        ytmp = [k.sb(es, "f_yt%d" % i, [128, TILE], F32) for i in range(2)]
        grep = k.sb(es, "f_g", [128, D], F32)
        brep = k.sb(es, "f_b", [128, D], F32)
        identb = k.sb(es, "f_idb", [128, 128], BF16)
        identf = k.sb(es, "f_idf", [128, 128], F32)
        st = k.sb(es, "f_st", [128, 4, 6], F32)
        mv = k.sb(es, "f_mv", [128, 8], F32)
        pT = [k.ps(es, "f_pT%d" % i, [128, 512], BF16) for i in range(2)]
        acc = [k.ps(es, "f_acc%d" % i, [128, 512], F32) for i in range(4)]
        pZ = [k.ps(es, "f_pZ%d" % i, [128, 512], F32) for i in range(2)]
        if ffn:
            stage = [k.sb(es, "f_stage%d" % i, [128, D], F32) for i in range(2)]
            hT = k.sb(es, "f_hT", [128, KC, TILE], BF16)
            wgs = [k.sb(es, "f_wg%d" % i, [128, KC, 128], BF16) for i in range(3)]
            wus = [k.sb(es, "f_wu%d" % i, [128, KC, 128], BF16) for i in range(3)]
            sgb = [k.sb(es, "f_sg%d" % i, [128, TILE], F32) for i in range(2)]
            s_stage = [k.dsem(es, "stage%d" % i) for i in range(2)]
            s_wgu = [k.dsem(es, "wgu%d" % i) for i in range(3)]
        s_xb = [k.dsem(es, "xb%d" % i) for i in range(4)]
        s_wd = [k.dsem(es, "wd%d" % i) for i in range(2)]
        s_xs = k.dsem(es, "xs")
        s_gb = k.dsem(es, "gb")
        s_out = k.dsem(es, "out")
        s_hid = k.dsem(es, "hid")

        t_gb = s_gb.done(nc.sync.dma_start(out=grep[:], in_=g_row.partition_broadcast(128)))
        t_gb = s_gb.done(nc.sync.dma_start(out=brep[:], in_=b_row.partition_broadcast(128)))
        for idt in (identb, identf):
            t0 = pool.done(nc.gpsimd.memset(idt[:], 1.0))
            pool.wait(t0)
            t_id = pool.done(nc.gpsimd.affine_select(out=idt[:], in_=idt[:], pattern=[[-1, 128]],
                                                     compare_op=ALU.is_equal, fill=0.0, base=0,
                                                     channel_multiplier=1))

        stage_free = [None, None]
        xb_free = [None] * 4
        pT_free = [None, None]
        acc_free = [None] * 4
        pZ_free = [None, None]
        wgu_free = [None] * 3
        wd_free = [None] * 2
        sg_free = [None, None]
        yt_free = [None, None]
        cnt = dict(stage=0, xb=0, pT=0, acc=0, pZ=0, wgu=0, wd=0, sg=0, yt=0)
        state = dict(hT_free=None, hid_free=None, xs_free=None, st_free=None, xs_ready=None)

        def load_xs(t):
            sp.wait(state["xs_free"])
            for s in range(4):
                r0 = t * TILE + s * 128
                state["xs_ready"] = s_xs.done(nc.sync.dma_start(out=xs[:, s, :], in_=X_in[r0:r0 + 128, :]))

        def emit_T(t, src, is_f32, nkc, dest, dest_free_key):
            for s in range(4):
                r0 = t * TILE + s * 128
                xi = cnt["xb"] % 4
                cnt["xb"] += 1
                if is_f32:
                    si = cnt["stage"] % 2
                    cnt["stage"] += 1
                    sp.wait(stage_free[si])
                    t_ld = s_stage[si].done(nc.sync.dma_start(out=stage[si][:], in_=src[r0:r0 + 128, :]))
                    pool.wait(t_ld, xb_free[xi])
                    t_c = pool.done(nc.gpsimd.tensor_copy(out=xb[xi][:], in_=stage[si][:]))
                    stage_free[si] = t_c
                else:
                    sp.wait(xb_free[xi])
                    t_c = s_xb[xi].done(nc.sync.dma_start(out=xb[xi][:], in_=src[r0:r0 + 128, :]))
                ngrp = (nkc + 3) // 4
                for q in range(ngrp):
                    n4 = min(4, nkc - 4 * q)
                    pi = cnt["pT"] % 2
                    cnt["pT"] += 1
                    pe.wait(t_c, pT_free[pi], t_id)
                    for j in range(n4):
                        kc = 4 * q + j
                        ins = nc.tensor.transpose(pT[pi][:, j * 128:(j + 1) * 128],
                                                  xb[xi][:, kc * 128:(kc + 1) * 128], identb[:])
                    t_tr = pe.done(ins)
                    ev = dve if (q % 2 == 0) else act
                    ev.wait(t_tr, state[dest_free_key])
                    dst = dest[:, 4 * q:4 * q + n4, s * 128:(s + 1) * 128]
                    srcp = pT[pi][:, 0:n4 * 128].rearrange("p (a b) -> p a b", b=128)
                    if ev is dve:
                        t_ev = dve.done(nc.vector.tensor_copy(out=dst, in_=srcp))
                    else:
                        t_ev = act.done(nc.scalar.copy(out=dst, in_=srcp))
                    pT_free[pi] = t_ev
                xb_free[xi] = t_tr
            return [dve.last(), act.last()]

        def emit_GU(t, hT_ready):
            for fc in range(FC):
                wi = cnt["wgu"] % 3
                cnt["wgu"] += 1
                sp.wait(wgu_free[wi], wtick)
                s_wgu[wi].done(nc.sync.dma_start(out=wgs[wi][:], in_=wg[fc]))
                t_w = s_wgu[wi].done(nc.sync.dma_start(out=wus[wi][:], in_=wu[fc]))
                if fc == 22:
                    load_xs(t)
                ag = cnt["acc"] % 4
                au = (cnt["acc"] + 1) % 4
                cnt["acc"] += 2
                pe.wait(t_w, hT_ready, acc_free[ag])
                for kc in range(KC):
                    ins = nc.tensor.matmul(acc[ag][:], lhsT=wgs[wi][:, kc, :], rhs=hT[:, kc, :],
                                           start=(kc == 0), stop=(kc == KC - 1))
                t_g = pe.done(ins)
                pe.wait(acc_free[au])
                for kc in range(KC):
                    ins = nc.tensor.matmul(acc[au][:], lhsT=wus[wi][:, kc, :], rhs=hT[:, kc, :],
                                           start=(kc == 0), stop=(kc == KC - 1))
                t_u = pe.done(ins)
                wgu_free[wi] = t_u
                gi = cnt["sg"] % 2
                cnt["sg"] += 1
                act.wait(t_g, sg_free[gi])
                t_s = act.done(nc.scalar.activation(out=sgb[gi][:], in_=acc[ag][:], func=AF.Silu))
                acc_free[ag] = t_s
                dve.wait(t_s, t_u, state["hid_free"])
                t_h = dve.done(nc.vector.tensor_tensor(out=hid[:, fc, :], in0=sgb[gi][:], in1=acc[au][:],
                                                       op=ALU.mult))
                sg_free[gi] = t_h
                acc_free[au] = t_h
            state["hT_free"] = t_u
            return t_h

        def emit_D(t, hid_ready):
            pend = None
            t_res = None

            def emit_tr(p):
                oc, yi, t_cp = p
                zi = cnt["pZ"] % 2
                cnt["pZ"] += 1
                pe.wait(t_cp, pZ_free[zi], t_id)
                for s in range(4):
                    ins = nc.tensor.transpose(pZ[zi][:, s * 128:(s + 1) * 128], ytmp[yi][:, s * 128:(s + 1) * 128],
                                              identf[:])
                t_tr = pe.done(ins)
                yt_free[yi] = t_tr
                dve.wait(t_tr, state["xs_ready"])
                t_r = dve.done(nc.vector.scalar_tensor_tensor(
                    out=xs[:, :, oc * 128:(oc + 1) * 128], in0=xs[:, :, oc * 128:(oc + 1) * 128],
                    scalar=float(cx), in1=pZ[zi][:].rearrange("p (a b) -> p a b", b=128),
                    op0=ALU.mult, op1=ALU.add))
                pZ_free[zi] = t_r
                return t_r

            for oc in range(KC):
                wi = cnt["wd"] % 2
                cnt["wd"] += 1
                sp.wait(wd_free[wi], wtick)
                t_w = s_wd[wi].done(nc.sync.dma_start(out=wds[wi][:], in_=wd[oc]))
                ai = cnt["acc"] % 4
                cnt["acc"] += 1
                pe.wait(t_w, hid_ready, acc_free[ai])
                for fc in range(nk):
                    ins = nc.tensor.matmul(acc[ai][:], lhsT=wds[wi][:, fc, :], rhs=hid[:, fc, :],
                                           start=(fc == 0), stop=(fc == nk - 1))
                t_m = pe.done(ins)
                wd_free[wi] = t_m
                yi = cnt["yt"] % 2
                cnt["yt"] += 1
                act.wait(t_m, yt_free[yi])
                t_cp = act.done(nc.scalar.copy(out=ytmp[yi][:], in_=acc[ai][:]))
                acc_free[ai] = t_cp
                if pend is not None:
                    t_res = emit_tr(pend)
                pend = (oc, yi, t_cp)
            state["hid_free"] = t_m
            t_res = emit_tr(pend)
            return t_res

        def emit_LN(t, t_res):
            t_st = None
            for s in range(4):
                dve.wait(t_res, state["st_free"])
                for c in range(4):
                    t_b = dve.done(nc.vector.bn_stats(out=st[:, c, :], in_=xs[:, s, c * 512:(c + 1) * 512]))
                dve.wait(t_b)
                t_a = dve.done(nc.vector.bn_aggr(out=mv[:, 0:2], in_=st[:].rearrange("p a b -> p (a b)")))
                dve.wait(t_a)
                t_e = dve.done(nc.vector.tensor_scalar(out=mv[:, 2:3], in0=mv[:, 1:2], scalar1=float(eps),
                                                       scalar2=None, op0=ALU.add))
                act.wait(t_e)
                t_q = act.done(nc.scalar.activation(out=mv[:, 3:4], in_=mv[:, 2:3], func=AF.Sqrt))
                dve.wait(t_q)
                t_r = dve.done(nc.vector.reciprocal(out=mv[:, 4:5], in_=mv[:, 3:4]))
                dve.wait(t_r)
                t_n = dve.done(nc.vector.scalar_tensor_tensor(out=mv[:, 5:6], in0=mv[:, 0:1], scalar=-1.0,
                                                              in1=mv[:, 4:5], op0=ALU.mult, op1=ALU.mult))
                act.wait(t_n)
                t_x = act.done(nc.scalar.activation(out=xs[:, s, :], in_=xs[:, s, :], func=AF.Identity,
                                                    bias=mv[:, 5:6], scale=mv[:, 4:5]))
                state["st_free"] = t_x
                dve.wait(t_x, t_gb)
                t_p = dve.done(nc.vector.tensor_tensor(out=xs[:, s, :], in0=xs[:, s, :], in1=grep[:], op=ALU.mult))
                dve.wait(t_p)
                t_p = dve.done(nc.vector.tensor_tensor(out=xs[:, s, :], in0=xs[:, s, :], in1=brep[:], op=ALU.add))
                pool.wait(t_p)
                r0 = t * TILE + s * 128
                t_st = s_out.done(nc.gpsimd.dma_start(out=X_out[r0:r0 + 128, :], in_=xs[:, s, :]))
            state["xs_free"] = t_st
            return t_st

        t_out = None
        if ffn:
            hT_ready = emit_T(0, X_in, True, KC, hT, "hT_free")
            for t in range(ntiles):
                hid_ready = emit_GU(t, hT_ready)
                if t + 1 < ntiles:
                    hT_ready = emit_T(t + 1, X_in, True, KC, hT, "hT_free")
                t_res = emit_D(t, hid_ready)
                t_out = emit_LN(t, t_res)
        else:
            for t in range(ntiles):
                if mode == "proj_tm":
                    hid_ready = emit_T(t, Y, False, nk, hid, "hid_free")
                else:
                    sp.wait(state["hid_free"])
                    for c in range(nk):
                        hid_ready = s_hid.done(nc.sync.dma_start(out=hid[:, c, :],
                                                                 in_=Y[c, :, t * TILE:(t + 1) * TILE]))
                load_xs(t)
                t_res = emit_D(t, hid_ready)
                t_out = emit_LN(t, t_res)
        drain(k, [t_out])
        return t_out


def drain(k, extra=()):
    engs = (k.pe, k.act, k.dve, k.pool, k.sp)
    fin = [e.last() for e in engs] + list(extra)
    for e in engs:
        e.wait(fin)


NH = 8
KINDS = ["q", "k", "v", "g", "mq", "mk", "mv"]


def prep_tm(k, es, w_src, name, kdim, ndim, gw=512):
    nc = k.nc
    nkc, ncg = kdim // 128, ndim // gw
    out = dram(nc, name, [ncg, 128, nkc, gw], BF16)
    ds = k.dsem(es, name)
    t = None
    for c in range(ncg):
        src = w_src[:, c * gw:(c + 1) * gw].rearrange("(kc p) f -> p kc f", p=128)
        t = ds.done(nc.gpsimd.dma_start(out=out[c], in_=src))
    return out, t


def attn_inproj_phase(k, X_in, w_in, wtick, cos_d, sin_d, ksc_d, outs, ntok):
    nc = k.nc
    pe, act, dve, pool, sp = k.pe, k.act, k.dve, k.pool, k.sp
    QT, KT, KTM, VTM, GTM, MQT, MKT, MVTM, KMEAN = outs
    ntiles = ntok // TILE
    with ExitStack() as es:
        stage = [k.sb(es, "a_stage%d" % i, [128, D], F32) for i in range(2)]
        xb = [k.sb(es, "a_xb%d" % i, [128, D], BF16) for i in range(4)]
        hT = k.sb(es, "a_hT", [128, KC, TILE], BF16)
        wt = [k.sb(es, "a_wt%d" % i, [128, KC, 512], BF16) for i in range(2)]
        xsb = [k.sb(es, "a_xsb%d" % i, [128, 512], F32) for i in range(2)]
        ra = [k.sb(es, "a_ra%d" % i, [128, 512], F32) for i in range(2)]
        rb = [k.sb(es, "a_rb%d" % i, [128, 512], F32) for i in range(2)]
        tmb = [k.sb(es, "a_tmb%d" % i, [128, 512], BF16) for i in range(4)]
        fmt = [k.sb(es, "a_fmt%d" % i, [128, 4, TILE], BF16) for i in range(2)]
        cosr = [k.sb(es, "a_cos%d" % i, [128, 4, 64], F32) for i in range(2)]
        sinr = [k.sb(es, "a_sin%d" % i, [128, 4, 64], F32) for i in range(2)]
        ksc = k.sb(es, "a_ksc", [128, NH], F32)
        onesb = k.sb(es, "a_ones", [128, 2], BF16)
        kmrow = [k.sb(es, "a_kmrow%d" % i, [1, 512], F32) for i in range(2)]
        identb = k.sb(es, "a_idb", [128, 128], BF16)
        pT_t = k.ps(es, "a_pT", [128, 512], BF16)
        pT = [pT_t[:, :], pT_t[:, :]]
        acc = [k.ps(es, "a_acc%d" % i, [128, 512], F32) for i in range(4)]
        ptr = [k.ps(es, "a_ptr%d" % i, [128, 512], BF16)[:, :] for i in range(2)]
        pkm_t = k.ps(es, "a_pkm", [1, 512], F32)
        pkm = [pkm_t, pkm_t]

        s_stage = [k.dsem(es, "astage%d" % i) for i in range(2)]
        s_wt = [k.dsem(es, "awt%d" % i) for i in range(2)]
        s_cs = [k.dsem(es, "acs%d" % i) for i in range(2)]
        s_c = k.dsem(es, "aconst")
        s_st = [k.dsem(es, "ast%d" % i) for i in range(4)]
        s_fm = [k.dsem(es, "afm%d" % i) for i in range(2)]
        s_km = [k.dsem(es, "akm%d" % i) for i in range(2)]

        t_c = s_c.done(nc.sync.dma_start(out=ksc[:], in_=ksc_d[:, :]))
        t0 = pool.done(nc.gpsimd.memset(onesb[:], 1.0))
        t0 = pool.done(nc.gpsimd.memset(identb[:], 1.0))
        pool.wait(t0)
        t_id = pool.done(nc.gpsimd.affine_select(out=identb[:], in_=identb[:], pattern=[[-1, 128]],
                                                 compare_op=ALU.is_equal, fill=0.0, base=0, channel_multiplier=1))

        stage_free = [None, None]
        xb_free = [None] * 4
        pT_free = [None, None]
        acc_free = [None] * 4
        wt_free = [None, None]
        xsb_free = [None, None]
        ra_free = [None, None]
        rb_free = [None, None]
        tmb_free = [None] * 4
        ptr_free = [None, None]
        fmt_free = [None, None]
        cs_free = [None, None]
        pkm_free = [None, None]
        kmrow_free = [None, None]
        cnt = dict(stage=0, xb=0, pT=0, acc=0, wt=0, xsb=0, r=0, tmb=0, ptr=0, fmt=0, km=0)
        state = dict(hT_free=None)

        def emit_T(t):
            for s in range(4):
                r0 = t * TILE + s * 128
                xi = cnt["xb"] % 4
                cnt["xb"] += 1
                si = cnt["stage"] % 2
                cnt["stage"] += 1
                sp.wait(stage_free[si])
                t_ld = s_stage[si].done(nc.sync.dma_start(out=stage[si][:], in_=X_in[r0:r0 + 128, :]))
                pool.wait(t_ld, xb_free[xi])
                t_cc = pool.done(nc.gpsimd.tensor_copy(out=xb[xi][:], in_=stage[si][:]))
                stage_free[si] = t_cc
                for q in range(4):
                    pi = 0
                    pe.wait(t_cc, pT_free[pi], t_id)
                    for j in range(4):
                        kc = 4 * q + j
                        ins = nc.tensor.transpose(pT[pi][:, j * 128:(j + 1) * 128],
                                                  xb[xi][:, kc * 128:(kc + 1) * 128], identb[:])
                    t_tr = pe.done(ins)
                    ev = dve if (q % 2 == 0) else act
                    ev.wait(t_tr, state["hT_free"])
                    dst = hT[:, 4 * q:4 * q + 4, s * 128:(s + 1) * 128]
                    srcp = pT[pi].rearrange("p (a b) -> p a b", b=128)
                    if ev is dve:
                        t_ev = dve.done(nc.vector.tensor_copy(out=dst, in_=srcp))
                    else:
                        t_ev = act.done(nc.scalar.copy(out=dst, in_=srcp))
                    pT_free[pi] = t_ev
                xb_free[xi] = t_tr
            return [dve.last(), act.last()]

        TMDST = {"k": KTM, "v": VTM, "g": GTM, "mv": MVTM}
        FMDST = {"q": QT, "k": KT, "mq": MQT, "mk": MKT}

        def post(t, cg, s, ai, t_m, ci, fj):
            kind = KINDS[cg // 2]
            dbg = getattr(k, "dbg", 9)
            if dbg < 4 and kind in ("q", "k"):
                kind = "mq" if kind == "q" else "mk"
            hb = (cg % 2) * 4
            r0 = t * TILE + s * 128
            j = cnt["tmb"] % 4
            cnt["tmb"] += 1
            if kind in ("q", "k"):
                xj = cnt["xsb"] % 2
                cnt["xsb"] += 1
                act.wait(t_m, xsb_free[xj])
                t_x = act.done(nc.scalar.copy(out=xsb[xj][:], in_=acc[ai][:]))
                acc_free[ai] = t_x
                r = cnt["r"] % 2
                cnt["r"] += 1
                x8 = xsb[xj][:].rearrange("p (a b) -> p a b", b=64)
                x42 = xsb[xj][:].rearrange("p (h a b) -> p h a b", a=2, b=64)
                rb42 = rb[r][:].rearrange("p (h a b) -> p h a b", a=2, b=64)
                cosb = cosr[ci][:, s, :].unsqueeze(1).to_broadcast([128, 8, 64])
                sinb = sinr[ci][:, s, :].unsqueeze(1).to_broadcast([128, 4, 64])
                dve.wait(t_x, ra_free[r], state["cs_ready"])
                t_a = dve.done(nc.vector.tensor_tensor(out=ra[r][:].rearrange("p (a b) -> p a b", b=64), in0=x8,
                                                       in1=cosb, op=ALU.mult))
                dve.wait(t_x, rb_free[r], state["cs_ready"])
                dve.done(nc.vector.tensor_tensor(out=rb42[:, :, 0, :], in0=x42[:, :, 1, :], in1=sinb, op=ALU.mult))
                t_b = dve.done(nc.vector.tensor_tensor(out=rb42[:, :, 1, :], in0=x42[:, :, 0, :], in1=sinb,
                                                       op=ALU.mult))
                xsb_free[xj] = [t_a, t_b]
                ra42 = ra[r][:].rearrange("p (h a b) -> p h a b", a=2, b=64)
                dve.wait(t_a, t_b, tmb_free[j])
                if kind == "q":
                    o42 = tmb[j][:].rearrange("p (h a b) -> p h a b", a=2, b=64)
                else:
                    o42 = ra42
                dve.done(nc.vector.tensor_tensor(out=o42[:, :, 0, :], in0=ra42[:, :, 0, :], in1=rb42[:, :, 0, :],
                                                 op=ALU.subtract))
                t_tm = dve.done(nc.vector.tensor_tensor(out=o42[:, :, 1, :], in0=ra42[:, :, 1, :],
                                                        in1=rb42[:, :, 1, :], op=ALU.add))
                if kind == "k":
                    dve.wait(t_tm, t_c)
                    t_tm = dve.done(nc.vector.tensor_tensor(
                        out=tmb[j][:].rearrange("p (h d) -> p h d", d=128),
                        in0=ra[r][:].rearrange("p (h d) -> p h d", d=128),
                        in1=ksc[:, hb:hb + 4].unsqueeze(2).to_broadcast([128, 4, 128]), op=ALU.mult))
                ra_free[r] = t_tm
                rb_free[r] = t_tm
            else:
                act.wait(t_m, tmb_free[j])
                if kind == "g":
                    t_tm = act.done(nc.scalar.activation(out=tmb[j][:], in_=acc[ai][:], func=AF.Silu))
                else:
                    t_tm = act.done(nc.scalar.copy(out=tmb[j][:], in_=acc[ai][:]))
                acc_free[ai] = t_tm
            frees = []
            if kind in TMDST and dbg >= 2:
                pool.wait(t_tm)
                frees.append(s_st[j].done(nc.gpsimd.dma_start(
                    out=TMDST[kind][r0:r0 + 128, hb * 128:hb * 128 + 512], in_=tmb[j][:])))
            tmb_free[j] = frees
            if kind in FMDST and dbg >= 3:
                return (t, cg, s, j, t_tm, fj, kind, hb, frees)
            return None

        def pe_post(p):
            t, cg, s, j, t_tm, fj, kind, hb, frees = p
            pi = cnt["ptr"] % 2
            cnt["ptr"] += 1
            pe.wait(t_tm, ptr_free[pi], t_id)
            for hh in range(4):
                ins = nc.tensor.transpose(ptr[pi][:, hh * 128:(hh + 1) * 128], tmb[j][:, hh * 128:(hh + 1) * 128],
                                          identb[:])
            t_tr = pe.done(ins)
            frees.append(t_tr)
            ev = dve if (s % 2 == 0) else act
            ev.wait(t_tr, fmt_free[fj] if s == 0 else None)
            dst = fmt[fj][:, :, s * 128:(s + 1) * 128]
            srcp = ptr[pi].rearrange("p (a b) -> p a b", b=128)
            if ev is dve:
                t_ev = dve.done(nc.vector.tensor_copy(out=dst, in_=srcp))
            else:
                t_ev = act.done(nc.scalar.copy(out=dst, in_=srcp))
            ptr_free[pi] = t_ev
            state["fm_evs"].append(t_ev)
            if kind == "mk" and not getattr(k, "no_kmean", False):
                b = s // 2
                kmi = 0
                if s % 2 == 0:
                    pe.wait(pkm_free[kmi])
                ins = nc.tensor.matmul(pkm[kmi][0:1, :], lhsT=onesb[:, 0:1], rhs=tmb[j][:], start=(s % 2 == 0),
                                       stop=(s % 2 == 1))
                t_k = pe.done(ins)
                frees.append(t_k)
                if s % 2 == 1:
                    act.wait(t_k, kmrow_free[kmi])
                    t_r = act.done(nc.scalar.activation(out=kmrow[kmi][:], in_=pkm[kmi][0:1, :], func=AF.Copy,
                                                        scale=1.0 / 256.0))
                    pkm_free[kmi] = t_r
                    pool.wait(t_r)
                    kmrow_free[kmi] = s_km[kmi].done(nc.gpsimd.dma_start(
                        out=KMEAN[2 * t + b:2 * t + b + 1, hb * 128:hb * 128 + 512], in_=kmrow[kmi][:]))
            if s == 3:
                pool.wait(state["fm_evs"])
                fmt_free[fj] = s_fm[fj].done(nc.gpsimd.dma_start(
                    out=FMDST[kind][hb:hb + 4, :, t * TILE:(t + 1) * TILE].rearrange("h d t -> d h t"),
                    in_=fmt[fj][:]))
                state["fm_evs"] = []

        state["fm_evs"] = []
        if getattr(k, "dbg", 9) == -1:
            drain(k, [t_c])
            return
        hT_ready = emit_T(0)
        if getattr(k, "dbg", 9) == 0:
            drain(k, [t_c])
            return
        for t in range(ntiles):
            ci = t % 2
            sp.wait(cs_free[ci])
            s_cs[ci].done(nc.sync.dma_start(out=cosr[ci][:], in_=cos_d[t * TILE:(t + 1) * TILE, :]
                                            .rearrange("(s p) j -> p s j", p=128)))
            state["cs_ready"] = s_cs[ci].done(nc.sync.dma_start(
                out=sinr[ci][:], in_=sin_d[t * TILE:(t + 1) * TILE, :].rearrange("(s p) j -> p s j", p=128)))
            pend = None
            for cg in range(14):
                wi = cnt["wt"] % 2
                cnt["wt"] += 1
                sp.wait(wt_free[wi], wtick)
                t_w = s_wt[wi].done(nc.sync.dma_start(out=wt[wi][:], in_=w_in[cg]))
                kind = KINDS[cg // 2]
                fj = None
                if kind in FMDST:
                    fj = cnt["fmt"] % 2
                    cnt["fmt"] += 1
                for s in range(4):
                    ai = cnt["acc"] % 4
                    cnt["acc"] += 1
                    pe.wait(t_w, hT_ready, acc_free[ai])
                    for kc in range(KC):
                        ins = nc.tensor.matmul(acc[ai][:], lhsT=hT[:, kc, s * 128:(s + 1) * 128], rhs=wt[wi][:, kc, :],
                                               start=(kc == 0), stop=(kc == KC - 1))
                    t_m = pe.done(ins)
                    if pend is not None:
                        pe_post(pend)
                    pend = post(t, cg, s, ai, t_m, ci, fj)
                wt_free[wi] = t_m
            state["hT_free"] = t_m
            if pend is not None:
                pe_post(pend)
                pend = None
            cs_free[ci] = [dve.last(), pool.last()]
            if t + 1 < ntiles:
                hT_ready = emit_T(t + 1)
        drain(k, [x.last() for x in s_st + s_fm + s_km])


def retention_phase(k, QT, KT, KTM, VTM, GTM, gng_row, gnb_row, causal_d, dec_d, qdec_d, YATT, ntok):
    nc = k.nc
    pe, act, dve, pool, sp = k.pe, k.act, k.dve, k.pool, k.sp
    ngrp = ntok // TILE
    with ExitStack() as es:
        qT4 = [k.sb(es, "r_qT%d" % i, [128, NH, TILE], BF16) for i in range(2)]
        kT4 = [k.sb(es, "r_kT%d" % i, [128, NH, TILE], BF16) for i in range(2)]
        ktm4 = [k.sb(es, "r_ktm%d" % i, [128, 4, 1024], BF16) for i in range(2)]
        vtm4 = [k.sb(es, "r_vtm%d" % i, [128, 4, 1024], BF16) for i in range(2)]
        gtm4 = [k.sb(es, "r_gtm%d" % i, [128, 4, 1024], BF16) for i in range(2)]
        S = k.sb(es, "r_S", [128, 1024], F32)
        Sb = [k.sb(es, "r_Sb%d" % i, [128, 1024], BF16) for i in range(2)]
        PT = [k.sb(es, "r_PT%d" % i, [128, 1024], BF16) for i in range(2)]
        ro = [k.sb(es, "r_ro%d" % i, [128, 1024], F32) for i in range(2)]
        ob = [k.sb(es, "r_ob%d" % i, [128, 1024], BF16) for i in range(2)]
        causal = k.sb(es, "r_causal", [128, 128], F32)
        dec = k.sb(es, "r_dec", [128, NH], F32)
        qdec = k.sb(es, "r_qdec", [128, NH], F32)
        gng = k.sb(es, "r_gng", [128, 1024], F32)
        gnb = k.sb(es, "r_gnb", [128, 1024], F32)
        st8 = k.sb(es, "r_st8", [128, NH, 6], F32)
        mv8 = k.sb(es, "r_mv8", [128, NH, 2], F32)
        rs = k.sb(es, "r_rs", [128, 3, NH], F32)
        pS = k.ps(es, "r_pS", [128, 1024], F32)
        pO = k.ps(es, "r_pO", [128, 1024], F32)
        pKV = k.ps(es, "r_pKV", [128, 1024], F32)
        s_ld = [k.dsem(es, "rld%d" % i) for i in range(2)]
        s_c = k.dsem(es, "rconst")
        s_o = [k.dsem(es, "rout%d" % i) for i in range(2)]

        s_c.done(nc.sync.dma_start(out=causal[:], in_=causal_d[:, :]))
        s_c.done(nc.sync.dma_start(out=dec[:], in_=dec_d[:, :]))
        s_c.done(nc.sync.dma_start(out=qdec[:], in_=qdec_d[:, :]))
        s_c.done(nc.sync.dma_start(out=gng[:], in_=gng_row.partition_broadcast(128)))
        t_c = s_c.done(nc.sync.dma_start(out=gnb[:], in_=gnb_row.partition_broadcast(128)))
        t0 = pool.done(nc.gpsimd.memset(S[:], 0.0))
        t_sb = pool.done(nc.gpsimd.memset(Sb[0][:], 0.0))
        t_S = t_sb

        def v3(ap):
            return ap.rearrange("p (h e) -> p h e", e=128)

        ld_free = [None, None]
        pS_free = pO_free = pKV_free = None
        PT_free = [None, None]
        ro_free = [None, None]
        ob_free = [None, None]
        Sb_free = [None, None]
        rs_free = None
        nchunk = 0
        for g in range(ngrp):
            li = g % 2
            sp.wait(ld_free[li])
            sl = slice(g * TILE, (g + 1) * TILE)
            s_ld[li].done(nc.sync.dma_start(out=qT4[li][:], in_=QT[:, :, sl].rearrange("h d t -> d h t")))
            s_ld[li].done(nc.sync.dma_start(out=kT4[li][:], in_=KT[:, :, sl].rearrange("h d t -> d h t")))
            s_ld[li].done(nc.sync.dma_start(out=ktm4[li][:], in_=KTM[sl, :].rearrange("(c p) f -> p c f", p=128)))
            s_ld[li].done(nc.sync.dma_start(out=vtm4[li][:], in_=VTM[sl, :].rearrange("(c p) f -> p c f", p=128)))
            t_ld = s_ld[li].done(nc.sync.dma_start(out=gtm4[li][:],
                                                   in_=GTM[sl, :].rearrange("(c p) f -> p c f", p=128)))
            for cc in range(4):
                c0 = cc * 128
                r0 = g * TILE + c0
                bi = nchunk % 2
                nchunk += 1
                pe.wait(t_ld, pS_free)
                for h in range(NH):
                    ins = nc.tensor.matmul(pS[:, h * 128:(h + 1) * 128], lhsT=kT4[li][:, h, c0:c0 + 128],
                                           rhs=qT4[li][:, h, c0:c0 + 128], start=True, stop=True)
                t_s = pe.done(ins)
                dve.wait(t_s, PT_free[bi], t_c)
                t_pt = dve.done(nc.vector.tensor_tensor(out=v3(PT[bi][:]), in0=v3(pS[:]),
                                                        in1=causal[:].unsqueeze(1).to_broadcast([128, NH, 128]),
                                                        op=ALU.mult))
                pS_free = t_pt
                pe.wait(t_pt, t_sb, pO_free)
                for h in range(NH):
                    hs = slice(h * 128, (h + 1) * 128)
                    nc.tensor.matmul(pO[:, hs], lhsT=PT[bi][:, hs], rhs=vtm4[li][:, cc, hs], start=True, stop=False)
                    ins = nc.tensor.matmul(pO[:, hs], lhsT=qT4[li][:, h, c0:c0 + 128], rhs=Sb[bi][:, hs],
                                           start=False, stop=True)
                t_o = pe.done(ins)
                PT_free[bi] = t_o
                Sb_free[bi] = t_o
                pe.wait(pKV_free)
                for h in range(NH):
                    hs = slice(h * 128, (h + 1) * 128)
                    ins = nc.tensor.matmul(pKV[:, hs], lhsT=ktm4[li][:, cc, hs], rhs=vtm4[li][:, cc, hs],
                                           start=True, stop=True)
                t_kv = pe.done(ins)
                dve.wait(t_kv, t_S)
                t_1 = dve.done(nc.vector.tensor_tensor(out=S[:], in0=S[:], in1=pKV[:], op=ALU.add))
                pKV_free = t_1
                dve.wait(t_1, t_c)
                t_2 = dve.done(nc.vector.tensor_tensor(out=v3(S[:]), in0=v3(S[:]),
                                                       in1=dec[:].unsqueeze(2).to_broadcast([128, NH, 128]),
                                                       op=ALU.mult))
                act.wait(t_2, Sb_free[1 - bi])
                t_sb = act.done(nc.scalar.copy(out=Sb[1 - bi][:], in_=S[:]))
                t_S = t_sb
                dve.wait(t_o, ro_free[bi], rs_free)
                t_r = dve.done(nc.vector.tensor_tensor(out=v3(ro[bi][:]), in0=v3(pO[:]),
                                                       in1=qdec[:].unsqueeze(2).to_broadcast([128, NH, 128]),
                                                       op=ALU.mult))
                pO_free = t_r
                dve.wait(t_r)
                for h in range(NH):
                    t_b = dve.done(nc.vector.bn_stats(out=st8[:, h, :], in_=ro[bi][:, h * 128:(h + 1) * 128]))
                dve.wait(t_b)
                for h in range(NH):
                    t_a = dve.done(nc.vector.bn_aggr(out=mv8[:, h, :], in_=st8[:, h, :]))
                dve.wait(t_a)
                t_e = dve.done(nc.vector.tensor_scalar(out=rs[:, 0, :], in0=mv8[:, :, 1], scalar1=LN_EPS, scalar2=None,
                                                       op0=ALU.add))
                act.wait(t_e)
                t_q = act.done(nc.scalar.activation(out=rs[:, 1, :], in_=rs[:, 0, :], func=AF.Sqrt))
                dve.wait(t_q)
                t_i = dve.done(nc.vector.reciprocal(out=rs[:, 2, :], in_=rs[:, 1, :]))
                dve.wait(t_i)
                t_nb = dve.done(nc.vector.scalar_tensor_tensor(out=rs[:, 1, :], in0=mv8[:, :, 0], scalar=-1.0,
                                                               in1=rs[:, 2, :], op0=ALU.mult, op1=ALU.mult))
                act.wait(t_nb)
                for h in range(NH):
                    hs = slice(h * 128, (h + 1) * 128)
                    t_p = act.done(nc.scalar.activation(out=ro[bi][:, hs], in_=ro[bi][:, hs], func=AF.Identity,
                                                        scale=rs[:, 2, h:h + 1], bias=rs[:, 1, h:h + 1]))
                rs_free = t_p
                dve.wait(t_p)
                t_p = dve.done(nc.vector.tensor_tensor(out=ro[bi][:], in0=ro[bi][:], in1=gng[:], op=ALU.mult))
                dve.wait(t_p)
                t_p = dve.done(nc.vector.tensor_tensor(out=ro[bi][:], in0=ro[bi][:], in1=gnb[:], op=ALU.add))
                dve.wait(t_p, ob_free[bi])
                t_f = dve.done(nc.vector.tensor_tensor(out=ob[bi][:], in0=ro[bi][:], in1=gtm4[li][:, cc, :],
                                                       op=ALU.mult))
                ro_free[bi] = t_f
                pool.wait(t_f)
                ob_free[bi] = s_o[bi].done(nc.gpsimd.dma_start(out=YATT[r0:r0 + 128, 0:1024], in_=ob[bi][:]))
            ld_free[li] = [pe.last(), dve.last()]
        drain(k, [x.last() for x in s_o])


def moba_phase(k, MQT, MKT, MVTM, KMEAN, causalT_d, YATT, ntok):
    nc = k.nc
    pe, act, dve, pool, sp = k.pe, k.act, k.dve, k.pool, k.sp
    nq = ntok // 128
    nblk = ntok // 256
    SC = 128.0 ** -0.5
    with ExitStack() as es:
        mqT = [k.sb(es, "m_q%d" % i, [128, ntok], BF16) for i in range(2)]
        mkT = [k.sb(es, "m_k%d" % i, [128, ntok], BF16) for i in range(2)]
        mv = [k.sb(es, "m_v%d" % i, [128, nq, 129], BF16) for i in range(2)]
        km = k.sb(es, "m_km", [nblk, 1024], F32)
        kmT = k.sb(es, "m_kmT", [128, NH, 32], BF16)
        identf = k.sb(es, "m_idf", [128, 128], F32)
        causalT = k.sb(es, "m_causal", [128, 128], F32)
        gm = k.sb(es, "m_gm", [128, 32], F32)
        top8 = k.sb(es, "m_top8", [128, 8], F32)
        sel = [k.sb(es, "m_sel%d" % i, [128, 32], F32) for i in range(2)]
        eo = [k.sb(es, "m_eo%d" % i, [128, 128], F32) for i in range(2)]
        PTo = [k.sb(es, "m_PTo%d" % i, [128, 2, 128], BF16) for i in range(2)]
        PTg = [k.sb(es, "m_PTg%d" % i, [128, 4, 128], BF16) for i in range(3)]
        O = [k.sb(es, "m_O%d" % i, [128, 132], F32) for i in range(2)]
        rinv = k.sb(es, "m_rinv", [128, 2], F32)
        mob = [k.sb(es, "m_mob%d" % i, [128, 128], BF16) for i in range(2)]
        pG = k.ps(es, "m_pG", [128, 512], F32)
        pSo = k.ps(es, "m_pSo", [128, 512], F32)
        pOo = k.ps(es, "m_pOo", [128, 512], F32)
        pS = [k.ps(es, "m_pS%d" % i, [128, 512], F32) for i in range(2)]
        pO2 = [k.ps(es, "m_pO2%d" % i, [128, 512], F32) for i in range(2)]
        s_h = [k.dsem(es, "mh%d" % i) for i in range(2)]
        s_c = k.dsem(es, "mconst")
        s_o = [k.dsem(es, "mout%d" % i) for i in range(2)]

        s_c.done(nc.sync.dma_start(out=km[:], in_=KMEAN[:, :]))
        t_c = s_c.done(nc.sync.dma_start(out=causalT[:], in_=causalT_d[:, :]))
        t0 = pool.done(nc.gpsimd.memset(identf[:], 1.0))
        pool.wait(t0)
        t_id = pool.done(nc.gpsimd.affine_select(out=identf[:], in_=identf[:], pattern=[[-1, 128]],
                                                 compare_op=ALU.is_equal, fill=0.0, base=0, channel_multiplier=1))
        for i in range(2):
            t_ones = pool.done(nc.gpsimd.memset(mv[i][:, :, 128:129], 1.0))
        t_prev = None
        for h in range(NH):
            pe.wait(t_c, t_id, t_prev)
            t_t = pe.done(nc.tensor.transpose(pG[:, 0:nblk], km[:, h * 128:(h + 1) * 128], identf[0:nblk, 0:nblk]))
            dve.wait(t_t)
            t_prev = dve.done(nc.vector.tensor_copy(out=kmT[:, h, 0:nblk], in_=pG[:, 0:nblk]))
        pG_free = t_prev

        h_free = [None, None]
        pSo_free = pOo_free = None
        pS_free = [None, None]
        pO2_free = [None, None]
        PTo_free = [None, None]
        PTg_free = [None] * 3
        O_free = [None, None]
        eo_free = [None, None]
        sel_free = [None, None]
        mob_free = [None, None]
        gm_t = None
        top_free = None
        rinv_free = None
        cnt = dict(pS=0, pO2=0, PTg=0, qi=0)

        for h in range(NH):
            hi = h % 2
            sp.wait(h_free[hi], t_ones)
            s_h[hi].done(nc.sync.dma_start(out=mqT[hi][:], in_=MQT[h]))
            s_h[hi].done(nc.sync.dma_start(out=mkT[hi][:], in_=MKT[h]))
            t_ld = s_h[hi].done(nc.sync.dma_start(
                out=mv[hi][:, :, 0:128], in_=MVTM[:, h * 128:(h + 1) * 128].rearrange("(c p) d -> p c d", p=128)))
            dve.wait(gm_t)
            gm_t = dve.done(nc.vector.memset(gm[:], -1e30))
            for i in range(nq):
                nb = i // 2
                qi = cnt["qi"] % 2
                cnt["qi"] += 1
                qs = slice(i * 128, (i + 1) * 128)
                r0 = i * 128
                use_sel = nb > 3
                if use_sel:
                    pe.wait(t_ld, pG_free)
                    t_g = pe.done(nc.tensor.matmul(pG[:, 0:32], lhsT=mqT[hi][:, qs], rhs=kmT[:, h, :],
                                                   start=True, stop=True))
                    dve.wait(t_g, gm_t, top_free)
                    t_gm = dve.done(nc.vector.tensor_copy(out=gm[:, 0:nb], in_=pG[:, 0:nb]))
                    pG_free = t_gm
                    dve.wait(t_gm)
                    t_t8 = dve.done(nc.vector.max(out=top8[:], in_=gm[:]))
                    dve.wait(t_t8, sel_free[qi])
                    t_sel = dve.done(nc.vector.tensor_scalar(out=sel[qi][:], in0=gm[:], scalar1=top8[:, 2:3],
                                                             scalar2=None, op0=ALU.is_ge))
                    gm_t = t_sel
                    top_free = t_sel
                ncs = 1 + (i % 2)
                pe.wait(t_ld, pSo_free)
                ins = nc.tensor.matmul(pSo[:, 0:128], lhsT=mkT[hi][:, qs], rhs=mqT[hi][:, qs], start=True, stop=True)
                if ncs == 2:
                    ins = nc.tensor.matmul(pSo[:, 128:256], lhsT=mkT[hi][:, (i - 1) * 128:i * 128], rhs=mqT[hi][:, qs],
                                           start=True, stop=True)
                t_so = pe.done(ins)
                act.wait(t_so, eo_free[qi], PTo_free[qi])
                t_e = act.done(nc.scalar.activation(out=eo[qi][:], in_=pSo[:, 0:128], func=AF.Exp, scale=SC))
                if ncs == 2:
                    t_e2 = act.done(nc.scalar.activation(out=PTo[qi][:, 1, :], in_=pSo[:, 128:256], func=AF.Exp,
                                                         scale=SC))
                else:
                    t_e2 = t_e
                pSo_free = t_e2
                dve.wait(t_e, t_c)
                t_pd = dve.done(nc.vector.tensor_tensor(out=PTo[qi][:, 0, :], in0=eo[qi][:], in1=causalT[:],
                                                        op=ALU.mult))
                eo_free[qi] = t_pd
                ng = (nb + 1) // 2

                def emit_S(g):
                    nbg = min(2, nb - 2 * g)
                    si = cnt["pS"] % 2
                    cnt["pS"] += 1
                    pe.wait(pS_free[si])
                    for b in range(nbg):
                        n = 2 * g + b
                        for c2 in range(2):
                            ks = slice(n * 256 + c2 * 128, n * 256 + (c2 + 1) * 128)
                            ins_ = nc.tensor.matmul(pS[si][:, (b * 2 + c2) * 128:(b * 2 + c2 + 1) * 128],
                                                    lhsT=mkT[hi][:, ks], rhs=mqT[hi][:, qs], start=True, stop=True)
                    t_s = pe.done(ins_)
                    pj = cnt["PTg"] % 3
                    cnt["PTg"] += 1
                    act.wait(t_s, PTg_free[pj])
                    t_p = act.done(nc.scalar.activation(
                        out=PTg[pj][:, 0:2 * nbg, :].rearrange("p a b -> p (a b)"), in_=pS[si][:, 0:nbg * 256],
                        func=AF.Exp, scale=SC))
                    pS_free[si] = t_p
                    return (g, nbg, pj, t_p)

                pend = emit_S(0) if ng > 0 else None
                pe.wait(t_pd, t_e2, pOo_free)
                ins = nc.tensor.matmul(pOo[:, 0:129], lhsT=PTo[qi][:, 0, :], rhs=mv[hi][:, i, :], start=True,
                                       stop=(ncs == 1))
                if ncs == 2:
                    ins = nc.tensor.matmul(pOo[:, 0:129], lhsT=PTo[qi][:, 1, :], rhs=mv[hi][:, i - 1, :], start=False,
                                           stop=True)
                t_oo = pe.done(ins)
                PTo_free[qi] = t_oo
                dve.wait(t_oo, O_free[qi])
                t_O = dve.done(nc.vector.tensor_copy(out=O[qi][:, 0:129], in_=pOo[:, 0:129]))
                pOo_free = t_O
                for g in range(ng):
                    nxt = emit_S(g + 1) if g + 1 < ng else None
                    _, nbg, pj, t_p = pend
                    oi = cnt["pO2"] % 2
                    cnt["pO2"] += 1
                    pe.wait(t_p, pO2_free[oi])
                    for b in range(nbg):
                        n = 2 * g + b
                        for c2 in range(2):
                            ins = nc.tensor.matmul(pO2[oi][:, b * 132:b * 132 + 129], lhsT=PTg[pj][:, b * 2 + c2, :],
                                                   rhs=mv[hi][:, n * 2 + c2, :], start=(c2 == 0), stop=(c2 == 1))
                    t_pv = pe.done(ins)
                    PTg_free[pj] = t_pv
                    for b in range(nbg):
                        n = 2 * g + b
                        dve.wait(t_pv, t_O)
                        if use_sel:
                            t_O = dve.done(nc.vector.scalar_tensor_tensor(
                                out=O[qi][:, 0:129], in0=pO2[oi][:, b * 132:b * 132 + 129], scalar=sel[qi][:, n:n + 1],
                                in1=O[qi][:, 0:129], op0=ALU.mult, op1=ALU.add))
                        else:
                            t_O = dve.done(nc.vector.tensor_tensor(out=O[qi][:, 0:129], in0=O[qi][:, 0:129],
                                                                   in1=pO2[oi][:, b * 132:b * 132 + 129], op=ALU.add))
                    pO2_free[oi] = t_O
                    pend = nxt
                if use_sel:
                    sel_free[qi] = t_O
                dve.wait(t_O, rinv_free)
                t_r = dve.done(nc.vector.reciprocal(out=rinv[:, 0:1], in_=O[qi][:, 128:129]))
                dve.wait(t_r, mob_free[qi])
                t_m = dve.done(nc.vector.tensor_scalar(out=mob[qi][:], in0=O[qi][:, 0:128], scalar1=rinv[:, 0:1],
                                                       scalar2=None, op0=ALU.mult))
                rinv_free = t_m
                O_free[qi] = t_m
                pool.wait(t_m)
                mob_free[qi] = s_o[qi].done(nc.gpsimd.dma_start(
                    out=YATT[r0:r0 + 128, 1024 + h * 128:1024 + (h + 1) * 128], in_=mob[qi][:]))
            h_free[hi] = pe.last()
        drain(k, [x.last() for x in s_o])


def consts(ntok, pos0):
    pos = (pos0 + np.arange(ntok)).astype(np.float32)
    invf = (10000.0 ** (-np.arange(64, dtype=np.float32) / np.float32(64))).astype(np.float32)
    ang = (pos[:, None] * invf[None, :]).astype(np.float32)
    lg = np.log1p(-np.exp2(-5.0 - np.arange(NH, dtype=np.float64)))
    p = np.arange(128, dtype=np.float64)
    c = np.arange(128)
    return dict(
        cos=np.cos(ang).astype(np.float32), sin=np.sin(ang).astype(np.float32),
        ksc=(128.0 ** -0.5 * np.exp(-(p[:, None] + 1) * lg[None, :])).astype(np.float32),
        qdec=np.exp((p[:, None] + 1) * lg[None, :]).astype(np.float32),
        dec=np.tile(np.exp(128.0 * lg)[None, :], (128, 1)).astype(np.float32),
        causal=(c[None, :] >= c[:, None]).astype(np.float32),
    )


DRNN = 2816
RC = DRNN // 128
GK = 1.5957691216057308


def rnn_inproj_phase(k, X_in, w_in, wtick, GATE, RNNBR, ntok):
    nc = k.nc
    pe, act, dve, pool, sp = k.pe, k.act, k.dve, k.pool, k.sp
    ntiles = ntok // TILE
    with ExitStack() as es:
        stage = [k.sb(es, "n_stage%d" % i, [128, D], F32) for i in range(2)]
        xb = [k.sb(es, "n_xb%d" % i, [128, D], BF16) for i in range(4)]
        hT = k.sb(es, "n_hT", [128, KC, TILE], BF16)
        ws = [k.sb(es, "n_w%d" % i, [128, KC, 128], BF16) for i in range(3)]
        xf = [k.sb(es, "n_xf%d" % i, [128, TILE], F32) for i in range(3)]
        t1 = [k.sb(es, "n_t1%d" % i, [128, TILE], F32) for i in range(2)]
        gb = [k.sb(es, "n_gb%d" % i, [128, TILE], BF16) for i in range(2)]
        identb = k.sb(es, "n_idb", [128, 128], BF16)
        pT = k.ps(es, "n_pT", [128, 512], BF16)
        acc = [k.ps(es, "n_acc%d" % i, [128, 512], F32) for i in range(4)]
        s_stage = [k.dsem(es, "nstage%d" % i) for i in range(2)]
        s_w = [k.dsem(es, "nw%d" % i) for i in range(3)]
        s_og = [k.dsem(es, "nog%d" % i) for i in range(2)]
        s_ox = [k.dsem(es, "nox%d" % i) for i in range(3)]
        t0 = pool.done(nc.gpsimd.memset(identb[:], 1.0))
        pool.wait(t0)
        t_id = pool.done(nc.gpsimd.affine_select(out=identb[:], in_=identb[:], pattern=[[-1, 128]],
                                                 compare_op=ALU.is_equal, fill=0.0, base=0, channel_multiplier=1))
        stage_free = [None, None]
        xb_free = [None] * 4
        w_free = [None] * 3
        acc_free = [None] * 4
        xf_free = [None] * 3
        t1_free = [None, None]
        gb_free = [None, None]
        cnt = dict(stage=0, xb=0, w=0, acc=0, xf=0, t1=0, gb=0)
        state = dict(hT_free=None, pT_free=None)

        def emit_T(t):
            for s in range(4):
                r0 = t * TILE + s * 128
                xi = cnt["xb"] % 4
                cnt["xb"] += 1
                si = cnt["stage"] % 2
                cnt["stage"] += 1
                sp.wait(stage_free[si])
                t_ld = s_stage[si].done(nc.sync.dma_start(out=stage[si][:], in_=X_in[r0:r0 + 128, :]))
                pool.wait(t_ld, xb_free[xi])
                t_cc = pool.done(nc.gpsimd.tensor_copy(out=xb[xi][:], in_=stage[si][:]))
                stage_free[si] = t_cc
                for q in range(4):
                    pe.wait(t_cc, state["pT_free"], t_id)
                    for j in range(4):
                        kc = 4 * q + j
                        ins = nc.tensor.transpose(pT[:, j * 128:(j + 1) * 128],
                                                  xb[xi][:, kc * 128:(kc + 1) * 128], identb[:])
                    t_tr = pe.done(ins)
                    ev = dve if (q % 2 == 0) else act
                    ev.wait(t_tr, state["hT_free"])
                    dst = hT[:, 4 * q:4 * q + 4, s * 128:(s + 1) * 128]
                    srcp = pT[:].rearrange("p (a b) -> p a b", b=128)
                    if ev is dve:
                        t_ev = dve.done(nc.vector.tensor_copy(out=dst, in_=srcp))
                    else:
                        t_ev = act.done(nc.scalar.copy(out=dst, in_=srcp))
                    state["pT_free"] = t_ev
                xb_free[xi] = t_tr
            return [dve.last(), act.last()]

        hT_ready = emit_T(0)
        for t in range(ntiles):
            sl = slice(t * TILE, (t + 1) * TILE)
            for oc in range(2 * RC):
                wi = cnt["w"] % 3
                cnt["w"] += 1
                sp.wait(w_free[wi], wtick)
                t_w = s_w[wi].done(nc.sync.dma_start(out=ws[wi][:], in_=w_in[oc]))
                ai = cnt["acc"] % 4
                cnt["acc"] += 1
                pe.wait(t_w, hT_ready, acc_free[ai])
                for kc in range(KC):
                    ins = nc.tensor.matmul(acc[ai][:], lhsT=ws[wi][:, kc, :], rhs=hT[:, kc, :],
                                           start=(kc == 0), stop=(kc == KC - 1))
                t_m = pe.done(ins)
                w_free[wi] = t_m
                xi = cnt["xf"] % 3
                cnt["xf"] += 1
                act.wait(t_m, xf_free[xi])
                t_x = act.done(nc.scalar.copy(out=xf[xi][:], in_=acc[ai][:]))
                acc_free[ai] = t_x
                if oc >= RC:
                    pool.wait(t_x)
                    xf_free[xi] = s_ox[xi].done(nc.gpsimd.dma_start(out=RNNBR[oc - RC, :, sl], in_=xf[xi][:]))
                else:
                    ti = cnt["t1"] % 2
                    cnt["t1"] += 1
                    gi = cnt["gb"] % 2
                    cnt["gb"] += 1
                    dve.wait(t_x, t1_free[ti])
                    t_a = dve.done(nc.vector.tensor_tensor(out=t1[ti][:], in0=xf[xi][:], in1=xf[xi][:], op=ALU.mult))
                    dve.wait(t_a)
                    t_a = dve.done(nc.vector.tensor_scalar(out=t1[ti][:], in0=t1[ti][:], scalar1=0.044715, scalar2=1.0,
                                                           op0=ALU.mult, op1=ALU.add))
                    dve.wait(t_a)
                    t_a = dve.done(nc.vector.tensor_tensor(out=t1[ti][:], in0=t1[ti][:], in1=xf[xi][:], op=ALU.mult))
                    act.wait(t_a)
                    t_e = act.done(nc.scalar.activation(out=t1[ti][:], in_=t1[ti][:], func=AF.Exp, scale=-GK))
                    dve.wait(t_e)
                    t_a = dve.done(nc.vector.tensor_scalar(out=t1[ti][:], in0=t1[ti][:], scalar1=1.0, scalar2=None,
                                                           op0=ALU.add))
                    dve.wait(t_a)
                    t_a = dve.done(nc.vector.reciprocal(out=t1[ti][:], in_=t1[ti][:]))
                    dve.wait(t_a, gb_free[gi])
                    t_g = dve.done(nc.vector.tensor_tensor(out=gb[gi][:], in0=t1[ti][:], in1=xf[xi][:], op=ALU.mult))
                    t1_free[ti] = t_g
                    xf_free[xi] = t_g
                    pool.wait(t_g)
                    gb_free[gi] = s_og[gi].done(nc.gpsimd.dma_start(out=GATE[oc, :, sl], in_=gb[gi][:]))
            state["hT_free"] = t_m
            if t + 1 < ntiles:
                hT_ready = emit_T(t + 1)
        drain(k, [x.last() for x in s_og + s_ox])


def prep_bd(k, es, w_src, name):
    nc = k.nc
    out = dram(nc, name, [DRNN, DRNN], BF16)
    z = k.sb(es, name + "_z", [128, DRNN], BF16)
    ds = k.dsem(es, name + "z")
    t0 = k.pool.done(nc.gpsimd.memset(z[:], 0.0))
    k.pool.wait(t0)
    t = None
    for c in range(RC):
        t = ds.done(nc.gpsimd.dma_start(out=out[c * 128:(c + 1) * 128, :], in_=z[:]))
    k.pool.wait(t)
    ds2 = k.dsem(es, name + "b")
    for g in range(16):
        t = ds2.done(nc.gpsimd.dma_start(out=out[g * 176:(g + 1) * 176, g * 176:(g + 1) * 176], in_=w_src[g]))
    return out, t


def rnn_core_phase(k, GATE, RNNBR, wa_bd, wx_bd, wtick, vecs, YRNN, ntok):
    nc = k.nc
    pe, act, dve, pool, sp = k.pe, k.act, k.dve, k.pool, k.sp
    ntiles = ntok // TILE
    with ExitStack() as es:
        wband = [k.sb(es, "c_wband%d" % i, [128, RC, 5, 128], BF16) for i in range(2)]
        vt = k.sb(es, "c_vt", [128, 9, RC], F32)
        cst = k.sb(es, "c_cst", [128, 6, RC], F32)
        hst = k.sb(es, "c_hst", [128, RC], F32)
        ones1 = k.sb(es, "c_ones", [128, 2], F32)
        NG = 4
        NX = 8
        xr = [k.sb(es, "c_xr%d" % i, [128, TILE + 4], F32) for i in range(NX)]
        u = k.sb(es, "c_u", [128, RC, TILE], F32)
        ub = k.sb(es, "c_ub", [128, RC, TILE], BF16)
        gt = [k.sb(es, "c_gt%d" % i, [128, TILE], BF16) for i in range(NG)]
        r_ = [k.sb(es, "c_r%d" % i, [128, TILE], F32) for i in range(NG)]
        i_ = [k.sb(es, "c_i%d" % i, [128, TILE], F32) for i in range(NG)]
        a_ = [k.sb(es, "c_a%d" % i, [128, TILE], F32) for i in range(NG)]
        b_ = [k.sb(es, "c_b%d" % i, [128, TILE], F32) for i in range(NG)]
        yb = [k.sb(es, "c_yb%d" % i, [128, TILE], BF16) for i in range(NG)]
        pR = [k.ps(es, "c_pR%d" % i, [128, 512], F32) for i in range(NG)]
        pI = [k.ps(es, "c_pI%d" % i, [128, 512], F32) for i in range(NG)]
        s_c = k.dsem(es, "cconst")
        s_x = [k.dsem(es, "cx%d" % i) for i in range(NX)]
        s_g = [k.dsem(es, "cg%d" % i) for i in range(NG)]
        s_o = [k.dsem(es, "co%d" % i) for i in range(NG)]

        sp.wait(wtick)
        for gi, wsrc in enumerate((wa_bd, wx_bd)):
            t0 = pool.done(nc.gpsimd.memset(wband[gi][:], 0.0))
            sp.wait(t0)
            for c in range(RC):
                lo, hi = max(0, c - 2), min(RC - 1, c + 2)
                t_wb = s_c.done(nc.sync.dma_start(
                    out=wband[gi][:, c, lo - c + 2:hi - c + 3, :],
                    in_=wsrc[lo * 128:(hi + 1) * 128, c * 128:(c + 1) * 128].rearrange("(j p) f -> p j f", p=128)))
        t_v = s_c.done(nc.sync.dma_start(out=vt[:].rearrange("p v c -> p (v c)"), in_=vecs[:, :]))
        t_wb = t_v
        act.wait(t_v)
        dve.wait(t_v)
        t = dve.done(nc.vector.tensor_scalar(out=cst[:, 0:2, :], in0=vt[:, 5:7, :], scalar1=-1.0, scalar2=None,
                                             op0=ALU.mult))
        t_e = act.done(nc.scalar.activation(out=cst[:, 4, :], in_=vt[:, 7, :], func=AF.Exp, scale=-1.0))
        dve.wait(t_e)
        t = dve.done(nc.vector.tensor_scalar(out=cst[:, 4, :], in0=cst[:, 4, :], scalar1=1.0, scalar2=None,
                                             op0=ALU.add))
        act.wait(t)
        t_e = act.done(nc.scalar.activation(out=cst[:, 5, :], in_=cst[:, 4, :], func=AF.Ln))
        dve.wait(t_e)
        dve.done(nc.vector.tensor_scalar(out=cst[:, 2, :], in0=cst[:, 5, :], scalar1=-8.0, scalar2=None, op0=ALU.mult))
        dve.done(nc.vector.tensor_scalar(out=cst[:, 3, :], in0=cst[:, 5, :], scalar1=-16.0, scalar2=None,
                                         op0=ALU.mult))
        dve.done(nc.vector.memset(ones1[:], 1.0))
        t_cst = dve.done(nc.vector.memset(hst[:], 0.0))
        for xi in range(NX):
            t_cst = dve.done(nc.vector.memset(xr[xi][:, 0:4], 0.0))

        xr_free = [None] * NX
        gt_free = [None] * NG
        pR_free = [None] * NG
        pI_free = [None] * NG
        buf_free = [None] * NG
        yb_free = [None] * NG
        u_free = None
        cntx = 0
        groups = [list(range(g0, min(RC, g0 + NG))) for g0 in range(0, RC, NG)]
        for t in range(ntiles):
            t0 = t * TILE
            sl = slice(t0, t0 + TILE)
            t_u = None
            for grp in groups:
                xs_ = {}
                tk = {}
                for c in grp:
                    xi = cntx % NX
                    cntx += 1
                    xs_[c] = xi
                    sp.wait(xr_free[xi], t_cst)
                    if t == 0:
                        tk[c] = s_x[xi].done(nc.sync.dma_start(out=xr[xi][:, 4:4 + TILE], in_=RNNBR[c, :, 0:TILE]))
                    else:
                        tk[c] = s_x[xi].done(nc.sync.dma_start(out=xr[xi][:, 0:4 + TILE],
                                                               in_=RNNBR[c, :, t0 - 4:t0 + TILE]))
                for c in grp:
                    xi = xs_[c]
                    act.wait(tk[c], u_free, t_cst)
                    tk[c] = act.done(nc.scalar.activation(out=u[:, c, :], in_=xr[xi][:, 4:4 + TILE], func=AF.Identity,
                                                          scale=vt[:, 3, c:c + 1], bias=vt[:, 4, c:c + 1]))
                for j in range(3):
                    for c in grp:
                        xi = xs_[c]
                        dve.wait(tk[c])
                        tk[c] = dve.done(nc.vector.scalar_tensor_tensor(
                            out=u[:, c, :], in0=xr[xi][:, 1 + j:1 + j + TILE], scalar=vt[:, j, c:c + 1],
                            in1=u[:, c, :], op0=ALU.mult, op1=ALU.add))
                for c in grp:
                    xr_free[xs_[c]] = tk[c]
                    act.wait(tk[c])
                    t_u = act.done(nc.scalar.copy(out=ub[:, c, :], in_=u[:, c, :]))
            for grp in groups:
                T = {}
                for j, c in enumerate(grp):
                    sp.wait(gt_free[j])
                    T[("g", c)] = s_g[j].done(nc.sync.dma_start(out=gt[j][:], in_=GATE[c, :, sl]))
                for j, c in enumerate(grp):
                    lo, hi = max(0, c - 2), min(RC - 1, c + 2)
                    pe.wait(t_u, t_wb, pR_free[j], pI_free[j])
                    for cc in range(lo, hi + 1):
                        nc.tensor.matmul(pR[j][:], lhsT=wband[0][:, c, cc - c + 2, :], rhs=ub[:, cc, :],
                                         start=(cc == lo), stop=(cc == hi))
                    for cc in range(lo, hi + 1):
                        ins = nc.tensor.matmul(pI[j][:], lhsT=wband[1][:, c, cc - c + 2, :], rhs=ub[:, cc, :],
                                               start=(cc == lo), stop=(cc == hi))
                    T[("m", c)] = pe.done(ins)
                for j, c in enumerate(grp):
                    act.wait(T[("m", c)], buf_free[j], t_cst)
                    T[("er", c)] = act.done(nc.scalar.activation(out=r_[j][:], in_=pR[j][:], func=AF.Exp, scale=-1.0,
                                                                 bias=cst[:, 0, c:c + 1]))
                    T[("ei", c)] = act.done(nc.scalar.activation(out=i_[j][:], in_=pI[j][:], func=AF.Exp, scale=-1.0,
                                                                 bias=cst[:, 1, c:c + 1]))
                    pR_free[j] = T[("er", c)]
                    pI_free[j] = T[("ei", c)]
                for j, c in enumerate(grp):
                    dve.wait(T[("er", c)])
                    T[("r1", c)] = dve.done(nc.vector.tensor_scalar(out=r_[j][:], in0=r_[j][:], scalar1=1.0,
                                                                    scalar2=None, op0=ALU.add))
                for j, c in enumerate(grp):
                    dve.wait(T[("r1", c)])
                    T[("r", c)] = dve.done(nc.vector.reciprocal(out=r_[j][:], in_=r_[j][:]))
                for j, c in enumerate(grp):
                    dve.wait(T[("ei", c)])
                    T[("i1", c)] = dve.done(nc.vector.tensor_scalar(out=i_[j][:], in0=i_[j][:], scalar1=1.0,
                                                                    scalar2=None, op0=ALU.add))
                for j, c in enumerate(grp):
                    dve.wait(T[("i1", c)])
                    T[("i2", c)] = dve.done(nc.vector.reciprocal(out=i_[j][:], in_=i_[j][:]))
                for j, c in enumerate(grp):
                    act.wait(T[("r", c)])
                    T[("a", c)] = act.done(nc.scalar.activation(out=a_[j][:], in_=r_[j][:], func=AF.Exp,
                                                                scale=cst[:, 2, c:c + 1]))
                    T[("a2", c)] = act.done(nc.scalar.activation(out=b_[j][:], in_=r_[j][:], func=AF.Exp,
                                                                 scale=cst[:, 3, c:c + 1]))
                for j, c in enumerate(grp):
                    act.wait(T[("a2", c)])
                    T[("ln", c)] = act.done(nc.scalar.activation(out=b_[j][:], in_=b_[j][:], func=AF.Ln, scale=-1.0,
                                                                 bias=ones1[:, 0:1]))
                for j, c in enumerate(grp):
                    act.wait(T[("ln", c)])
                    T[("sq", c)] = act.done(nc.scalar.activation(out=b_[j][:], in_=b_[j][:], func=AF.Exp, scale=0.5))
                for j, c in enumerate(grp):
                    dve.wait(T[("i2", c)])
                    T[("iu", c)] = dve.done(nc.vector.tensor_tensor(out=i_[j][:], in0=i_[j][:], in1=u[:, c, :],
                                                                    op=ALU.mult))
                for j, c in enumerate(grp):
                    dve.wait(T[("iu", c)], T[("sq", c)])
                    T[("b", c)] = dve.done(nc.vector.tensor_tensor(out=b_[j][:], in0=b_[j][:], in1=i_[j][:],
                                                                   op=ALU.mult))
                for j, c in enumerate(grp):
                    dve.wait(T[("b", c)], T[("a", c)], T[("a2", c)], t_cst)
                    T[("sc", c)] = dve.done(nc.vector.tensor_tensor_scan(
                        out=r_[j][:], data0=a_[j][:], data1=b_[j][:], initial=hst[:, c:c + 1],
                        op0=ALU.mult, op1=ALU.add))
                for j, c in enumerate(grp):
                    dve.wait(T[("sc", c)])
                    T[("h", c)] = dve.done(nc.vector.tensor_copy(out=hst[:, c:c + 1], in_=r_[j][:, TILE - 1:TILE]))
                for j, c in enumerate(grp):
                    dve.wait(T[("sc", c)], T[("g", c)], yb_free[j])
                    T[("y", c)] = dve.done(nc.vector.tensor_tensor(out=yb[j][:], in0=r_[j][:], in1=gt[j][:],
                                                                   op=ALU.mult))
                    gt_free[j] = T[("y", c)]
                    buf_free[j] = [T[("y", c)], T[("h", c)]]
                for j, c in enumerate(grp):
                    pool.wait(T[("y", c)])
                    yb_free[j] = s_o[j].done(nc.gpsimd.dma_start(out=YRNN[c, :, sl], in_=yb[j][:]))
            u_free = [pe.last(), dve.last()]
        drain(k, [x.last() for x in s_o])


BETA_UNUSED = None
IN_SPECS = [
    ("x", None), ("ln_g", [2, 3, D]), ("ln_b", [2, 3, D]),
    ("ffn_w_gate", [2, 2, D, DFF]), ("ffn_w_up", [2, 2, D, DFF]), ("ffn_w_down", [2, 2, DFF, D]),
    ("attn_w_in", [1, D, 7168]), ("ret_gn_g", [1, 1024]), ("ret_gn_b", [1, 1024]), ("attn_w_out", [1, D, D]),
    ("rnn_w_in", [1, D, 2 * DRNN]), ("rnn_gate_a_w", [1, 16, 176, 176]), ("rnn_gate_x_w", [1, 16, 176, 176]),
    ("rnn_w_out", [1, DRNN, D]),
    ("c_cos", None), ("c_sin", None), ("c_ksc", [128, NH]), ("c_qdec", [128, NH]), ("c_dec", [128, NH]),
    ("c_causal", [128, 128]), ("c_vecs", [128, 9 * RC]),
]


_DBG = False


def build_program(ntok, upto=99):
    nc = bass.Bass("TRN2", target_bir_lowering=False)
    I = {}
    for name, shape in IN_SPECS:
        if name == "x":
            shape = [ntok, D]
        elif name in ("c_cos", "c_sin"):
            shape = [ntok, 64]
        I[name] = dram(nc, name, shape, F32, "ExternalInput")
    out = dram(nc, "out", [ntok, D], F32, "ExternalOutput")
    dk = "ExternalOutput" if _DBG else "Internal"
    XA = dram(nc, "XA", [ntok, D], F32, dk)
    XB = dram(nc, "XB", [ntok, D], F32, dk)
    with ExitStack() as es:
        k = K(nc, es)

        def prep_ffn(l, j):
            a, t1 = prep_ws(k, es, I["ffn_w_gate"][l, j], "wg%d%d" % (l, j), D, DFF)
            b, t2 = prep_ws(k, es, I["ffn_w_up"][l, j], "wu%d%d" % (l, j), D, DFF)
            c, t3 = prep_ws(k, es, I["ffn_w_down"][l, j], "wd%d%d" % (l, j), DFF, D)
            return a, b, c, [t1, t2, t3]

        def ffn(l, j, w, src, dst):
            sublayer_phase(k, "ffn", src, dst, w[2], FC, w[3], I["ln_g"][l, 2 * j], I["ln_b"][l, 2 * j], ntok,
                           2.0 * ALPHA, 4.0 * LN_EPS, wg=w[0], wu=w[1])

        w00 = prep_ffn(0, 0)
        ffn(0, 0, w00, I["x"], XA if upto > 0 else out)
        if upto <= 0:
            return nc
        win, t_win = prep_tm(k, es, I["attn_w_in"][0], "awin", D, 7168)
        wout, t_wout = prep_ws(k, es, I["attn_w_out"][0], "awout", D, D)
        w01 = prep_ffn(0, 1)
        sc = [dram(nc, "QT", [NH, 128, ntok], BF16), dram(nc, "KT", [NH, 128, ntok], BF16),
              dram(nc, "KTM", [ntok, 1024], BF16), dram(nc, "VTM", [ntok, 1024], BF16),
              dram(nc, "GTM", [ntok, 1024], BF16), dram(nc, "MQT", [NH, 128, ntok], BF16),
              dram(nc, "MKT", [NH, 128, ntok], BF16), dram(nc, "MVTM", [ntok, 1024], BF16),
              dram(nc, "KMEAN", [ntok // 256, 1024], F32)]
        YATT = dram(nc, "YATT", [ntok, 2048], BF16)
        attn_inproj_phase(k, XA, win, [t_win], I["c_cos"], I["c_sin"], I["c_ksc"], sc, ntok)
        QT, KT, KTM, VTM, GTM, MQT, MKT, MVTM, KMEAN = sc
        retention_phase(k, QT, KT, KTM, VTM, GTM, I["ret_gn_g"][0], I["ret_gn_b"][0], I["c_causal"], I["c_dec"],
                        I["c_qdec"], YATT, ntok)
        moba_phase(k, MQT, MKT, MVTM, KMEAN, I["c_causal"], YATT, ntok)
        sublayer_phase(k, "proj_tm", XA, XB if upto > 1 else out, wout, KC, [t_wout], I["ln_g"][0, 1], I["ln_b"][0, 1],
                       ntok, ALPHA, LN_EPS, Y=YATT)
        if upto <= 1:
            return nc
        w10 = prep_ffn(1, 0)
        ffn(0, 1, w01, XB, XA)
        rin, t_rin = prep_ws(k, es, I["rnn_w_in"][0], "rwin", D, 2 * DRNN)
        wa, t_wa = prep_bd(k, es, I["rnn_gate_a_w"][0], "rwa")
        wx, t_wx = prep_bd(k, es, I["rnn_gate_x_w"][0], "rwx")
        rout, t_rout = prep_ws(k, es, I["rnn_w_out"][0], "rwout", DRNN, D)
        w11 = prep_ffn(1, 1)
        ffn(1, 0, w10, XA, XB)
        GATE = dram(nc, "GATE", [RC, 128, ntok], BF16)
        RNNBR = dram(nc, "RNNBR", [RC, 128, ntok], F32)
        YRNN = dram(nc, "YRNN", [RC, 128, ntok], BF16, dk)
        rnn_inproj_phase(k, XB, rin, [t_rin], GATE, RNNBR, ntok)
        rnn_core_phase(k, GATE, RNNBR, wa, wx, [t_wa, t_wx], I["c_vecs"], YRNN, ntok)
        sublayer_phase(k, "proj_fm", XB, XA if upto > 2 else out, rout, RC, [t_rout], I["ln_g"][1, 1], I["ln_b"][1, 1],
                       ntok, ALPHA, LN_EPS, Y=YRNN)
        if upto <= 2:
            return nc
        ffn(1, 1, w11, XA, out)
    return nc


_PROG = {}
_LAST = {}
SPREAD = True


def kernel(x, ln_g, ln_b, ffn_w_gate, ffn_w_up, ffn_w_down, attn_w_in, ret_gn_g, ret_gn_b, attn_w_out,
           rnn_w_in, rnn_conv_w, rnn_conv_b, rnn_gate_a_w, rnn_gate_a_b, rnn_gate_x_w, rnn_gate_x_b,
           rnn_lambda, rnn_w_out):
    f32 = lambda a: np.ascontiguousarray(np.asarray(a, dtype=np.float32))
    x = f32(x)
    B, S, _ = x.shape
    if S not in _PROG:
        _PROG[S] = build_program(S)
    nc = _PROG[S]
    C = consts(S, 0)
    vecs = np.zeros((9, DRNN), np.float32)
    vecs[0:4] = f32(rnn_conv_w)[0]
    vecs[4] = f32(rnn_conv_b)[0]
    vecs[5] = f32(rnn_gate_a_b)[0]
    vecs[6] = f32(rnn_gate_x_b)[0]
    vecs[7] = f32(rnn_lambda)[0]
    shared = dict(ln_g=f32(ln_g), ln_b=f32(ln_b), ffn_w_gate=f32(ffn_w_gate), ffn_w_up=f32(ffn_w_up),
                  ffn_w_down=f32(ffn_w_down), attn_w_in=f32(attn_w_in), ret_gn_g=f32(ret_gn_g),
                  ret_gn_b=f32(ret_gn_b), attn_w_out=f32(attn_w_out), rnn_w_in=f32(rnn_w_in),
                  rnn_gate_a_w=f32(rnn_gate_a_w), rnn_gate_x_w=f32(rnn_gate_x_w), rnn_w_out=f32(rnn_w_out),
                  c_cos=C["cos"], c_sin=C["sin"], c_ksc=C["ksc"], c_qdec=C["qdec"], c_dec=C["dec"],
                  c_causal=C["causal"],
                  c_vecs=np.ascontiguousarray(vecs.reshape(9, RC, 128).transpose(2, 0, 1).reshape(128, 9 * RC)))
    if B == 4 and SPREAD:
        act_cores = [0, 1, 4, 5]
        zeros = {n: np.zeros_like(v) for n, v in shared.items()}
        zeros["x"] = np.zeros_like(x[0])
        in_maps = [zeros] * NCORES
        in_maps = list(in_maps)
        for b, c in enumerate(act_cores):
            in_maps[c] = dict(shared, x=x[b])
        res = run_bass_kernel_spmd(nc, in_maps, core_ids=list(range(NCORES)))
        outs = [res.results[c] for c in act_cores]
    else:
        in_maps = [dict(shared, x=x[b]) for b in range(B)]
        res = run_bass_kernel_spmd(nc, in_maps, core_ids=list(range(B)))
        outs = res.results
    _LAST["r"] = outs[0]
    return np.stack([np.asarray(r["out"], dtype=np.float32) for r in outs], axis=0)
```

```python
import math
from contextlib import ExitStack

import numpy as np
import concourse.bass as bass
import concourse.mybir as mybir
from concourse.bass_utils import run_bass_kernel_spmd

F32 = mybir.dt.float32
BF16 = mybir.dt.bfloat16
AF = mybir.ActivationFunctionType
ALU = mybir.AluOpType

D = 2048
DFF = 5632
KC = D // 128
FC = DFF // 128
DEPTH = 2
ALPHA = (2.0 * DEPTH) ** 0.25
LN_EPS = 1e-5
NCORES = 8
TOK = 4096
TILE = 512


class EngW:
    def __init__(self, nc, es, name, eng):
        self.nc, self.name, self.eng = nc, name, eng
        self.sem = es.enter_context(nc.semaphore("pg_" + name))
        self.count = 0
        self.waited = {}

    def wait(self, *tickets):
        for t in tickets:
            if t is None:
                continue
            if isinstance(t, list):
                self.wait(*t)
                continue
            sem, val, key = t
            if self.waited.get(key, 0) >= val:
                continue
            self.eng.wait_ge(sem, val)
            self.waited[key] = val

    def done(self, inst):
        self.count += 1
        inst.then_inc(self.sem, 1)
        return (self.sem, self.count, self.name)

    def last(self):
        return (self.sem, self.count, self.name) if self.count else None


class DmaSem:
    _n = 0

    def __init__(self, nc, es, name):
        DmaSem._n += 1
        self.key = "dma_%d" % DmaSem._n
        self.sem = es.enter_context(nc.semaphore(self.key))
        self.count = 0

    def done(self, inst):
        self.count += 16
        inst.then_inc(self.sem, 16)
        return (self.sem, self.count, self.key)

    def last(self):
        return (self.sem, self.count, self.key) if self.count else None


class Ring:
    def __init__(self, bufs):
        self.bufs = bufs
        self.free = [None] * len(bufs)
        self.i = 0

    def next(self):
        j = self.i % len(self.bufs)
        self.i += 1
        return j


class K:
    def __init__(self, nc, es):
        self.nc, self.es = nc, es
        self.pe = EngW(nc, es, "pe", nc.tensor)
        self.act = EngW(nc, es, "act", nc.scalar)
        self.dve = EngW(nc, es, "dve", nc.vector)
        self.pool = EngW(nc, es, "pool", nc.gpsimd)
        self.sp = EngW(nc, es, "sp", nc.sync)
        self.sem_pool = []
        self.nsem = 0

    _uid = 0

    def sb(self, es, name, shape, dt):
        K._uid += 1
        return es.enter_context(self.nc.sbuf_tensor("%s_%d" % (name, K._uid), shape, dt))

    def ps(self, es, name, shape, dt):
        K._uid += 1
        return es.enter_context(self.nc.psum_tensor("%s_%d" % (name, K._uid), shape, dt))

    def dsem(self, es, name):
        if self.sem_pool and self.nsem >= 80:
            self.sem_pool.sort(key=lambda d: d.count)
            ds = self.sem_pool.pop(0)
        else:
            ds = DmaSem(self.nc, self.es, name)
            self.nsem += 1
        es.callback(self.sem_pool.append, ds)
        return ds


def dram(nc, name, shape, dt, kind="Internal"):
    return nc.dram_tensor(name, list(shape), dt, kind=kind).ap()


def prep_ws(k, es, w_src, name, kdim, ndim):
    nc = k.nc
    nkc, nnc = kdim // 128, ndim // 128
    out = dram(nc, name, [nnc, 128, nkc, 128], BF16)
    ds = k.dsem(es, name)
    t = None
    for c in range(nnc):
        src = w_src[:, c * 128:(c + 1) * 128].rearrange("(kc p) f -> p kc f", p=128)
        t = ds.done(nc.gpsimd.dma_start(out=out[c], in_=src))
    return out, t


def sublayer_phase(k, mode, X_in, X_out, wd, nk, wtick, g_row, b_row, ntok, cx, eps, wg=None, wu=None, Y=None):
    nc = k.nc
    pe, act, dve, pool, sp = k.pe, k.act, k.dve, k.pool, k.sp
    ntiles = ntok // TILE
    ffn = mode == "ffn"
    with ExitStack() as es:
        xs = k.sb(es, "f_xs", [128, 4, D], F32)
        xbw = D if mode != "proj_tm" else nk * 128
        xb = [k.sb(es, "f_xb%d" % i, [128, xbw], BF16) for i in range(4)]
        hid = k.sb(es, "f_hid", [128, nk, TILE], BF16)
        wds = [k.sb(es, "f_wd%d" % i, [128, nk, 128], BF16) for i in range(2)]
        ytmp = [k.sb(es, "f_yt%d" % i, [128, TILE], F32) for i in range(2)]
        grep = k.sb(es, "f_g", [128, D], F32)
        brep = k.sb(es, "f_b", [128, D], F32)
        identb = k.sb(es, "f_idb", [128, 128], BF16)
        identf = k.sb(es, "f_idf", [128, 128], F32)
        st = k.sb(es, "f_st", [128, 4, 6], F32)
        mv = k.sb(es, "f_mv", [128, 8], F32)
        pT = [k.ps(es, "f_pT%d" % i, [128, 512], BF16) for i in range(2)]
        acc = [k.ps(es, "f_acc%d" % i, [128, 512], F32) for i in range(4)]
        pZ = [k.ps(es, "f_pZ%d" % i, [128, 512], F32) for i in range(2)]
        if ffn:
            stage = [k.sb(es, "f_stage%d" % i, [128, D], F32) for i in range(2)]
            hT = k.sb(es, "f_hT", [128, KC, TILE], BF16)
            wgs = [k.sb(es, "f_wg%d" % i, [128, KC, 128], BF16) for i in range(3)]
            wus = [k.sb(es, "f_wu%d" % i, [128, KC, 128], BF16) for i in range(3)]
            sgb = [k.sb(es, "f_sg%d" % i, [128, TILE], F32) for i in range(2)]
            s_stage = [k.dsem(es, "stage%d" % i) for i in range(2)]
            s_wgu = [k.dsem(es, "wgu%d" % i) for i in range(3)]
        s_xb = [k.dsem(es, "xb%d" % i) for i in range(4)]
        s_wd = [k.dsem(es, "wd%d" % i) for i in range(2)]
        s_xs = k.dsem(es, "xs")
        s_gb = k.dsem(es, "gb")
        s_out = k.dsem(es, "out")
        s_hid = k.dsem(es, "hid")

        t_gb = s_gb.done(nc.sync.dma_start(out=grep[:], in_=g_row.partition_broadcast(128)))
        t_gb = s_gb.done(nc.sync.dma_start(out=brep[:], in_=b_row.partition_broadcast(128)))
        for idt in (identb, identf):
            t0 = pool.done(nc.gpsimd.memset(idt[:], 1.0))
            pool.wait(t0)
            t_id = pool.done(nc.gpsimd.affine_select(out=idt[:], in_=idt[:], pattern=[[-1, 128]],
                                                     compare_op=ALU.is_equal, fill=0.0, base=0,
                                                     channel_multiplier=1))

        stage_free = [None, None]
        xb_free = [None] * 4
        pT_free = [None, None]
        acc_free = [None] * 4
        pZ_free = [None, None]
        wgu_free = [None] * 3
        wd_free = [None] * 2
        sg_free = [None, None]
        yt_free = [None, None]
        cnt = dict(stage=0, xb=0, pT=0, acc=0, pZ=0, wgu=0, wd=0, sg=0, yt=0)
        state = dict(hT_free=None, hid_free=None, xs_free=None, st_free=None, xs_ready=None)

        def load_xs(t):
            sp.wait(state["xs_free"])
            for s in range(4):
                r0 = t * TILE + s * 128
                state["xs_ready"] = s_xs.done(nc.sync.dma_start(out=xs[:, s, :], in_=X_in[r0:r0 + 128, :]))

        def emit_T_load(t, src, is_f32, subs=(0, 1, 2, 3)):
            res = []
            for s in subs:
                r0 = t * TILE + s * 128
                xi = cnt["xb"] % 4
                cnt["xb"] += 1
                if is_f32:
                    si = cnt["stage"] % 2
                    cnt["stage"] += 1
                    sp.wait(stage_free[si])
                    t_ld = s_stage[si].done(nc.sync.dma_start(out=stage[si][:], in_=src[r0:r0 + 128, :]))
                    pool.wait(t_ld, xb_free[xi])
                    t_c = pool.done(nc.gpsimd.tensor_copy(out=xb[xi][:], in_=stage[si][:]))
                    stage_free[si] = t_c
                else:
                    sp.wait(xb_free[xi])
                    t_c = s_xb[xi].done(nc.sync.dma_start(out=xb[xi][:], in_=src[r0:r0 + 128, :]))
                res.append((xi, t_c))
            return res

        def emit_T_pe(loads, nkc, dest, dest_free_key):
            for s, (xi, t_c) in enumerate(loads):
                ngrp = (nkc + 3) // 4
                for q in range(ngrp):
                    n4 = min(4, nkc - 4 * q)
                    pi = cnt["pT"] % 2
                    cnt["pT"] += 1
                    pe.wait(t_c, pT_free[pi], t_id)
                    for j in range(n4):
                        kc = 4 * q + j
                        ins = nc.tensor.transpose(pT[pi][:, j * 128:(j + 1) * 128],
                                                  xb[xi][:, kc * 128:(kc + 1) * 128], identb[:])
                    t_tr = pe.done(ins)
                    ev = dve if (q % 2 == 0) else act
                    ev.wait(t_tr, state[dest_free_key])
                    dst = dest[:, 4 * q:4 * q + n4, s * 128:(s + 1) * 128]
                    srcp = pT[pi][:, 0:n4 * 128].rearrange("p (a b) -> p a b", b=128)
                    if ev is dve:
                        t_ev = dve.done(nc.vector.tensor_copy(out=dst, in_=srcp))
                    else:
                        t_ev = act.done(nc.scalar.copy(out=dst, in_=srcp))
                    pT_free[pi] = t_ev
                xb_free[xi] = t_tr
            return [dve.last(), act.last()]

        def emit_T(t, src, is_f32, nkc, dest, dest_free_key):
            return emit_T_pe(emit_T_load(t, src, is_f32), nkc, dest, dest_free_key)

        def emit_GU(t, hT_ready):
            for fc in range(FC):
                wi = cnt["wgu"] % 3
                cnt["wgu"] += 1
                sp.wait(wgu_free[wi], wtick)
                s_wgu[wi].done(nc.sync.dma_start(out=wgs[wi][:], in_=wg[fc]))
                t_w = s_wgu[wi].done(nc.sync.dma_start(out=wus[wi][:], in_=wu[fc]))
                if fc == 22:
                    load_xs(t)
                if fc in (4, 8, 12, 16) and t + 1 < ntiles:
                    if fc == 4:
                        state["next_loads"] = []
                    state["next_loads"] += emit_T_load(t + 1, X_in, True, subs=((fc - 4) // 4,))
                ag = cnt["acc"] % 4
                au = (cnt["acc"] + 1) % 4
                cnt["acc"] += 2
                pe.wait(t_w, hT_ready, acc_free[ag])
                for kc in range(KC):
                    ins = nc.tensor.matmul(acc[ag][:], lhsT=wgs[wi][:, kc, :], rhs=hT[:, kc, :],
                                           start=(kc == 0), stop=(kc == KC - 1))
                t_g = pe.done(ins)
                pe.wait(acc_free[au])
                for kc in range(KC):
                    ins = nc.tensor.matmul(acc[au][:], lhsT=wus[wi][:, kc, :], rhs=hT[:, kc, :],
                                           start=(kc == 0), stop=(kc == KC - 1))
                t_u = pe.done(ins)
                wgu_free[wi] = t_u
                gi = cnt["sg"] % 2
                cnt["sg"] += 1
                act.wait(t_g, sg_free[gi])
                t_s = act.done(nc.scalar.activation(out=sgb[gi][:], in_=acc[ag][:], func=AF.Silu))
                acc_free[ag] = t_s
                dve.wait(t_s, t_u, state["hid_free"])
                t_h = dve.done(nc.vector.tensor_tensor(out=hid[:, fc, :], in0=sgb[gi][:], in1=acc[au][:],
                                                       op=ALU.mult))
                sg_free[gi] = t_h
                acc_free[au] = t_h
            state["hT_free"] = t_u
            return t_h

        def emit_D(t, hid_ready):
            pend = None
            t_res = None

            def emit_tr(p):
                oc, yi, t_cp = p
                zi = cnt["pZ"] % 2
                cnt["pZ"] += 1
                pe.wait(t_cp, pZ_free[zi], t_id)
                for s in range(4):
                    ins = nc.tensor.transpose(pZ[zi][:, s * 128:(s + 1) * 128], ytmp[yi][:, s * 128:(s + 1) * 128],
                                              identf[:])
                t_tr = pe.done(ins)
                yt_free[yi] = t_tr
                dve.wait(t_tr, state["xs_ready"])
                t_r = dve.done(nc.vector.scalar_tensor_tensor(
                    out=xs[:, :, oc * 128:(oc + 1) * 128], in0=xs[:, :, oc * 128:(oc + 1) * 128],
                    scalar=float(cx), in1=pZ[zi][:].rearrange("p (a b) -> p a b", b=128),
                    op0=ALU.mult, op1=ALU.add))
                pZ_free[zi] = t_r
                return t_r

            for oc in range(KC):
                wi = cnt["wd"] % 2
                cnt["wd"] += 1
                sp.wait(wd_free[wi], wtick)
                t_w = s_wd[wi].done(nc.sync.dma_start(out=wds[wi][:], in_=wd[oc]))
                ai = cnt["acc"] % 4
                cnt["acc"] += 1
                pe.wait(t_w, hid_ready, acc_free[ai])
                for fc in range(nk):
                    ins = nc.tensor.matmul(acc[ai][:], lhsT=wds[wi][:, fc, :], rhs=hid[:, fc, :],
                                           start=(fc == 0), stop=(fc == nk - 1))
                t_m = pe.done(ins)
                wd_free[wi] = t_m
                yi = cnt["yt"] % 2
                cnt["yt"] += 1
                act.wait(t_m, yt_free[yi])
                t_cp = act.done(nc.scalar.copy(out=ytmp[yi][:], in_=acc[ai][:]))
                acc_free[ai] = t_cp
                if pend is not None:
                    t_res = emit_tr(pend)
                pend = (oc, yi, t_cp)
            state["hid_free"] = t_m
            t_res = emit_tr(pend)
            return t_res

        def emit_LN(t, t_res):
            t_st = None
            for s in range(4):
                dve.wait(t_res, state["st_free"])
                for c in range(4):
                    t_b = dve.done(nc.vector.bn_stats(out=st[:, c, :], in_=xs[:, s, c * 512:(c + 1) * 512]))
                dve.wait(t_b)
                t_a = dve.done(nc.vector.bn_aggr(out=mv[:, 0:2], in_=st[:].rearrange("p a b -> p (a b)")))
                dve.wait(t_a)
                t_e = dve.done(nc.vector.tensor_scalar(out=mv[:, 2:3], in0=mv[:, 1:2], scalar1=float(eps),
                                                       scalar2=None, op0=ALU.add))
                act.wait(t_e)
                t_q = act.done(nc.scalar.activation(out=mv[:, 3:4], in_=mv[:, 2:3], func=AF.Sqrt))
                dve.wait(t_q)
                t_r = dve.done(nc.vector.reciprocal(out=mv[:, 4:5], in_=mv[:, 3:4]))
                dve.wait(t_r)
                t_n = dve.done(nc.vector.scalar_tensor_tensor(out=mv[:, 5:6], in0=mv[:, 0:1], scalar=-1.0,
                                                              in1=mv[:, 4:5], op0=ALU.mult, op1=ALU.mult))
                act.wait(t_n)
                t_x = act.done(nc.scalar.activation(out=xs[:, s, :], in_=xs[:, s, :], func=AF.Identity,
                                                    bias=mv[:, 5:6], scale=mv[:, 4:5]))
                state["st_free"] = t_x
                dve.wait(t_x, t_gb)
                t_p = dve.done(nc.vector.tensor_tensor(out=xs[:, s, :], in0=xs[:, s, :], in1=grep[:], op=ALU.mult))
                dve.wait(t_p)
                t_p = dve.done(nc.vector.tensor_tensor(out=xs[:, s, :], in0=xs[:, s, :], in1=brep[:], op=ALU.add))
                pool.wait(t_p)
                r0 = t * TILE + s * 128
                t_st = s_out.done(nc.gpsimd.dma_start(out=X_out[r0:r0 + 128, :], in_=xs[:, s, :]))
            state["xs_free"] = t_st
            return t_st

        t_out = None
        if ffn:
            hT_ready = emit_T(0, X_in, True, KC, hT, "hT_free")
            for t in range(ntiles):
                hid_ready = emit_GU(t, hT_ready)
                if t + 1 < ntiles:
                    hT_ready = emit_T_pe(state["next_loads"], KC, hT, "hT_free")
                t_res = emit_D(t, hid_ready)
                t_out = emit_LN(t, t_res)
        else:
            for t in range(ntiles):
                if mode == "proj_tm":
                    hid_ready = emit_T(t, Y, False, nk, hid, "hid_free")
                else:
                    sp.wait(state["hid_free"])
                    for c in range(nk):
                        hid_ready = s_hid.done(nc.sync.dma_start(out=hid[:, c, :],
                                                                 in_=Y[c, :, t * TILE:(t + 1) * TILE]))
                load_xs(t)
                t_res = emit_D(t, hid_ready)
                t_out = emit_LN(t, t_res)
        drain(k, [t_out])
        return t_out


def drain(k, extra=()):
    engs = (k.pe, k.act, k.dve, k.pool, k.sp)
    fin = [e.last() for e in engs] + list(extra)
    for e in engs:
        e.wait(fin)


NH = 8
KINDS = ["q", "k", "v", "g", "mq", "mk", "mv"]


def prep_tm(k, es, w_src, name, kdim, ndim, gw=512):
    nc = k.nc
    nkc, ncg = kdim // 128, ndim // gw
    out = dram(nc, name, [ncg, 128, nkc, gw], BF16)
    ds = k.dsem(es, name)
    t = None
    for c in range(ncg):
        src = w_src[:, c * gw:(c + 1) * gw].rearrange("(kc p) f -> p kc f", p=128)
        t = ds.done(nc.gpsimd.dma_start(out=out[c], in_=src))
    return out, t


def attn_inproj_phase(k, X_in, w_in, wtick, cos_d, sin_d, ksc_d, outs, ntok):
    nc = k.nc
    pe, act, dve, pool, sp = k.pe, k.act, k.dve, k.pool, k.sp
    QT, KT, KTM, VTM, GTM, MQT, MKT, MVTM, KMEAN = outs
    ntiles = ntok // TILE
    with ExitStack() as es:
        stage = [k.sb(es, "a_stage%d" % i, [128, D], F32) for i in range(2)]
        xb = [k.sb(es, "a_xb%d" % i, [128, D], BF16) for i in range(4)]
        hT = k.sb(es, "a_hT", [128, KC, TILE], BF16)
        wt = [k.sb(es, "a_wt%d" % i, [128, KC, 512], BF16) for i in range(2)]
        xsb = [k.sb(es, "a_xsb%d" % i, [128, 512], F32) for i in range(2)]
        ra = [k.sb(es, "a_ra%d" % i, [128, 512], F32) for i in range(2)]
        rb = [k.sb(es, "a_rb%d" % i, [128, 512], F32) for i in range(2)]
        tmb = [k.sb(es, "a_tmb%d" % i, [128, 512], BF16) for i in range(4)]
        fmt = [k.sb(es, "a_fmt%d" % i, [128, 4, TILE], BF16) for i in range(2)]
        cosr = [k.sb(es, "a_cos%d" % i, [128, 4, 64], F32) for i in range(2)]
        sinr = [k.sb(es, "a_sin%d" % i, [128, 4, 64], F32) for i in range(2)]
        ksc = k.sb(es, "a_ksc", [128, NH], F32)
        onesb = k.sb(es, "a_ones", [128, 2], BF16)
        kmrow = [k.sb(es, "a_kmrow%d" % i, [1, 512], F32) for i in range(2)]
        identb = k.sb(es, "a_idb", [128, 128], BF16)
        pT_t = k.ps(es, "a_pT", [128, 512], BF16)
        pT = [pT_t[:, :], pT_t[:, :]]
        acc = [k.ps(es, "a_acc%d" % i, [128, 512], F32) for i in range(4)]
        ptr = [k.ps(es, "a_ptr%d" % i, [128, 512], BF16)[:, :] for i in range(2)]
        pkm_t = k.ps(es, "a_pkm", [1, 512], F32)
        pkm = [pkm_t, pkm_t]

        s_stage = [k.dsem(es, "astage%d" % i) for i in range(2)]
        s_wt = [k.dsem(es, "awt%d" % i) for i in range(2)]
        s_cs = [k.dsem(es, "acs%d" % i) for i in range(2)]
        s_c = k.dsem(es, "aconst")
        s_st = [k.dsem(es, "ast%d" % i) for i in range(4)]
        s_fm = [k.dsem(es, "afm%d" % i) for i in range(2)]
        s_km = [k.dsem(es, "akm%d" % i) for i in range(2)]

        t_c = s_c.done(nc.sync.dma_start(out=ksc[:], in_=ksc_d[:, :]))
        t0 = pool.done(nc.gpsimd.memset(onesb[:], 1.0))
        t0 = pool.done(nc.gpsimd.memset(identb[:], 1.0))
        pool.wait(t0)
        t_id = pool.done(nc.gpsimd.affine_select(out=identb[:], in_=identb[:], pattern=[[-1, 128]],
                                                 compare_op=ALU.is_equal, fill=0.0, base=0, channel_multiplier=1))

        stage_free = [None, None]
        xb_free = [None] * 4
        pT_free = [None, None]
        acc_free = [None] * 4
        wt_free = [None, None]
        xsb_free = [None, None]
        ra_free = [None, None]
        rb_free = [None, None]
        tmb_free = [None] * 4
        ptr_free = [None, None]
        fmt_free = [None, None]
        cs_free = [None, None]
        pkm_free = [None, None]
        kmrow_free = [None, None]
        cnt = dict(stage=0, xb=0, pT=0, acc=0, wt=0, xsb=0, r=0, tmb=0, ptr=0, fmt=0, km=0)
        state = dict(hT_free=None)

        def emit_T_load(t, subs):
            res = []
            for s in subs:
                r0 = t * TILE + s * 128
                xi = cnt["xb"] % 4
                cnt["xb"] += 1
                si = cnt["stage"] % 2
                cnt["stage"] += 1
                sp.wait(stage_free[si])
                t_ld = s_stage[si].done(nc.sync.dma_start(out=stage[si][:], in_=X_in[r0:r0 + 128, :]))
                pool.wait(t_ld, xb_free[xi])
                t_cc = pool.done(nc.gpsimd.tensor_copy(out=xb[xi][:], in_=stage[si][:]))
                stage_free[si] = t_cc
                res.append((xi, t_cc))
            return res

        def emit_T(t, loads=None):
            if loads is None:
                loads = emit_T_load(t, (0, 1, 2, 3))
            for s, (xi, t_cc) in enumerate(loads):
                for q in range(4):
                    pi = 0
                    pe.wait(t_cc, pT_free[pi], t_id)
                    for j in range(4):
                        kc = 4 * q + j
                        ins = nc.tensor.transpose(pT[pi][:, j * 128:(j + 1) * 128],
                                                  xb[xi][:, kc * 128:(kc + 1) * 128], identb[:])
                    t_tr = pe.done(ins)
                    ev = dve if (q % 2 == 0) else act
                    ev.wait(t_tr, state["hT_free"])
                    dst = hT[:, 4 * q:4 * q + 4, s * 128:(s + 1) * 128]
                    srcp = pT[pi].rearrange("p (a b) -> p a b", b=128)
                    if ev is dve:
                        t_ev = dve.done(nc.vector.tensor_copy(out=dst, in_=srcp))
                    else:
                        t_ev = act.done(nc.scalar.copy(out=dst, in_=srcp))
                    pT_free[pi] = t_ev
                xb_free[xi] = t_tr
            return [dve.last(), act.last()]

        TMDST = {"k": KTM, "v": VTM, "g": GTM, "mv": MVTM}
        FMDST = {"q": QT, "k": KT, "mq": MQT, "mk": MKT}

        def post(t, cg, s, ai, t_m, ci, fj):
            kind = KINDS[cg // 2]
            dbg = getattr(k, "dbg", 9)
            if dbg < 4 and kind in ("q", "k"):
                kind = "mq" if kind == "q" else "mk"
            hb = (cg % 2) * 4
            r0 = t * TILE + s * 128
            j = cnt["tmb"] % 4
            cnt["tmb"] += 1
            if kind in ("q", "k"):
                xj = cnt["xsb"] % 2
                cnt["xsb"] += 1
                act.wait(t_m, xsb_free[xj])
                t_x = act.done(nc.scalar.copy(out=xsb[xj][:], in_=acc[ai][:]))
                acc_free[ai] = t_x
                r = cnt["r"] % 2
                cnt["r"] += 1
                x8 = xsb[xj][:].rearrange("p (a b) -> p a b", b=64)
                x42 = xsb[xj][:].rearrange("p (h a b) -> p h a b", a=2, b=64)
                rb42 = rb[r][:].rearrange("p (h a b) -> p h a b", a=2, b=64)
                cosb = cosr[ci][:, s, :].unsqueeze(1).to_broadcast([128, 8, 64])
                sinb = sinr[ci][:, s, :].unsqueeze(1).to_broadcast([128, 4, 64])
                dve.wait(t_x, ra_free[r], state["cs_ready"])
                t_a = dve.done(nc.vector.tensor_tensor(out=ra[r][:].rearrange("p (a b) -> p a b", b=64), in0=x8,
                                                       in1=cosb, op=ALU.mult))
                dve.wait(t_x, rb_free[r], state["cs_ready"])
                dve.done(nc.vector.tensor_tensor(out=rb42[:, :, 0, :], in0=x42[:, :, 1, :], in1=sinb, op=ALU.mult))
                t_b = dve.done(nc.vector.tensor_tensor(out=rb42[:, :, 1, :], in0=x42[:, :, 0, :], in1=sinb,
                                                       op=ALU.mult))
                xsb_free[xj] = [t_a, t_b]
                ra42 = ra[r][:].rearrange("p (h a b) -> p h a b", a=2, b=64)
                dve.wait(t_a, t_b, tmb_free[j])
                if kind == "q":
                    o42 = tmb[j][:].rearrange("p (h a b) -> p h a b", a=2, b=64)
                else:
                    o42 = ra42
                dve.done(nc.vector.tensor_tensor(out=o42[:, :, 0, :], in0=ra42[:, :, 0, :], in1=rb42[:, :, 0, :],
                                                 op=ALU.subtract))
                t_tm = dve.done(nc.vector.tensor_tensor(out=o42[:, :, 1, :], in0=ra42[:, :, 1, :],
                                                        in1=rb42[:, :, 1, :], op=ALU.add))
                if kind == "k":
                    dve.wait(t_tm, t_c)
                    t_tm = dve.done(nc.vector.tensor_tensor(
                        out=tmb[j][:].rearrange("p (h d) -> p h d", d=128),
                        in0=ra[r][:].rearrange("p (h d) -> p h d", d=128),
                        in1=ksc[:, hb:hb + 4].unsqueeze(2).to_broadcast([128, 4, 128]), op=ALU.mult))
                ra_free[r] = t_tm
                rb_free[r] = t_tm
            else:
                act.wait(t_m, tmb_free[j])
                if kind == "g":
                    t_tm = act.done(nc.scalar.activation(out=tmb[j][:], in_=acc[ai][:], func=AF.Silu))
                else:
                    t_tm = act.done(nc.scalar.copy(out=tmb[j][:], in_=acc[ai][:]))
                acc_free[ai] = t_tm
            frees = []
            if kind in TMDST and dbg >= 2:
                pool.wait(t_tm)
                frees.append(s_st[j].done(nc.gpsimd.dma_start(
                    out=TMDST[kind][r0:r0 + 128, hb * 128:hb * 128 + 512], in_=tmb[j][:])))
            tmb_free[j] = frees
            if kind in FMDST and dbg >= 3:
                return (t, cg, s, j, t_tm, fj, kind, hb, frees)
            return None

        def pe_post(p):
            t, cg, s, j, t_tm, fj, kind, hb, frees = p
            pi = cnt["ptr"] % 2
            cnt["ptr"] += 1
            pe.wait(t_tm, ptr_free[pi], t_id)
            for hh in range(4):
                ins = nc.tensor.transpose(ptr[pi][:, hh * 128:(hh + 1) * 128], tmb[j][:, hh * 128:(hh + 1) * 128],
                                          identb[:])
            t_tr = pe.done(ins)
            frees.append(t_tr)
            ev = dve if (s % 2 == 0) else act
            ev.wait(t_tr, fmt_free[fj] if s == 0 else None)
            dst = fmt[fj][:, :, s * 128:(s + 1) * 128]
            srcp = ptr[pi].rearrange("p (a b) -> p a b", b=128)
            if ev is dve:
                t_ev = dve.done(nc.vector.tensor_copy(out=dst, in_=srcp))
            else:
                t_ev = act.done(nc.scalar.copy(out=dst, in_=srcp))
            ptr_free[pi] = t_ev
            state["fm_evs"].append(t_ev)
            if kind == "mk" and not getattr(k, "no_kmean", False):
                b = s // 2
                kmi = 0
                if s % 2 == 0:
                    pe.wait(pkm_free[kmi])
                ins = nc.tensor.matmul(pkm[kmi][0:1, :], lhsT=onesb[:, 0:1], rhs=tmb[j][:], start=(s % 2 == 0),
                                       stop=(s % 2 == 1))
                t_k = pe.done(ins)
                frees.append(t_k)
                if s % 2 == 1:
                    act.wait(t_k, kmrow_free[kmi])
                    t_r = act.done(nc.scalar.activation(out=kmrow[kmi][:], in_=pkm[kmi][0:1, :], func=AF.Copy,
                                                        scale=1.0 / 256.0))
                    pkm_free[kmi] = t_r
                    pool.wait(t_r)
                    kmrow_free[kmi] = s_km[kmi].done(nc.gpsimd.dma_start(
                        out=KMEAN[2 * t + b:2 * t + b + 1, hb * 128:hb * 128 + 512], in_=kmrow[kmi][:]))
            if s == 3:
                pool.wait(state["fm_evs"])
                fmt_free[fj] = s_fm[fj].done(nc.gpsimd.dma_start(
                    out=FMDST[kind][hb:hb + 4, :, t * TILE:(t + 1) * TILE].rearrange("h d t -> d h t"),
                    in_=fmt[fj][:]))
                state["fm_evs"] = []

        state["fm_evs"] = []
        if getattr(k, "dbg", 9) == -1:
            drain(k, [t_c])
            return
        hT_ready = emit_T(0)
        if getattr(k, "dbg", 9) == 0:
            drain(k, [t_c])
            return
        for t in range(ntiles):
            ci = t % 2
            sp.wait(cs_free[ci])
            s_cs[ci].done(nc.sync.dma_start(out=cosr[ci][:], in_=cos_d[t * TILE:(t + 1) * TILE, :]
                                            .rearrange("(s p) j -> p s j", p=128)))
            state["cs_ready"] = s_cs[ci].done(nc.sync.dma_start(
                out=sinr[ci][:], in_=sin_d[t * TILE:(t + 1) * TILE, :].rearrange("(s p) j -> p s j", p=128)))
            pend = None
            for cg in range(14):
                wi = cnt["wt"] % 2
                cnt["wt"] += 1
                sp.wait(wt_free[wi], wtick)
                t_w = s_wt[wi].done(nc.sync.dma_start(out=wt[wi][:], in_=w_in[cg]))
                kind = KINDS[cg // 2]
                if cg in (1, 3, 5, 7) and t + 1 < ntiles:
                    if cg == 1:
                        state["next_loads"] = []
                    state["next_loads"] += emit_T_load(t + 1, ((cg - 1) // 2,))
                fj = None
                if kind in FMDST:
                    fj = cnt["fmt"] % 2
                    cnt["fmt"] += 1
                for s in range(4):
                    ai = cnt["acc"] % 4
                    cnt["acc"] += 1
                    pe.wait(t_w, hT_ready, acc_free[ai])
                    for kc in range(KC):
                        ins = nc.tensor.matmul(acc[ai][:], lhsT=hT[:, kc, s * 128:(s + 1) * 128], rhs=wt[wi][:, kc, :],
                                               start=(kc == 0), stop=(kc == KC - 1))
                    t_m = pe.done(ins)
                    if pend is not None:
                        pe_post(pend)
                    pend = post(t, cg, s, ai, t_m, ci, fj)
                wt_free[wi] = t_m
            state["hT_free"] = t_m
            if pend is not None:
                pe_post(pend)
                pend = None
            cs_free[ci] = [dve.last(), pool.last()]
            if t + 1 < ntiles:
                hT_ready = emit_T(t + 1, state["next_loads"])
        drain(k, [x.last() for x in s_st + s_fm + s_km])


def retention_phase(k, QT, KT, KTM, VTM, GTM, gng_row, gnb_row, causal_d, dec_d, qdec_d, YATT, ntok):
    nc = k.nc
    pe, act, dve, pool, sp = k.pe, k.act, k.dve, k.pool, k.sp
    ngrp = ntok // TILE
    with ExitStack() as es:
        qT4 = [k.sb(es, "r_qT%d" % i, [128, NH, TILE], BF16) for i in range(2)]
        kT4 = [k.sb(es, "r_kT%d" % i, [128, NH, TILE], BF16) for i in range(2)]
        ktm4 = [k.sb(es, "r_ktm%d" % i, [128, 4, 1024], BF16) for i in range(2)]
        vtm4 = [k.sb(es, "r_vtm%d" % i, [128, 4, 1024], BF16) for i in range(2)]
        gtm4 = [k.sb(es, "r_gtm%d" % i, [128, 4, 1024], BF16) for i in range(2)]
        S = k.sb(es, "r_S", [128, 1024], F32)
        Sb = [k.sb(es, "r_Sb%d" % i, [128, 1024], BF16) for i in range(2)]
        PT = [k.sb(es, "r_PT%d" % i, [128, 1024], BF16) for i in range(2)]
        ro = [k.sb(es, "r_ro%d" % i, [128, 1024], F32) for i in range(2)]
        ob = [k.sb(es, "r_ob%d" % i, [128, 1024], BF16) for i in range(2)]
        causal = k.sb(es, "r_causal", [128, 128], F32)
        dec = k.sb(es, "r_dec", [128, NH], F32)
        qdec = k.sb(es, "r_qdec", [128, NH], F32)
        gng = k.sb(es, "r_gng", [128, 1024], F32)
        gnb = k.sb(es, "r_gnb", [128, 1024], F32)
        st8 = k.sb(es, "r_st8", [128, NH, 6], F32)
        mv8 = k.sb(es, "r_mv8", [128, NH, 2], F32)
        rs = k.sb(es, "r_rs", [128, 3, NH], F32)
        pS = k.ps(es, "r_pS", [128, 1024], F32)
        pO = k.ps(es, "r_pO", [128, 1024], F32)
        pKV = k.ps(es, "r_pKV", [128, 1024], F32)
        s_ld = [k.dsem(es, "rld%d" % i) for i in range(2)]
        s_c = k.dsem(es, "rconst")
        s_o = [k.dsem(es, "rout%d" % i) for i in range(2)]

        s_c.done(nc.sync.dma_start(out=causal[:], in_=causal_d[:, :]))
        s_c.done(nc.sync.dma_start(out=dec[:], in_=dec_d[:, :]))
        s_c.done(nc.sync.dma_start(out=qdec[:], in_=qdec_d[:, :]))
        s_c.done(nc.sync.dma_start(out=gng[:], in_=gng_row.partition_broadcast(128)))
        t_c = s_c.done(nc.sync.dma_start(out=gnb[:], in_=gnb_row.partition_broadcast(128)))
        t0 = pool.done(nc.gpsimd.memset(S[:], 0.0))
        t_sb = pool.done(nc.gpsimd.memset(Sb[0][:], 0.0))
        t_S = t_sb

        def v3(ap):
            return ap.rearrange("p (h e) -> p h e", e=128)

        ld_free = [None, None]
        pS_free = pO_free = pKV_free = None
        PT_free = [None, None]
        ro_free = [None, None]
        ob_free = [None, None]
        Sb_free = [None, None]
        rs_free = None
        nchunk = 0
        for g in range(ngrp):
            li = g % 2
            sp.wait(ld_free[li])
            sl = slice(g * TILE, (g + 1) * TILE)
            s_ld[li].done(nc.sync.dma_start(out=qT4[li][:], in_=QT[:, :, sl].rearrange("h d t -> d h t")))
            s_ld[li].done(nc.sync.dma_start(out=kT4[li][:], in_=KT[:, :, sl].rearrange("h d t -> d h t")))
            s_ld[li].done(nc.sync.dma_start(out=ktm4[li][:], in_=KTM[sl, :].rearrange("(c p) f -> p c f", p=128)))
            s_ld[li].done(nc.sync.dma_start(out=vtm4[li][:], in_=VTM[sl, :].rearrange("(c p) f -> p c f", p=128)))
            t_ld = s_ld[li].done(nc.sync.dma_start(out=gtm4[li][:],
                                                   in_=GTM[sl, :].rearrange("(c p) f -> p c f", p=128)))
            for cc in range(4):
                c0 = cc * 128
                r0 = g * TILE + c0
                bi = nchunk % 2
                nchunk += 1
                pe.wait(t_ld, pS_free)
                for h in range(NH):
                    ins = nc.tensor.matmul(pS[:, h * 128:(h + 1) * 128], lhsT=kT4[li][:, h, c0:c0 + 128],
                                           rhs=qT4[li][:, h, c0:c0 + 128], start=True, stop=True)
                t_s = pe.done(ins)
                dve.wait(t_s, PT_free[bi], t_c)
                t_pt = dve.done(nc.vector.tensor_tensor(out=v3(PT[bi][:]), in0=v3(pS[:]),
                                                        in1=causal[:].unsqueeze(1).to_broadcast([128, NH, 128]),
                                                        op=ALU.mult))
                pS_free = t_pt
                pe.wait(t_pt, t_sb, pO_free)
                for h in range(NH):
                    hs = slice(h * 128, (h + 1) * 128)
                    nc.tensor.matmul(pO[:, hs], lhsT=PT[bi][:, hs], rhs=vtm4[li][:, cc, hs], start=True, stop=False)
                    ins = nc.tensor.matmul(pO[:, hs], lhsT=qT4[li][:, h, c0:c0 + 128], rhs=Sb[bi][:, hs],
                                           start=False, stop=True)
                t_o = pe.done(ins)
                PT_free[bi] = t_o
                Sb_free[bi] = t_o
                pe.wait(pKV_free)
                for h in range(NH):
                    hs = slice(h * 128, (h + 1) * 128)
                    ins = nc.tensor.matmul(pKV[:, hs], lhsT=ktm4[li][:, cc, hs], rhs=vtm4[li][:, cc, hs],
                                           start=True, stop=True)
                t_kv = pe.done(ins)
                dve.wait(t_kv, t_S)
                t_1 = dve.done(nc.vector.tensor_tensor(out=S[:], in0=S[:], in1=pKV[:], op=ALU.add))
                pKV_free = t_1
                dve.wait(t_1, t_c)
                t_2 = dve.done(nc.vector.tensor_tensor(out=v3(S[:]), in0=v3(S[:]),
                                                       in1=dec[:].unsqueeze(2).to_broadcast([128, NH, 128]),
                                                       op=ALU.mult))
                act.wait(t_2, Sb_free[1 - bi])
                t_sb = act.done(nc.scalar.copy(out=Sb[1 - bi][:], in_=S[:]))
                t_S = t_sb
                dve.wait(t_o, ro_free[bi], rs_free)
                t_r = dve.done(nc.vector.tensor_tensor(out=v3(ro[bi][:]), in0=v3(pO[:]),
                                                       in1=qdec[:].unsqueeze(2).to_broadcast([128, NH, 128]),
                                                       op=ALU.mult))
                pO_free = t_r
                dve.wait(t_r)
                for h in range(NH):
                    t_b = dve.done(nc.vector.bn_stats(out=st8[:, h, :], in_=ro[bi][:, h * 128:(h + 1) * 128]))
                dve.wait(t_b)
                for h in range(NH):
                    t_a = dve.done(nc.vector.bn_aggr(out=mv8[:, h, :], in_=st8[:, h, :]))
                dve.wait(t_a)
                t_e = dve.done(nc.vector.tensor_scalar(out=rs[:, 0, :], in0=mv8[:, :, 1], scalar1=LN_EPS, scalar2=None,
                                                       op0=ALU.add))
                act.wait(t_e)
                t_q = act.done(nc.scalar.activation(out=rs[:, 1, :], in_=rs[:, 0, :], func=AF.Sqrt))
                dve.wait(t_q)
                t_i = dve.done(nc.vector.reciprocal(out=rs[:, 2, :], in_=rs[:, 1, :]))
                dve.wait(t_i)
                t_nb = dve.done(nc.vector.scalar_tensor_tensor(out=rs[:, 1, :], in0=mv8[:, :, 0], scalar=-1.0,
                                                               in1=rs[:, 2, :], op0=ALU.mult, op1=ALU.mult))
                act.wait(t_nb)
                for h in range(NH):
                    hs = slice(h * 128, (h + 1) * 128)
                    t_p = act.done(nc.scalar.activation(out=ro[bi][:, hs], in_=ro[bi][:, hs], func=AF.Identity,
                                                        scale=rs[:, 2, h:h + 1], bias=rs[:, 1, h:h + 1]))
                rs_free = t_p
                dve.wait(t_p)
                t_p = dve.done(nc.vector.tensor_tensor(out=ro[bi][:], in0=ro[bi][:], in1=gng[:], op=ALU.mult))
                dve.wait(t_p)
                t_p = dve.done(nc.vector.tensor_tensor(out=ro[bi][:], in0=ro[bi][:], in1=gnb[:], op=ALU.add))
                dve.wait(t_p, ob_free[bi])
                t_f = dve.done(nc.vector.tensor_tensor(out=ob[bi][:], in0=ro[bi][:], in1=gtm4[li][:, cc, :],
                                                       op=ALU.mult))
                ro_free[bi] = t_f
                pool.wait(t_f)
                ob_free[bi] = s_o[bi].done(nc.gpsimd.dma_start(out=YATT[r0:r0 + 128, 0:1024], in_=ob[bi][:]))
            ld_free[li] = [pe.last(), dve.last()]
        drain(k, [x.last() for x in s_o])


def moba_phase(k, MQT, MKT, MVTM, KMEAN, causalT_d, YATT, ntok):
    nc = k.nc
    pe, act, dve, pool, sp = k.pe, k.act, k.dve, k.pool, k.sp
    nq = ntok // 128
    nblk = ntok // 256
    SC = 128.0 ** -0.5
    with ExitStack() as es:
        mqT = [k.sb(es, "m_q%d" % i, [128, ntok], BF16) for i in range(2)]
        mkT = [k.sb(es, "m_k%d" % i, [128, ntok], BF16) for i in range(2)]
        mv = [k.sb(es, "m_v%d" % i, [128, nq, 129], BF16) for i in range(2)]
        km = k.sb(es, "m_km", [nblk, 1024], F32)
        kmT = k.sb(es, "m_kmT", [128, NH, 32], BF16)
        identf = k.sb(es, "m_idf", [128, 128], F32)
        causalT = k.sb(es, "m_causal", [128, 128], F32)
        gm = k.sb(es, "m_gm", [128, 32], F32)
        top8 = k.sb(es, "m_top8", [128, 8], F32)
        sel = [k.sb(es, "m_sel%d" % i, [128, 32], F32) for i in range(2)]
        eo = [k.sb(es, "m_eo%d" % i, [128, 128], F32) for i in range(2)]
        PTo = [k.sb(es, "m_PTo%d" % i, [128, 2, 128], BF16) for i in range(2)]
        PTg = [k.sb(es, "m_PTg%d" % i, [128, 4, 128], BF16) for i in range(3)]
        O = [k.sb(es, "m_O%d" % i, [128, 132], F32) for i in range(2)]
        rinv = k.sb(es, "m_rinv", [128, 2], F32)
        mob = [k.sb(es, "m_mob%d" % i, [128, 128], BF16) for i in range(2)]
        pG = k.ps(es, "m_pG", [128, 512], F32)
        pSo = k.ps(es, "m_pSo", [128, 512], F32)
        pOo = k.ps(es, "m_pOo", [128, 512], F32)
        pS = [k.ps(es, "m_pS%d" % i, [128, 512], F32) for i in range(2)]
        pO2 = [k.ps(es, "m_pO2%d" % i, [128, 512], F32) for i in range(2)]
        s_h = [k.dsem(es, "mh%d" % i) for i in range(2)]
        s_c = k.dsem(es, "mconst")
        s_o = [k.dsem(es, "mout%d" % i) for i in range(2)]

        s_c.done(nc.sync.dma_start(out=km[:], in_=KMEAN[:, :]))
        t_c = s_c.done(nc.sync.dma_start(out=causalT[:], in_=causalT_d[:, :]))
        t0 = pool.done(nc.gpsimd.memset(identf[:], 1.0))
        pool.wait(t0)
        t_id = pool.done(nc.gpsimd.affine_select(out=identf[:], in_=identf[:], pattern=[[-1, 128]],
                                                 compare_op=ALU.is_equal, fill=0.0, base=0, channel_multiplier=1))
        for i in range(2):
            t_ones = pool.done(nc.gpsimd.memset(mv[i][:, :, 128:129], 1.0))
        t_prev = None
        for h in range(NH):
            pe.wait(t_c, t_id, t_prev)
            t_t = pe.done(nc.tensor.transpose(pG[:, 0:nblk], km[:, h * 128:(h + 1) * 128], identf[0:nblk, 0:nblk]))
            dve.wait(t_t)
            t_prev = dve.done(nc.vector.tensor_copy(out=kmT[:, h, 0:nblk], in_=pG[:, 0:nblk]))
        pG_free = t_prev

        h_free = [None, None]
        pSo_free = pOo_free = None
        pS_free = [None, None]
        pO2_free = [None, None]
        PTo_free = [None, None]
        PTg_free = [None] * 3
        O_free = [None, None]
        eo_free = [None, None]
        sel_free = [None, None]
        mob_free = [None, None]
        gm_t = None
        top_free = None
        rinv_free = None
        cnt = dict(pS=0, pO2=0, PTg=0, qi=0)

        for h in range(NH):
            hi = h % 2
            sp.wait(h_free[hi], t_ones)
            s_h[hi].done(nc.sync.dma_start(out=mqT[hi][:], in_=MQT[h]))
            s_h[hi].done(nc.sync.dma_start(out=mkT[hi][:], in_=MKT[h]))
            t_ld = s_h[hi].done(nc.sync.dma_start(
                out=mv[hi][:, :, 0:128], in_=MVTM[:, h * 128:(h + 1) * 128].rearrange("(c p) d -> p c d", p=128)))
            dve.wait(gm_t)
            gm_t = dve.done(nc.vector.memset(gm[:], -1e30))
            for i in range(nq):
                nb = i // 2
                qi = cnt["qi"] % 2
                cnt["qi"] += 1
                qs = slice(i * 128, (i + 1) * 128)
                r0 = i * 128
                use_sel = nb > 3
                if use_sel:
                    pe.wait(t_ld, pG_free)
                    t_g = pe.done(nc.tensor.matmul(pG[:, 0:32], lhsT=mqT[hi][:, qs], rhs=kmT[:, h, :],
                                                   start=True, stop=True))
                    dve.wait(t_g, gm_t, top_free)
                    t_gm = dve.done(nc.vector.tensor_copy(out=gm[:, 0:nb], in_=pG[:, 0:nb]))
                    pG_free = t_gm
                    dve.wait(t_gm)
                    t_t8 = dve.done(nc.vector.max(out=top8[:], in_=gm[:]))
                    dve.wait(t_t8, sel_free[qi])
                    t_sel = dve.done(nc.vector.tensor_scalar(out=sel[qi][:], in0=gm[:], scalar1=top8[:, 2:3],
                                                             scalar2=None, op0=ALU.is_ge))
                    gm_t = t_sel
                    top_free = t_sel
                ncs = 1 + (i % 2)
                pe.wait(t_ld, pSo_free)
                ins = nc.tensor.matmul(pSo[:, 0:128], lhsT=mkT[hi][:, qs], rhs=mqT[hi][:, qs], start=True, stop=True)
                if ncs == 2:
                    ins = nc.tensor.matmul(pSo[:, 128:256], lhsT=mkT[hi][:, (i - 1) * 128:i * 128], rhs=mqT[hi][:, qs],
                                           start=True, stop=True)
                t_so = pe.done(ins)
                act.wait(t_so, eo_free[qi], PTo_free[qi])
                t_e = act.done(nc.scalar.activation(out=eo[qi][:], in_=pSo[:, 0:128], func=AF.Exp, scale=SC))
                if ncs == 2:
                    t_e2 = act.done(nc.scalar.activation(out=PTo[qi][:, 1, :], in_=pSo[:, 128:256], func=AF.Exp,
                                                         scale=SC))
                else:
                    t_e2 = t_e
                pSo_free = t_e2
                dve.wait(t_e, t_c)
                t_pd = dve.done(nc.vector.tensor_tensor(out=PTo[qi][:, 0, :], in0=eo[qi][:], in1=causalT[:],
                                                        op=ALU.mult))
                eo_free[qi] = t_pd
                ng = (nb + 1) // 2

                def emit_S(g):
                    nbg = min(2, nb - 2 * g)
                    si = cnt["pS"] % 2
                    cnt["pS"] += 1
                    pe.wait(pS_free[si])
                    for b in range(nbg):
                        n = 2 * g + b
                        for c2 in range(2):
                            ks = slice(n * 256 + c2 * 128, n * 256 + (c2 + 1) * 128)
                            ins_ = nc.tensor.matmul(pS[si][:, (b * 2 + c2) * 128:(b * 2 + c2 + 1) * 128],
                                                    lhsT=mkT[hi][:, ks], rhs=mqT[hi][:, qs], start=True, stop=True)
                    t_s = pe.done(ins_)
                    pj = cnt["PTg"] % 3
                    cnt["PTg"] += 1
                    act.wait(t_s, PTg_free[pj])
                    t_p = act.done(nc.scalar.activation(
                        out=PTg[pj][:, 0:2 * nbg, :].rearrange("p a b -> p (a b)"), in_=pS[si][:, 0:nbg * 256],
                        func=AF.Exp, scale=SC))
                    pS_free[si] = t_p
                    return (g, nbg, pj, t_p)

                pend = emit_S(0) if ng > 0 else None
                pe.wait(t_pd, t_e2, pOo_free)
                ins = nc.tensor.matmul(pOo[:, 0:129], lhsT=PTo[qi][:, 0, :], rhs=mv[hi][:, i, :], start=True,
                                       stop=(ncs == 1))
                if ncs == 2:
                    ins = nc.tensor.matmul(pOo[:, 0:129], lhsT=PTo[qi][:, 1, :], rhs=mv[hi][:, i - 1, :], start=False,
                                           stop=True)
                t_oo = pe.done(ins)
                PTo_free[qi] = t_oo
                dve.wait(t_oo, O_free[qi])
                t_O = dve.done(nc.vector.tensor_copy(out=O[qi][:, 0:129], in_=pOo[:, 0:129]))
                pOo_free = t_O
                for g in range(ng):
                    nxt = emit_S(g + 1) if g + 1 < ng else None
                    _, nbg, pj, t_p = pend
                    oi = cnt["pO2"] % 2
                    cnt["pO2"] += 1
                    pe.wait(t_p, pO2_free[oi])
                    for b in range(nbg):
                        n = 2 * g + b
                        for c2 in range(2):
                            ins = nc.tensor.matmul(pO2[oi][:, b * 132:b * 132 + 129], lhsT=PTg[pj][:, b * 2 + c2, :],
                                                   rhs=mv[hi][:, n * 2 + c2, :], start=(c2 == 0), stop=(c2 == 1))
                    t_pv = pe.done(ins)
                    PTg_free[pj] = t_pv
                    for b in range(nbg):
                        n = 2 * g + b
                        dve.wait(t_pv, t_O)
                        if use_sel:
                            t_O = dve.done(nc.vector.scalar_tensor_tensor(
                                out=O[qi][:, 0:129], in0=pO2[oi][:, b * 132:b * 132 + 129], scalar=sel[qi][:, n:n + 1],
                                in1=O[qi][:, 0:129], op0=ALU.mult, op1=ALU.add))
                        else:
                            t_O = dve.done(nc.vector.tensor_tensor(out=O[qi][:, 0:129], in0=O[qi][:, 0:129],
                                                                   in1=pO2[oi][:, b * 132:b * 132 + 129], op=ALU.add))
                    pO2_free[oi] = t_O
                    pend = nxt
                if use_sel:
                    sel_free[qi] = t_O
                dve.wait(t_O, rinv_free)
                t_r = dve.done(nc.vector.reciprocal(out=rinv[:, 0:1], in_=O[qi][:, 128:129]))
                dve.wait(t_r, mob_free[qi])
                t_m = dve.done(nc.vector.tensor_scalar(out=mob[qi][:], in0=O[qi][:, 0:128], scalar1=rinv[:, 0:1],
                                                       scalar2=None, op0=ALU.mult))
                rinv_free = t_m
                O_free[qi] = t_m
                pool.wait(t_m)
                mob_free[qi] = s_o[qi].done(nc.gpsimd.dma_start(
                    out=YATT[r0:r0 + 128, 1024 + h * 128:1024 + (h + 1) * 128], in_=mob[qi][:]))
            h_free[hi] = pe.last()
        drain(k, [x.last() for x in s_o])


def consts(ntok, pos0):
    pos = (pos0 + np.arange(ntok)).astype(np.float32)
    invf = (10000.0 ** (-np.arange(64, dtype=np.float32) / np.float32(64))).astype(np.float32)
    ang = (pos[:, None] * invf[None, :]).astype(np.float32)
    lg = np.log1p(-np.exp2(-5.0 - np.arange(NH, dtype=np.float64)))
    p = np.arange(128, dtype=np.float64)
    c = np.arange(128)
    return dict(
        cos=np.cos(ang).astype(np.float32), sin=np.sin(ang).astype(np.float32),
        ksc=(128.0 ** -0.5 * np.exp(-(p[:, None] + 1) * lg[None, :])).astype(np.float32),
        qdec=np.exp((p[:, None] + 1) * lg[None, :]).astype(np.float32),
        dec=np.tile(np.exp(128.0 * lg)[None, :], (128, 1)).astype(np.float32),
        causal=(c[None, :] >= c[:, None]).astype(np.float32),
    )


DRNN = 2816
RC = DRNN // 128
GK = 1.5957691216057308


def rnn_inproj_phase(k, X_in, w_in, wtick, GATE, RNNBR, ntok):
    nc = k.nc
    pe, act, dve, pool, sp = k.pe, k.act, k.dve, k.pool, k.sp
    ntiles = ntok // TILE
    with ExitStack() as es:
        stage = [k.sb(es, "n_stage%d" % i, [128, D], F32) for i in range(2)]
        xb = [k.sb(es, "n_xb%d" % i, [128, D], BF16) for i in range(4)]
        hT = k.sb(es, "n_hT", [128, KC, TILE], BF16)
        ws = [k.sb(es, "n_w%d" % i, [128, KC, 128], BF16) for i in range(3)]
        xf = [k.sb(es, "n_xf%d" % i, [128, TILE], F32) for i in range(3)]
        t1 = [k.sb(es, "n_t1%d" % i, [128, TILE], F32) for i in range(2)]
        gb = [k.sb(es, "n_gb%d" % i, [128, TILE], BF16) for i in range(2)]
        identb = k.sb(es, "n_idb", [128, 128], BF16)
        pT = k.ps(es, "n_pT", [128, 512], BF16)
        acc = [k.ps(es, "n_acc%d" % i, [128, 512], F32) for i in range(4)]
        s_stage = [k.dsem(es, "nstage%d" % i) for i in range(2)]
        s_w = [k.dsem(es, "nw%d" % i) for i in range(3)]
        s_og = [k.dsem(es, "nog%d" % i) for i in range(2)]
        s_ox = [k.dsem(es, "nox%d" % i) for i in range(3)]
        t0 = pool.done(nc.gpsimd.memset(identb[:], 1.0))
        pool.wait(t0)
        t_id = pool.done(nc.gpsimd.affine_select(out=identb[:], in_=identb[:], pattern=[[-1, 128]],
                                                 compare_op=ALU.is_equal, fill=0.0, base=0, channel_multiplier=1))
        stage_free = [None, None]
        xb_free = [None] * 4
        w_free = [None] * 3
        acc_free = [None] * 4
        xf_free = [None] * 3
        t1_free = [None, None]
        gb_free = [None, None]
        cnt = dict(stage=0, xb=0, w=0, acc=0, xf=0, t1=0, gb=0)
        state = dict(hT_free=None, pT_free=None)

        def emit_T_load(t, subs):
            res = []
            for s in subs:
                r0 = t * TILE + s * 128
                xi = cnt["xb"] % 4
                cnt["xb"] += 1
                si = cnt["stage"] % 2
                cnt["stage"] += 1
                sp.wait(stage_free[si])
                t_ld = s_stage[si].done(nc.sync.dma_start(out=stage[si][:], in_=X_in[r0:r0 + 128, :]))
                pool.wait(t_ld, xb_free[xi])
                t_cc = pool.done(nc.gpsimd.tensor_copy(out=xb[xi][:], in_=stage[si][:]))
                stage_free[si] = t_cc
                res.append((xi, t_cc))
            return res

        def emit_T(t, loads=None):
            if loads is None:
                loads = emit_T_load(t, (0, 1, 2, 3))
            for s, (xi, t_cc) in enumerate(loads):
                for q in range(4):
                    pe.wait(t_cc, state["pT_free"], t_id)
                    for j in range(4):
                        kc = 4 * q + j
                        ins = nc.tensor.transpose(pT[:, j * 128:(j + 1) * 128],
                                                  xb[xi][:, kc * 128:(kc + 1) * 128], identb[:])
                    t_tr = pe.done(ins)
                    ev = dve if (q % 2 == 0) else act
                    ev.wait(t_tr, state["hT_free"])
                    dst = hT[:, 4 * q:4 * q + 4, s * 128:(s + 1) * 128]
                    srcp = pT[:].rearrange("p (a b) -> p a b", b=128)
                    if ev is dve:
                        t_ev = dve.done(nc.vector.tensor_copy(out=dst, in_=srcp))
                    else:
                        t_ev = act.done(nc.scalar.copy(out=dst, in_=srcp))
                    state["pT_free"] = t_ev
                xb_free[xi] = t_tr
            return [dve.last(), act.last()]

        hT_ready = emit_T(0)
        for t in range(ntiles):
            sl = slice(t * TILE, (t + 1) * TILE)
            for oc in range(2 * RC):
                if oc in (4, 10, 16, 22) and t + 1 < ntiles:
                    if oc == 4:
                        state["next_loads"] = []
                    state["next_loads"] += emit_T_load(t + 1, ((oc - 4) // 6,))
                wi = cnt["w"] % 3
                cnt["w"] += 1
                sp.wait(w_free[wi], wtick)
                t_w = s_w[wi].done(nc.sync.dma_start(out=ws[wi][:], in_=w_in[oc]))
                ai = cnt["acc"] % 4
                cnt["acc"] += 1
                pe.wait(t_w, hT_ready, acc_free[ai])
                for kc in range(KC):
                    ins = nc.tensor.matmul(acc[ai][:], lhsT=ws[wi][:, kc, :], rhs=hT[:, kc, :],
                                           start=(kc == 0), stop=(kc == KC - 1))
                t_m = pe.done(ins)
                w_free[wi] = t_m
                xi = cnt["xf"] % 3
                cnt["xf"] += 1
                act.wait(t_m, xf_free[xi])
                t_x = act.done(nc.scalar.copy(out=xf[xi][:], in_=acc[ai][:]))
                acc_free[ai] = t_x
                if oc >= RC:
                    pool.wait(t_x)
                    xf_free[xi] = s_ox[xi].done(nc.gpsimd.dma_start(out=RNNBR[oc - RC, :, sl], in_=xf[xi][:]))
                else:
                    ti = cnt["t1"] % 2
                    cnt["t1"] += 1
                    gi = cnt["gb"] % 2
                    cnt["gb"] += 1
                    dve.wait(t_x, t1_free[ti])
                    t_a = dve.done(nc.vector.tensor_tensor(out=t1[ti][:], in0=xf[xi][:], in1=xf[xi][:], op=ALU.mult))
                    dve.wait(t_a)
                    t_a = dve.done(nc.vector.tensor_scalar(out=t1[ti][:], in0=t1[ti][:], scalar1=0.044715, scalar2=1.0,
                                                           op0=ALU.mult, op1=ALU.add))
                    dve.wait(t_a)
                    t_a = dve.done(nc.vector.tensor_tensor(out=t1[ti][:], in0=t1[ti][:], in1=xf[xi][:], op=ALU.mult))
                    act.wait(t_a)
                    t_e = act.done(nc.scalar.activation(out=t1[ti][:], in_=t1[ti][:], func=AF.Exp, scale=-GK))
                    dve.wait(t_e)
                    t_a = dve.done(nc.vector.tensor_scalar(out=t1[ti][:], in0=t1[ti][:], scalar1=1.0, scalar2=None,
                                                           op0=ALU.add))
                    dve.wait(t_a)
                    t_a = dve.done(nc.vector.reciprocal(out=t1[ti][:], in_=t1[ti][:]))
                    dve.wait(t_a, gb_free[gi])
                    t_g = dve.done(nc.vector.tensor_tensor(out=gb[gi][:], in0=t1[ti][:], in1=xf[xi][:], op=ALU.mult))
                    t1_free[ti] = t_g
                    xf_free[xi] = t_g
                    pool.wait(t_g)
                    gb_free[gi] = s_og[gi].done(nc.gpsimd.dma_start(out=GATE[oc, :, sl], in_=gb[gi][:]))
            state["hT_free"] = t_m
            if t + 1 < ntiles:
                hT_ready = emit_T(t + 1, state["next_loads"])
        drain(k, [x.last() for x in s_og + s_ox])


def prep_bd(k, es, w_src, name):
    nc = k.nc
    out = dram(nc, name, [DRNN, DRNN], BF16)
    z = k.sb(es, name + "_z", [128, DRNN], BF16)
    ds = k.dsem(es, name + "z")
    t0 = k.pool.done(nc.gpsimd.memset(z[:], 0.0))
    k.pool.wait(t0)
    t = None
    for c in range(RC):
        t = ds.done(nc.gpsimd.dma_start(out=out[c * 128:(c + 1) * 128, :], in_=z[:]))
    k.pool.wait(t)
    ds2 = k.dsem(es, name + "b")
    for g in range(16):
        t = ds2.done(nc.gpsimd.dma_start(out=out[g * 176:(g + 1) * 176, g * 176:(g + 1) * 176], in_=w_src[g]))
    return out, t


def rnn_core_phase(k, GATE, RNNBR, wa_bd, wx_bd, wtick, vecs, YRNN, ntok):
    nc = k.nc
    pe, act, dve, pool, sp = k.pe, k.act, k.dve, k.pool, k.sp
    ntiles = ntok // TILE
    with ExitStack() as es:
        wband = [k.sb(es, "c_wband%d" % i, [128, RC, 5, 128], BF16) for i in range(2)]
        vt = k.sb(es, "c_vt", [128, 9, RC], F32)
        cst = k.sb(es, "c_cst", [128, 6, RC], F32)
        hst = k.sb(es, "c_hst", [128, RC], F32)
        ones1 = k.sb(es, "c_ones", [128, 2], F32)
        NG = 4
        NX = 8
        xr = [k.sb(es, "c_xr%d" % i, [128, TILE + 4], F32) for i in range(NX)]
        u = k.sb(es, "c_u", [128, RC, TILE], F32)
        ub = k.sb(es, "c_ub", [128, RC, TILE], BF16)
        gt = [k.sb(es, "c_gt%d" % i, [128, TILE], BF16) for i in range(NG)]
        r_ = [k.sb(es, "c_r%d" % i, [128, TILE], F32) for i in range(NG)]
        i_ = [k.sb(es, "c_i%d" % i, [128, TILE], F32) for i in range(NG)]
        a_ = [k.sb(es, "c_a%d" % i, [128, TILE], F32) for i in range(NG)]
        b_ = [k.sb(es, "c_b%d" % i, [128, TILE], F32) for i in range(NG)]
        yb = [k.sb(es, "c_yb%d" % i, [128, TILE], BF16) for i in range(NG)]
        pR = [k.ps(es, "c_pR%d" % i, [128, 512], F32) for i in range(NG)]
        pI = [k.ps(es, "c_pI%d" % i, [128, 512], F32) for i in range(NG)]
        s_c = k.dsem(es, "cconst")
        s_x = [k.dsem(es, "cx%d" % i) for i in range(NX)]
        s_g = [k.dsem(es, "cg%d" % i) for i in range(NG)]
        s_o = [k.dsem(es, "co%d" % i) for i in range(NG)]

        sp.wait(wtick)
        for gi, wsrc in enumerate((wa_bd, wx_bd)):
            t0 = pool.done(nc.gpsimd.memset(wband[gi][:], 0.0))
            sp.wait(t0)
            for c in range(RC):
                lo, hi = max(0, c - 2), min(RC - 1, c + 2)
                t_wb = s_c.done(nc.sync.dma_start(
                    out=wband[gi][:, c, lo - c + 2:hi - c + 3, :],
                    in_=wsrc[lo * 128:(hi + 1) * 128, c * 128:(c + 1) * 128].rearrange("(j p) f -> p j f", p=128)))
        t_v = s_c.done(nc.sync.dma_start(out=vt[:].rearrange("p v c -> p (v c)"), in_=vecs[:, :]))
        t_wb = t_v
        act.wait(t_v)
        dve.wait(t_v)
        t = dve.done(nc.vector.tensor_scalar(out=cst[:, 0:2, :], in0=vt[:, 5:7, :], scalar1=-1.0, scalar2=None,
                                             op0=ALU.mult))
        t_e = act.done(nc.scalar.activation(out=cst[:, 4, :], in_=vt[:, 7, :], func=AF.Exp, scale=-1.0))
        dve.wait(t_e)
        t = dve.done(nc.vector.tensor_scalar(out=cst[:, 4, :], in0=cst[:, 4, :], scalar1=1.0, scalar2=None,
                                             op0=ALU.add))
        act.wait(t)
        t_e = act.done(nc.scalar.activation(out=cst[:, 5, :], in_=cst[:, 4, :], func=AF.Ln))
        dve.wait(t_e)
        dve.done(nc.vector.tensor_scalar(out=cst[:, 2, :], in0=cst[:, 5, :], scalar1=-8.0, scalar2=None, op0=ALU.mult))
        dve.done(nc.vector.tensor_scalar(out=cst[:, 3, :], in0=cst[:, 5, :], scalar1=-16.0, scalar2=None,
                                         op0=ALU.mult))
        dve.done(nc.vector.memset(ones1[:], 1.0))
        t_cst = dve.done(nc.vector.memset(hst[:], 0.0))
        for xi in range(NX):
            t_cst = dve.done(nc.vector.memset(xr[xi][:, 0:4], 0.0))

        xr_free = [None] * NX
        gt_free = [None] * NG
        pR_free = [None] * NG
        pI_free = [None] * NG
        buf_free = [None] * NG
        yb_free = [None] * NG
        u_free = None
        cntx = 0
        groups = [list(range(g0, min(RC, g0 + NG))) for g0 in range(0, RC, NG)]
        for t in range(ntiles):
            t0 = t * TILE
            sl = slice(t0, t0 + TILE)
            t_u = None
            for grp in groups:
                xs_ = {}
                tk = {}
                for c in grp:
                    xi = cntx % NX
                    cntx += 1
                    xs_[c] = xi
                    sp.wait(xr_free[xi], t_cst)
                    if t == 0:
                        tk[c] = s_x[xi].done(nc.sync.dma_start(out=xr[xi][:, 4:4 + TILE], in_=RNNBR[c, :, 0:TILE]))
                    else:
                        tk[c] = s_x[xi].done(nc.sync.dma_start(out=xr[xi][:, 0:4 + TILE],
                                                               in_=RNNBR[c, :, t0 - 4:t0 + TILE]))
                for c in grp:
                    xi = xs_[c]
                    act.wait(tk[c], u_free, t_cst)
                    tk[c] = act.done(nc.scalar.activation(out=u[:, c, :], in_=xr[xi][:, 4:4 + TILE], func=AF.Identity,
                                                          scale=vt[:, 3, c:c + 1], bias=vt[:, 4, c:c + 1]))
                for j in range(3):
                    for c in grp:
                        xi = xs_[c]
                        dve.wait(tk[c])
                        tk[c] = dve.done(nc.vector.scalar_tensor_tensor(
                            out=u[:, c, :], in0=xr[xi][:, 1 + j:1 + j + TILE], scalar=vt[:, j, c:c + 1],
                            in1=u[:, c, :], op0=ALU.mult, op1=ALU.add))
                for c in grp:
                    xr_free[xs_[c]] = tk[c]
                    act.wait(tk[c])
                    t_u = act.done(nc.scalar.copy(out=ub[:, c, :], in_=u[:, c, :]))
            for grp in groups:
                T = {}
                for j, c in enumerate(grp):
                    sp.wait(gt_free[j])
                    T[("g", c)] = s_g[j].done(nc.sync.dma_start(out=gt[j][:], in_=GATE[c, :, sl]))
                for j, c in enumerate(grp):
                    lo, hi = max(0, c - 2), min(RC - 1, c + 2)
                    pe.wait(t_u, t_wb, pR_free[j], pI_free[j])
                    for cc in range(lo, hi + 1):
                        nc.tensor.matmul(pR[j][:], lhsT=wband[0][:, c, cc - c + 2, :], rhs=ub[:, cc, :],
                                         start=(cc == lo), stop=(cc == hi))
                    for cc in range(lo, hi + 1):
                        ins = nc.tensor.matmul(pI[j][:], lhsT=wband[1][:, c, cc - c + 2, :], rhs=ub[:, cc, :],
                                               start=(cc == lo), stop=(cc == hi))
                    T[("m", c)] = pe.done(ins)
                for j, c in enumerate(grp):
                    act.wait(T[("m", c)], buf_free[j], t_cst)
                    T[("er", c)] = act.done(nc.scalar.activation(out=r_[j][:], in_=pR[j][:], func=AF.Exp, scale=-1.0,
                                                                 bias=cst[:, 0, c:c + 1]))
                    T[("ei", c)] = act.done(nc.scalar.activation(out=i_[j][:], in_=pI[j][:], func=AF.Exp, scale=-1.0,
                                                                 bias=cst[:, 1, c:c + 1]))
                    pR_free[j] = T[("er", c)]
                    pI_free[j] = T[("ei", c)]
                for j, c in enumerate(grp):
                    dve.wait(T[("er", c)])
                    T[("r1", c)] = dve.done(nc.vector.tensor_scalar(out=r_[j][:], in0=r_[j][:], scalar1=1.0,
                                                                    scalar2=None, op0=ALU.add))
                for j, c in enumerate(grp):
                    dve.wait(T[("r1", c)])
                    T[("r", c)] = dve.done(nc.vector.reciprocal(out=r_[j][:], in_=r_[j][:]))
                for j, c in enumerate(grp):
                    dve.wait(T[("ei", c)])
                    T[("i1", c)] = dve.done(nc.vector.tensor_scalar(out=i_[j][:], in0=i_[j][:], scalar1=1.0,
                                                                    scalar2=None, op0=ALU.add))
                for j, c in enumerate(grp):
                    dve.wait(T[("i1", c)])
                    T[("i2", c)] = dve.done(nc.vector.reciprocal(out=i_[j][:], in_=i_[j][:]))
                for j, c in enumerate(grp):
                    act.wait(T[("r", c)])
                    T[("a", c)] = act.done(nc.scalar.activation(out=a_[j][:], in_=r_[j][:], func=AF.Exp,
                                                                scale=cst[:, 2, c:c + 1]))
                    T[("a2", c)] = act.done(nc.scalar.activation(out=b_[j][:], in_=r_[j][:], func=AF.Exp,
                                                                 scale=cst[:, 3, c:c + 1]))
                for j, c in enumerate(grp):
                    act.wait(T[("a2", c)])
                    T[("ln", c)] = act.done(nc.scalar.activation(out=b_[j][:], in_=b_[j][:], func=AF.Ln, scale=-1.0,
                                                                 bias=ones1[:, 0:1]))
                for j, c in enumerate(grp):
                    act.wait(T[("ln", c)])
                    T[("sq", c)] = act.done(nc.scalar.activation(out=b_[j][:], in_=b_[j][:], func=AF.Exp, scale=0.5))
                for j, c in enumerate(grp):
                    dve.wait(T[("i2", c)])
                    T[("iu", c)] = dve.done(nc.vector.tensor_tensor(out=i_[j][:], in0=i_[j][:], in1=u[:, c, :],
                                                                    op=ALU.mult))
                for j, c in enumerate(grp):
                    dve.wait(T[("iu", c)], T[("sq", c)])
                    T[("b", c)] = dve.done(nc.vector.tensor_tensor(out=b_[j][:], in0=b_[j][:], in1=i_[j][:],
                                                                   op=ALU.mult))
                for j, c in enumerate(grp):
                    dve.wait(T[("b", c)], T[("a", c)], T[("a2", c)], t_cst)
                    T[("sc", c)] = dve.done(nc.vector.tensor_tensor_scan(
                        out=r_[j][:], data0=a_[j][:], data1=b_[j][:], initial=hst[:, c:c + 1],
                        op0=ALU.mult, op1=ALU.add))
                for j, c in enumerate(grp):
                    dve.wait(T[("sc", c)])
                    T[("h", c)] = dve.done(nc.vector.tensor_copy(out=hst[:, c:c + 1], in_=r_[j][:, TILE - 1:TILE]))
                for j, c in enumerate(grp):
                    dve.wait(T[("sc", c)], T[("g", c)], yb_free[j])
                    T[("y", c)] = dve.done(nc.vector.tensor_tensor(out=yb[j][:], in0=r_[j][:], in1=gt[j][:],
                                                                   op=ALU.mult))
                    gt_free[j] = T[("y", c)]
                    buf_free[j] = [T[("y", c)], T[("h", c)]]
                for j, c in enumerate(grp):
                    pool.wait(T[("y", c)])
                    yb_free[j] = s_o[j].done(nc.gpsimd.dma_start(out=YRNN[c, :, sl], in_=yb[j][:]))
            u_free = [pe.last(), dve.last()]
        drain(k, [x.last() for x in s_o])


BETA_UNUSED = None
IN_SPECS = [
    ("x", None), ("ln_g", [2, 3, D]), ("ln_b", [2, 3, D]),
    ("ffn_w_gate", [2, 2, D, DFF]), ("ffn_w_up", [2, 2, D, DFF]), ("ffn_w_down", [2, 2, DFF, D]),
    ("attn_w_in", [1, D, 7168]), ("ret_gn_g", [1, 1024]), ("ret_gn_b", [1, 1024]), ("attn_w_out", [1, D, D]),
    ("rnn_w_in", [1, D, 2 * DRNN]), ("rnn_gate_a_w", [1, 16, 176, 176]), ("rnn_gate_x_w", [1, 16, 176, 176]),
    ("rnn_w_out", [1, DRNN, D]),
    ("c_cos", None), ("c_sin", None), ("c_ksc", [128, NH]), ("c_qdec", [128, NH]), ("c_dec", [128, NH]),
    ("c_causal", [128, 128]), ("c_vecs", [128, 9 * RC]),
]


_DBG = False


def build_program(ntok, upto=99):
    nc = bass.Bass("TRN2", target_bir_lowering=False)
    I = {}
    for name, shape in IN_SPECS:
        if name == "x":
            shape = [ntok, D]
        elif name in ("c_cos", "c_sin"):
            shape = [ntok, 64]
        I[name] = dram(nc, name, shape, F32, "ExternalInput")
    out = dram(nc, "out", [ntok, D], F32, "ExternalOutput")
    dk = "ExternalOutput" if _DBG else "Internal"
    XA = dram(nc, "XA", [ntok, D], F32, dk)
    XB = dram(nc, "XB", [ntok, D], F32, dk)
    with ExitStack() as es:
        k = K(nc, es)

        def prep_ffn(l, j):
            a, t1 = prep_ws(k, es, I["ffn_w_gate"][l, j], "wg%d%d" % (l, j), D, DFF)
            b, t2 = prep_ws(k, es, I["ffn_w_up"][l, j], "wu%d%d" % (l, j), D, DFF)
            c, t3 = prep_ws(k, es, I["ffn_w_down"][l, j], "wd%d%d" % (l, j), DFF, D)
            return a, b, c, [t1, t2, t3]

        def ffn(l, j, w, src, dst):
            sublayer_phase(k, "ffn", src, dst, w[2], FC, w[3], I["ln_g"][l, 2 * j], I["ln_b"][l, 2 * j], ntok,
                           2.0 * ALPHA, 4.0 * LN_EPS, wg=w[0], wu=w[1])

        w00 = prep_ffn(0, 0)
        ffn(0, 0, w00, I["x"], XA if upto > 0 else out)
        if upto <= 0:
            return nc
        win, t_win = prep_tm(k, es, I["attn_w_in"][0], "awin", D, 7168)
        wout, t_wout = prep_ws(k, es, I["attn_w_out"][0], "awout", D, D)
        w01 = prep_ffn(0, 1)
        sc = [dram(nc, "QT", [NH, 128, ntok], BF16), dram(nc, "KT", [NH, 128, ntok], BF16),
              dram(nc, "KTM", [ntok, 1024], BF16), dram(nc, "VTM", [ntok, 1024], BF16),
              dram(nc, "GTM", [ntok, 1024], BF16), dram(nc, "MQT", [NH, 128, ntok], BF16),
              dram(nc, "MKT", [NH, 128, ntok], BF16), dram(nc, "MVTM", [ntok, 1024], BF16),
              dram(nc, "KMEAN", [ntok // 256, 1024], F32)]
        YATT = dram(nc, "YATT", [ntok, 2048], BF16)
        attn_inproj_phase(k, XA, win, [t_win], I["c_cos"], I["c_sin"], I["c_ksc"], sc, ntok)
        QT, KT, KTM, VTM, GTM, MQT, MKT, MVTM, KMEAN = sc
        retention_phase(k, QT, KT, KTM, VTM, GTM, I["ret_gn_g"][0], I["ret_gn_b"][0], I["c_causal"], I["c_dec"],
                        I["c_qdec"], YATT, ntok)
        moba_phase(k, MQT, MKT, MVTM, KMEAN, I["c_causal"], YATT, ntok)
        sublayer_phase(k, "proj_tm", XA, XB if upto > 1 else out, wout, KC, [t_wout], I["ln_g"][0, 1], I["ln_b"][0, 1],
                       ntok, ALPHA, LN_EPS, Y=YATT)
        if upto <= 1:
            return nc
        w10 = prep_ffn(1, 0)
        ffn(0, 1, w01, XB, XA)
        rin, t_rin = prep_ws(k, es, I["rnn_w_in"][0], "rwin", D, 2 * DRNN)
        wa, t_wa = prep_bd(k, es, I["rnn_gate_a_w"][0], "rwa")
        wx, t_wx = prep_bd(k, es, I["rnn_gate_x_w"][0], "rwx")
        rout, t_rout = prep_ws(k, es, I["rnn_w_out"][0], "rwout", DRNN, D)
        w11 = prep_ffn(1, 1)
        ffn(1, 0, w10, XA, XB)
        GATE = dram(nc, "GATE", [RC, 128, ntok], BF16)
        RNNBR = dram(nc, "RNNBR", [RC, 128, ntok], F32)
        YRNN = dram(nc, "YRNN", [RC, 128, ntok], BF16, dk)
        rnn_inproj_phase(k, XB, rin, [t_rin], GATE, RNNBR, ntok)
        rnn_core_phase(k, GATE, RNNBR, wa, wx, [t_wa, t_wx], I["c_vecs"], YRNN, ntok)
        sublayer_phase(k, "proj_fm", XB, XA if upto > 2 else out, rout, RC, [t_rout], I["ln_g"][1, 1], I["ln_b"][1, 1],
                       ntok, ALPHA, LN_EPS, Y=YRNN)
        if upto <= 2:
            return nc
        ffn(1, 1, w11, XA, out)
    return nc


_PROG = {}
_LAST = {}
SPREAD = True


def kernel(x, ln_g, ln_b, ffn_w_gate, ffn_w_up, ffn_w_down, attn_w_in, ret_gn_g, ret_gn_b, attn_w_out,
           rnn_w_in, rnn_conv_w, rnn_conv_b, rnn_gate_a_w, rnn_gate_a_b, rnn_gate_x_w, rnn_gate_x_b,
           rnn_lambda, rnn_w_out):
    f32 = lambda a: np.ascontiguousarray(np.asarray(a, dtype=np.float32))
    x = f32(x)
    B, S, _ = x.shape
    if S not in _PROG:
        _PROG[S] = build_program(S)
    nc = _PROG[S]
    C = consts(S, 0)
    vecs = np.zeros((9, DRNN), np.float32)
    vecs[0:4] = f32(rnn_conv_w)[0]
    vecs[4] = f32(rnn_conv_b)[0]
    vecs[5] = f32(rnn_gate_a_b)[0]
    vecs[6] = f32(rnn_gate_x_b)[0]
    vecs[7] = f32(rnn_lambda)[0]
    shared = dict(ln_g=f32(ln_g), ln_b=f32(ln_b), ffn_w_gate=f32(ffn_w_gate), ffn_w_up=f32(ffn_w_up),
                  ffn_w_down=f32(ffn_w_down), attn_w_in=f32(attn_w_in), ret_gn_g=f32(ret_gn_g),
                  ret_gn_b=f32(ret_gn_b), attn_w_out=f32(attn_w_out), rnn_w_in=f32(rnn_w_in),
                  rnn_gate_a_w=f32(rnn_gate_a_w), rnn_gate_x_w=f32(rnn_gate_x_w), rnn_w_out=f32(rnn_w_out),
                  c_cos=C["cos"], c_sin=C["sin"], c_ksc=C["ksc"], c_qdec=C["qdec"], c_dec=C["dec"],
                  c_causal=C["causal"],
                  c_vecs=np.ascontiguousarray(vecs.reshape(9, RC, 128).transpose(2, 0, 1).reshape(128, 9 * RC)))
    if B == 4 and SPREAD:
        act_cores = [0, 1, 4, 5]
        zeros = {n: np.zeros_like(v) for n, v in shared.items()}
        zeros["x"] = np.zeros_like(x[0])
        in_maps = [zeros] * NCORES
        in_maps = list(in_maps)
        for b, c in enumerate(act_cores):
            in_maps[c] = dict(shared, x=x[b])
        res = run_bass_kernel_spmd(nc, in_maps, core_ids=list(range(NCORES)))
        outs = [res.results[c] for c in act_cores]
    else:
        in_maps = [dict(shared, x=x[b]) for b in range(B)]
        res = run_bass_kernel_spmd(nc, in_maps, core_ids=list(range(B)))
        outs = res.results
    _LAST["r"] = outs[0]
    return np.stack([np.asarray(r["out"], dtype=np.float32) for r in outs], axis=0)
```

```python
import math
from contextlib import ExitStack

import numpy as np
import concourse.bass as bass
import concourse.mybir as mybir
from concourse.bass_utils import run_bass_kernel_spmd

F32 = mybir.dt.float32
BF16 = mybir.dt.bfloat16
AF = mybir.ActivationFunctionType
ALU = mybir.AluOpType

D = 2048
DFF = 5632
KC = D // 128
FC = DFF // 128
DEPTH = 2
ALPHA = (2.0 * DEPTH) ** 0.25
LN_EPS = 1e-5
NCORES = 8
TOK = 4096
TILE = 512


class EngW:
    def __init__(self, nc, es, name, eng):
        self.nc, self.name, self.eng = nc, name, eng
        self.sem = es.enter_context(nc.semaphore("pg_" + name))
        self.count = 0
        self.waited = {}

    def wait(self, *tickets):
        for t in tickets:
            if t is None:
                continue
            if isinstance(t, list):
                self.wait(*t)
                continue
            sem, val, key = t
            if self.waited.get(key, 0) >= val:
                continue
            self.eng.wait_ge(sem, val)
            self.waited[key] = val

    def done(self, inst):
        self.count += 1
        inst.then_inc(self.sem, 1)
        return (self.sem, self.count, self.name)

    def last(self):
        return (self.sem, self.count, self.name) if self.count else None


class DmaSem:
    _n = 0

    def __init__(self, nc, es, name):
        DmaSem._n += 1
        self.key = "dma_%d" % DmaSem._n
        self.sem = es.enter_context(nc.semaphore(self.key))
        self.count = 0

    def done(self, inst):
        self.count += 16
        inst.then_inc(self.sem, 16)
        return (self.sem, self.count, self.key)

    def last(self):
        return (self.sem, self.count, self.key) if self.count else None


class Ring:
    def __init__(self, bufs):
        self.bufs = bufs
        self.free = [None] * len(bufs)
        self.i = 0

    def next(self):
        j = self.i % len(self.bufs)
        self.i += 1
        return j


class K:
    def __init__(self, nc, es):
        self.nc, self.es = nc, es
        self.pe = EngW(nc, es, "pe", nc.tensor)
        self.act = EngW(nc, es, "act", nc.scalar)
        self.dve = EngW(nc, es, "dve", nc.vector)
        self.pool = EngW(nc, es, "pool", nc.gpsimd)
        self.sp = EngW(nc, es, "sp", nc.sync)
        self.sem_pool = []
        self.nsem = 0

    _uid = 0

    def sb(self, es, name, shape, dt):
        K._uid += 1
        return es.enter_context(self.nc.sbuf_tensor("%s_%d" % (name, K._uid), shape, dt))

    def ps(self, es, name, shape, dt):
        K._uid += 1
        return es.enter_context(self.nc.psum_tensor("%s_%d" % (name, K._uid), shape, dt))

    def dsem(self, es, name):
        if self.sem_pool and self.nsem >= 80:
            self.sem_pool.sort(key=lambda d: d.count)
            ds = self.sem_pool.pop(0)
        else:
            ds = DmaSem(self.nc, self.es, name)
            self.nsem += 1
        es.callback(self.sem_pool.append, ds)
        return ds


def dram(nc, name, shape, dt, kind="Internal"):
    return nc.dram_tensor(name, list(shape), dt, kind=kind).ap()


def prep_ws(k, es, w_src, name, kdim, ndim):
    nc = k.nc
    nkc, nnc = kdim // 128, ndim // 128
    out = dram(nc, name, [nnc, 128, nkc, 128], BF16)
    ds = k.dsem(es, name)
    t = None
    for c in range(nnc):
        src = w_src[:, c * 128:(c + 1) * 128].rearrange("(kc p) f -> p kc f", p=128)
        t = ds.done(nc.gpsimd.dma_start(out=out[c], in_=src))
    return out, t


def sublayer_phase(k, mode, X_in, X_out, wd, nk, wtick, g_row, b_row, ntok, cx, eps, wg=None, wu=None, Y=None):
    nc = k.nc
    pe, act, dve, pool, sp = k.pe, k.act, k.dve, k.pool, k.sp
    ntiles = ntok // TILE
    ffn = mode == "ffn"
    with ExitStack() as es:
        xs = k.sb(es, "f_xs", [128, 4, D], F32)
        xbw = D if mode != "proj_tm" else nk * 128
        xb = [k.sb(es, "f_xb%d" % i, [128, xbw], BF16) for i in range(4)]
        hid = k.sb(es, "f_hid", [128, nk, TILE], BF16)
        wds = [k.sb(es, "f_wd%d" % i, [128, nk, 128], BF16) for i in range(2)]
        ytmp = [k.sb(es, "f_yt%d" % i, [128, TILE], F32) for i in range(2)]
        grep = k.sb(es, "f_g", [128, D], F32)
        brep = k.sb(es, "f_b", [128, D], F32)
        identb = k.sb(es, "f_idb", [128, 128], BF16)
        identf = k.sb(es, "f_idf", [128, 128], F32)
        st = k.sb(es, "f_st", [128, 4, 6], F32)
        mv = k.sb(es, "f_mv", [128, 8], F32)
        pT = [k.ps(es, "f_pT%d" % i, [128, 512], BF16) for i in range(2)]
        acc = [k.ps(es, "f_acc%d" % i, [128, 512], F32) for i in range(4)]
        pZ = [k.ps(es, "f_pZ%d" % i, [128, 512], F32) for i in range(2)]
        if ffn:
            stage = [k.sb(es, "f_stage%d" % i, [128, D], F32) for i in range(2)]
            hT = k.sb(es, "f_hT", [128, KC, TILE], BF16)
            wgs = [k.sb(es, "f_wg%d" % i, [128, KC, 128], BF16) for i in range(3)]
            wus = [k.sb(es, "f_wu%d" % i, [128, KC, 128], BF16) for i in range(3)]
            sgb = [k.sb(es, "f_sg%d" % i, [128, TILE], F32) for i in range(2)]
            s_stage = [k.dsem(es, "stage%d" % i) for i in range(2)]
            s_wgu = [k.dsem(es, "wgu%d" % i) for i in range(3)]
        s_xb = [k.dsem(es, "xb%d" % i) for i in range(4)]
        s_wd = [k.dsem(es, "wd%d" % i) for i in range(2)]
        s_xs = k.dsem(es, "xs")
        s_gb = k.dsem(es, "gb")
        s_out = k.dsem(es, "out")
        s_hid = k.dsem(es, "hid")

        t_gb = s_gb.done(nc.sync.dma_start(out=grep[:], in_=g_row.partition_broadcast(128)))
        t_gb = s_gb.done(nc.sync.dma_start(out=brep[:], in_=b_row.partition_broadcast(128)))
        for idt in (identb, identf):
            t0 = pool.done(nc.gpsimd.memset(idt[:], 1.0))
            pool.wait(t0)
            t_id = pool.done(nc.gpsimd.affine_select(out=idt[:], in_=idt[:], pattern=[[-1, 128]],
                                                     compare_op=ALU.is_equal, fill=0.0, base=0,
                                                     channel_multiplier=1))

        stage_free = [None, None]
        xb_free = [None] * 4
        pT_free = [None, None]
        acc_free = [None] * 4
        pZ_free = [None, None]
        wgu_free = [None] * 3
        wd_free = [None] * 2
        sg_free = [None, None]
        yt_free = [None, None]
        cnt = dict(stage=0, xb=0, pT=0, acc=0, pZ=0, wgu=0, wd=0, sg=0, yt=0)
        state = dict(hT_free=None, hid_free=None, xs_free=None, st_free=None, xs_ready=None)

        def load_xs(t):
            sp.wait(state["xs_free"])
            for s in range(4):
                r0 = t * TILE + s * 128
                state["xs_ready"] = s_xs.done(nc.sync.dma_start(out=xs[:, s, :], in_=X_in[r0:r0 + 128, :]))

        def emit_T_load(t, src, is_f32, subs=(0, 1, 2, 3)):
            res = []
            for s in subs:
                r0 = t * TILE + s * 128
                xi = cnt["xb"] % 4
                cnt["xb"] += 1
                if is_f32:
                    si = cnt["stage"] % 2
                    cnt["stage"] += 1
                    sp.wait(stage_free[si])
                    t_ld = s_stage[si].done(nc.sync.dma_start(out=stage[si][:], in_=src[r0:r0 + 128, :]))
                    act.wait(t_ld, xb_free[xi])
                    t_c = act.done(nc.scalar.copy(out=xb[xi][:], in_=stage[si][:]))
                    stage_free[si] = t_c
                else:
                    sp.wait(xb_free[xi])
                    t_c = s_xb[xi].done(nc.sync.dma_start(out=xb[xi][:], in_=src[r0:r0 + 128, :]))
                res.append((xi, t_c))
            return res

        def emit_T_pe(loads, nkc, dest, dest_free_key):
            for s, (xi, t_c) in enumerate(loads):
                ngrp = (nkc + 3) // 4
                for q in range(ngrp):
                    n4 = min(4, nkc - 4 * q)
                    pi = cnt["pT"] % 2
                    cnt["pT"] += 1
                    pe.wait(t_c, pT_free[pi], t_id)
                    for j in range(n4):
                        kc = 4 * q + j
                        ins = nc.tensor.transpose(pT[pi][:, j * 128:(j + 1) * 128],
                                                  xb[xi][:, kc * 128:(kc + 1) * 128], identb[:])
                    t_tr = pe.done(ins)
                    ev = dve if (q % 2 == 0) else act
                    ev.wait(t_tr, state[dest_free_key])
                    dst = dest[:, 4 * q:4 * q + n4, s * 128:(s + 1) * 128]
                    srcp = pT[pi][:, 0:n4 * 128].rearrange("p (a b) -> p a b", b=128)
                    if ev is dve:
                        t_ev = dve.done(nc.vector.tensor_copy(out=dst, in_=srcp))
                    else:
                        t_ev = act.done(nc.scalar.copy(out=dst, in_=srcp))
                    pT_free[pi] = t_ev
                xb_free[xi] = t_tr
            return [dve.last(), act.last()]

        def emit_T(t, src, is_f32, nkc, dest, dest_free_key):
            return emit_T_pe(emit_T_load(t, src, is_f32), nkc, dest, dest_free_key)

        def emit_GU(t, hT_ready):
            for fc in range(FC):
                wi = cnt["wgu"] % 3
                cnt["wgu"] += 1
                sp.wait(wgu_free[wi], wtick)
                s_wgu[wi].done(nc.sync.dma_start(out=wgs[wi][:], in_=wg[fc]))
                t_w = s_wgu[wi].done(nc.sync.dma_start(out=wus[wi][:], in_=wu[fc]))
                if fc == 22:
                    load_xs(t)
                if fc in (4, 8, 12, 16) and t + 1 < ntiles:
                    if fc == 4:
                        state["next_loads"] = []
                    state["next_loads"] += emit_T_load(t + 1, X_in, True, subs=((fc - 4) // 4,))
                ag = cnt["acc"] % 4
                au = (cnt["acc"] + 1) % 4
                cnt["acc"] += 2
                pe.wait(t_w, hT_ready, acc_free[ag])
                for kc in range(KC):
                    ins = nc.tensor.matmul(acc[ag][:], lhsT=wgs[wi][:, kc, :], rhs=hT[:, kc, :],
                                           start=(kc == 0), stop=(kc == KC - 1))
                t_g = pe.done(ins)
                pe.wait(acc_free[au])
                for kc in range(KC):
                    ins = nc.tensor.matmul(acc[au][:], lhsT=wus[wi][:, kc, :], rhs=hT[:, kc, :],
                                           start=(kc == 0), stop=(kc == KC - 1))
                t_u = pe.done(ins)
                wgu_free[wi] = t_u
                gi = cnt["sg"] % 2
                cnt["sg"] += 1
                act.wait(t_g, sg_free[gi])
                t_s = act.done(nc.scalar.activation(out=sgb[gi][:], in_=acc[ag][:], func=AF.Silu))
                acc_free[ag] = t_s
                dve.wait(t_s, t_u, state["hid_free"])
                t_h = dve.done(nc.vector.tensor_tensor(out=hid[:, fc, :], in0=sgb[gi][:], in1=acc[au][:],
                                                       op=ALU.mult))
                sg_free[gi] = t_h
                acc_free[au] = t_h
            state["hT_free"] = t_u
            return t_h

        def emit_D(t, hid_ready):
            pend = None
            t_res = None

            def emit_tr(p):
                oc, yi, t_cp = p
                zi = cnt["pZ"] % 2
                cnt["pZ"] += 1
                pe.wait(t_cp, pZ_free[zi], t_id)
                for s in range(4):
                    ins = nc.tensor.transpose(pZ[zi][:, s * 128:(s + 1) * 128], ytmp[yi][:, s * 128:(s + 1) * 128],
                                              identf[:])
                t_tr = pe.done(ins)
                yt_free[yi] = t_tr
                dve.wait(t_tr, state["xs_ready"])
                t_r = dve.done(nc.vector.scalar_tensor_tensor(
                    out=xs[:, :, oc * 128:(oc + 1) * 128], in0=xs[:, :, oc * 128:(oc + 1) * 128],
                    scalar=float(cx), in1=pZ[zi][:].rearrange("p (a b) -> p a b", b=128),
                    op0=ALU.mult, op1=ALU.add))
                pZ_free[zi] = t_r
                return t_r

            for oc in range(KC):
                wi = cnt["wd"] % 2
                cnt["wd"] += 1
                sp.wait(wd_free[wi], wtick)
                t_w = s_wd[wi].done(nc.sync.dma_start(out=wds[wi][:], in_=wd[oc]))
                ai = cnt["acc"] % 4
                cnt["acc"] += 1
                pe.wait(t_w, hid_ready, acc_free[ai])
                for fc in range(nk):
                    ins = nc.tensor.matmul(acc[ai][:], lhsT=wds[wi][:, fc, :], rhs=hid[:, fc, :],
                                           start=(fc == 0), stop=(fc == nk - 1))
                t_m = pe.done(ins)
                wd_free[wi] = t_m
                yi = cnt["yt"] % 2
                cnt["yt"] += 1
                act.wait(t_m, yt_free[yi])
                t_cp = act.done(nc.scalar.copy(out=ytmp[yi][:], in_=acc[ai][:]))
                acc_free[ai] = t_cp
                if pend is not None:
                    t_res = emit_tr(pend)
                pend = (oc, yi, t_cp)
            state["hid_free"] = t_m
            t_res = emit_tr(pend)
            return t_res

        def emit_LN(t, t_res):
            t_st = None
            for s in range(4):
                dve.wait(t_res, state["st_free"])
                for c in range(4):
                    t_b = dve.done(nc.vector.bn_stats(out=st[:, c, :], in_=xs[:, s, c * 512:(c + 1) * 512]))
                dve.wait(t_b)
                t_a = dve.done(nc.vector.bn_aggr(out=mv[:, 0:2], in_=st[:].rearrange("p a b -> p (a b)")))
                dve.wait(t_a)
                t_e = dve.done(nc.vector.tensor_scalar(out=mv[:, 2:3], in0=mv[:, 1:2], scalar1=float(eps),
                                                       scalar2=None, op0=ALU.add))
                act.wait(t_e)
                t_q = act.done(nc.scalar.activation(out=mv[:, 3:4], in_=mv[:, 2:3], func=AF.Sqrt))
                dve.wait(t_q)
                t_r = dve.done(nc.vector.reciprocal(out=mv[:, 4:5], in_=mv[:, 3:4]))
                dve.wait(t_r)
                t_n = dve.done(nc.vector.scalar_tensor_tensor(out=mv[:, 5:6], in0=mv[:, 0:1], scalar=-1.0,
                                                              in1=mv[:, 4:5], op0=ALU.mult, op1=ALU.mult))
                act.wait(t_n)
                t_x = act.done(nc.scalar.activation(out=xs[:, s, :], in_=xs[:, s, :], func=AF.Identity,
                                                    bias=mv[:, 5:6], scale=mv[:, 4:5]))
                state["st_free"] = t_x
                dve.wait(t_x, t_gb)
                t_p = dve.done(nc.vector.tensor_tensor(out=xs[:, s, :], in0=xs[:, s, :], in1=grep[:], op=ALU.mult))
                dve.wait(t_p)
                t_p = dve.done(nc.vector.tensor_tensor(out=xs[:, s, :], in0=xs[:, s, :], in1=brep[:], op=ALU.add))
                pool.wait(t_p)
                r0 = t * TILE + s * 128
                t_st = s_out.done(nc.gpsimd.dma_start(out=X_out[r0:r0 + 128, :], in_=xs[:, s, :]))
            state["xs_free"] = t_st
            return t_st

        t_out = None
        if ffn:
            hT_ready = emit_T(0, X_in, True, KC, hT, "hT_free")
            for t in range(ntiles):
                hid_ready = emit_GU(t, hT_ready)
                if t + 1 < ntiles:
                    hT_ready = emit_T_pe(state["next_loads"], KC, hT, "hT_free")
                t_res = emit_D(t, hid_ready)
                t_out = emit_LN(t, t_res)
        else:
            for t in range(ntiles):
                if mode == "proj_tm":
                    hid_ready = emit_T(t, Y, False, nk, hid, "hid_free")
                else:
                    sp.wait(state["hid_free"])
                    for c in range(nk):
                        hid_ready = s_hid.done(nc.sync.dma_start(out=hid[:, c, :],
                                                                 in_=Y[c, :, t * TILE:(t + 1) * TILE]))
                load_xs(t)
                t_res = emit_D(t, hid_ready)
                t_out = emit_LN(t, t_res)
        drain(k, [t_out])
        return t_out


def drain(k, extra=()):
    engs = (k.pe, k.act, k.dve, k.pool, k.sp)
    fin = [e.last() for e in engs] + list(extra)
    for e in engs:
        e.wait(fin)


NH = 8
KINDS = ["q", "k", "v", "g", "mq", "mk", "mv"]


def prep_tm(k, es, w_src, name, kdim, ndim, gw=512):
    nc = k.nc
    nkc, ncg = kdim // 128, ndim // gw
    out = dram(nc, name, [ncg, 128, nkc, gw], BF16)
    ds = k.dsem(es, name)
    t = None
    for c in range(ncg):
        src = w_src[:, c * gw:(c + 1) * gw].rearrange("(kc p) f -> p kc f", p=128)
        t = ds.done(nc.gpsimd.dma_start(out=out[c], in_=src))
    return out, t


def attn_inproj_phase(k, X_in, w_in, wtick, cos_d, sin_d, ksc_d, outs, ntok):
    nc = k.nc
    pe, act, dve, pool, sp = k.pe, k.act, k.dve, k.pool, k.sp
    QT, KT, KTM, VTM, GTM, MQT, MKT, MVTM, KMEAN = outs
    ntiles = ntok // TILE
    with ExitStack() as es:
        stage = [k.sb(es, "a_stage%d" % i, [128, D], F32) for i in range(2)]
        xb = [k.sb(es, "a_xb%d" % i, [128, D], BF16) for i in range(4)]
        hT = k.sb(es, "a_hT", [128, KC, TILE], BF16)
        wt = [k.sb(es, "a_wt%d" % i, [128, KC, 512], BF16) for i in range(2)]
        xsb = [k.sb(es, "a_xsb%d" % i, [128, 512], F32) for i in range(2)]
        ra = [k.sb(es, "a_ra%d" % i, [128, 512], F32) for i in range(2)]
        rb = [k.sb(es, "a_rb%d" % i, [128, 512], F32) for i in range(2)]
        tmb = [k.sb(es, "a_tmb%d" % i, [128, 512], BF16) for i in range(4)]
        fmt = [k.sb(es, "a_fmt%d" % i, [128, 4, TILE], BF16) for i in range(2)]
        cosr = [k.sb(es, "a_cos%d" % i, [128, 4, 64], F32) for i in range(2)]
        sinr = [k.sb(es, "a_sin%d" % i, [128, 4, 64], F32) for i in range(2)]
        ksc = k.sb(es, "a_ksc", [128, NH], F32)
        onesb = k.sb(es, "a_ones", [128, 2], BF16)
        kmrow = [k.sb(es, "a_kmrow%d" % i, [1, 512], F32) for i in range(2)]
        identb = k.sb(es, "a_idb", [128, 128], BF16)
        pT_t = k.ps(es, "a_pT", [128, 512], BF16)
        pT = [pT_t[:, :], pT_t[:, :]]
        acc = [k.ps(es, "a_acc%d" % i, [128, 512], F32) for i in range(4)]
        ptr = [k.ps(es, "a_ptr%d" % i, [128, 512], BF16)[:, :] for i in range(2)]
        pkm_t = k.ps(es, "a_pkm", [1, 512], F32)
        pkm = [pkm_t, pkm_t]

        s_stage = [k.dsem(es, "astage%d" % i) for i in range(2)]
        s_wt = [k.dsem(es, "awt%d" % i) for i in range(2)]
        s_cs = [k.dsem(es, "acs%d" % i) for i in range(2)]
        s_c = k.dsem(es, "aconst")
        s_st = [k.dsem(es, "ast%d" % i) for i in range(4)]
        s_fm = [k.dsem(es, "afm%d" % i) for i in range(2)]
        s_km = [k.dsem(es, "akm%d" % i) for i in range(2)]

        t_c = s_c.done(nc.sync.dma_start(out=ksc[:], in_=ksc_d[:, :]))
        t0 = pool.done(nc.gpsimd.memset(onesb[:], 1.0))
        t0 = pool.done(nc.gpsimd.memset(identb[:], 1.0))
        pool.wait(t0)
        t_id = pool.done(nc.gpsimd.affine_select(out=identb[:], in_=identb[:], pattern=[[-1, 128]],
                                                 compare_op=ALU.is_equal, fill=0.0, base=0, channel_multiplier=1))

        stage_free = [None, None]
        xb_free = [None] * 4
        pT_free = [None, None]
        acc_free = [None] * 4
        wt_free = [None, None]
        xsb_free = [None, None]
        ra_free = [None, None]
        rb_free = [None, None]
        tmb_free = [None] * 4
        ptr_free = [None, None]
        fmt_free = [None, None]
        cs_free = [None, None]
        pkm_free = [None, None]
        kmrow_free = [None, None]
        cnt = dict(stage=0, xb=0, pT=0, acc=0, wt=0, xsb=0, r=0, tmb=0, ptr=0, fmt=0, km=0)
        state = dict(hT_free=None)

        def emit_T_load(t, subs):
            res = []
            for s in subs:
                r0 = t * TILE + s * 128
                xi = cnt["xb"] % 4
                cnt["xb"] += 1
                si = cnt["stage"] % 2
                cnt["stage"] += 1
                sp.wait(stage_free[si])
                t_ld = s_stage[si].done(nc.sync.dma_start(out=stage[si][:], in_=X_in[r0:r0 + 128, :]))
                act.wait(t_ld, xb_free[xi])
                t_cc = act.done(nc.scalar.copy(out=xb[xi][:], in_=stage[si][:]))
                stage_free[si] = t_cc
                res.append((xi, t_cc))
            return res

        def emit_T(t, loads=None):
            if loads is None:
                loads = emit_T_load(t, (0, 1, 2, 3))
            for s, (xi, t_cc) in enumerate(loads):
                for q in range(4):
                    pi = 0
                    pe.wait(t_cc, pT_free[pi], t_id)
                    for j in range(4):
                        kc = 4 * q + j
                        ins = nc.tensor.transpose(pT[pi][:, j * 128:(j + 1) * 128],
                                                  xb[xi][:, kc * 128:(kc + 1) * 128], identb[:])
                    t_tr = pe.done(ins)
                    ev = dve if (q % 2 == 0) else act
                    ev.wait(t_tr, state["hT_free"])
                    dst = hT[:, 4 * q:4 * q + 4, s * 128:(s + 1) * 128]
                    srcp = pT[pi].rearrange("p (a b) -> p a b", b=128)
                    if ev is dve:
                        t_ev = dve.done(nc.vector.tensor_copy(out=dst, in_=srcp))
                    else:
                        t_ev = act.done(nc.scalar.copy(out=dst, in_=srcp))
                    pT_free[pi] = t_ev
                xb_free[xi] = t_tr
            return [dve.last(), act.last()]

        TMDST = {"k": KTM, "v": VTM, "g": GTM, "mv": MVTM}
        FMDST = {"q": QT, "k": KT, "mq": MQT, "mk": MKT}

        def post(t, cg, s, ai, t_m, ci, fj):
            kind = KINDS[cg // 2]
            dbg = getattr(k, "dbg", 9)
            if dbg < 4 and kind in ("q", "k"):
                kind = "mq" if kind == "q" else "mk"
            hb = (cg % 2) * 4
            r0 = t * TILE + s * 128
            j = cnt["tmb"] % 4
            cnt["tmb"] += 1
            if kind in ("q", "k"):
                xj = cnt["xsb"] % 2
                cnt["xsb"] += 1
                act.wait(t_m, xsb_free[xj])
                t_x = act.done(nc.scalar.copy(out=xsb[xj][:], in_=acc[ai][:]))
                acc_free[ai] = t_x
                r = cnt["r"] % 2
                cnt["r"] += 1
                x8 = xsb[xj][:].rearrange("p (a b) -> p a b", b=64)
                x42 = xsb[xj][:].rearrange("p (h a b) -> p h a b", a=2, b=64)
                rb42 = rb[r][:].rearrange("p (h a b) -> p h a b", a=2, b=64)
                cosb = cosr[ci][:, s, :].unsqueeze(1).to_broadcast([128, 8, 64])
                sinb = sinr[ci][:, s, :].unsqueeze(1).to_broadcast([128, 4, 64])
                dve.wait(t_x, ra_free[r], state["cs_ready"])
                t_a = dve.done(nc.vector.tensor_tensor(out=ra[r][:].rearrange("p (a b) -> p a b", b=64), in0=x8,
                                                       in1=cosb, op=ALU.mult))
                dve.wait(t_x, rb_free[r], state["cs_ready"])
                dve.done(nc.vector.tensor_tensor(out=rb42[:, :, 0, :], in0=x42[:, :, 1, :], in1=sinb, op=ALU.mult))
                t_b = dve.done(nc.vector.tensor_tensor(out=rb42[:, :, 1, :], in0=x42[:, :, 0, :], in1=sinb,
                                                       op=ALU.mult))
                xsb_free[xj] = [t_a, t_b]
                ra42 = ra[r][:].rearrange("p (h a b) -> p h a b", a=2, b=64)
                dve.wait(t_a, t_b, tmb_free[j])
                if kind == "q":
                    o42 = tmb[j][:].rearrange("p (h a b) -> p h a b", a=2, b=64)
                else:
                    o42 = ra42
                dve.done(nc.vector.tensor_tensor(out=o42[:, :, 0, :], in0=ra42[:, :, 0, :], in1=rb42[:, :, 0, :],
                                                 op=ALU.subtract))
                t_tm = dve.done(nc.vector.tensor_tensor(out=o42[:, :, 1, :], in0=ra42[:, :, 1, :],
                                                        in1=rb42[:, :, 1, :], op=ALU.add))
                if kind == "k":
                    dve.wait(t_tm, t_c)
                    t_tm = dve.done(nc.vector.tensor_tensor(
                        out=tmb[j][:].rearrange("p (h d) -> p h d", d=128),
                        in0=ra[r][:].rearrange("p (h d) -> p h d", d=128),
                        in1=ksc[:, hb:hb + 4].unsqueeze(2).to_broadcast([128, 4, 128]), op=ALU.mult))
                ra_free[r] = t_tm
                rb_free[r] = t_tm
            else:
                act.wait(t_m, tmb_free[j])
                if kind == "g":
                    t_tm = act.done(nc.scalar.activation(out=tmb[j][:], in_=acc[ai][:], func=AF.Silu))
                else:
                    t_tm = act.done(nc.scalar.copy(out=tmb[j][:], in_=acc[ai][:]))
                acc_free[ai] = t_tm
            frees = []
            if kind in TMDST and dbg >= 2:
                pool.wait(t_tm)
                frees.append(s_st[j].done(nc.gpsimd.dma_start(
                    out=TMDST[kind][r0:r0 + 128, hb * 128:hb * 128 + 512], in_=tmb[j][:])))
            tmb_free[j] = frees
            if kind in FMDST and dbg >= 3:
                return (t, cg, s, j, t_tm, fj, kind, hb, frees)
            return None

        def pe_post(p):
            t, cg, s, j, t_tm, fj, kind, hb, frees = p
            pi = cnt["ptr"] % 2
            cnt["ptr"] += 1
            pe.wait(t_tm, ptr_free[pi], t_id)
            for hh in range(4):
                ins = nc.tensor.transpose(ptr[pi][:, hh * 128:(hh + 1) * 128], tmb[j][:, hh * 128:(hh + 1) * 128],
                                          identb[:])
            t_tr = pe.done(ins)
            frees.append(t_tr)
            ev = dve if (s % 2 == 0) else act
            ev.wait(t_tr, fmt_free[fj] if s == 0 else None)
            dst = fmt[fj][:, :, s * 128:(s + 1) * 128]
            srcp = ptr[pi].rearrange("p (a b) -> p a b", b=128)
            if ev is dve:
                t_ev = dve.done(nc.vector.tensor_copy(out=dst, in_=srcp))
            else:
                t_ev = act.done(nc.scalar.copy(out=dst, in_=srcp))
            ptr_free[pi] = t_ev
            state["fm_evs"].append(t_ev)
            if kind == "mk" and not getattr(k, "no_kmean", False):
                b = s // 2
                kmi = 0
                if s % 2 == 0:
                    pe.wait(pkm_free[kmi])
                ins = nc.tensor.matmul(pkm[kmi][0:1, :], lhsT=onesb[:, 0:1], rhs=tmb[j][:], start=(s % 2 == 0),
                                       stop=(s % 2 == 1))
                t_k = pe.done(ins)
                frees.append(t_k)
                if s % 2 == 1:
                    act.wait(t_k, kmrow_free[kmi])
                    t_r = act.done(nc.scalar.activation(out=kmrow[kmi][:], in_=pkm[kmi][0:1, :], func=AF.Copy,
                                                        scale=1.0 / 256.0))
                    pkm_free[kmi] = t_r
                    pool.wait(t_r)
                    kmrow_free[kmi] = s_km[kmi].done(nc.gpsimd.dma_start(
                        out=KMEAN[2 * t + b:2 * t + b + 1, hb * 128:hb * 128 + 512], in_=kmrow[kmi][:]))
            if s == 3:
                pool.wait(state["fm_evs"])
                fmt_free[fj] = s_fm[fj].done(nc.gpsimd.dma_start(
                    out=FMDST[kind][hb:hb + 4, :, t * TILE:(t + 1) * TILE].rearrange("h d t -> d h t"),
                    in_=fmt[fj][:]))
                state["fm_evs"] = []

        state["fm_evs"] = []
        if getattr(k, "dbg", 9) == -1:
            drain(k, [t_c])
            return
        hT_ready = emit_T(0)
        if getattr(k, "dbg", 9) == 0:
            drain(k, [t_c])
            return
        for t in range(ntiles):
            ci = t % 2
            sp.wait(cs_free[ci])
            s_cs[ci].done(nc.sync.dma_start(out=cosr[ci][:], in_=cos_d[t * TILE:(t + 1) * TILE, :]
                                            .rearrange("(s p) j -> p s j", p=128)))
            state["cs_ready"] = s_cs[ci].done(nc.sync.dma_start(
                out=sinr[ci][:], in_=sin_d[t * TILE:(t + 1) * TILE, :].rearrange("(s p) j -> p s j", p=128)))
            pend = None
            for cg in range(14):
                wi = cnt["wt"] % 2
                cnt["wt"] += 1
                sp.wait(wt_free[wi], wtick)
                t_w = s_wt[wi].done(nc.sync.dma_start(out=wt[wi][:], in_=w_in[cg]))
                kind = KINDS[cg // 2]
                if cg in (1, 3, 5, 7) and t + 1 < ntiles:
                    if cg == 1:
                        state["next_loads"] = []
                    state["next_loads"] += emit_T_load(t + 1, ((cg - 1) // 2,))
                fj = None
                if kind in FMDST:
                    fj = cnt["fmt"] % 2
                    cnt["fmt"] += 1
                for s in range(4):
                    ai = cnt["acc"] % 4
                    cnt["acc"] += 1
                    pe.wait(t_w, hT_ready, acc_free[ai])
                    for kc in range(KC):
                        ins = nc.tensor.matmul(acc[ai][:], lhsT=hT[:, kc, s * 128:(s + 1) * 128], rhs=wt[wi][:, kc, :],
                                               start=(kc == 0), stop=(kc == KC - 1))
                    t_m = pe.done(ins)
                    if pend is not None:
                        pe_post(pend)
                    pend = post(t, cg, s, ai, t_m, ci, fj)
                wt_free[wi] = t_m
            state["hT_free"] = t_m
            if pend is not None:
                pe_post(pend)
                pend = None
            cs_free[ci] = [dve.last(), pool.last()]
            if t + 1 < ntiles:
                hT_ready = emit_T(t + 1, state["next_loads"])
        drain(k, [x.last() for x in s_st + s_fm + s_km])


def retention_phase(k, QT, KT, KTM, VTM, GTM, gng_row, gnb_row, causal_d, dec_d, qdec_d, YATT, ntok):
    nc = k.nc
    pe, act, dve, pool, sp = k.pe, k.act, k.dve, k.pool, k.sp
    ngrp = ntok // TILE
    with ExitStack() as es:
        qT4 = [k.sb(es, "r_qT%d" % i, [128, NH, TILE], BF16) for i in range(2)]
        kT4 = [k.sb(es, "r_kT%d" % i, [128, NH, TILE], BF16) for i in range(2)]
        ktm4 = [k.sb(es, "r_ktm%d" % i, [128, 4, 1024], BF16) for i in range(2)]
        vtm4 = [k.sb(es, "r_vtm%d" % i, [128, 4, 1024], BF16) for i in range(2)]
        gtm4 = [k.sb(es, "r_gtm%d" % i, [128, 4, 1024], BF16) for i in range(2)]
        S = k.sb(es, "r_S", [128, 1024], F32)
        Sb = [k.sb(es, "r_Sb%d" % i, [128, 1024], BF16) for i in range(2)]
        PT = [k.sb(es, "r_PT%d" % i, [128, 1024], BF16) for i in range(2)]
        ro = [k.sb(es, "r_ro%d" % i, [128, 1024], F32) for i in range(2)]
        ob = [k.sb(es, "r_ob%d" % i, [128, 1024], BF16) for i in range(2)]
        causal = k.sb(es, "r_causal", [128, 128], F32)
        dec = k.sb(es, "r_dec", [128, NH], F32)
        qdec = k.sb(es, "r_qdec", [128, NH], F32)
        gng = k.sb(es, "r_gng", [128, 1024], F32)
        gnb = k.sb(es, "r_gnb", [128, 1024], F32)
        st8 = k.sb(es, "r_st8", [128, NH, 6], F32)
        mv8 = k.sb(es, "r_mv8", [128, NH, 2], F32)
        rs = k.sb(es, "r_rs", [128, 3, NH], F32)
        pS = k.ps(es, "r_pS", [128, 1024], F32)
        pO = k.ps(es, "r_pO", [128, 1024], F32)
        pKV = k.ps(es, "r_pKV", [128, 1024], F32)
        s_ld = [k.dsem(es, "rld%d" % i) for i in range(2)]
        s_c = k.dsem(es, "rconst")
        s_o = [k.dsem(es, "rout%d" % i) for i in range(2)]

        s_c.done(nc.sync.dma_start(out=causal[:], in_=causal_d[:, :]))
        s_c.done(nc.sync.dma_start(out=dec[:], in_=dec_d[:, :]))
        s_c.done(nc.sync.dma_start(out=qdec[:], in_=qdec_d[:, :]))
        s_c.done(nc.sync.dma_start(out=gng[:], in_=gng_row.partition_broadcast(128)))
        t_c = s_c.done(nc.sync.dma_start(out=gnb[:], in_=gnb_row.partition_broadcast(128)))
        t0 = pool.done(nc.gpsimd.memset(S[:], 0.0))
        t_sb = pool.done(nc.gpsimd.memset(Sb[0][:], 0.0))
        t_S = t_sb

        def v3(ap):
            return ap.rearrange("p (h e) -> p h e", e=128)

        ld_free = [None, None]
        pS_free = pO_free = pKV_free = None
        PT_free = [None, None]
        ro_free = [None, None]
        ob_free = [None, None]
        Sb_free = [None, None]
        rs_free = None
        nchunk = 0
        for g in range(ngrp):
            li = g % 2
            sp.wait(ld_free[li])
            sl = slice(g * TILE, (g + 1) * TILE)
            s_ld[li].done(nc.sync.dma_start(out=qT4[li][:], in_=QT[:, :, sl].rearrange("h d t -> d h t")))
            s_ld[li].done(nc.sync.dma_start(out=kT4[li][:], in_=KT[:, :, sl].rearrange("h d t -> d h t")))
            s_ld[li].done(nc.sync.dma_start(out=ktm4[li][:], in_=KTM[sl, :].rearrange("(c p) f -> p c f", p=128)))
            s_ld[li].done(nc.sync.dma_start(out=vtm4[li][:], in_=VTM[sl, :].rearrange("(c p) f -> p c f", p=128)))
            t_ld = s_ld[li].done(nc.sync.dma_start(out=gtm4[li][:],
                                                   in_=GTM[sl, :].rearrange("(c p) f -> p c f", p=128)))
            for cc in range(4):
                c0 = cc * 128
                r0 = g * TILE + c0
                bi = nchunk % 2
                nchunk += 1
                pe.wait(t_ld, pS_free)
                for h in range(NH):
                    ins = nc.tensor.matmul(pS[:, h * 128:(h + 1) * 128], lhsT=kT4[li][:, h, c0:c0 + 128],
                                           rhs=qT4[li][:, h, c0:c0 + 128], start=True, stop=True)
                t_s = pe.done(ins)
                dve.wait(t_s, PT_free[bi], t_c)
                t_pt = dve.done(nc.vector.tensor_tensor(out=v3(PT[bi][:]), in0=v3(pS[:]),
                                                        in1=causal[:].unsqueeze(1).to_broadcast([128, NH, 128]),
                                                        op=ALU.mult))
                pS_free = t_pt
                pe.wait(t_pt, t_sb, pO_free)
                for h in range(NH):
                    hs = slice(h * 128, (h + 1) * 128)
                    nc.tensor.matmul(pO[:, hs], lhsT=PT[bi][:, hs], rhs=vtm4[li][:, cc, hs], start=True, stop=False)
                    ins = nc.tensor.matmul(pO[:, hs], lhsT=qT4[li][:, h, c0:c0 + 128], rhs=Sb[bi][:, hs],
                                           start=False, stop=True)
                t_o = pe.done(ins)
                PT_free[bi] = t_o
                Sb_free[bi] = t_o
                pe.wait(pKV_free)
                for h in range(NH):
                    hs = slice(h * 128, (h + 1) * 128)
                    ins = nc.tensor.matmul(pKV[:, hs], lhsT=ktm4[li][:, cc, hs], rhs=vtm4[li][:, cc, hs],
                                           start=True, stop=True)
                t_kv = pe.done(ins)
                dve.wait(t_kv, t_S)
                t_1 = dve.done(nc.vector.tensor_tensor(out=S[:], in0=S[:], in1=pKV[:], op=ALU.add))
                pKV_free = t_1
                dve.wait(t_1, t_c)
                t_2 = dve.done(nc.vector.tensor_tensor(out=v3(S[:]), in0=v3(S[:]),
                                                       in1=dec[:].unsqueeze(2).to_broadcast([128, NH, 128]),
                                                       op=ALU.mult))
                act.wait(t_2, Sb_free[1 - bi])
                t_sb = act.done(nc.scalar.copy(out=Sb[1 - bi][:], in_=S[:]))
                t_S = t_sb
                dve.wait(t_o, ro_free[bi], rs_free)
                t_r = dve.done(nc.vector.tensor_tensor(out=v3(ro[bi][:]), in0=v3(pO[:]),
                                                       in1=qdec[:].unsqueeze(2).to_broadcast([128, NH, 128]),
                                                       op=ALU.mult))
                pO_free = t_r
                dve.wait(t_r)
                for h in range(NH):
                    t_b = dve.done(nc.vector.bn_stats(out=st8[:, h, :], in_=ro[bi][:, h * 128:(h + 1) * 128]))
                dve.wait(t_b)
                for h in range(NH):
                    t_a = dve.done(nc.vector.bn_aggr(out=mv8[:, h, :], in_=st8[:, h, :]))
                dve.wait(t_a)
                t_e = dve.done(nc.vector.tensor_scalar(out=rs[:, 0, :], in0=mv8[:, :, 1], scalar1=LN_EPS, scalar2=None,
                                                       op0=ALU.add))
                act.wait(t_e)
                t_q = act.done(nc.scalar.activation(out=rs[:, 1, :], in_=rs[:, 0, :], func=AF.Sqrt))
                dve.wait(t_q)
                t_i = dve.done(nc.vector.reciprocal(out=rs[:, 2, :], in_=rs[:, 1, :]))
                dve.wait(t_i)
                t_nb = dve.done(nc.vector.scalar_tensor_tensor(out=rs[:, 1, :], in0=mv8[:, :, 0], scalar=-1.0,
                                                               in1=rs[:, 2, :], op0=ALU.mult, op1=ALU.mult))
                act.wait(t_nb)
                for h in range(NH):
                    hs = slice(h * 128, (h + 1) * 128)
                    t_p = act.done(nc.scalar.activation(out=ro[bi][:, hs], in_=ro[bi][:, hs], func=AF.Identity,
                                                        scale=rs[:, 2, h:h + 1], bias=rs[:, 1, h:h + 1]))
                rs_free = t_p
                dve.wait(t_p)
                t_p = dve.done(nc.vector.tensor_tensor(out=ro[bi][:], in0=ro[bi][:], in1=gng[:], op=ALU.mult))
                dve.wait(t_p)
                t_p = dve.done(nc.vector.tensor_tensor(out=ro[bi][:], in0=ro[bi][:], in1=gnb[:], op=ALU.add))
                dve.wait(t_p, ob_free[bi])
                t_f = dve.done(nc.vector.tensor_tensor(out=ob[bi][:], in0=ro[bi][:], in1=gtm4[li][:, cc, :],
                                                       op=ALU.mult))
                ro_free[bi] = t_f
                pool.wait(t_f)
                ob_free[bi] = s_o[bi].done(nc.gpsimd.dma_start(out=YATT[r0:r0 + 128, 0:1024], in_=ob[bi][:]))
            ld_free[li] = [pe.last(), dve.last()]
        drain(k, [x.last() for x in s_o])


def moba_phase(k, MQT, MKT, MVTM, KMEAN, causalT_d, YATT, ntok):
    nc = k.nc
    pe, act, dve, pool, sp = k.pe, k.act, k.dve, k.pool, k.sp
    nq = ntok // 128
    nblk = ntok // 256
    SC = 128.0 ** -0.5
    with ExitStack() as es:
        mqT = [k.sb(es, "m_q%d" % i, [128, ntok], BF16) for i in range(2)]
        mkT = [k.sb(es, "m_k%d" % i, [128, ntok], BF16) for i in range(2)]
        mv = [k.sb(es, "m_v%d" % i, [128, nq, 129], BF16) for i in range(2)]
        km = k.sb(es, "m_km", [nblk, 1024], F32)
        kmT = k.sb(es, "m_kmT", [128, NH, 32], BF16)
        identf = k.sb(es, "m_idf", [128, 128], F32)
        causalT = k.sb(es, "m_causal", [128, 128], F32)
        gm = k.sb(es, "m_gm", [128, 32], F32)
        top8 = k.sb(es, "m_top8", [128, 8], F32)
        sel = [k.sb(es, "m_sel%d" % i, [128, 32], F32) for i in range(2)]
        eo = [k.sb(es, "m_eo%d" % i, [128, 128], F32) for i in range(2)]
        PTo = [k.sb(es, "m_PTo%d" % i, [128, 2, 128], BF16) for i in range(2)]
        PTg = [k.sb(es, "m_PTg%d" % i, [128, 4, 128], BF16) for i in range(3)]
        O = [k.sb(es, "m_O%d" % i, [128, 132], F32) for i in range(2)]
        rinv = k.sb(es, "m_rinv", [128, 2], F32)
        mob = [k.sb(es, "m_mob%d" % i, [128, 128], BF16) for i in range(2)]
        pG = k.ps(es, "m_pG", [128, 512], F32)
        pSo = k.ps(es, "m_pSo", [128, 512], F32)
        pOo = k.ps(es, "m_pOo", [128, 512], F32)
        pS = [k.ps(es, "m_pS%d" % i, [128, 512], F32) for i in range(2)]
        pO2 = [k.ps(es, "m_pO2%d" % i, [128, 512], F32) for i in range(2)]
        s_h = [k.dsem(es, "mh%d" % i) for i in range(2)]
        s_c = k.dsem(es, "mconst")
        s_o = [k.dsem(es, "mout%d" % i) for i in range(2)]

        s_c.done(nc.sync.dma_start(out=km[:], in_=KMEAN[:, :]))
        t_c = s_c.done(nc.sync.dma_start(out=causalT[:], in_=causalT_d[:, :]))
        t0 = pool.done(nc.gpsimd.memset(identf[:], 1.0))
        pool.wait(t0)
        t_id = pool.done(nc.gpsimd.affine_select(out=identf[:], in_=identf[:], pattern=[[-1, 128]],
                                                 compare_op=ALU.is_equal, fill=0.0, base=0, channel_multiplier=1))
        for i in range(2):
            t_ones = pool.done(nc.gpsimd.memset(mv[i][:, :, 128:129], 1.0))
        t_prev = None
        for h in range(NH):
            pe.wait(t_c, t_id, t_prev)
            t_t = pe.done(nc.tensor.transpose(pG[:, 0:nblk], km[:, h * 128:(h + 1) * 128], identf[0:nblk, 0:nblk]))
            dve.wait(t_t)
            t_prev = dve.done(nc.vector.tensor_copy(out=kmT[:, h, 0:nblk], in_=pG[:, 0:nblk]))
        pG_free = t_prev

        h_free = [None, None]
        pSo_free = pOo_free = None
        pS_free = [None, None]
        pO2_free = [None, None]
        PTo_free = [None, None]
        PTg_free = [None] * 3
        O_free = [None, None]
        eo_free = [None, None]
        sel_free = [None, None]
        mob_free = [None, None]
        gm_t = None
        top_free = None
        rinv_free = None
        cnt = dict(pS=0, pO2=0, PTg=0, qi=0)

        for h in range(NH):
            hi = h % 2
            sp.wait(h_free[hi], t_ones)
            s_h[hi].done(nc.sync.dma_start(out=mqT[hi][:], in_=MQT[h]))
            s_h[hi].done(nc.sync.dma_start(out=mkT[hi][:], in_=MKT[h]))
            t_ld = s_h[hi].done(nc.sync.dma_start(
                out=mv[hi][:, :, 0:128], in_=MVTM[:, h * 128:(h + 1) * 128].rearrange("(c p) d -> p c d", p=128)))
            dve.wait(gm_t)
            gm_t = dve.done(nc.vector.memset(gm[:], -1e30))
            for i in range(nq):
                nb = i // 2
                qi = cnt["qi"] % 2
                cnt["qi"] += 1
                qs = slice(i * 128, (i + 1) * 128)
                r0 = i * 128
                use_sel = nb > 3
                if use_sel:
                    pe.wait(t_ld, pG_free)
                    t_g = pe.done(nc.tensor.matmul(pG[:, 0:32], lhsT=mqT[hi][:, qs], rhs=kmT[:, h, :],
                                                   start=True, stop=True))
                    dve.wait(t_g, gm_t, top_free)
                    t_gm = dve.done(nc.vector.tensor_copy(out=gm[:, 0:nb], in_=pG[:, 0:nb]))
                    pG_free = t_gm
                    dve.wait(t_gm)
                    t_t8 = dve.done(nc.vector.max(out=top8[:], in_=gm[:]))
                    dve.wait(t_t8, sel_free[qi])
                    t_sel = dve.done(nc.vector.tensor_scalar(out=sel[qi][:], in0=gm[:], scalar1=top8[:, 2:3],
                                                             scalar2=None, op0=ALU.is_ge))
                    gm_t = t_sel
                    top_free = t_sel
                ncs = 1 + (i % 2)
                pe.wait(t_ld, pSo_free)
                ins = nc.tensor.matmul(pSo[:, 0:128], lhsT=mkT[hi][:, qs], rhs=mqT[hi][:, qs], start=True, stop=True)
                if ncs == 2:
                    ins = nc.tensor.matmul(pSo[:, 128:256], lhsT=mkT[hi][:, (i - 1) * 128:i * 128], rhs=mqT[hi][:, qs],
                                           start=True, stop=True)
                t_so = pe.done(ins)
                act.wait(t_so, eo_free[qi], PTo_free[qi])
                t_e = act.done(nc.scalar.activation(out=eo[qi][:], in_=pSo[:, 0:128], func=AF.Exp, scale=SC))
                if ncs == 2:
                    t_e2 = act.done(nc.scalar.activation(out=PTo[qi][:, 1, :], in_=pSo[:, 128:256], func=AF.Exp,
                                                         scale=SC))
                else:
                    t_e2 = t_e
                pSo_free = t_e2
                dve.wait(t_e, t_c)
                t_pd = dve.done(nc.vector.tensor_tensor(out=PTo[qi][:, 0, :], in0=eo[qi][:], in1=causalT[:],
                                                        op=ALU.mult))
                eo_free[qi] = t_pd
                ng = (nb + 1) // 2

                def emit_S(g):
                    nbg = min(2, nb - 2 * g)
                    si = cnt["pS"] % 2
                    cnt["pS"] += 1
                    pe.wait(pS_free[si])
                    for b in range(nbg):
                        n = 2 * g + b
                        for c2 in range(2):
                            ks = slice(n * 256 + c2 * 128, n * 256 + (c2 + 1) * 128)
                            ins_ = nc.tensor.matmul(pS[si][:, (b * 2 + c2) * 128:(b * 2 + c2 + 1) * 128],
                                                    lhsT=mkT[hi][:, ks], rhs=mqT[hi][:, qs], start=True, stop=True)
                    t_s = pe.done(ins_)
                    pj = cnt["PTg"] % 3
                    cnt["PTg"] += 1
                    act.wait(t_s, PTg_free[pj])
                    t_p = act.done(nc.scalar.activation(
                        out=PTg[pj][:, 0:2 * nbg, :].rearrange("p a b -> p (a b)"), in_=pS[si][:, 0:nbg * 256],
                        func=AF.Exp, scale=SC))
                    pS_free[si] = t_p
                    return (g, nbg, pj, t_p)

                pend = emit_S(0) if ng > 0 else None
                pe.wait(t_pd, t_e2, pOo_free)
                ins = nc.tensor.matmul(pOo[:, 0:129], lhsT=PTo[qi][:, 0, :], rhs=mv[hi][:, i, :], start=True,
                                       stop=(ncs == 1))
                if ncs == 2:
                    ins = nc.tensor.matmul(pOo[:, 0:129], lhsT=PTo[qi][:, 1, :], rhs=mv[hi][:, i - 1, :], start=False,
                                           stop=True)
                t_oo = pe.done(ins)
                PTo_free[qi] = t_oo
                dve.wait(t_oo, O_free[qi])
                t_O = dve.done(nc.vector.tensor_copy(out=O[qi][:, 0:129], in_=pOo[:, 0:129]))
                pOo_free = t_O
                for g in range(ng):
                    nxt = emit_S(g + 1) if g + 1 < ng else None
                    _, nbg, pj, t_p = pend
                    oi = cnt["pO2"] % 2
                    cnt["pO2"] += 1
                    pe.wait(t_p, pO2_free[oi])
                    for b in range(nbg):
                        n = 2 * g + b
                        for c2 in range(2):
                            ins = nc.tensor.matmul(pO2[oi][:, b * 132:b * 132 + 129], lhsT=PTg[pj][:, b * 2 + c2, :],
                                                   rhs=mv[hi][:, n * 2 + c2, :], start=(c2 == 0), stop=(c2 == 1))
                    t_pv = pe.done(ins)
                    PTg_free[pj] = t_pv
                    for b in range(nbg):
                        n = 2 * g + b
                        dve.wait(t_pv, t_O)
                        if use_sel:
                            t_O = dve.done(nc.vector.scalar_tensor_tensor(
                                out=O[qi][:, 0:129], in0=pO2[oi][:, b * 132:b * 132 + 129], scalar=sel[qi][:, n:n + 1],
                                in1=O[qi][:, 0:129], op0=ALU.mult, op1=ALU.add))
                        else:
                            t_O = dve.done(nc.vector.tensor_tensor(out=O[qi][:, 0:129], in0=O[qi][:, 0:129],
                                                                   in1=pO2[oi][:, b * 132:b * 132 + 129], op=ALU.add))
                    pO2_free[oi] = t_O
                    pend = nxt
                if use_sel:
                    sel_free[qi] = t_O
                dve.wait(t_O, rinv_free)
                t_r = dve.done(nc.vector.reciprocal(out=rinv[:, 0:1], in_=O[qi][:, 128:129]))
                dve.wait(t_r, mob_free[qi])
                t_m = dve.done(nc.vector.tensor_scalar(out=mob[qi][:], in0=O[qi][:, 0:128], scalar1=rinv[:, 0:1],
                                                       scalar2=None, op0=ALU.mult))
                rinv_free = t_m
                O_free[qi] = t_m
                pool.wait(t_m)
                mob_free[qi] = s_o[qi].done(nc.gpsimd.dma_start(
                    out=YATT[r0:r0 + 128, 1024 + h * 128:1024 + (h + 1) * 128], in_=mob[qi][:]))
            h_free[hi] = pe.last()
        drain(k, [x.last() for x in s_o])


def consts(ntok, pos0):
    pos = (pos0 + np.arange(ntok)).astype(np.float32)
    invf = (10000.0 ** (-np.arange(64, dtype=np.float32) / np.float32(64))).astype(np.float32)
    ang = (pos[:, None] * invf[None, :]).astype(np.float32)
    lg = np.log1p(-np.exp2(-5.0 - np.arange(NH, dtype=np.float64)))
    p = np.arange(128, dtype=np.float64)
    c = np.arange(128)
    return dict(
        cos=np.cos(ang).astype(np.float32), sin=np.sin(ang).astype(np.float32),
        ksc=(128.0 ** -0.5 * np.exp(-(p[:, None] + 1) * lg[None, :])).astype(np.float32),
        qdec=np.exp((p[:, None] + 1) * lg[None, :]).astype(np.float32),
        dec=np.tile(np.exp(128.0 * lg)[None, :], (128, 1)).astype(np.float32),
        causal=(c[None, :] >= c[:, None]).astype(np.float32),
    )


DRNN = 2816
RC = DRNN // 128
GK = 1.5957691216057308


def rnn_inproj_phase(k, X_in, w_in, wtick, GATE, RNNBR, ntok):
    nc = k.nc
    pe, act, dve, pool, sp = k.pe, k.act, k.dve, k.pool, k.sp
    ntiles = ntok // TILE
    with ExitStack() as es:
        stage = [k.sb(es, "n_stage%d" % i, [128, D], F32) for i in range(2)]
        xb = [k.sb(es, "n_xb%d" % i, [128, D], BF16) for i in range(4)]
        hT = k.sb(es, "n_hT", [128, KC, TILE], BF16)
        ws = [k.sb(es, "n_w%d" % i, [128, KC, 128], BF16) for i in range(3)]
        xf = [k.sb(es, "n_xf%d" % i, [128, TILE], F32) for i in range(3)]
        t1 = [k.sb(es, "n_t1%d" % i, [128, TILE], F32) for i in range(2)]
        gb = [k.sb(es, "n_gb%d" % i, [128, TILE], BF16) for i in range(2)]
        identb = k.sb(es, "n_idb", [128, 128], BF16)
        pT = k.ps(es, "n_pT", [128, 512], BF16)
        acc = [k.ps(es, "n_acc%d" % i, [128, 512], F32) for i in range(4)]
        s_stage = [k.dsem(es, "nstage%d" % i) for i in range(2)]
        s_w = [k.dsem(es, "nw%d" % i) for i in range(3)]
        s_og = [k.dsem(es, "nog%d" % i) for i in range(2)]
        s_ox = [k.dsem(es, "nox%d" % i) for i in range(3)]
        t0 = pool.done(nc.gpsimd.memset(identb[:], 1.0))
        pool.wait(t0)
        t_id = pool.done(nc.gpsimd.affine_select(out=identb[:], in_=identb[:], pattern=[[-1, 128]],
                                                 compare_op=ALU.is_equal, fill=0.0, base=0, channel_multiplier=1))
        stage_free = [None, None]
        xb_free = [None] * 4
        w_free = [None] * 3
        acc_free = [None] * 4
        xf_free = [None] * 3
        t1_free = [None, None]
        gb_free = [None, None]
        cnt = dict(stage=0, xb=0, w=0, acc=0, xf=0, t1=0, gb=0)
        state = dict(hT_free=None, pT_free=None)

        def emit_T_load(t, subs):
            res = []
            for s in subs:
                r0 = t * TILE + s * 128
                xi = cnt["xb"] % 4
                cnt["xb"] += 1
                si = cnt["stage"] % 2
                cnt["stage"] += 1
                sp.wait(stage_free[si])
                t_ld = s_stage[si].done(nc.sync.dma_start(out=stage[si][:], in_=X_in[r0:r0 + 128, :]))
                act.wait(t_ld, xb_free[xi])
                t_cc = act.done(nc.scalar.copy(out=xb[xi][:], in_=stage[si][:]))
                stage_free[si] = t_cc
                res.append((xi, t_cc))
            return res

        def emit_T(t, loads=None):
            if loads is None:
                loads = emit_T_load(t, (0, 1, 2, 3))
            for s, (xi, t_cc) in enumerate(loads):
                for q in range(4):
                    pe.wait(t_cc, state["pT_free"], t_id)
                    for j in range(4):
                        kc = 4 * q + j
                        ins = nc.tensor.transpose(pT[:, j * 128:(j + 1) * 128],
                                                  xb[xi][:, kc * 128:(kc + 1) * 128], identb[:])
                    t_tr = pe.done(ins)
                    ev = dve if (q % 2 == 0) else act
                    ev.wait(t_tr, state["hT_free"])
                    dst = hT[:, 4 * q:4 * q + 4, s * 128:(s + 1) * 128]
                    srcp = pT[:].rearrange("p (a b) -> p a b", b=128)
                    if ev is dve:
                        t_ev = dve.done(nc.vector.tensor_copy(out=dst, in_=srcp))
                    else:
                        t_ev = act.done(nc.scalar.copy(out=dst, in_=srcp))
                    state["pT_free"] = t_ev
                xb_free[xi] = t_tr
            return [dve.last(), act.last()]

        hT_ready = emit_T(0)
        for t in range(ntiles):
            sl = slice(t * TILE, (t + 1) * TILE)
            for oi_ in range(2 * RC):
                oc = (oi_ // 2) + (RC if oi_ % 2 else 0)
                if oi_ in (4, 10, 16, 22) and t + 1 < ntiles:
                    if oi_ == 4:
                        state["next_loads"] = []
                    state["next_loads"] += emit_T_load(t + 1, ((oi_ - 4) // 6,))
                wi = cnt["w"] % 3
                cnt["w"] += 1
                sp.wait(w_free[wi], wtick)
                t_w = s_w[wi].done(nc.sync.dma_start(out=ws[wi][:], in_=w_in[oc]))
                ai = cnt["acc"] % 4
                cnt["acc"] += 1
                pe.wait(t_w, hT_ready, acc_free[ai])
                for kc in range(KC):
                    ins = nc.tensor.matmul(acc[ai][:], lhsT=ws[wi][:, kc, :], rhs=hT[:, kc, :],
                                           start=(kc == 0), stop=(kc == KC - 1))
                t_m = pe.done(ins)
                w_free[wi] = t_m
                xi = cnt["xf"] % 3
                cnt["xf"] += 1
                act.wait(t_m, xf_free[xi])
                t_x = act.done(nc.scalar.copy(out=xf[xi][:], in_=acc[ai][:]))
                acc_free[ai] = t_x
                if oc >= RC:
                    pool.wait(t_x)
                    xf_free[xi] = s_ox[xi].done(nc.gpsimd.dma_start(out=RNNBR[oc - RC, :, sl], in_=xf[xi][:]))
                else:
                    ti = cnt["t1"] % 2
                    cnt["t1"] += 1
                    gi = cnt["gb"] % 2
                    cnt["gb"] += 1
                    dve.wait(t_x, t1_free[ti])
                    t_a = dve.done(nc.vector.tensor_tensor(out=t1[ti][:], in0=xf[xi][:], in1=xf[xi][:], op=ALU.mult))
                    dve.wait(t_a)
                    t_a = dve.done(nc.vector.tensor_scalar(out=t1[ti][:], in0=t1[ti][:], scalar1=0.044715, scalar2=1.0,
                                                           op0=ALU.mult, op1=ALU.add))
                    dve.wait(t_a)
                    t_a = dve.done(nc.vector.tensor_tensor(out=t1[ti][:], in0=t1[ti][:], in1=xf[xi][:], op=ALU.mult))
                    act.wait(t_a)
                    t_a = act.done(nc.scalar.activation(out=t1[ti][:], in_=t1[ti][:], func=AF.Sigmoid, scale=GK))
                    dve.wait(t_a, gb_free[gi])
                    t_g = dve.done(nc.vector.tensor_tensor(out=gb[gi][:], in0=t1[ti][:], in1=xf[xi][:], op=ALU.mult))
                    t1_free[ti] = t_g
                    xf_free[xi] = t_g
                    pool.wait(t_g)
                    gb_free[gi] = s_og[gi].done(nc.gpsimd.dma_start(out=GATE[oc, :, sl], in_=gb[gi][:]))
            state["hT_free"] = t_m
            if t + 1 < ntiles:
                hT_ready = emit_T(t + 1, state["next_loads"])
        drain(k, [x.last() for x in s_og + s_ox])


def prep_bd(k, es, w_src, name):
    nc = k.nc
    out = dram(nc, name, [DRNN, DRNN], BF16)
    z = k.sb(es, name + "_z", [128, DRNN], BF16)
    ds = k.dsem(es, name + "z")
    t0 = k.pool.done(nc.gpsimd.memset(z[:], 0.0))
    k.pool.wait(t0)
    t = None
    for c in range(RC):
        t = ds.done(nc.gpsimd.dma_start(out=out[c * 128:(c + 1) * 128, :], in_=z[:]))
    k.pool.wait(t)
    ds2 = k.dsem(es, name + "b")
    for g in range(16):
        t = ds2.done(nc.gpsimd.dma_start(out=out[g * 176:(g + 1) * 176, g * 176:(g + 1) * 176], in_=w_src[g]))
    return out, t


def rnn_core_phase(k, GATE, RNNBR, wa_bd, wx_bd, wtick, vecs, YRNN, ntok):
    nc = k.nc
    pe, act, dve, pool, sp = k.pe, k.act, k.dve, k.pool, k.sp
    ntiles = ntok // TILE
    with ExitStack() as es:
        wband = [k.sb(es, "c_wband%d" % i, [128, RC, 5, 128], BF16) for i in range(2)]
        vt = k.sb(es, "c_vt", [128, 9, RC], F32)
        cst = k.sb(es, "c_cst", [128, 6, RC], F32)
        hst = k.sb(es, "c_hst", [128, RC], F32)
        ones1 = k.sb(es, "c_ones", [128, 2], F32)
        NG = 4
        NX = 8
        xr = [k.sb(es, "c_xr%d" % i, [128, TILE + 4], F32) for i in range(NX)]
        u = k.sb(es, "c_u", [128, RC, TILE], F32)
        ub = k.sb(es, "c_ub", [128, RC, TILE], BF16)
        gt = [k.sb(es, "c_gt%d" % i, [128, TILE], BF16) for i in range(NG)]
        r_ = [k.sb(es, "c_r%d" % i, [128, TILE], F32) for i in range(NG)]
        i_ = [k.sb(es, "c_i%d" % i, [128, TILE], F32) for i in range(NG)]
        a_ = [k.sb(es, "c_a%d" % i, [128, TILE], F32) for i in range(NG)]
        b_ = [k.sb(es, "c_b%d" % i, [128, TILE], F32) for i in range(NG)]
        yb = [k.sb(es, "c_yb%d" % i, [128, TILE], BF16) for i in range(NG)]
        pR = [k.ps(es, "c_pR%d" % i, [128, 512], F32) for i in range(NG)]
        pI = [k.ps(es, "c_pI%d" % i, [128, 512], F32) for i in range(NG)]
        s_c = k.dsem(es, "cconst")
        s_x = [k.dsem(es, "cx%d" % i) for i in range(NX)]
        s_g = [k.dsem(es, "cg%d" % i) for i in range(NG)]
        s_o = [k.dsem(es, "co%d" % i) for i in range(NG)]

        sp.wait(wtick)
        for gi, wsrc in enumerate((wa_bd, wx_bd)):
            t0 = pool.done(nc.gpsimd.memset(wband[gi][:], 0.0))
            sp.wait(t0)
            for c in range(RC):
                lo, hi = max(0, c - 2), min(RC - 1, c + 2)
                t_wb = s_c.done(nc.sync.dma_start(
                    out=wband[gi][:, c, lo - c + 2:hi - c + 3, :],
                    in_=wsrc[lo * 128:(hi + 1) * 128, c * 128:(c + 1) * 128].rearrange("(j p) f -> p j f", p=128)))
        t_v = s_c.done(nc.sync.dma_start(out=vt[:].rearrange("p v c -> p (v c)"), in_=vecs[:, :]))
        t_wb = t_v
        act.wait(t_v)
        dve.wait(t_v)
        t = dve.done(nc.vector.tensor_scalar(out=cst[:, 0:2, :], in0=vt[:, 5:7, :], scalar1=-1.0, scalar2=None,
                                             op0=ALU.mult))
        t_e = act.done(nc.scalar.activation(out=cst[:, 4, :], in_=vt[:, 7, :], func=AF.Exp, scale=-1.0))
        dve.wait(t_e)
        t = dve.done(nc.vector.tensor_scalar(out=cst[:, 4, :], in0=cst[:, 4, :], scalar1=1.0, scalar2=None,
                                             op0=ALU.add))
        act.wait(t)
        t_e = act.done(nc.scalar.activation(out=cst[:, 5, :], in_=cst[:, 4, :], func=AF.Ln))
        dve.wait(t_e)
        dve.done(nc.vector.tensor_scalar(out=cst[:, 2, :], in0=cst[:, 5, :], scalar1=-8.0, scalar2=None, op0=ALU.mult))
        dve.done(nc.vector.tensor_scalar(out=cst[:, 3, :], in0=cst[:, 5, :], scalar1=-16.0, scalar2=None,
                                         op0=ALU.mult))
        dve.done(nc.vector.memset(ones1[:], 1.0))
        t_cst = dve.done(nc.vector.memset(hst[:], 0.0))
        for xi in range(NX):
            t_cst = dve.done(nc.vector.memset(xr[xi][:, 0:4], 0.0))

        xr_free = [None] * NX
        gt_free = [None] * NG
        pR_free = [None] * NG
        pI_free = [None] * NG
        buf_free = [None] * NG
        yb_free = [None] * NG
        u_free = None
        cntx = 0
        groups = [list(range(g0, min(RC, g0 + NG))) for g0 in range(0, RC, NG)]
        for t in range(ntiles):
            t0 = t * TILE
            sl = slice(t0, t0 + TILE)
            t_u = None
            for grp in groups:
                xs_ = {}
                tk = {}
                for c in grp:
                    xi = cntx % NX
                    cntx += 1
                    xs_[c] = xi
                    sp.wait(xr_free[xi], t_cst)
                    if t == 0:
                        tk[c] = s_x[xi].done(nc.sync.dma_start(out=xr[xi][:, 4:4 + TILE], in_=RNNBR[c, :, 0:TILE]))
                    else:
                        tk[c] = s_x[xi].done(nc.sync.dma_start(out=xr[xi][:, 0:4 + TILE],
                                                               in_=RNNBR[c, :, t0 - 4:t0 + TILE]))
                for c in grp:
                    xi = xs_[c]
                    act.wait(tk[c], u_free, t_cst)
                    tk[c] = act.done(nc.scalar.activation(out=u[:, c, :], in_=xr[xi][:, 4:4 + TILE], func=AF.Identity,
                                                          scale=vt[:, 3, c:c + 1], bias=vt[:, 4, c:c + 1]))
                for j in range(3):
                    for c in grp:
                        xi = xs_[c]
                        dve.wait(tk[c])
                        tk[c] = dve.done(nc.vector.scalar_tensor_tensor(
                            out=u[:, c, :], in0=xr[xi][:, 1 + j:1 + j + TILE], scalar=vt[:, j, c:c + 1],
                            in1=u[:, c, :], op0=ALU.mult, op1=ALU.add))
                for c in grp:
                    xr_free[xs_[c]] = tk[c]
                    act.wait(tk[c])
                    t_u = act.done(nc.scalar.copy(out=ub[:, c, :], in_=u[:, c, :]))
            for grp in groups:
                T = {}
                for j, c in enumerate(grp):
                    sp.wait(gt_free[j])
                    T[("g", c)] = s_g[j].done(nc.sync.dma_start(out=gt[j][:], in_=GATE[c, :, sl]))
                for j, c in enumerate(grp):
                    lo, hi = max(0, c - 2), min(RC - 1, c + 2)
                    pe.wait(t_u, t_wb, pR_free[j], pI_free[j])
                    for cc in range(lo, hi + 1):
                        nc.tensor.matmul(pR[j][:], lhsT=wband[0][:, c, cc - c + 2, :], rhs=ub[:, cc, :],
                                         start=(cc == lo), stop=(cc == hi))
                    for cc in range(lo, hi + 1):
                        ins = nc.tensor.matmul(pI[j][:], lhsT=wband[1][:, c, cc - c + 2, :], rhs=ub[:, cc, :],
                                               start=(cc == lo), stop=(cc == hi))
                    T[("m", c)] = pe.done(ins)
                for j, c in enumerate(grp):
                    act.wait(T[("m", c)], buf_free[j], t_cst)
                    T[("r", c)] = act.done(nc.scalar.activation(out=r_[j][:], in_=pR[j][:], func=AF.Sigmoid,
                                                                bias=vt[:, 5, c:c + 1]))
                    T[("i2", c)] = act.done(nc.scalar.activation(out=i_[j][:], in_=pI[j][:], func=AF.Sigmoid,
                                                                 bias=vt[:, 6, c:c + 1]))
                    pR_free[j] = T[("r", c)]
                    pI_free[j] = T[("i2", c)]
                for j, c in enumerate(grp):
                    act.wait(T[("r", c)])
                    T[("a", c)] = act.done(nc.scalar.activation(out=a_[j][:], in_=r_[j][:], func=AF.Exp,
                                                                scale=cst[:, 2, c:c + 1]))
                    T[("a2", c)] = act.done(nc.scalar.activation(out=b_[j][:], in_=r_[j][:], func=AF.Exp,
                                                                 scale=cst[:, 3, c:c + 1]))
                for j, c in enumerate(grp):
                    act.wait(T[("a2", c)])
                    T[("ln", c)] = act.done(nc.scalar.activation(out=b_[j][:], in_=b_[j][:], func=AF.Ln, scale=-1.0,
                                                                 bias=ones1[:, 0:1]))
                for j, c in enumerate(grp):
                    act.wait(T[("ln", c)])
                    T[("sq", c)] = act.done(nc.scalar.activation(out=b_[j][:], in_=b_[j][:], func=AF.Exp, scale=0.5))
                for j, c in enumerate(grp):
                    dve.wait(T[("i2", c)])
                    T[("iu", c)] = dve.done(nc.vector.tensor_tensor(out=i_[j][:], in0=i_[j][:], in1=u[:, c, :],
                                                                    op=ALU.mult))
                for j, c in enumerate(grp):
                    dve.wait(T[("iu", c)], T[("sq", c)])
                    T[("b", c)] = dve.done(nc.vector.tensor_tensor(out=b_[j][:], in0=b_[j][:], in1=i_[j][:],
                                                                   op=ALU.mult))
                for j, c in enumerate(grp):
                    dve.wait(T[("b", c)], T[("a", c)], T[("a2", c)], t_cst)
                    T[("sc", c)] = dve.done(nc.vector.tensor_tensor_scan(
                        out=r_[j][:], data0=a_[j][:], data1=b_[j][:], initial=hst[:, c:c + 1],
                        op0=ALU.mult, op1=ALU.add))
                for j, c in enumerate(grp):
                    dve.wait(T[("sc", c)])
                    T[("h", c)] = dve.done(nc.vector.tensor_copy(out=hst[:, c:c + 1], in_=r_[j][:, TILE - 1:TILE]))
                for j, c in enumerate(grp):
                    dve.wait(T[("sc", c)], T[("g", c)], yb_free[j])
                    T[("y", c)] = dve.done(nc.vector.tensor_tensor(out=yb[j][:], in0=r_[j][:], in1=gt[j][:],
                                                                   op=ALU.mult))
                    gt_free[j] = T[("y", c)]
                    buf_free[j] = [T[("y", c)], T[("h", c)]]
                for j, c in enumerate(grp):
                    pool.wait(T[("y", c)])
                    yb_free[j] = s_o[j].done(nc.gpsimd.dma_start(out=YRNN[c, :, sl], in_=yb[j][:]))
            u_free = [pe.last(), dve.last()]
        drain(k, [x.last() for x in s_o])


BETA_UNUSED = None
IN_SPECS = [
    ("x", None), ("ln_g", [2, 3, D]), ("ln_b", [2, 3, D]),
    ("ffn_w_gate", [2, 2, D, DFF]), ("ffn_w_up", [2, 2, D, DFF]), ("ffn_w_down", [2, 2, DFF, D]),
    ("attn_w_in", [1, D, 7168]), ("ret_gn_g", [1, 1024]), ("ret_gn_b", [1, 1024]), ("attn_w_out", [1, D, D]),
    ("rnn_w_in", [1, D, 2 * DRNN]), ("rnn_gate_a_w", [1, 16, 176, 176]), ("rnn_gate_x_w", [1, 16, 176, 176]),
    ("rnn_w_out", [1, DRNN, D]),
    ("c_cos", None), ("c_sin", None), ("c_ksc", [128, NH]), ("c_qdec", [128, NH]), ("c_dec", [128, NH]),
    ("c_causal", [128, 128]), ("c_vecs", [128, 9 * RC]),
]


_DBG = False


def build_program(ntok, upto=99):
    nc = bass.Bass("TRN2", target_bir_lowering=False)
    I = {}
    for name, shape in IN_SPECS:
        if name == "x":
            shape = [ntok, D]
        elif name in ("c_cos", "c_sin"):
            shape = [ntok, 64]
        I[name] = dram(nc, name, shape, F32, "ExternalInput")
    out = dram(nc, "out", [ntok, D], F32, "ExternalOutput")
    dk = "ExternalOutput" if _DBG else "Internal"
    XA = dram(nc, "XA", [ntok, D], F32, dk)
    XB = dram(nc, "XB", [ntok, D], F32, dk)
    with ExitStack() as es:
        k = K(nc, es)

        def prep_ffn(l, j):
            a, t1 = prep_ws(k, es, I["ffn_w_gate"][l, j], "wg%d%d" % (l, j), D, DFF)
            b, t2 = prep_ws(k, es, I["ffn_w_up"][l, j], "wu%d%d" % (l, j), D, DFF)
            c, t3 = prep_ws(k, es, I["ffn_w_down"][l, j], "wd%d%d" % (l, j), DFF, D)
            return a, b, c, [t1, t2, t3]

        def ffn(l, j, w, src, dst):
            sublayer_phase(k, "ffn", src, dst, w[2], FC, w[3], I["ln_g"][l, 2 * j], I["ln_b"][l, 2 * j], ntok,
                           2.0 * ALPHA, 4.0 * LN_EPS, wg=w[0], wu=w[1])

        w00 = prep_ffn(0, 0)
        ffn(0, 0, w00, I["x"], XA if upto > 0 else out)
        if upto <= 0:
            return nc
        win, t_win = prep_tm(k, es, I["attn_w_in"][0], "awin", D, 7168)
        wout, t_wout = prep_ws(k, es, I["attn_w_out"][0], "awout", D, D)
        w01 = prep_ffn(0, 1)
        sc = [dram(nc, "QT", [NH, 128, ntok], BF16), dram(nc, "KT", [NH, 128, ntok], BF16),
              dram(nc, "KTM", [ntok, 1024], BF16), dram(nc, "VTM", [ntok, 1024], BF16),
              dram(nc, "GTM", [ntok, 1024], BF16), dram(nc, "MQT", [NH, 128, ntok], BF16),
              dram(nc, "MKT", [NH, 128, ntok], BF16), dram(nc, "MVTM", [ntok, 1024], BF16),
              dram(nc, "KMEAN", [ntok // 256, 1024], F32)]
        YATT = dram(nc, "YATT", [ntok, 2048], BF16)
        attn_inproj_phase(k, XA, win, [t_win], I["c_cos"], I["c_sin"], I["c_ksc"], sc, ntok)
        QT, KT, KTM, VTM, GTM, MQT, MKT, MVTM, KMEAN = sc
        retention_phase(k, QT, KT, KTM, VTM, GTM, I["ret_gn_g"][0], I["ret_gn_b"][0], I["c_causal"], I["c_dec"],
                        I["c_qdec"], YATT, ntok)
        moba_phase(k, MQT, MKT, MVTM, KMEAN, I["c_causal"], YATT, ntok)
        sublayer_phase(k, "proj_tm", XA, XB if upto > 1 else out, wout, KC, [t_wout], I["ln_g"][0, 1], I["ln_b"][0, 1],
                       ntok, ALPHA, LN_EPS, Y=YATT)
        if upto <= 1:
            return nc
        w10 = prep_ffn(1, 0)
        ffn(0, 1, w01, XB, XA)
        rin, t_rin = prep_ws(k, es, I["rnn_w_in"][0], "rwin", D, 2 * DRNN)
        wa, t_wa = prep_bd(k, es, I["rnn_gate_a_w"][0], "rwa")
        wx, t_wx = prep_bd(k, es, I["rnn_gate_x_w"][0], "rwx")
        rout, t_rout = prep_ws(k, es, I["rnn_w_out"][0], "rwout", DRNN, D)
        w11 = prep_ffn(1, 1)
        ffn(1, 0, w10, XA, XB)
        GATE = dram(nc, "GATE", [RC, 128, ntok], BF16)
        RNNBR = dram(nc, "RNNBR", [RC, 128, ntok], F32)
        YRNN = dram(nc, "YRNN", [RC, 128, ntok], BF16, dk)
        rnn_inproj_phase(k, XB, rin, [t_rin], GATE, RNNBR, ntok)
        rnn_core_phase(k, GATE, RNNBR, wa, wx, [t_wa, t_wx], I["c_vecs"], YRNN, ntok)
        sublayer_phase(k, "proj_fm", XB, XA if upto > 2 else out, rout, RC, [t_rout], I["ln_g"][1, 1], I["ln_b"][1, 1],
                       ntok, ALPHA, LN_EPS, Y=YRNN)
        if upto <= 2:
            return nc
        ffn(1, 1, w11, XA, out)
    return nc


_PROG = {}
_LAST = {}
SPREAD = True


def kernel(x, ln_g, ln_b, ffn_w_gate, ffn_w_up, ffn_w_down, attn_w_in, ret_gn_g, ret_gn_b, attn_w_out,
           rnn_w_in, rnn_conv_w, rnn_conv_b, rnn_gate_a_w, rnn_gate_a_b, rnn_gate_x_w, rnn_gate_x_b,
           rnn_lambda, rnn_w_out):
    f32 = lambda a: np.ascontiguousarray(np.asarray(a, dtype=np.float32))
    x = f32(x)
    B, S, _ = x.shape
    if S not in _PROG:
        _PROG[S] = build_program(S)
    nc = _PROG[S]
    C = consts(S, 0)
    vecs = np.zeros((9, DRNN), np.float32)
    vecs[0:4] = f32(rnn_conv_w)[0]
    vecs[4] = f32(rnn_conv_b)[0]
    vecs[5] = f32(rnn_gate_a_b)[0]
    vecs[6] = f32(rnn_gate_x_b)[0]
    vecs[7] = f32(rnn_lambda)[0]
    shared = dict(ln_g=f32(ln_g), ln_b=f32(ln_b), ffn_w_gate=f32(ffn_w_gate), ffn_w_up=f32(ffn_w_up),
                  ffn_w_down=f32(ffn_w_down), attn_w_in=f32(attn_w_in), ret_gn_g=f32(ret_gn_g),
                  ret_gn_b=f32(ret_gn_b), attn_w_out=f32(attn_w_out), rnn_w_in=f32(rnn_w_in),
                  rnn_gate_a_w=f32(rnn_gate_a_w), rnn_gate_x_w=f32(rnn_gate_x_w), rnn_w_out=f32(rnn_w_out),
                  c_cos=C["cos"], c_sin=C["sin"], c_ksc=C["ksc"], c_qdec=C["qdec"], c_dec=C["dec"],
                  c_causal=C["causal"],
                  c_vecs=np.ascontiguousarray(vecs.reshape(9, RC, 128).transpose(2, 0, 1).reshape(128, 9 * RC)))
    if B == 4 and SPREAD:
        act_cores = [0, 1, 4, 5]
        zeros = {n: np.zeros_like(v) for n, v in shared.items()}
        zeros["x"] = np.zeros_like(x[0])
        in_maps = [zeros] * NCORES
        in_maps = list(in_maps)
        for b, c in enumerate(act_cores):
            in_maps[c] = dict(shared, x=x[b])
        res = run_bass_kernel_spmd(nc, in_maps, core_ids=list(range(NCORES)))
        outs = [res.results[c] for c in act_cores]
    else:
        in_maps = [dict(shared, x=x[b]) for b in range(B)]
        res = run_bass_kernel_spmd(nc, in_maps, core_ids=list(range(B)))
        outs = res.results
    _LAST["r"] = outs[0]
    return np.stack([np.asarray(r["out"], dtype=np.float32) for r in outs], axis=0)
```

```python
import math
from contextlib import ExitStack

import numpy as np
import concourse.bass as bass
import concourse.mybir as mybir
from concourse.bass_utils import run_bass_kernel_spmd

F32 = mybir.dt.float32
BF16 = mybir.dt.bfloat16
AF = mybir.ActivationFunctionType
ALU = mybir.AluOpType

D = 2048
DFF = 5632
KC = D // 128
FC = DFF // 128
DEPTH = 2
ALPHA = (2.0 * DEPTH) ** 0.25
LN_EPS = 1e-5
NCORES = 8
TOK = 4096
TILE = 512


class EngW:
    def __init__(self, nc, es, name, eng):
        self.nc, self.name, self.eng = nc, name, eng
        self.sem = es.enter_context(nc.semaphore("pg_" + name))
        self.count = 0
        self.waited = {}

    def wait(self, *tickets):
        for t in tickets:
            if t is None:
                continue
            if isinstance(t, list):
                self.wait(*t)
                continue
            sem, val, key = t
            if self.waited.get(key, 0) >= val:
                continue
            self.eng.wait_ge(sem, val)
            self.waited[key] = val

    def done(self, inst):
        self.count += 1
        inst.then_inc(self.sem, 1)
        return (self.sem, self.count, self.name)

    def last(self):
        return (self.sem, self.count, self.name) if self.count else None


class DmaSem:
    _n = 0

    def __init__(self, nc, es, name):
        DmaSem._n += 1
        self.key = "dma_%d" % DmaSem._n
        self.sem = es.enter_context(nc.semaphore(self.key))
        self.count = 0

    def done(self, inst):
        self.count += 16
        inst.then_inc(self.sem, 16)
        return (self.sem, self.count, self.key)

    def last(self):
        return (self.sem, self.count, self.key) if self.count else None


class Ring:
    def __init__(self, bufs):
        self.bufs = bufs
        self.free = [None] * len(bufs)
        self.i = 0

    def next(self):
        j = self.i % len(self.bufs)
        self.i += 1
        return j


class K:
    def __init__(self, nc, es):
        self.nc, self.es = nc, es
        self.pe = EngW(nc, es, "pe", nc.tensor)
        self.act = EngW(nc, es, "act", nc.scalar)
        self.dve = EngW(nc, es, "dve", nc.vector)
        self.pool = EngW(nc, es, "pool", nc.gpsimd)
        self.sp = EngW(nc, es, "sp", nc.sync)
        self.sem_pool = []
        self.nsem = 0
        self.identb = self.sb(es, "g_idb", [128, 128], BF16)
        self.identf = self.sb(es, "g_idf", [128, 128], F32)
        for idt in (self.identb, self.identf):
            t0 = self.pool.done(nc.gpsimd.memset(idt[:], 1.0))
            self.pool.wait(t0)
            self.t_id = self.pool.done(nc.gpsimd.affine_select(out=idt[:], in_=idt[:], pattern=[[-1, 128]],
                                                               compare_op=ALU.is_equal, fill=0.0, base=0,
                                                               channel_multiplier=1))

    _uid = 0

    def sb(self, es, name, shape, dt):
        K._uid += 1
        return es.enter_context(self.nc.sbuf_tensor("%s_%d" % (name, K._uid), shape, dt))

    def ps(self, es, name, shape, dt):
        K._uid += 1
        return es.enter_context(self.nc.psum_tensor("%s_%d" % (name, K._uid), shape, dt))

    def dsem(self, es, name):
        if self.sem_pool and self.nsem >= 80:
            self.sem_pool.sort(key=lambda d: d.count)
            ds = self.sem_pool.pop(0)
        else:
            ds = DmaSem(self.nc, self.es, name)
            self.nsem += 1
        es.callback(self.sem_pool.append, ds)
        return ds


def dram(nc, name, shape, dt, kind="Internal"):
    return nc.dram_tensor(name, list(shape), dt, kind=kind).ap()


def prep_ws(k, es, w_src, name, kdim, ndim):
    nc = k.nc
    nkc, nnc = kdim // 128, ndim // 128
    out = dram(nc, name, [nnc, 128, nkc, 128], BF16)
    ds = k.dsem(es, name)
    t = None
    for c in range(nnc):
        src = w_src[:, c * 128:(c + 1) * 128].rearrange("(kc p) f -> p kc f", p=128)
        t = ds.done(nc.gpsimd.dma_start(out=out[c], in_=src))
    return out, t


def sublayer_phase(k, mode, X_in, X_out, wd, nk, wtick, g_row, b_row, ntok, cx, eps, wg=None, wu=None, Y=None):
    nc = k.nc
    pe, act, dve, pool, sp = k.pe, k.act, k.dve, k.pool, k.sp
    ntiles = ntok // TILE
    ffn = mode == "ffn"
    with ExitStack() as es:
        xs = k.sb(es, "f_xs", [128, 4, D], F32)
        xbw = D if mode != "proj_tm" else nk * 128
        xb = [k.sb(es, "f_xb%d" % i, [128, xbw], BF16) for i in range(4)]
        hid = k.sb(es, "f_hid", [128, nk, TILE], BF16)
        wds = [k.sb(es, "f_wd%d" % i, [128, nk, 128], BF16) for i in range(2)]
        ytmp = [k.sb(es, "f_yt%d" % i, [128, TILE], F32) for i in range(2)]
        grep = k.sb(es, "f_g", [128, D], F32)
        brep = k.sb(es, "f_b", [128, D], F32)
        identb, identf, t_id = k.identb, k.identf, k.t_id
        st = k.sb(es, "f_st", [128, 4, 6], F32)
        mv = k.sb(es, "f_mv", [128, 8], F32)
        pT = [k.ps(es, "f_pT%d" % i, [128, 512], BF16) for i in range(2)]
        acc = [k.ps(es, "f_acc%d" % i, [128, 512], F32) for i in range(4)]
        pZ = [k.ps(es, "f_pZ%d" % i, [128, 512], F32) for i in range(2)]
        if ffn:
            stage = [k.sb(es, "f_stage%d" % i, [128, D], F32) for i in range(2)]
            hT = k.sb(es, "f_hT", [128, KC, TILE], BF16)
            wgs = [k.sb(es, "f_wg%d" % i, [128, KC, 128], BF16) for i in range(3)]
            wus = [k.sb(es, "f_wu%d" % i, [128, KC, 128], BF16) for i in range(3)]
            sgb = [k.sb(es, "f_sg%d" % i, [128, TILE], F32) for i in range(2)]
            s_stage = [k.dsem(es, "stage%d" % i) for i in range(2)]
            s_wgu = [k.dsem(es, "wgu%d" % i) for i in range(3)]
        s_xb = [k.dsem(es, "xb%d" % i) for i in range(4)]
        s_wd = [k.dsem(es, "wd%d" % i) for i in range(2)]
        s_xs = k.dsem(es, "xs")
        s_gb = k.dsem(es, "gb")
        s_out = k.dsem(es, "out")
        s_hid = k.dsem(es, "hid")

        t_gb = s_gb.done(nc.sync.dma_start(out=grep[:], in_=g_row.partition_broadcast(128)))
        t_gb = s_gb.done(nc.sync.dma_start(out=brep[:], in_=b_row.partition_broadcast(128)))

        stage_free = [None, None]
        xb_free = [None] * 4
        pT_free = [None, None]
        acc_free = [None] * 4
        pZ_free = [None, None]
        wgu_free = [None] * 3
        wd_free = [None] * 2
        sg_free = [None, None]
        yt_free = [None, None]
        cnt = dict(stage=0, xb=0, pT=0, acc=0, pZ=0, wgu=0, wd=0, sg=0, yt=0)
        state = dict(hT_free=None, hid_free=None, xs_free=None, st_free=None, xs_ready=None)

        def load_xs(t):
            sp.wait(state["xs_free"])
            for s in range(4):
                r0 = t * TILE + s * 128
                state["xs_ready"] = s_xs.done(nc.sync.dma_start(out=xs[:, s, :], in_=X_in[r0:r0 + 128, :]))

        def emit_T_load(t, src, is_f32, subs=(0, 1, 2, 3)):
            res = []
            for s in subs:
                r0 = t * TILE + s * 128
                xi = cnt["xb"] % 4
                cnt["xb"] += 1
                if is_f32:
                    si = cnt["stage"] % 2
                    cnt["stage"] += 1
                    sp.wait(stage_free[si])
                    t_ld = s_stage[si].done(nc.sync.dma_start(out=stage[si][:], in_=src[r0:r0 + 128, :]))
                    act.wait(t_ld, xb_free[xi])
                    t_c = act.done(nc.scalar.copy(out=xb[xi][:], in_=stage[si][:]))
                    stage_free[si] = t_c
                else:
                    sp.wait(xb_free[xi])
                    t_c = s_xb[xi].done(nc.sync.dma_start(out=xb[xi][:], in_=src[r0:r0 + 128, :]))
                res.append((xi, t_c))
            return res

        def emit_T_pe(loads, nkc, dest, dest_free_key):
            for s, (xi, t_c) in enumerate(loads):
                ngrp = (nkc + 3) // 4
                for q in range(ngrp):
                    n4 = min(4, nkc - 4 * q)
                    pi = cnt["pT"] % 2
                    cnt["pT"] += 1
                    pe.wait(t_c, pT_free[pi], t_id)
                    for j in range(n4):
                        kc = 4 * q + j
                        ins = nc.tensor.transpose(pT[pi][:, j * 128:(j + 1) * 128],
                                                  xb[xi][:, kc * 128:(kc + 1) * 128], identb[:])
                    t_tr = pe.done(ins)
                    ev = dve if (q % 2 == 0) else act
                    ev.wait(t_tr, state[dest_free_key])
                    dst = dest[:, 4 * q:4 * q + n4, s * 128:(s + 1) * 128]
                    srcp = pT[pi][:, 0:n4 * 128].rearrange("p (a b) -> p a b", b=128)
                    if ev is dve:
                        t_ev = dve.done(nc.vector.tensor_copy(out=dst, in_=srcp))
                    else:
                        t_ev = act.done(nc.scalar.copy(out=dst, in_=srcp))
                    pT_free[pi] = t_ev
                xb_free[xi] = t_tr
            return [dve.last(), act.last()]

        def emit_T(t, src, is_f32, nkc, dest, dest_free_key):
            return emit_T_pe(emit_T_load(t, src, is_f32), nkc, dest, dest_free_key)

        def emit_GU(t, hT_ready):
            for fc in range(FC):
                wi = cnt["wgu"] % 3
                cnt["wgu"] += 1
                sp.wait(wgu_free[wi], wtick)
                s_wgu[wi].done(nc.sync.dma_start(out=wgs[wi][:], in_=wg[fc]))
                t_w = s_wgu[wi].done(nc.sync.dma_start(out=wus[wi][:], in_=wu[fc]))
                if fc == 22:
                    load_xs(t)
                if fc in (4, 8, 12, 16) and t + 1 < ntiles:
                    if fc == 4:
                        state["next_loads"] = []
                    state["next_loads"] += emit_T_load(t + 1, X_in, True, subs=((fc - 4) // 4,))
                ag = cnt["acc"] % 4
                au = (cnt["acc"] + 1) % 4
                cnt["acc"] += 2
                pe.wait(t_w, hT_ready, acc_free[ag])
                for kc in range(KC):
                    ins = nc.tensor.matmul(acc[ag][:], lhsT=wgs[wi][:, kc, :], rhs=hT[:, kc, :],
                                           start=(kc == 0), stop=(kc == KC - 1))
                t_g = pe.done(ins)
                pe.wait(acc_free[au])
                for kc in range(KC):
                    ins = nc.tensor.matmul(acc[au][:], lhsT=wus[wi][:, kc, :], rhs=hT[:, kc, :],
                                           start=(kc == 0), stop=(kc == KC - 1))
                t_u = pe.done(ins)
                wgu_free[wi] = t_u
                gi = cnt["sg"] % 2
                cnt["sg"] += 1
                act.wait(t_g, sg_free[gi])
                t_s = act.done(nc.scalar.activation(out=sgb[gi][:], in_=acc[ag][:], func=AF.Silu))
                acc_free[ag] = t_s
                dve.wait(t_s, t_u, state["hid_free"])
                t_h = dve.done(nc.vector.tensor_tensor(out=hid[:, fc, :], in0=sgb[gi][:], in1=acc[au][:],
                                                       op=ALU.mult))
                sg_free[gi] = t_h
                acc_free[au] = t_h
            state["hT_free"] = t_u
            return t_h

        def emit_D(t, hid_ready):
            pend = None
            t_res = None

            def emit_tr(p):
                oc, yi, t_cp = p
                zi = cnt["pZ"] % 2
                cnt["pZ"] += 1
                pe.wait(t_cp, pZ_free[zi], t_id)
                for s in range(4):
                    ins = nc.tensor.transpose(pZ[zi][:, s * 128:(s + 1) * 128], ytmp[yi][:, s * 128:(s + 1) * 128],
                                              identf[:])
                t_tr = pe.done(ins)
                yt_free[yi] = t_tr
                dve.wait(t_tr, state["xs_ready"])
                t_r = dve.done(nc.vector.scalar_tensor_tensor(
                    out=xs[:, :, oc * 128:(oc + 1) * 128], in0=xs[:, :, oc * 128:(oc + 1) * 128],
                    scalar=float(cx), in1=pZ[zi][:].rearrange("p (a b) -> p a b", b=128),
                    op0=ALU.mult, op1=ALU.add))
                pZ_free[zi] = t_r
                return t_r

            for oc in range(KC):
                wi = cnt["wd"] % 2
                cnt["wd"] += 1
                sp.wait(wd_free[wi], wtick)
                t_w = s_wd[wi].done(nc.sync.dma_start(out=wds[wi][:], in_=wd[oc]))
                ai = cnt["acc"] % 4
                cnt["acc"] += 1
                pe.wait(t_w, hid_ready, acc_free[ai])
                for fc in range(nk):
                    ins = nc.tensor.matmul(acc[ai][:], lhsT=wds[wi][:, fc, :], rhs=hid[:, fc, :],
                                           start=(fc == 0), stop=(fc == nk - 1))
                t_m = pe.done(ins)
                wd_free[wi] = t_m
                yi = cnt["yt"] % 2
                cnt["yt"] += 1
                act.wait(t_m, yt_free[yi])
                t_cp = act.done(nc.scalar.copy(out=ytmp[yi][:], in_=acc[ai][:]))
                acc_free[ai] = t_cp
                if pend is not None:
                    t_res = emit_tr(pend)
                pend = (oc, yi, t_cp)
            state["hid_free"] = t_m
            t_res = emit_tr(pend)
            return t_res

        def emit_LN(t, t_res):
            t_st = None
            for s in range(4):
                dve.wait(t_res, state["st_free"])
                for c in range(4):
                    t_b = dve.done(nc.vector.bn_stats(out=st[:, c, :], in_=xs[:, s, c * 512:(c + 1) * 512]))
                dve.wait(t_b)
                t_a = dve.done(nc.vector.bn_aggr(out=mv[:, 0:2], in_=st[:].rearrange("p a b -> p (a b)")))
                dve.wait(t_a)
                t_e = dve.done(nc.vector.tensor_scalar(out=mv[:, 2:3], in0=mv[:, 1:2], scalar1=float(eps),
                                                       scalar2=None, op0=ALU.add))
                act.wait(t_e)
                t_q = act.done(nc.scalar.activation(out=mv[:, 3:4], in_=mv[:, 2:3], func=AF.Sqrt))
                dve.wait(t_q)
                t_r = dve.done(nc.vector.reciprocal(out=mv[:, 4:5], in_=mv[:, 3:4]))
                dve.wait(t_r)
                t_n = dve.done(nc.vector.scalar_tensor_tensor(out=mv[:, 5:6], in0=mv[:, 0:1], scalar=-1.0,
                                                              in1=mv[:, 4:5], op0=ALU.mult, op1=ALU.mult))
                act.wait(t_n)
                t_x = act.done(nc.scalar.activation(out=xs[:, s, :], in_=xs[:, s, :], func=AF.Identity,
                                                    bias=mv[:, 5:6], scale=mv[:, 4:5]))
                state["st_free"] = t_x
                dve.wait(t_x, t_gb)
                t_p = dve.done(nc.vector.tensor_tensor(out=xs[:, s, :], in0=xs[:, s, :], in1=grep[:], op=ALU.mult))
                dve.wait(t_p)
                t_p = dve.done(nc.vector.tensor_tensor(out=xs[:, s, :], in0=xs[:, s, :], in1=brep[:], op=ALU.add))
                pool.wait(t_p)
                r0 = t * TILE + s * 128
                t_st = s_out.done(nc.gpsimd.dma_start(out=X_out[r0:r0 + 128, :], in_=xs[:, s, :]))
            state["xs_free"] = t_st
            return t_st

        t_out = None
        if ffn:
            hT_ready = emit_T(0, X_in, True, KC, hT, "hT_free")
            for t in range(ntiles):
                hid_ready = emit_GU(t, hT_ready)
                if t + 1 < ntiles:
                    hT_ready = emit_T_pe(state["next_loads"], KC, hT, "hT_free")
                t_res = emit_D(t, hid_ready)
                t_out = emit_LN(t, t_res)
        else:
            for t in range(ntiles):
                if mode == "proj_tm":
                    hid_ready = emit_T(t, Y, False, nk, hid, "hid_free")
                else:
                    sp.wait(state["hid_free"])
                    for c in range(nk):
                        hid_ready = s_hid.done(nc.sync.dma_start(out=hid[:, c, :],
                                                                 in_=Y[c, :, t * TILE:(t + 1) * TILE]))
                load_xs(t)
                t_res = emit_D(t, hid_ready)
                t_out = emit_LN(t, t_res)
        drain(k, [t_out])
        return t_out


def drain(k, extra=()):
    engs = (k.pe, k.act, k.dve, k.pool, k.sp)
    fin = [e.last() for e in engs] + list(extra)
    for e in engs:
        e.wait(fin)


NH = 8
KINDS = ["q", "k", "v", "g", "mq", "mk", "mv"]


def prep_tm(k, es, w_src, name, kdim, ndim, gw=512):
    nc = k.nc
    nkc, ncg = kdim // 128, ndim // gw
    out = dram(nc, name, [ncg, 128, nkc, gw], BF16)
    ds = k.dsem(es, name)
    t = None
    for c in range(ncg):
        src = w_src[:, c * gw:(c + 1) * gw].rearrange("(kc p) f -> p kc f", p=128)
        t = ds.done(nc.gpsimd.dma_start(out=out[c], in_=src))
    return out, t


def attn_inproj_phase(k, X_in, w_in, wtick, cos_d, sin_d, ksc_d, outs, ntok):
    nc = k.nc
    pe, act, dve, pool, sp = k.pe, k.act, k.dve, k.pool, k.sp
    QT, KT, KTM, VTM, GTM, MQT, MKT, MVTM, KMEAN = outs
    ntiles = ntok // TILE
    with ExitStack() as es:
        stage = [k.sb(es, "a_stage%d" % i, [128, D], F32) for i in range(2)]
        xb = [k.sb(es, "a_xb%d" % i, [128, D], BF16) for i in range(4)]
        hT = k.sb(es, "a_hT", [128, KC, TILE], BF16)
        wt = [k.sb(es, "a_wt%d" % i, [128, KC, 512], BF16) for i in range(2)]
        xsb = [k.sb(es, "a_xsb%d" % i, [128, 512], F32) for i in range(2)]
        ra = [k.sb(es, "a_ra%d" % i, [128, 512], F32) for i in range(2)]
        rb = [k.sb(es, "a_rb%d" % i, [128, 512], F32) for i in range(2)]
        tmb = [k.sb(es, "a_tmb%d" % i, [128, 512], BF16) for i in range(4)]
        fmt = [k.sb(es, "a_fmt%d" % i, [128, 4, TILE], BF16) for i in range(2)]
        cosr = [k.sb(es, "a_cos%d" % i, [128, 4, 64], F32) for i in range(2)]
        sinr = [k.sb(es, "a_sin%d" % i, [128, 4, 64], F32) for i in range(2)]
        ksc = k.sb(es, "a_ksc", [128, NH], F32)
        onesb = k.sb(es, "a_ones", [128, 2], BF16)
        kmrow = [k.sb(es, "a_kmrow%d" % i, [1, 512], F32) for i in range(2)]
        identb, t_id = k.identb, k.t_id
        pT_t = k.ps(es, "a_pT", [128, 512], BF16)
        pT = [pT_t[:, :], pT_t[:, :]]
        acc = [k.ps(es, "a_acc%d" % i, [128, 512], F32) for i in range(4)]
        ptr = [k.ps(es, "a_ptr%d" % i, [128, 512], BF16)[:, :] for i in range(2)]
        pkm_t = k.ps(es, "a_pkm", [1, 512], F32)
        pkm = [pkm_t, pkm_t]

        s_stage = [k.dsem(es, "astage%d" % i) for i in range(2)]
        s_wt = [k.dsem(es, "awt%d" % i) for i in range(2)]
        s_cs = [k.dsem(es, "acs%d" % i) for i in range(2)]
        s_c = k.dsem(es, "aconst")
        s_st = [k.dsem(es, "ast%d" % i) for i in range(4)]
        s_fm = [k.dsem(es, "afm%d" % i) for i in range(2)]
        s_km = [k.dsem(es, "akm%d" % i) for i in range(2)]

        t_c = s_c.done(nc.sync.dma_start(out=ksc[:], in_=ksc_d[:, :]))
        t_id = [t_id, dve.done(nc.vector.memset(onesb[:], 1.0))]

        stage_free = [None, None]
        xb_free = [None] * 4
        pT_free = [None, None]
        acc_free = [None] * 4
        wt_free = [None, None]
        xsb_free = [None, None]
        ra_free = [None, None]
        rb_free = [None, None]
        tmb_free = [None] * 4
        ptr_free = [None, None]
        fmt_free = [None, None]
        cs_free = [None, None]
        pkm_free = [None, None]
        kmrow_free = [None, None]
        cnt = dict(stage=0, xb=0, pT=0, acc=0, wt=0, xsb=0, r=0, tmb=0, ptr=0, fmt=0, km=0)
        state = dict(hT_free=None)

        def emit_T_load(t, subs):
            res = []
            for s in subs:
                r0 = t * TILE + s * 128
                xi = cnt["xb"] % 4
                cnt["xb"] += 1
                si = cnt["stage"] % 2
                cnt["stage"] += 1
                sp.wait(stage_free[si])
                t_ld = s_stage[si].done(nc.sync.dma_start(out=stage[si][:], in_=X_in[r0:r0 + 128, :]))
                act.wait(t_ld, xb_free[xi])
                t_cc = act.done(nc.scalar.copy(out=xb[xi][:], in_=stage[si][:]))
                stage_free[si] = t_cc
                res.append((xi, t_cc))
            return res

        def emit_T(t, loads=None):
            if loads is None:
                loads = emit_T_load(t, (0, 1, 2, 3))
            for s, (xi, t_cc) in enumerate(loads):
                for q in range(4):
                    pi = 0
                    pe.wait(t_cc, pT_free[pi], t_id)
                    for j in range(4):
                        kc = 4 * q + j
                        ins = nc.tensor.transpose(pT[pi][:, j * 128:(j + 1) * 128],
                                                  xb[xi][:, kc * 128:(kc + 1) * 128], identb[:])
                    t_tr = pe.done(ins)
                    ev = dve if (q % 2 == 0) else act
                    ev.wait(t_tr, state["hT_free"])
                    dst = hT[:, 4 * q:4 * q + 4, s * 128:(s + 1) * 128]
                    srcp = pT[pi].rearrange("p (a b) -> p a b", b=128)
                    if ev is dve:
                        t_ev = dve.done(nc.vector.tensor_copy(out=dst, in_=srcp))
                    else:
                        t_ev = act.done(nc.scalar.copy(out=dst, in_=srcp))
                    pT_free[pi] = t_ev
                xb_free[xi] = t_tr
            return [dve.last(), act.last()]

        TMDST = {"k": KTM, "v": VTM, "g": GTM, "mv": MVTM}
        FMDST = {"q": QT, "k": KT, "mq": MQT, "mk": MKT}

        def post(t, cg, s, ai, t_m, ci, fj):
            kind = KINDS[cg // 2]
            dbg = getattr(k, "dbg", 9)
            if dbg < 4 and kind in ("q", "k"):
                kind = "mq" if kind == "q" else "mk"
            hb = (cg % 2) * 4
            r0 = t * TILE + s * 128
            j = cnt["tmb"] % 4
            cnt["tmb"] += 1
            if kind in ("q", "k"):
                xj = cnt["xsb"] % 2
                cnt["xsb"] += 1
                act.wait(t_m, xsb_free[xj])
                t_x = act.done(nc.scalar.copy(out=xsb[xj][:], in_=acc[ai][:]))
                acc_free[ai] = t_x
                r = cnt["r"] % 2
                cnt["r"] += 1
                x8 = xsb[xj][:].rearrange("p (a b) -> p a b", b=64)
                x42 = xsb[xj][:].rearrange("p (h a b) -> p h a b", a=2, b=64)
                rb42 = rb[r][:].rearrange("p (h a b) -> p h a b", a=2, b=64)
                cosb = cosr[ci][:, s, :].unsqueeze(1).to_broadcast([128, 8, 64])
                sinb = sinr[ci][:, s, :].unsqueeze(1).to_broadcast([128, 4, 64])
                dve.wait(t_x, ra_free[r], state["cs_ready"])
                t_a = dve.done(nc.vector.tensor_tensor(out=ra[r][:].rearrange("p (a b) -> p a b", b=64), in0=x8,
                                                       in1=cosb, op=ALU.mult))
                dve.wait(t_x, rb_free[r], state["cs_ready"])
                dve.done(nc.vector.tensor_tensor(out=rb42[:, :, 0, :], in0=x42[:, :, 1, :], in1=sinb, op=ALU.mult))
                t_b = dve.done(nc.vector.tensor_tensor(out=rb42[:, :, 1, :], in0=x42[:, :, 0, :], in1=sinb,
                                                       op=ALU.mult))
                xsb_free[xj] = [t_a, t_b]
                ra42 = ra[r][:].rearrange("p (h a b) -> p h a b", a=2, b=64)
                dve.wait(t_a, t_b, tmb_free[j])
                if kind == "q":
                    o42 = tmb[j][:].rearrange("p (h a b) -> p h a b", a=2, b=64)
                else:
                    o42 = ra42
                dve.done(nc.vector.tensor_tensor(out=o42[:, :, 0, :], in0=ra42[:, :, 0, :], in1=rb42[:, :, 0, :],
                                                 op=ALU.subtract))
                t_tm = dve.done(nc.vector.tensor_tensor(out=o42[:, :, 1, :], in0=ra42[:, :, 1, :],
                                                        in1=rb42[:, :, 1, :], op=ALU.add))
                if kind == "k":
                    dve.wait(t_tm, t_c)
                    t_tm = dve.done(nc.vector.tensor_tensor(
                        out=tmb[j][:].rearrange("p (h d) -> p h d", d=128),
                        in0=ra[r][:].rearrange("p (h d) -> p h d", d=128),
                        in1=ksc[:, hb:hb + 4].unsqueeze(2).to_broadcast([128, 4, 128]), op=ALU.mult))
                ra_free[r] = t_tm
                rb_free[r] = t_tm
            else:
                act.wait(t_m, tmb_free[j])
                if kind == "g":
                    t_tm = act.done(nc.scalar.activation(out=tmb[j][:], in_=acc[ai][:], func=AF.Silu))
                else:
                    t_tm = act.done(nc.scalar.copy(out=tmb[j][:], in_=acc[ai][:]))
                acc_free[ai] = t_tm
            frees = []
            if kind in TMDST and dbg >= 2:
                pool.wait(t_tm)
                frees.append(s_st[j].done(nc.gpsimd.dma_start(
                    out=TMDST[kind][r0:r0 + 128, hb * 128:hb * 128 + 512], in_=tmb[j][:])))
            tmb_free[j] = frees
            if kind in FMDST and dbg >= 3:
                return (t, cg, s, j, t_tm, fj, kind, hb, frees)
            return None

        def pe_post(p):
            t, cg, s, j, t_tm, fj, kind, hb, frees = p
            pi = cnt["ptr"] % 2
            cnt["ptr"] += 1
            pe.wait(t_tm, ptr_free[pi], t_id)
            for hh in range(4):
                ins = nc.tensor.transpose(ptr[pi][:, hh * 128:(hh + 1) * 128], tmb[j][:, hh * 128:(hh + 1) * 128],
                                          identb[:])
            t_tr = pe.done(ins)
            frees.append(t_tr)
            ev = dve if (s % 2 == 0) else act
            ev.wait(t_tr, fmt_free[fj] if s == 0 else None)
            dst = fmt[fj][:, :, s * 128:(s + 1) * 128]
            srcp = ptr[pi].rearrange("p (a b) -> p a b", b=128)
            if ev is dve:
                t_ev = dve.done(nc.vector.tensor_copy(out=dst, in_=srcp))
            else:
                t_ev = act.done(nc.scalar.copy(out=dst, in_=srcp))
            ptr_free[pi] = t_ev
            state["fm_evs"].append(t_ev)
            if kind == "mk" and not getattr(k, "no_kmean", False):
                b = s // 2
                kmi = 0
                if s % 2 == 0:
                    pe.wait(pkm_free[kmi])
                ins = nc.tensor.matmul(pkm[kmi][0:1, :], lhsT=onesb[:, 0:1], rhs=tmb[j][:], start=(s % 2 == 0),
                                       stop=(s % 2 == 1))
                t_k = pe.done(ins)
                frees.append(t_k)
                if s % 2 == 1:
                    act.wait(t_k, kmrow_free[kmi])
                    t_r = act.done(nc.scalar.activation(out=kmrow[kmi][:], in_=pkm[kmi][0:1, :], func=AF.Copy,
                                                        scale=1.0 / 256.0))
                    pkm_free[kmi] = t_r
                    pool.wait(t_r)
                    kmrow_free[kmi] = s_km[kmi].done(nc.gpsimd.dma_start(
                        out=KMEAN[2 * t + b:2 * t + b + 1, hb * 128:hb * 128 + 512], in_=kmrow[kmi][:]))
            if s == 3:
                pool.wait(state["fm_evs"])
                fmt_free[fj] = s_fm[fj].done(nc.gpsimd.dma_start(
                    out=FMDST[kind][hb:hb + 4, :, t * TILE:(t + 1) * TILE].rearrange("h d t -> d h t"),
                    in_=fmt[fj][:]))
                state["fm_evs"] = []

        state["fm_evs"] = []
        if getattr(k, "dbg", 9) == -1:
            drain(k, [t_c])
            return
        hT_ready = emit_T(0)
        if getattr(k, "dbg", 9) == 0:
            drain(k, [t_c])
            return
        for t in range(ntiles):
            ci = t % 2
            sp.wait(cs_free[ci])
            s_cs[ci].done(nc.sync.dma_start(out=cosr[ci][:], in_=cos_d[t * TILE:(t + 1) * TILE, :]
                                            .rearrange("(s p) j -> p s j", p=128)))
            state["cs_ready"] = s_cs[ci].done(nc.sync.dma_start(
                out=sinr[ci][:], in_=sin_d[t * TILE:(t + 1) * TILE, :].rearrange("(s p) j -> p s j", p=128)))
            pend = None
            for cg in range(14):
                wi = cnt["wt"] % 2
                cnt["wt"] += 1
                sp.wait(wt_free[wi], wtick)
                t_w = s_wt[wi].done(nc.sync.dma_start(out=wt[wi][:], in_=w_in[cg]))
                kind = KINDS[cg // 2]
                if cg in (1, 3, 5, 7) and t + 1 < ntiles:
                    if cg == 1:
                        state["next_loads"] = []
                    state["next_loads"] += emit_T_load(t + 1, ((cg - 1) // 2,))
                fj = None
                if kind in FMDST:
                    fj = cnt["fmt"] % 2
                    cnt["fmt"] += 1
                for s in range(4):
                    ai = cnt["acc"] % 4
                    cnt["acc"] += 1
                    pe.wait(t_w, hT_ready, acc_free[ai])
                    for kc in range(KC):
                        ins = nc.tensor.matmul(acc[ai][:], lhsT=hT[:, kc, s * 128:(s + 1) * 128], rhs=wt[wi][:, kc, :],
                                               start=(kc == 0), stop=(kc == KC - 1))
                    t_m = pe.done(ins)
                    if pend is not None:
                        pe_post(pend)
                    pend = post(t, cg, s, ai, t_m, ci, fj)
                wt_free[wi] = t_m
            state["hT_free"] = t_m
            if pend is not None:
                pe_post(pend)
                pend = None
            cs_free[ci] = [dve.last(), pool.last()]
            if t + 1 < ntiles:
                hT_ready = emit_T(t + 1, state["next_loads"])
        drain(k, [x.last() for x in s_st + s_fm + s_km])


def retention_phase(k, QT, KT, KTM, VTM, GTM, gng_row, gnb_row, causal_d, dec_d, qdec_d, YATT, ntok):
    nc = k.nc
    pe, act, dve, pool, sp = k.pe, k.act, k.dve, k.pool, k.sp
    ngrp = ntok // TILE
    with ExitStack() as es:
        qT4 = [k.sb(es, "r_qT%d" % i, [128, NH, TILE], BF16) for i in range(2)]
        kT4 = [k.sb(es, "r_kT%d" % i, [128, NH, TILE], BF16) for i in range(2)]
        ktm4 = [k.sb(es, "r_ktm%d" % i, [128, 4, 1024], BF16) for i in range(2)]
        vtm4 = [k.sb(es, "r_vtm%d" % i, [128, 4, 1024], BF16) for i in range(2)]
        gtm4 = [k.sb(es, "r_gtm%d" % i, [128, 4, 1024], BF16) for i in range(2)]
        S = k.sb(es, "r_S", [128, 1024], F32)
        Sb = [k.sb(es, "r_Sb%d" % i, [128, 1024], BF16) for i in range(2)]
        PT = [k.sb(es, "r_PT%d" % i, [128, 1024], BF16) for i in range(2)]
        ro = [k.sb(es, "r_ro%d" % i, [128, 1024], F32) for i in range(2)]
        ob = [k.sb(es, "r_ob%d" % i, [128, 1024], BF16) for i in range(2)]
        causal = k.sb(es, "r_causal", [128, 128], F32)
        dec = k.sb(es, "r_dec", [128, NH], F32)
        qdec = k.sb(es, "r_qdec", [128, NH], F32)
        gng = k.sb(es, "r_gng", [128, 1024], F32)
        gnb = k.sb(es, "r_gnb", [128, 1024], F32)
        st8 = k.sb(es, "r_st8", [128, NH, 6], F32)
        mv8 = k.sb(es, "r_mv8", [128, NH, 2], F32)
        rs = k.sb(es, "r_rs", [128, 3, NH], F32)
        pS = k.ps(es, "r_pS", [128, 1024], F32)
        pO = k.ps(es, "r_pO", [128, 1024], F32)
        pKV = k.ps(es, "r_pKV", [128, 1024], F32)
        s_ld = [k.dsem(es, "rld%d" % i) for i in range(2)]
        s_c = k.dsem(es, "rconst")
        s_o = [k.dsem(es, "rout%d" % i) for i in range(2)]

        s_c.done(nc.sync.dma_start(out=causal[:], in_=causal_d[:, :]))
        s_c.done(nc.sync.dma_start(out=dec[:], in_=dec_d[:, :]))
        s_c.done(nc.sync.dma_start(out=qdec[:], in_=qdec_d[:, :]))
        s_c.done(nc.sync.dma_start(out=gng[:], in_=gng_row.partition_broadcast(128)))
        t_c = s_c.done(nc.sync.dma_start(out=gnb[:], in_=gnb_row.partition_broadcast(128)))
        t0 = dve.done(nc.vector.memset(S[:], 0.0))
        t_sb = dve.done(nc.vector.memset(Sb[0][:], 0.0))
        t_S = t_sb

        def v3(ap):
            return ap.rearrange("p (h e) -> p h e", e=128)

        ld_free = [None, None]
        pS_free = pO_free = pKV_free = None
        PT_free = [None, None]
        ro_free = [None, None]
        ob_free = [None, None]
        Sb_free = [None, None]
        rs_free = None
        nchunk = 0
        for g in range(ngrp):
            li = g % 2
            sp.wait(ld_free[li])
            sl = slice(g * TILE, (g + 1) * TILE)
            s_ld[li].done(nc.sync.dma_start(out=qT4[li][:], in_=QT[:, :, sl].rearrange("h d t -> d h t")))
            s_ld[li].done(nc.sync.dma_start(out=kT4[li][:], in_=KT[:, :, sl].rearrange("h d t -> d h t")))
            s_ld[li].done(nc.sync.dma_start(out=ktm4[li][:], in_=KTM[sl, :].rearrange("(c p) f -> p c f", p=128)))
            s_ld[li].done(nc.sync.dma_start(out=vtm4[li][:], in_=VTM[sl, :].rearrange("(c p) f -> p c f", p=128)))
            t_ld = s_ld[li].done(nc.sync.dma_start(out=gtm4[li][:],
                                                   in_=GTM[sl, :].rearrange("(c p) f -> p c f", p=128)))
            for cc in range(4):
                c0 = cc * 128
                r0 = g * TILE + c0
                bi = nchunk % 2
                nchunk += 1
                pe.wait(t_ld, pS_free)
                for h in range(NH):
                    ins = nc.tensor.matmul(pS[:, h * 128:(h + 1) * 128], lhsT=kT4[li][:, h, c0:c0 + 128],
                                           rhs=qT4[li][:, h, c0:c0 + 128], start=True, stop=True)
                t_s = pe.done(ins)
                dve.wait(t_s, PT_free[bi], t_c)
                t_pt = dve.done(nc.vector.tensor_tensor(out=v3(PT[bi][:]), in0=v3(pS[:]),
                                                        in1=causal[:].unsqueeze(1).to_broadcast([128, NH, 128]),
                                                        op=ALU.mult))
                pS_free = t_pt
                pe.wait(t_pt, t_sb, pO_free)
                for h in range(NH):
                    hs = slice(h * 128, (h + 1) * 128)
                    nc.tensor.matmul(pO[:, hs], lhsT=PT[bi][:, hs], rhs=vtm4[li][:, cc, hs], start=True, stop=False)
                    ins = nc.tensor.matmul(pO[:, hs], lhsT=qT4[li][:, h, c0:c0 + 128], rhs=Sb[bi][:, hs],
                                           start=False, stop=True)
                t_o = pe.done(ins)
                PT_free[bi] = t_o
                Sb_free[bi] = t_o
                pe.wait(pKV_free)
                for h in range(NH):
                    hs = slice(h * 128, (h + 1) * 128)
                    ins = nc.tensor.matmul(pKV[:, hs], lhsT=ktm4[li][:, cc, hs], rhs=vtm4[li][:, cc, hs],
                                           start=True, stop=True)
                t_kv = pe.done(ins)
                dve.wait(t_kv, t_S)
                t_1 = dve.done(nc.vector.tensor_tensor(out=S[:], in0=S[:], in1=pKV[:], op=ALU.add))
                pKV_free = t_1
                dve.wait(t_1, t_c)
                t_2 = dve.done(nc.vector.tensor_tensor(out=v3(S[:]), in0=v3(S[:]),
                                                       in1=dec[:].unsqueeze(2).to_broadcast([128, NH, 128]),
                                                       op=ALU.mult))
                act.wait(t_2, Sb_free[1 - bi])
                t_sb = act.done(nc.scalar.copy(out=Sb[1 - bi][:], in_=S[:]))
                t_S = t_sb
                dve.wait(t_o, ro_free[bi], rs_free)
                t_r = dve.done(nc.vector.tensor_tensor(out=v3(ro[bi][:]), in0=v3(pO[:]),
                                                       in1=qdec[:].unsqueeze(2).to_broadcast([128, NH, 128]),
                                                       op=ALU.mult))
                pO_free = t_r
                dve.wait(t_r)
                for h in range(NH):
                    t_b = dve.done(nc.vector.bn_stats(out=st8[:, h, :], in_=ro[bi][:, h * 128:(h + 1) * 128]))
                dve.wait(t_b)
                for h in range(NH):
                    t_a = dve.done(nc.vector.bn_aggr(out=mv8[:, h, :], in_=st8[:, h, :]))
                dve.wait(t_a)
                t_e = dve.done(nc.vector.tensor_scalar(out=rs[:, 0, :], in0=mv8[:, :, 1], scalar1=LN_EPS, scalar2=None,
                                                       op0=ALU.add))
                act.wait(t_e)
                t_q = act.done(nc.scalar.activation(out=rs[:, 1, :], in_=rs[:, 0, :], func=AF.Sqrt))
                dve.wait(t_q)
                t_i = dve.done(nc.vector.reciprocal(out=rs[:, 2, :], in_=rs[:, 1, :]))
                dve.wait(t_i)
                t_nb = dve.done(nc.vector.scalar_tensor_tensor(out=rs[:, 1, :], in0=mv8[:, :, 0], scalar=-1.0,
                                                               in1=rs[:, 2, :], op0=ALU.mult, op1=ALU.mult))
                act.wait(t_nb)
                for h in range(NH):
                    hs = slice(h * 128, (h + 1) * 128)
                    t_p = act.done(nc.scalar.activation(out=ro[bi][:, hs], in_=ro[bi][:, hs], func=AF.Identity,
                                                        scale=rs[:, 2, h:h + 1], bias=rs[:, 1, h:h + 1]))
                rs_free = t_p
                dve.wait(t_p)
                t_p = dve.done(nc.vector.tensor_tensor(out=ro[bi][:], in0=ro[bi][:], in1=gng[:], op=ALU.mult))
                dve.wait(t_p)
                t_p = dve.done(nc.vector.tensor_tensor(out=ro[bi][:], in0=ro[bi][:], in1=gnb[:], op=ALU.add))
                dve.wait(t_p, ob_free[bi])
                t_f = dve.done(nc.vector.tensor_tensor(out=ob[bi][:], in0=ro[bi][:], in1=gtm4[li][:, cc, :],
                                                       op=ALU.mult))
                ro_free[bi] = t_f
                pool.wait(t_f)
                ob_free[bi] = s_o[bi].done(nc.gpsimd.dma_start(out=YATT[r0:r0 + 128, 0:1024], in_=ob[bi][:]))
            ld_free[li] = [pe.last(), dve.last()]
        drain(k, [x.last() for x in s_o])


def moba_phase(k, MQT, MKT, MVTM, KMEAN, causalT_d, YATT, ntok):
    nc = k.nc
    pe, act, dve, pool, sp = k.pe, k.act, k.dve, k.pool, k.sp
    nq = ntok // 128
    nblk = ntok // 256
    SC = 128.0 ** -0.5
    with ExitStack() as es:
        mqT = [k.sb(es, "m_q%d" % i, [128, ntok], BF16) for i in range(2)]
        mkT = [k.sb(es, "m_k%d" % i, [128, ntok], BF16) for i in range(2)]
        mv = [k.sb(es, "m_v%d" % i, [128, nq, 129], BF16) for i in range(2)]
        km = k.sb(es, "m_km", [nblk, 1024], F32)
        kmT = k.sb(es, "m_kmT", [128, NH, 32], BF16)
        identf = k.identf
        causalT = k.sb(es, "m_causal", [128, 128], F32)
        gm = k.sb(es, "m_gm", [128, 32], F32)
        top8 = k.sb(es, "m_top8", [128, 8], F32)
        sel = [k.sb(es, "m_sel%d" % i, [128, 32], F32) for i in range(2)]
        eo = [k.sb(es, "m_eo%d" % i, [128, 128], F32) for i in range(2)]
        PTo = [k.sb(es, "m_PTo%d" % i, [128, 2, 128], BF16) for i in range(2)]
        PTg = [k.sb(es, "m_PTg%d" % i, [128, 4, 128], BF16) for i in range(3)]
        O = [k.sb(es, "m_O%d" % i, [128, 132], F32) for i in range(2)]
        rinv = k.sb(es, "m_rinv", [128, 2], F32)
        mob = [k.sb(es, "m_mob%d" % i, [128, 128], BF16) for i in range(2)]
        pG = k.ps(es, "m_pG", [128, 512], F32)
        pSo = k.ps(es, "m_pSo", [128, 512], F32)
        pOo = k.ps(es, "m_pOo", [128, 512], F32)
        pS = [k.ps(es, "m_pS%d" % i, [128, 512], F32) for i in range(2)]
        pO2 = [k.ps(es, "m_pO2%d" % i, [128, 512], F32) for i in range(2)]
        s_h = [k.dsem(es, "mh%d" % i) for i in range(2)]
        s_c = k.dsem(es, "mconst")
        s_o = [k.dsem(es, "mout%d" % i) for i in range(2)]

        s_c.done(nc.sync.dma_start(out=km[:], in_=KMEAN[:, :]))
        t_c = s_c.done(nc.sync.dma_start(out=causalT[:], in_=causalT_d[:, :]))
        t_id = k.t_id
        for i in range(2):
            t_ones = dve.done(nc.vector.memset(mv[i][:, :, 128:129], 1.0))
        t_prev = None
        for h in range(NH):
            pe.wait(t_c, t_id, t_prev)
            t_t = pe.done(nc.tensor.transpose(pG[:, 0:nblk], km[:, h * 128:(h + 1) * 128], identf[0:nblk, 0:nblk]))
            dve.wait(t_t)
            t_prev = dve.done(nc.vector.tensor_copy(out=kmT[:, h, 0:nblk], in_=pG[:, 0:nblk]))
        pG_free = t_prev

        h_free = [None, None]
        pSo_free = pOo_free = None
        pS_free = [None, None]
        pO2_free = [None, None]
        PTo_free = [None, None]
        PTg_free = [None] * 3
        O_free = [None, None]
        eo_free = [None, None]
        sel_free = [None, None]
        mob_free = [None, None]
        gm_t = None
        top_free = None
        rinv_free = None
        cnt = dict(pS=0, pO2=0, PTg=0, qi=0)

        for h in range(NH):
            hi = h % 2
            sp.wait(h_free[hi], t_ones)
            s_h[hi].done(nc.sync.dma_start(out=mqT[hi][:], in_=MQT[h]))
            s_h[hi].done(nc.sync.dma_start(out=mkT[hi][:], in_=MKT[h]))
            t_ld = s_h[hi].done(nc.sync.dma_start(
                out=mv[hi][:, :, 0:128], in_=MVTM[:, h * 128:(h + 1) * 128].rearrange("(c p) d -> p c d", p=128)))
            dve.wait(gm_t)
            gm_t = dve.done(nc.vector.memset(gm[:], -1e30))
            for i in range(nq):
                nb = i // 2
                qi = cnt["qi"] % 2
                cnt["qi"] += 1
                qs = slice(i * 128, (i + 1) * 128)
                r0 = i * 128
                use_sel = nb > 3
                if use_sel:
                    pe.wait(t_ld, pG_free)
                    t_g = pe.done(nc.tensor.matmul(pG[:, 0:32], lhsT=mqT[hi][:, qs], rhs=kmT[:, h, :],
                                                   start=True, stop=True))
                    dve.wait(t_g, gm_t, top_free)
                    t_gm = dve.done(nc.vector.tensor_copy(out=gm[:, 0:nb], in_=pG[:, 0:nb]))
                    pG_free = t_gm
                    dve.wait(t_gm)
                    t_t8 = dve.done(nc.vector.max(out=top8[:], in_=gm[:]))
                    dve.wait(t_t8, sel_free[qi])
                    t_sel = dve.done(nc.vector.tensor_scalar(out=sel[qi][:], in0=gm[:], scalar1=top8[:, 2:3],
                                                             scalar2=None, op0=ALU.is_ge))
                    gm_t = t_sel
                    top_free = t_sel
                ncs = 1 + (i % 2)
                pe.wait(t_ld, pSo_free)
                ins = nc.tensor.matmul(pSo[:, 0:128], lhsT=mkT[hi][:, qs], rhs=mqT[hi][:, qs], start=True, stop=True)
                if ncs == 2:
                    ins = nc.tensor.matmul(pSo[:, 128:256], lhsT=mkT[hi][:, (i - 1) * 128:i * 128], rhs=mqT[hi][:, qs],
                                           start=True, stop=True)
                t_so = pe.done(ins)
                act.wait(t_so, eo_free[qi], PTo_free[qi])
                t_e = act.done(nc.scalar.activation(out=eo[qi][:], in_=pSo[:, 0:128], func=AF.Exp, scale=SC))
                if ncs == 2:
                    t_e2 = act.done(nc.scalar.activation(out=PTo[qi][:, 1, :], in_=pSo[:, 128:256], func=AF.Exp,
                                                         scale=SC))
                else:
                    t_e2 = t_e
                pSo_free = t_e2
                dve.wait(t_e, t_c)
                t_pd = dve.done(nc.vector.tensor_tensor(out=PTo[qi][:, 0, :], in0=eo[qi][:], in1=causalT[:],
                                                        op=ALU.mult))
                eo_free[qi] = t_pd
                ng = (nb + 1) // 2

                def emit_S(g):
                    nbg = min(2, nb - 2 * g)
                    si = cnt["pS"] % 2
                    cnt["pS"] += 1
                    pe.wait(pS_free[si])
                    for b in range(nbg):
                        n = 2 * g + b
                        for c2 in range(2):
                            ks = slice(n * 256 + c2 * 128, n * 256 + (c2 + 1) * 128)
                            ins_ = nc.tensor.matmul(pS[si][:, (b * 2 + c2) * 128:(b * 2 + c2 + 1) * 128],
                                                    lhsT=mkT[hi][:, ks], rhs=mqT[hi][:, qs], start=True, stop=True)
                    t_s = pe.done(ins_)
                    pj = cnt["PTg"] % 3
                    cnt["PTg"] += 1
                    act.wait(t_s, PTg_free[pj])
                    t_p = act.done(nc.scalar.activation(
                        out=PTg[pj][:, 0:2 * nbg, :].rearrange("p a b -> p (a b)"), in_=pS[si][:, 0:nbg * 256],
                        func=AF.Exp, scale=SC))
                    pS_free[si] = t_p
                    return (g, nbg, pj, t_p)

                pend = emit_S(0) if ng > 0 else None
                pe.wait(t_pd, t_e2, pOo_free)
                ins = nc.tensor.matmul(pOo[:, 0:129], lhsT=PTo[qi][:, 0, :], rhs=mv[hi][:, i, :], start=True,
                                       stop=(ncs == 1))
                if ncs == 2:
                    ins = nc.tensor.matmul(pOo[:, 0:129], lhsT=PTo[qi][:, 1, :], rhs=mv[hi][:, i - 1, :], start=False,
                                           stop=True)
                t_oo = pe.done(ins)
                PTo_free[qi] = t_oo
                dve.wait(t_oo, O_free[qi])
                t_O = dve.done(nc.vector.tensor_copy(out=O[qi][:, 0:129], in_=pOo[:, 0:129]))
                pOo_free = t_O
                for g in range(ng):
                    nxt = emit_S(g + 1) if g + 1 < ng else None
                    _, nbg, pj, t_p = pend
                    oi = cnt["pO2"] % 2
                    cnt["pO2"] += 1
                    pe.wait(t_p, pO2_free[oi])
                    for b in range(nbg):
                        n = 2 * g + b
                        for c2 in range(2):
                            ins = nc.tensor.matmul(pO2[oi][:, b * 132:b * 132 + 129], lhsT=PTg[pj][:, b * 2 + c2, :],
                                                   rhs=mv[hi][:, n * 2 + c2, :], start=(c2 == 0), stop=(c2 == 1))
                    t_pv = pe.done(ins)
                    PTg_free[pj] = t_pv
                    for b in range(nbg):
                        n = 2 * g + b
                        dve.wait(t_pv, t_O)
                        if use_sel:
                            t_O = dve.done(nc.vector.scalar_tensor_tensor(
                                out=O[qi][:, 0:129], in0=pO2[oi][:, b * 132:b * 132 + 129], scalar=sel[qi][:, n:n + 1],
                                in1=O[qi][:, 0:129], op0=ALU.mult, op1=ALU.add))
                        else:
                            t_O = dve.done(nc.vector.tensor_tensor(out=O[qi][:, 0:129], in0=O[qi][:, 0:129],
                                                                   in1=pO2[oi][:, b * 132:b * 132 + 129], op=ALU.add))
                    pO2_free[oi] = t_O
                    pend = nxt
                if use_sel:
                    sel_free[qi] = t_O
                dve.wait(t_O, rinv_free)
                t_r = dve.done(nc.vector.reciprocal(out=rinv[:, 0:1], in_=O[qi][:, 128:129]))
                dve.wait(t_r, mob_free[qi])
                t_m = dve.done(nc.vector.tensor_scalar(out=mob[qi][:], in0=O[qi][:, 0:128], scalar1=rinv[:, 0:1],
                                                       scalar2=None, op0=ALU.mult))
                rinv_free = t_m
                O_free[qi] = t_m
                pool.wait(t_m)
                mob_free[qi] = s_o[qi].done(nc.gpsimd.dma_start(
                    out=YATT[r0:r0 + 128, 1024 + h * 128:1024 + (h + 1) * 128], in_=mob[qi][:]))
            h_free[hi] = pe.last()
        drain(k, [x.last() for x in s_o])


def consts(ntok, pos0):
    pos = (pos0 + np.arange(ntok)).astype(np.float32)
    invf = (10000.0 ** (-np.arange(64, dtype=np.float32) / np.float32(64))).astype(np.float32)
    ang = (pos[:, None] * invf[None, :]).astype(np.float32)
    lg = np.log1p(-np.exp2(-5.0 - np.arange(NH, dtype=np.float64)))
    p = np.arange(128, dtype=np.float64)
    c = np.arange(128)
    return dict(
        cos=np.cos(ang).astype(np.float32), sin=np.sin(ang).astype(np.float32),
        ksc=(128.0 ** -0.5 * np.exp(-(p[:, None] + 1) * lg[None, :])).astype(np.float32),
        qdec=np.exp((p[:, None] + 1) * lg[None, :]).astype(np.float32),
        dec=np.tile(np.exp(128.0 * lg)[None, :], (128, 1)).astype(np.float32),
        causal=(c[None, :] >= c[:, None]).astype(np.float32),
    )


DRNN = 2816
RC = DRNN // 128
GK = 1.5957691216057308


def rnn_inproj_phase(k, X_in, w_in, wtick, GATE, RNNBR, ntok):
    nc = k.nc
    pe, act, dve, pool, sp = k.pe, k.act, k.dve, k.pool, k.sp
    ntiles = ntok // TILE
    with ExitStack() as es:
        stage = [k.sb(es, "n_stage%d" % i, [128, D], F32) for i in range(2)]
        xb = [k.sb(es, "n_xb%d" % i, [128, D], BF16) for i in range(4)]
        hT = k.sb(es, "n_hT", [128, KC, TILE], BF16)
        ws = [k.sb(es, "n_w%d" % i, [128, KC, 128], BF16) for i in range(3)]
        xf = [k.sb(es, "n_xf%d" % i, [128, TILE], F32) for i in range(3)]
        t1 = [k.sb(es, "n_t1%d" % i, [128, TILE], F32) for i in range(2)]
        gb = [k.sb(es, "n_gb%d" % i, [128, TILE], BF16) for i in range(2)]
        identb, t_id = k.identb, k.t_id
        pT = k.ps(es, "n_pT", [128, 512], BF16)
        acc = [k.ps(es, "n_acc%d" % i, [128, 512], F32) for i in range(4)]
        s_stage = [k.dsem(es, "nstage%d" % i) for i in range(2)]
        s_w = [k.dsem(es, "nw%d" % i) for i in range(3)]
        s_og = [k.dsem(es, "nog%d" % i) for i in range(2)]
        s_ox = [k.dsem(es, "nox%d" % i) for i in range(3)]
        stage_free = [None, None]
        xb_free = [None] * 4
        w_free = [None] * 3
        acc_free = [None] * 4
        xf_free = [None] * 3
        t1_free = [None, None]
        gb_free = [None, None]
        cnt = dict(stage=0, xb=0, w=0, acc=0, xf=0, t1=0, gb=0)
        state = dict(hT_free=None, pT_free=None)

        def emit_T_load(t, subs):
            res = []
            for s in subs:
                r0 = t * TILE + s * 128
                xi = cnt["xb"] % 4
                cnt["xb"] += 1
                si = cnt["stage"] % 2
                cnt["stage"] += 1
                sp.wait(stage_free[si])
                t_ld = s_stage[si].done(nc.sync.dma_start(out=stage[si][:], in_=X_in[r0:r0 + 128, :]))
                act.wait(t_ld, xb_free[xi])
                t_cc = act.done(nc.scalar.copy(out=xb[xi][:], in_=stage[si][:]))
                stage_free[si] = t_cc
                res.append((xi, t_cc))
            return res

        def emit_T(t, loads=None):
            if loads is None:
                loads = emit_T_load(t, (0, 1, 2, 3))
            for s, (xi, t_cc) in enumerate(loads):
                for q in range(4):
                    pe.wait(t_cc, state["pT_free"], t_id)
                    for j in range(4):
                        kc = 4 * q + j
                        ins = nc.tensor.transpose(pT[:, j * 128:(j + 1) * 128],
                                                  xb[xi][:, kc * 128:(kc + 1) * 128], identb[:])
                    t_tr = pe.done(ins)
                    ev = dve if (q % 2 == 0) else act
                    ev.wait(t_tr, state["hT_free"])
                    dst = hT[:, 4 * q:4 * q + 4, s * 128:(s + 1) * 128]
                    srcp = pT[:].rearrange("p (a b) -> p a b", b=128)
                    if ev is dve:
                        t_ev = dve.done(nc.vector.tensor_copy(out=dst, in_=srcp))
                    else:
                        t_ev = act.done(nc.scalar.copy(out=dst, in_=srcp))
                    state["pT_free"] = t_ev
                xb_free[xi] = t_tr
            return [dve.last(), act.last()]

        hT_ready = emit_T(0)
        for t in range(ntiles):
            sl = slice(t * TILE, (t + 1) * TILE)
            for oi_ in range(2 * RC):
                oc = (oi_ // 2) + (RC if oi_ % 2 else 0)
                if oi_ in (4, 10, 16, 22) and t + 1 < ntiles:
                    if oi_ == 4:
                        state["next_loads"] = []
                    state["next_loads"] += emit_T_load(t + 1, ((oi_ - 4) // 6,))
                wi = cnt["w"] % 3
                cnt["w"] += 1
                sp.wait(w_free[wi], wtick)
                t_w = s_w[wi].done(nc.sync.dma_start(out=ws[wi][:], in_=w_in[oc]))
                ai = cnt["acc"] % 4
                cnt["acc"] += 1
                pe.wait(t_w, hT_ready, acc_free[ai])
                for kc in range(KC):
                    ins = nc.tensor.matmul(acc[ai][:], lhsT=ws[wi][:, kc, :], rhs=hT[:, kc, :],
                                           start=(kc == 0), stop=(kc == KC - 1))
                t_m = pe.done(ins)
                w_free[wi] = t_m
                xi = cnt["xf"] % 3
                cnt["xf"] += 1
                act.wait(t_m, xf_free[xi])
                t_x = act.done(nc.scalar.copy(out=xf[xi][:], in_=acc[ai][:]))
                acc_free[ai] = t_x
                if oc >= RC:
                    pool.wait(t_x)
                    xf_free[xi] = s_ox[xi].done(nc.gpsimd.dma_start(out=RNNBR[oc - RC, :, sl], in_=xf[xi][:]))
                else:
                    ti = cnt["t1"] % 2
                    cnt["t1"] += 1
                    gi = cnt["gb"] % 2
                    cnt["gb"] += 1
                    dve.wait(t_x, t1_free[ti])
                    t_a = dve.done(nc.vector.tensor_tensor(out=t1[ti][:], in0=xf[xi][:], in1=xf[xi][:], op=ALU.mult))
                    dve.wait(t_a)
                    t_a = dve.done(nc.vector.tensor_scalar(out=t1[ti][:], in0=t1[ti][:], scalar1=0.044715, scalar2=1.0,
                                                           op0=ALU.mult, op1=ALU.add))
                    dve.wait(t_a)
                    t_a = dve.done(nc.vector.tensor_tensor(out=t1[ti][:], in0=t1[ti][:], in1=xf[xi][:], op=ALU.mult))
                    act.wait(t_a)
                    t_a = act.done(nc.scalar.activation(out=t1[ti][:], in_=t1[ti][:], func=AF.Sigmoid, scale=GK))
                    dve.wait(t_a, gb_free[gi])
                    t_g = dve.done(nc.vector.tensor_tensor(out=gb[gi][:], in0=t1[ti][:], in1=xf[xi][:], op=ALU.mult))
                    t1_free[ti] = t_g
                    xf_free[xi] = t_g
                    pool.wait(t_g)
                    gb_free[gi] = s_og[gi].done(nc.gpsimd.dma_start(out=GATE[oc, :, sl], in_=gb[gi][:]))
            state["hT_free"] = t_m
            if t + 1 < ntiles:
                hT_ready = emit_T(t + 1, state["next_loads"])
        drain(k, [x.last() for x in s_og + s_ox])


def prep_bd(k, es, w_src, name):
    nc = k.nc
    out = dram(nc, name, [DRNN, DRNN], BF16)
    z = k.sb(es, name + "_z", [128, DRNN], BF16)
    ds = k.dsem(es, name + "z")
    t0 = k.pool.done(nc.gpsimd.memset(z[:], 0.0))
    k.pool.wait(t0)
    t = None
    for c in range(RC):
        t = ds.done(nc.gpsimd.dma_start(out=out[c * 128:(c + 1) * 128, :], in_=z[:]))
    k.pool.wait(t)
    ds2 = k.dsem(es, name + "b")
    for g in range(16):
        t = ds2.done(nc.gpsimd.dma_start(out=out[g * 176:(g + 1) * 176, g * 176:(g + 1) * 176], in_=w_src[g]))
    return out, t


def rnn_core_phase(k, GATE, RNNBR, wa_bd, wx_bd, wtick, vecs, YRNN, ntok):
    nc = k.nc
    pe, act, dve, pool, sp = k.pe, k.act, k.dve, k.pool, k.sp
    ntiles = ntok // TILE
    with ExitStack() as es:
        wband = [k.sb(es, "c_wband%d" % i, [128, RC, 5, 128], BF16) for i in range(2)]
        vt = k.sb(es, "c_vt", [128, 9, RC], F32)
        cst = k.sb(es, "c_cst", [128, 6, RC], F32)
        hst = k.sb(es, "c_hst", [128, RC], F32)
        ones1 = k.sb(es, "c_ones", [128, 2], F32)
        NG = 4
        NX = 8
        xr = [k.sb(es, "c_xr%d" % i, [128, TILE + 4], F32) for i in range(NX)]
        u = k.sb(es, "c_u", [128, RC, TILE], F32)
        ub = k.sb(es, "c_ub", [128, RC, TILE], BF16)
        gt = [k.sb(es, "c_gt%d" % i, [128, TILE], BF16) for i in range(NG)]
        r_ = [k.sb(es, "c_r%d" % i, [128, TILE], F32) for i in range(NG)]
        i_ = [k.sb(es, "c_i%d" % i, [128, TILE], F32) for i in range(NG)]
        a_ = [k.sb(es, "c_a%d" % i, [128, TILE], F32) for i in range(NG)]
        b_ = [k.sb(es, "c_b%d" % i, [128, TILE], F32) for i in range(NG)]
        yb = [k.sb(es, "c_yb%d" % i, [128, TILE], BF16) for i in range(NG)]
        pR = [k.ps(es, "c_pR%d" % i, [128, 512], F32) for i in range(NG)]
        pI = [k.ps(es, "c_pI%d" % i, [128, 512], F32) for i in range(NG)]
        s_c = k.dsem(es, "cconst")
        s_x = [k.dsem(es, "cx%d" % i) for i in range(NX)]
        s_g = [k.dsem(es, "cg%d" % i) for i in range(NG)]
        s_o = [k.dsem(es, "co%d" % i) for i in range(NG)]

        sp.wait(wtick)
        for gi, wsrc in enumerate((wa_bd, wx_bd)):
            t0 = dve.done(nc.vector.memset(wband[gi][:], 0.0))
            sp.wait(t0)
            for c in range(RC):
                lo, hi = max(0, c - 2), min(RC - 1, c + 2)
                t_wb = s_c.done(nc.sync.dma_start(
                    out=wband[gi][:, c, lo - c + 2:hi - c + 3, :],
                    in_=wsrc[lo * 128:(hi + 1) * 128, c * 128:(c + 1) * 128].rearrange("(j p) f -> p j f", p=128)))
        t_v = s_c.done(nc.sync.dma_start(out=vt[:].rearrange("p v c -> p (v c)"), in_=vecs[:, :]))
        t_wb = t_v
        act.wait(t_v)
        dve.wait(t_v)
        t = dve.done(nc.vector.tensor_scalar(out=cst[:, 0:2, :], in0=vt[:, 5:7, :], scalar1=-1.0, scalar2=None,
                                             op0=ALU.mult))
        t_e = act.done(nc.scalar.activation(out=cst[:, 4, :], in_=vt[:, 7, :], func=AF.Exp, scale=-1.0))
        dve.wait(t_e)
        t = dve.done(nc.vector.tensor_scalar(out=cst[:, 4, :], in0=cst[:, 4, :], scalar1=1.0, scalar2=None,
                                             op0=ALU.add))
        act.wait(t)
        t_e = act.done(nc.scalar.activation(out=cst[:, 5, :], in_=cst[:, 4, :], func=AF.Ln))
        dve.wait(t_e)
        dve.done(nc.vector.tensor_scalar(out=cst[:, 2, :], in0=cst[:, 5, :], scalar1=-8.0, scalar2=None, op0=ALU.mult))
        dve.done(nc.vector.tensor_scalar(out=cst[:, 3, :], in0=cst[:, 5, :], scalar1=-16.0, scalar2=None,
                                         op0=ALU.mult))
        dve.done(nc.vector.memset(ones1[:], 1.0))
        t_cst = dve.done(nc.vector.memset(hst[:], 0.0))
        for xi in range(NX):
            t_cst = dve.done(nc.vector.memset(xr[xi][:, 0:4], 0.0))

        xr_free = [None] * NX
        gt_free = [None] * NG
        pR_free = [None] * NG
        pI_free = [None] * NG
        buf_free = [None] * NG
        yb_free = [None] * NG
        u_free = None
        cntx = 0
        groups = [list(range(g0, min(RC, g0 + NG))) for g0 in range(0, RC, NG)]
        for t in range(ntiles):
            t0 = t * TILE
            sl = slice(t0, t0 + TILE)
            t_u = None
            for grp in groups:
                xs_ = {}
                tk = {}
                for c in grp:
                    xi = cntx % NX
                    cntx += 1
                    xs_[c] = xi
                    sp.wait(xr_free[xi], t_cst)
                    if t == 0:
                        tk[c] = s_x[xi].done(nc.sync.dma_start(out=xr[xi][:, 4:4 + TILE], in_=RNNBR[c, :, 0:TILE]))
                    else:
                        tk[c] = s_x[xi].done(nc.sync.dma_start(out=xr[xi][:, 0:4 + TILE],
                                                               in_=RNNBR[c, :, t0 - 4:t0 + TILE]))
                for c in grp:
                    xi = xs_[c]
                    act.wait(tk[c], u_free, t_cst)
                    tk[c] = act.done(nc.scalar.activation(out=u[:, c, :], in_=xr[xi][:, 4:4 + TILE], func=AF.Identity,
                                                          scale=vt[:, 3, c:c + 1], bias=vt[:, 4, c:c + 1]))
                for j in range(3):
                    for c in grp:
                        xi = xs_[c]
                        dve.wait(tk[c])
                        tk[c] = dve.done(nc.vector.scalar_tensor_tensor(
                            out=u[:, c, :], in0=xr[xi][:, 1 + j:1 + j + TILE], scalar=vt[:, j, c:c + 1],
                            in1=u[:, c, :], op0=ALU.mult, op1=ALU.add))
                for c in grp:
                    xr_free[xs_[c]] = tk[c]
                    act.wait(tk[c])
                    t_u = act.done(nc.scalar.copy(out=ub[:, c, :], in_=u[:, c, :]))
            for grp in groups:
                T = {}
                for j, c in enumerate(grp):
                    sp.wait(gt_free[j])
                    T[("g", c)] = s_g[j].done(nc.sync.dma_start(out=gt[j][:], in_=GATE[c, :, sl]))
                for j, c in enumerate(grp):
                    lo, hi = max(0, c - 2), min(RC - 1, c + 2)
                    pe.wait(t_u, t_wb, pR_free[j], pI_free[j])
                    for cc in range(lo, hi + 1):
                        nc.tensor.matmul(pR[j][:], lhsT=wband[0][:, c, cc - c + 2, :], rhs=ub[:, cc, :],
                                         start=(cc == lo), stop=(cc == hi))
                    for cc in range(lo, hi + 1):
                        ins = nc.tensor.matmul(pI[j][:], lhsT=wband[1][:, c, cc - c + 2, :], rhs=ub[:, cc, :],
                                               start=(cc == lo), stop=(cc == hi))
                    T[("m", c)] = pe.done(ins)
                for j, c in enumerate(grp):
                    act.wait(T[("m", c)], buf_free[j], t_cst)
                    T[("r", c)] = act.done(nc.scalar.activation(out=r_[j][:], in_=pR[j][:], func=AF.Sigmoid,
                                                                bias=vt[:, 5, c:c + 1]))
                    T[("i2", c)] = act.done(nc.scalar.activation(out=i_[j][:], in_=pI[j][:], func=AF.Sigmoid,
                                                                 bias=vt[:, 6, c:c + 1]))
                    pR_free[j] = T[("r", c)]
                    pI_free[j] = T[("i2", c)]
                for j, c in enumerate(grp):
                    act.wait(T[("r", c)])
                    T[("a", c)] = act.done(nc.scalar.activation(out=a_[j][:], in_=r_[j][:], func=AF.Exp,
                                                                scale=cst[:, 2, c:c + 1]))
                    T[("a2", c)] = act.done(nc.scalar.activation(out=b_[j][:], in_=r_[j][:], func=AF.Exp,
                                                                 scale=cst[:, 3, c:c + 1]))
                for j, c in enumerate(grp):
                    act.wait(T[("a2", c)])
                    T[("ln", c)] = act.done(nc.scalar.activation(out=b_[j][:], in_=b_[j][:], func=AF.Ln, scale=-1.0,
                                                                 bias=ones1[:, 0:1]))
                for j, c in enumerate(grp):
                    act.wait(T[("ln", c)])
                    T[("sq", c)] = act.done(nc.scalar.activation(out=b_[j][:], in_=b_[j][:], func=AF.Exp, scale=0.5))
                for j, c in enumerate(grp):
                    dve.wait(T[("i2", c)])
                    T[("iu", c)] = dve.done(nc.vector.tensor_tensor(out=i_[j][:], in0=i_[j][:], in1=u[:, c, :],
                                                                    op=ALU.mult))
                for j, c in enumerate(grp):
                    dve.wait(T[("iu", c)], T[("sq", c)])
                    T[("b", c)] = dve.done(nc.vector.tensor_tensor(out=b_[j][:], in0=b_[j][:], in1=i_[j][:],
                                                                   op=ALU.mult))
                for j, c in enumerate(grp):
                    dve.wait(T[("b", c)], T[("a", c)], T[("a2", c)], t_cst)
                    T[("sc", c)] = dve.done(nc.vector.tensor_tensor_scan(
                        out=r_[j][:], data0=a_[j][:], data1=b_[j][:], initial=hst[:, c:c + 1],
                        op0=ALU.mult, op1=ALU.add))
                for j, c in enumerate(grp):
                    dve.wait(T[("sc", c)])
                    T[("h", c)] = dve.done(nc.vector.tensor_copy(out=hst[:, c:c + 1], in_=r_[j][:, TILE - 1:TILE]))
                for j, c in enumerate(grp):
                    dve.wait(T[("sc", c)], T[("g", c)], yb_free[j])
                    T[("y", c)] = dve.done(nc.vector.tensor_tensor(out=yb[j][:], in0=r_[j][:], in1=gt[j][:],
                                                                   op=ALU.mult))
                    gt_free[j] = T[("y", c)]
                    buf_free[j] = [T[("y", c)], T[("h", c)]]
                for j, c in enumerate(grp):
                    pool.wait(T[("y", c)])
                    yb_free[j] = s_o[j].done(nc.gpsimd.dma_start(out=YRNN[c, :, sl], in_=yb[j][:]))
            u_free = [pe.last(), dve.last()]
        drain(k, [x.last() for x in s_o])


BETA_UNUSED = None
IN_SPECS = [
    ("x", None), ("ln_g", [2, 3, D]), ("ln_b", [2, 3, D]),
    ("ffn_w_gate", [2, 2, D, DFF]), ("ffn_w_up", [2, 2, D, DFF]), ("ffn_w_down", [2, 2, DFF, D]),
    ("attn_w_in", [1, D, 7168]), ("ret_gn_g", [1, 1024]), ("ret_gn_b", [1, 1024]), ("attn_w_out", [1, D, D]),
    ("rnn_w_in", [1, D, 2 * DRNN]), ("rnn_gate_a_w", [1, 16, 176, 176]), ("rnn_gate_x_w", [1, 16, 176, 176]),
    ("rnn_w_out", [1, DRNN, D]),
    ("c_cos", None), ("c_sin", None), ("c_ksc", [128, NH]), ("c_qdec", [128, NH]), ("c_dec", [128, NH]),
    ("c_causal", [128, 128]), ("c_vecs", [128, 9 * RC]),
]


_DBG = False


def build_program(ntok, upto=99):
    nc = bass.Bass("TRN2", target_bir_lowering=False)
    I = {}
    for name, shape in IN_SPECS:
        if name == "x":
            shape = [ntok, D]
        elif name in ("c_cos", "c_sin"):
            shape = [ntok, 64]
        I[name] = dram(nc, name, shape, F32, "ExternalInput")
    out = dram(nc, "out", [ntok, D], F32, "ExternalOutput")
    dk = "ExternalOutput" if _DBG else "Internal"
    XA = dram(nc, "XA", [ntok, D], F32, dk)
    XB = dram(nc, "XB", [ntok, D], F32, dk)
    with ExitStack() as es:
        k = K(nc, es)

        def prep_ffn(l, j):
            a, t1 = prep_ws(k, es, I["ffn_w_gate"][l, j], "wg%d%d" % (l, j), D, DFF)
            b, t2 = prep_ws(k, es, I["ffn_w_up"][l, j], "wu%d%d" % (l, j), D, DFF)
            c, t3 = prep_ws(k, es, I["ffn_w_down"][l, j], "wd%d%d" % (l, j), DFF, D)
            return a, b, c, [t1, t2, t3]

        def ffn(l, j, w, src, dst):
            sublayer_phase(k, "ffn", src, dst, w[2], FC, w[3], I["ln_g"][l, 2 * j], I["ln_b"][l, 2 * j], ntok,
                           2.0 * ALPHA, 4.0 * LN_EPS, wg=w[0], wu=w[1])

        w00 = prep_ffn(0, 0)
        ffn(0, 0, w00, I["x"], XA if upto > 0 else out)
        if upto <= 0:
            return nc
        win, t_win = prep_tm(k, es, I["attn_w_in"][0], "awin", D, 7168)
        wout, t_wout = prep_ws(k, es, I["attn_w_out"][0], "awout", D, D)
        w01 = prep_ffn(0, 1)
        sc = [dram(nc, "QT", [NH, 128, ntok], BF16), dram(nc, "KT", [NH, 128, ntok], BF16),
              dram(nc, "KTM", [ntok, 1024], BF16), dram(nc, "VTM", [ntok, 1024], BF16),
              dram(nc, "GTM", [ntok, 1024], BF16), dram(nc, "MQT", [NH, 128, ntok], BF16),
              dram(nc, "MKT", [NH, 128, ntok], BF16), dram(nc, "MVTM", [ntok, 1024], BF16),
              dram(nc, "KMEAN", [ntok // 256, 1024], F32)]
        YATT = dram(nc, "YATT", [ntok, 2048], BF16)
        attn_inproj_phase(k, XA, win, [t_win], I["c_cos"], I["c_sin"], I["c_ksc"], sc, ntok)
        QT, KT, KTM, VTM, GTM, MQT, MKT, MVTM, KMEAN = sc
        retention_phase(k, QT, KT, KTM, VTM, GTM, I["ret_gn_g"][0], I["ret_gn_b"][0], I["c_causal"], I["c_dec"],
                        I["c_qdec"], YATT, ntok)
        moba_phase(k, MQT, MKT, MVTM, KMEAN, I["c_causal"], YATT, ntok)
        sublayer_phase(k, "proj_tm", XA, XB if upto > 1 else out, wout, KC, [t_wout], I["ln_g"][0, 1], I["ln_b"][0, 1],
                       ntok, ALPHA, LN_EPS, Y=YATT)
        if upto <= 1:
            return nc
        w10 = prep_ffn(1, 0)
        ffn(0, 1, w01, XB, XA)
        rin, t_rin = prep_ws(k, es, I["rnn_w_in"][0], "rwin", D, 2 * DRNN)
        wa, t_wa = prep_bd(k, es, I["rnn_gate_a_w"][0], "rwa")
        wx, t_wx = prep_bd(k, es, I["rnn_gate_x_w"][0], "rwx")
        rout, t_rout = prep_ws(k, es, I["rnn_w_out"][0], "rwout", DRNN, D)
        w11 = prep_ffn(1, 1)
        ffn(1, 0, w10, XA, XB)
        GATE = dram(nc, "GATE", [RC, 128, ntok], BF16)
        RNNBR = dram(nc, "RNNBR", [RC, 128, ntok], F32)
        YRNN = dram(nc, "YRNN", [RC, 128, ntok], BF16, dk)
        rnn_inproj_phase(k, XB, rin, [t_rin], GATE, RNNBR, ntok)
        rnn_core_phase(k, GATE, RNNBR, wa, wx, [t_wa, t_wx], I["c_vecs"], YRNN, ntok)
        sublayer_phase(k, "proj_fm", XB, XA if upto > 2 else out, rout, RC, [t_rout], I["ln_g"][1, 1], I["ln_b"][1, 1],
                       ntok, ALPHA, LN_EPS, Y=YRNN)
        if upto <= 2:
            return nc
        ffn(1, 1, w11, XA, out)
    return nc


_PROG = {}
_LAST = {}
SPREAD = True


def kernel(x, ln_g, ln_b, ffn_w_gate, ffn_w_up, ffn_w_down, attn_w_in, ret_gn_g, ret_gn_b, attn_w_out,
           rnn_w_in, rnn_conv_w, rnn_conv_b, rnn_gate_a_w, rnn_gate_a_b, rnn_gate_x_w, rnn_gate_x_b,
           rnn_lambda, rnn_w_out):
    f32 = lambda a: np.ascontiguousarray(np.asarray(a, dtype=np.float32))
    x = f32(x)
    B, S, _ = x.shape
    if S not in _PROG:
        _PROG[S] = build_program(S)
    nc = _PROG[S]
    C = consts(S, 0)
    vecs = np.zeros((9, DRNN), np.float32)
    vecs[0:4] = f32(rnn_conv_w)[0]
    vecs[4] = f32(rnn_conv_b)[0]
    vecs[5] = f32(rnn_gate_a_b)[0]
    vecs[6] = f32(rnn_gate_x_b)[0]
    vecs[7] = f32(rnn_lambda)[0]
    shared = dict(ln_g=f32(ln_g), ln_b=f32(ln_b), ffn_w_gate=f32(ffn_w_gate), ffn_w_up=f32(ffn_w_up),
                  ffn_w_down=f32(ffn_w_down), attn_w_in=f32(attn_w_in), ret_gn_g=f32(ret_gn_g),
                  ret_gn_b=f32(ret_gn_b), attn_w_out=f32(attn_w_out), rnn_w_in=f32(rnn_w_in),
                  rnn_gate_a_w=f32(rnn_gate_a_w), rnn_gate_x_w=f32(rnn_gate_x_w), rnn_w_out=f32(rnn_w_out),
                  c_cos=C["cos"], c_sin=C["sin"], c_ksc=C["ksc"], c_qdec=C["qdec"], c_dec=C["dec"],
                  c_causal=C["causal"],
                  c_vecs=np.ascontiguousarray(vecs.reshape(9, RC, 128).transpose(2, 0, 1).reshape(128, 9 * RC)))
    if B == 4 and SPREAD:
        act_cores = [0, 1, 4, 5]
        zeros = {n: np.zeros_like(v) for n, v in shared.items()}
        zeros["x"] = np.zeros_like(x[0])
        in_maps = [zeros] * NCORES
        in_maps = list(in_maps)
        for b, c in enumerate(act_cores):
            in_maps[c] = dict(shared, x=x[b])
        res = run_bass_kernel_spmd(nc, in_maps, core_ids=list(range(NCORES)))
        outs = [res.results[c] for c in act_cores]
    else:
        in_maps = [dict(shared, x=x[b]) for b in range(B)]
        res = run_bass_kernel_spmd(nc, in_maps, core_ids=list(range(B)))
        outs = res.results
    _LAST["r"] = outs[0]
    return np.stack([np.asarray(r["out"], dtype=np.float32) for r in outs], axis=0)
```
